# Optimizing a Trainium2 kernel written in Bass

```python
import jax, jax.numpy as jnp
from jax import lax
import numpy as np

D_MODEL = 2048
BATCH = 2
SEQ = 8192
DEPTH = 4

N_MIXERS = 2
N_CONV_LAYERS = (DEPTH + 1) // 2
N_GLA_LAYERS = DEPTH // 2
CONV_WIDTH = 31
GLA_HEADS = 4
GLA_DK = D_MODEL // 2
GLA_DV = D_MODEL
GLA_HEAD_K = GLA_DK // GLA_HEADS
GLA_HEAD_V = GLA_DV // GLA_HEADS
GLA_GATE_RANK = 16
GLA_GATE_TAU = 16.0
GLA_CHUNK = 64
GLA_IN_WIDTH = 2 * GLA_DK + 2 * GLA_DV + GLA_GATE_RANK
D_FF = -(-8 * D_MODEL // (3 * 256)) * 256
N_MOD = 6
EPS = 1e-6

kernel_name = "hybrid_conformerconv_gla_sandwich_adaln"


def rms_norm(x, gain):
    xf = x.astype(jnp.float32)
    y = xf * lax.rsqrt(jnp.mean(xf * xf, axis=-1, keepdims=True) + EPS)
    return (y * gain.astype(jnp.float32)).astype(x.dtype)


def layer_norm(x, gain, bias):
    xf = x.astype(jnp.float32)
    mu = jnp.mean(xf, axis=-1, keepdims=True)
    xc = xf - mu
    y = xc * lax.rsqrt(jnp.mean(xc * xc, axis=-1, keepdims=True) + EPS)
    return (y * gain.astype(jnp.float32) + bias.astype(jnp.float32)).astype(x.dtype)


def conformer_conv(h, w_pw1, b_pw1, w_dw, b_dw, ln_g, ln_b, w_pw2, b_pw2):
    u = h @ w_pw1 + b_pw1
    a, g = jnp.split(u, 2, axis=-1)
    u = a * jax.nn.sigmoid(g)
    u = lax.conv_general_dilated(
        u, w_dw[:, None, :], window_strides=(1,),
        padding=((CONV_WIDTH - 1, 0),),
        dimension_numbers=("NWC", "WIO", "NWC"),
        feature_group_count=D_MODEL) + b_dw
    u = jax.nn.silu(layer_norm(u, ln_g, ln_b))
    return u @ w_pw2 + b_pw2


def gla_mixer(h, w_in, w_gate_up, b_gate, norm_g, w_out):
    bsz, L, _ = h.shape
    proj = h @ w_in
    q, k, v, r, a = jnp.split(
        proj, [GLA_DK, 2 * GLA_DK, 2 * GLA_DK + GLA_DV, 2 * GLA_DK + 2 * GLA_DV], axis=-1)
    g = jax.nn.log_sigmoid((a @ w_gate_up + b_gate).astype(jnp.float32)) / GLA_GATE_TAU
    nc = L // GLA_CHUNK

    def to_chunks(t, dh):
        t = t.astype(jnp.float32).reshape(bsz, nc, GLA_CHUNK, GLA_HEADS, dh)
        return t.transpose(1, 0, 3, 2, 4)

    qc = to_chunks(q * (GLA_HEAD_K ** -0.5), GLA_HEAD_K)
    kc = to_chunks(k, GLA_HEAD_K)
    vc = to_chunks(v, GLA_HEAD_V)
    gc = to_chunks(g, GLA_HEAD_K)
    causal = jnp.tril(jnp.ones((GLA_CHUNK, GLA_CHUNK), dtype=bool))[:, :, None]

    def step(S, inp):
        qb, kb, vb, gb = inp
        b = jnp.cumsum(gb, axis=2)
        o_inter = jnp.einsum("bhcd,bhde->bhce", qb * jnp.exp(b), S)
        diff = jnp.where(causal, b[:, :, :, None, :] - b[:, :, None, :, :], -jnp.inf)
        scores = jnp.einsum("bhid,bhjd,bhijd->bhij", qb, kb, jnp.exp(diff))
        o_intra = jnp.einsum("bhij,bhje->bhie", scores, vb)
        b_last = b[:, :, -1:, :]
        S_new = S * jnp.exp(b_last[:, :, 0, :])[..., None] + jnp.einsum(
            "bhcd,bhce->bhde", kb * jnp.exp(b_last - b), vb)
        return S_new, o_inter + o_intra

    S0 = jnp.zeros((bsz, GLA_HEADS, GLA_HEAD_K, GLA_HEAD_V), jnp.float32)
    _, o = lax.scan(step, S0, (qc, kc, vc, gc))
    o = o.transpose(1, 0, 3, 2, 4).reshape(bsz, L, GLA_HEADS, GLA_HEAD_V)
    o = rms_norm(o, norm_g.reshape(GLA_HEADS, GLA_HEAD_V))
    o = o.reshape(bsz, L, GLA_DV).astype(h.dtype) * jax.nn.silu(r)
    return o @ w_out


def swiglu_ffn(h, w_in, w_out):
    gate, up = jnp.split(h @ w_in, 2, axis=-1)
    return (jax.nn.silu(gate) * up) @ w_out


def setup_inputs(seed: int = 0) -> dict:
    key = jax.random.key(seed)
    ks = iter(jax.random.split(key, 32))
    D = D_MODEL

    def nrm(shape, scale):
        return jax.random.normal(next(ks), shape, jnp.float32) * scale

    def gain(shape):
        return 1.0 + nrm(shape, 0.02)

    return {
        "x": nrm((BATCH, SEQ, D), 1.0),
        "c": nrm((BATCH, D), 1.0),
        "w_ada": nrm((DEPTH, D, N_MOD * D), D ** -0.5),
        "b_ada": nrm((DEPTH, N_MOD * D), 0.02),
        "pre_mix_g": gain((DEPTH, D)),
        "post_mix_g": gain((DEPTH, D)),
        "pre_ffn_g": gain((DEPTH, D)),
        "post_ffn_g": gain((DEPTH, D)),
        "conv_w_pw1": nrm((N_CONV_LAYERS, D, 2 * D), D ** -0.5),
        "conv_b_pw1": nrm((N_CONV_LAYERS, 2 * D), 0.02),
        "conv_w_dw": nrm((N_CONV_LAYERS, CONV_WIDTH, D), CONV_WIDTH ** -0.5),
        "conv_b_dw": nrm((N_CONV_LAYERS, D), 0.02),
        "conv_ln_g": gain((N_CONV_LAYERS, D)),
        "conv_ln_b": nrm((N_CONV_LAYERS, D), 0.02),
        "conv_w_pw2": nrm((N_CONV_LAYERS, D, D), D ** -0.5),
        "conv_b_pw2": nrm((N_CONV_LAYERS, D), 0.02),
        "gla_w_in": nrm((N_GLA_LAYERS, D, GLA_IN_WIDTH), D ** -0.5),
        "gla_w_gate_up": nrm((N_GLA_LAYERS, GLA_GATE_RANK, GLA_DK), GLA_GATE_RANK ** -0.5),
        "gla_b_gate": nrm((N_GLA_LAYERS, GLA_DK), 0.02),
        "gla_norm_g": gain((N_GLA_LAYERS, GLA_DV)),
        "gla_w_out": nrm((N_GLA_LAYERS, GLA_DV, D), GLA_DV ** -0.5),
        "ffn_w_in": nrm((DEPTH, D, 2 * D_FF), D ** -0.5),
        "ffn_w_out": nrm((DEPTH, D_FF, D), D_FF ** -0.5),
    }


def reference(x, c, w_ada, b_ada, pre_mix_g, post_mix_g, pre_ffn_g, post_ffn_g,
              conv_w_pw1, conv_b_pw1, conv_w_dw, conv_b_dw, conv_ln_g, conv_ln_b,
              conv_w_pw2, conv_b_pw2,
              gla_w_in, gla_w_gate_up, gla_b_gate, gla_norm_g, gla_w_out,
              ffn_w_in, ffn_w_out):
    c_act = jax.nn.silu(c)
    for i in range(DEPTH):
        mod = c_act @ w_ada[i] + b_ada[i]
        sh1, sc1, gt1, sh2, sc2, gt2 = jnp.split(mod[:, None, :], N_MOD, axis=-1)

        h = rms_norm(x, pre_mix_g[i]) * (1.0 + sc1) + sh1
        j = i // N_MIXERS
        if i % N_MIXERS == 0:
            y = conformer_conv(h, conv_w_pw1[j], conv_b_pw1[j], conv_w_dw[j], conv_b_dw[j],
                               conv_ln_g[j], conv_ln_b[j], conv_w_pw2[j], conv_b_pw2[j])
        else:
            y = gla_mixer(h, gla_w_in[j], gla_w_gate_up[j], gla_b_gate[j],
                          gla_norm_g[j], gla_w_out[j])
        x = x + gt1 * rms_norm(y, post_mix_g[i])

        h = rms_norm(x, pre_ffn_g[i]) * (1.0 + sc2) + sh2
        y = swiglu_ffn(h, ffn_w_in[i], ffn_w_out[i])
        x = x + gt2 * rms_norm(y, post_ffn_g[i])
    return x
```

```python
import contextlib
import numpy as np
import concourse.bass as bass
import concourse.mybir as mybir
from concourse.bass_utils import run_bass_kernel_spmd

F32 = mybir.dt.float32
BF16 = mybir.dt.bfloat16
AF = mybir.ActivationFunctionType
ALU = mybir.AluOpType

D = 2048
KC = 16
SEQ = 8192
NCORE = 8
TOK = 2048
TH = 1024
DFF = 5632
JC = 44
DEPTH = 4
CW = 31
HALO = 32
EPS = 1e-6
DK = 1024
DV = 2048
NH = 4
GIN = 6160
NWB = 4


class Sched:
    ENG = ["pe", "act", "dve", "pool", "sp"]

    def __init__(self, nc):
        self.nc = nc
        self.ops = {e: [] for e in self.ENG}
        self.cnt = {e: 0 for e in self.ENG}
        self.seen = {e: {} for e in self.ENG}
        self.last_w = {}
        self.readers = {}
        self.dma_cnt = {}
        self.dma_keys = []

    def _add(self, eng, fn, reads, writes, tok_kind, dma_key=None):
        deps = []
        for r in reads:
            t = self.last_w.get(r)
            if t is not None:
                deps.append(t)
        for w in writes:
            t = self.last_w.get(w)
            if t is not None:
                deps.append(t)
            deps.extend(self.readers.get(w, ()))
        if tok_kind == "eng":
            self.cnt[eng] += 1
            tok = ("eng", eng, self.cnt[eng])
        else:
            if dma_key not in self.dma_cnt:
                self.dma_cnt[dma_key] = 0
                self.dma_keys.append(dma_key)
            self.dma_cnt[dma_key] += 16
            tok = ("dma", dma_key, self.dma_cnt[dma_key])
        need = {}
        for t in deps:
            if t[0] == "eng" and t[1] == eng and tok_kind == "eng" and eng == "pe":
                continue
            k = (t[0], t[1])
            if t[2] > need.get(k, 0):
                need[k] = t[2]
        waits = []
        seen = self.seen[eng]
        for k, v in need.items():
            if seen.get(k, 0) >= v:
                continue
            seen[k] = v
            waits.append((k, v))
        self.ops[eng].append((waits, fn, tok))
        for r in reads:
            self.readers.setdefault(r, []).append(tok)
        for w in writes:
            self.last_w[w] = tok
            self.readers[w] = []
        return tok

    def op(self, eng, fn, reads=(), writes=()):
        return self._add(eng, fn, reads, writes, "eng")

    def dma(self, fn, reads=(), writes=(), key=None, eng="sp"):
        return self._add(eng, fn, reads, writes, "dma", dma_key=key)

    def emit(self, final_wait_tokens=()):
        nc = self.nc
        with contextlib.ExitStack() as es:
            sems = {}
            for e in self.ENG:
                sems[("eng", e)] = es.enter_context(nc.semaphore("s_" + e))
            for i, k in enumerate(self.dma_keys):
                sems[("dma", k)] = es.enter_context(nc.semaphore("d%d" % i))
            block = es.enter_context(nc.Block())
            eng_map = {"pe": block.tensor, "act": block.scalar, "dve": block.vector,
                       "pool": block.gpsimd, "sp": block.sync}
            for e in self.ENG:
                ops = self.ops[e]
                extra = final_wait_tokens if e == "sp" else ()

                def body(eh, ops=ops, e=e, extra=extra):
                    for waits, fn, tok in ops:
                        for k, v in waits:
                            eh.wait_ge(sems[k], v)
                        ins = fn(eh)
                        if tok[0] == "eng":
                            ins.then_inc(sems[("eng", e)], 1)
                        else:
                            ins.then_inc(sems[("dma", tok[1])], 16)
                    for t in extra:
                        eh.wait_ge(sems[(t[0], t[1])], t[2])
                eng_map[e](body)


class Region:
    def __init__(self, nc, es, name, nbytes):
        self.name = name
        self.nbytes = nbytes
        self.t = es.enter_context(nc.sbuf_tensor(name, [128, nbytes // 4], F32))

    def view(self, off, dtype, shape):
        esz = 4 if dtype == F32 else 2
        n = 1
        for x in shape:
            n *= x
        assert off % 4 == 0 and (n * esz) % 4 == 0 and off + n * esz <= self.nbytes, (self.name, off, n, esz)
        a = self.t[:, off // 4:(off + n * esz) // 4]
        if dtype != F32:
            a = a.bitcast(dtype)
        if len(shape) == 2:
            a = a.rearrange("p (a b) -> p a b", a=shape[0])
        keys = [(self.name, pg) for pg in range(off // 1024, (off + n * esz + 1023) // 1024)]
        return a, keys


class Builder:
    def __init__(self, nc, es, dram):
        self.nc = nc
        self.es = es
        self.dr = dram
        self.s = Sched(nc)
        s = self.s
        self.R1 = Region(nc, es, "R1", 64 * 1024)
        self.R2 = Region(nc, es, "R2", 88 * 1024)
        self.R3 = Region(nc, es, "R3", 24 * 1024)
        self.wb = [es.enter_context(nc.sbuf_tensor("wb%d" % i, [128, 2048], BF16)) for i in range(NWB)]
        self.wi = 0
        self.P = [es.enter_context(nc.psum_tensor("P%d" % i, [128, 1024], F32)) for i in range(4)]
        self.ones = es.enter_context(nc.sbuf_tensor("ones", [128, 128], BF16))
        self.one32 = es.enter_context(nc.sbuf_tensor("one32", [128, 1], F32))
        self.vecs = es.enter_context(nc.sbuf_tensor("vecs_sb", [128, self.dr["vecs"].shape[1]], F32))
        self.cact = es.enter_context(nc.sbuf_tensor("cact", [128, 16], BF16))
        self.modrow = self.R3.t[0:1, 4096:6144]
        self.modc = es.enter_context(nc.sbuf_tensor("modc", [128, DEPTH * 96], F32))
        self.lay = es.enter_context(nc.sbuf_tensor("lay", [128, DEPTH * 96], F32))
        s.op("dve", lambda e: e.memset(self.ones[:], 1.0), writes=["ones"])
        s.op("dve", lambda e: e.memset(self.one32[:], 1.0), writes=["one32"])
        s.dma(lambda q: q.dma_start(out=self.vecs[:], in_=self.dr["vecs"]), writes=["vecs"], key="vecs")
        self.pending = []
        self.ident = es.enter_context(nc.sbuf_tensor("ident", [128, 128], BF16))
        self.tri4 = es.enter_context(nc.sbuf_tensor("tri4", [128, 512], F32))
        s.dma(lambda q: q.dma_start(out=self.ident[:], in_=self.dr["consts"][:, 0:128]), writes=["ident"],
              key="ident", eng="pool")
        s.dma(lambda q: q.dma_start(out=self.tri4[:], in_=self.dr["consts"][:, 128:640]), writes=["tri4"], key="tri4")
        self.tribf = es.enter_context(nc.sbuf_tensor("tribf", [128, 128], BF16))
        s.dma(lambda q: q.dma_start(out=self.tribf[:], in_=self.dr["consts"][:, 128:256]), writes=["tribf"],
              key="tribf", eng="pool")

    def pk(self, i, tt=None):
        if tt is None:
            return [("P", i, 0), ("P", i, 1)]
        return [("P", i, tt)]

    def load_w(self, src_ap, ncols):
        i = self.wi % NWB
        self.wi += 1
        t = self.wb[i]
        key = ("wb", i)
        self.s.dma(lambda q: q.dma_start(out=t[:, 0:ncols], in_=src_ap), writes=[key], key=key, eng="pool")
        return t, key

    def flush_pending(self):
        p = self.pending
        self.pending = []
        for f in p:
            f()

    def stats_mm(self, Pi, src_ap, src_keys, first, last):
        def f():
            def emit(e):
                ins = None
                for tt in range(2):
                    ins = e.matmul(self.P[Pi][:, tt * 512:(tt + 1) * 512], lhsT=self.ones[:],
                                   rhs=src_ap[:, tt * 512:(tt + 1) * 512], start=first, stop=last)
                return ins
            self.s.op("pe", emit, reads=list(src_keys) + ["ones"], writes=self.pk(Pi))
        self.pending.append(f)

    def vcol(self, name, c):
        o = self.dr["vec_off"][name]
        return self.vecs[:, o + c:o + c + 1]

    def prologue_mod(self, layers):
        s = self.s
        nc = self.nc
        dr = self.dr
        co = dr["vec_off"]["c"]
        s.op("act", lambda e: e.activation(out=self.cact[:], in_=self.vecs[:, co:co + 16], func=AF.Silu),
             reads=["vecs"], writes=["cact"])
        for i in layers:
            for nb in range(6):
                for kc in range(KC):
                    t, key = self.load_w(dr["w_ada%d" % i][kc * 128:(kc + 1) * 128, nb * 2048:(nb + 1) * 2048], 2048)

                    def emit(e, t=t, kc=kc):
                        ins = None
                        for q in range(4):
                            ins = e.matmul(self.P[q // 2][0:1, (q % 2) * 512:(q % 2 + 1) * 512],
                                           lhsT=self.cact[:, kc:kc + 1], rhs=t[:, q * 512:(q + 1) * 512],
                                           start=(kc == 0), stop=(kc == KC - 1))
                        return ins
                    s.op("pe", emit, reads=[key, "cact"], writes=self.pk(0) + self.pk(1))
                s.op("act", lambda e: e.activation(out=self.modrow[0:1, 0:1024], in_=self.P[0][0:1, :], func=AF.Identity),
                     reads=self.pk(0), writes=[("R3", pg) for pg in range(16, 20)])
                s.op("act", lambda e: e.activation(out=self.modrow[0:1, 1024:2048], in_=self.P[1][0:1, :], func=AF.Identity),
                     reads=self.pk(1), writes=[("R3", pg) for pg in range(20, 24)])

                def emit_t(e):
                    ins = None
                    for j in range(16):
                        ins = e.matmul(self.P[2][:, j:j + 1], lhsT=self.modrow[0:1, j * 128:(j + 1) * 128],
                                       rhs=self.one32[0:1, 0:1], start=True, stop=True)
                    return ins
                s.op("pe", emit_t, reads=[("R3", pg) for pg in range(16, 24)] + ["one32"], writes=self.pk(2))
                bo = dr["vec_off"]["b_ada%d" % i] + nb * 16
                s.op("dve", lambda e, i=i, nb=nb, bo=bo: e.tensor_tensor(
                    out=self.modc[:, i * 96 + nb * 16:i * 96 + nb * 16 + 16], in0=self.P[2][:, 0:16],
                    in1=self.vecs[:, bo:bo + 16], op=ALU.add),
                    reads=self.pk(2) + ["vecs"], writes=[("modc", i, nb)])
            m = lambda nb, i=i: self.modc[:, i * 96 + nb * 16:i * 96 + nb * 16 + 16]
            L = lambda k, i=i: self.lay[:, i * 96 + k * 16:i * 96 + k * 16 + 16]
            vo = dr["vec_off"]
            g = lambda nm, i=i, vo=vo: self.vecs[:, vo[nm % i]:vo[nm % i] + 16]
            rd = [("modc", i, nb) for nb in range(6)] + ["vecs"]
            wr = [("lay", i)]
            s.op("dve", lambda e, m=m, L=L, g=g: e.scalar_tensor_tensor(
                out=L(0), in0=m(1), scalar=1.0, in1=g("pre_mix_g%d"), op0=ALU.add, op1=ALU.mult), reads=rd, writes=wr)
            s.op("dve", lambda e, m=m, L=L: e.tensor_copy(out=L(1), in_=m(0)), reads=rd, writes=wr)
            s.op("dve", lambda e, m=m, L=L, g=g: e.tensor_tensor(
                out=L(2), in0=m(2), in1=g("post_mix_g%d"), op=ALU.mult), reads=rd, writes=wr)
            s.op("dve", lambda e, m=m, L=L, g=g: e.scalar_tensor_tensor(
                out=L(3), in0=m(4), scalar=1.0, in1=g("pre_ffn_g%d"), op0=ALU.add, op1=ALU.mult), reads=rd, writes=wr)
            s.op("dve", lambda e, m=m, L=L: e.tensor_copy(out=L(4), in_=m(3)), reads=rd, writes=wr)
            s.op("dve", lambda e, m=m, L=L, g=g: e.tensor_tensor(
                out=L(5), in0=m(5), in1=g("post_ffn_g%d"), op=ALU.mult), reads=rd, writes=wr)

    def mod_units(self, i):
        s = self.s
        dr = self.dr
        units = []
        for nb in range(6):
            for kc in range(KC):
                def unit(nb=nb, kc=kc):
                    t, key = self.load_w(dr["w_ada%d" % i][kc * 128:(kc + 1) * 128, nb * 2048:(nb + 1) * 2048], 2048)

                    def emit(e):
                        ins = None
                        for si in range(16):
                            ins = e.matmul(self.P[2][:, si:si + 1], lhsT=t[:, si * 128:(si + 1) * 128],
                                           rhs=self.cact[:, kc:kc + 1], start=(kc == 0 and si == 0),
                                           stop=(kc == KC - 1), skip_group_check=True)
                        return ins
                    s.op("pe", emit, reads=[key, "cact"], writes=self.pk(2, 0))
                    if kc == KC - 1:
                        bo = dr["vec_off"]["b_ada%d" % i] + nb * 16
                        s.op("dve", lambda e: e.tensor_tensor(
                            out=self.modc[:, i * 96 + nb * 16:i * 96 + nb * 16 + 16], in0=self.P[2][:, 0:16],
                            in1=self.vecs[:, bo:bo + 16], op=ALU.add),
                            reads=self.pk(2, 0) + ["vecs"], writes=[("modc", i, nb)])
                units.append(unit)
        units.append(lambda: self.mod_derive(i))
        return units

    def mod_derive(self, i):
        s = self.s
        dr = self.dr
        m = lambda nb, i=i: self.modc[:, i * 96 + nb * 16:i * 96 + nb * 16 + 16]
        L = lambda k, i=i: self.lay[:, i * 96 + k * 16:i * 96 + k * 16 + 16]
        vo = dr["vec_off"]
        g = lambda nm, i=i, vo=vo: self.vecs[:, vo[nm % i]:vo[nm % i] + 16]
        rd = [("modc", i, nb) for nb in range(6)] + ["vecs"]
        wr = [("lay", i)]
        s.op("dve", lambda e: e.scalar_tensor_tensor(
            out=L(0), in0=m(1), scalar=1.0, in1=g("pre_mix_g%d"), op0=ALU.add, op1=ALU.mult), reads=rd, writes=wr)
        s.op("dve", lambda e: e.tensor_copy(out=L(1), in_=m(0)), reads=rd, writes=wr)
        s.op("dve", lambda e: e.tensor_tensor(out=L(2), in0=m(2), in1=g("post_mix_g%d"), op=ALU.mult), reads=rd, writes=wr)
        s.op("dve", lambda e: e.scalar_tensor_tensor(
            out=L(3), in0=m(4), scalar=1.0, in1=g("pre_ffn_g%d"), op0=ALU.add, op1=ALU.mult), reads=rd, writes=wr)
        s.op("dve", lambda e: e.tensor_copy(out=L(4), in_=m(3)), reads=rd, writes=wr)
        s.op("dve", lambda e: e.tensor_tensor(out=L(5), in0=m(5), in1=g("post_ffn_g%d"), op=ALU.mult), reads=rd, writes=wr)

    def lcol(self, i, k, c):
        o = i * 96 + k * 16 + c
        return self.lay[:, o:o + 1]

    def hT(self, c, tt=None):
        if tt is None:
            return self.R1.view(c * 2048, BF16, (1024,))
        return self.R1.view(c * 2048 + tt * 1024, BF16, (512,))

    def prenorm(self, xin, xin_name, hf, layer, ka, kb):
        s = self.s
        xs = []
        for c in range(KC):
            a, keys = self.R2.view(c * 4096, F32, (1024,))
            xs.append((a, keys))
            s.dma(lambda q, a=a, c=c: q.dma_start(out=a, in_=xin[c, :, hf * TH:(hf + 1) * TH]),
                  reads=[(xin_name, c, hf)], writes=keys, key=("xst", c))
        for c in range(KC):
            a, keys = xs[c]
            sq, sqk = self.R3.view((c % 2) * 2048, BF16, (1024,))
            if c % 2 == 0:
                s.op("act", lambda e, a=a, sq=sq: e.activation(out=sq, in_=a, func=AF.Square), reads=keys, writes=sqk)
            else:
                s.op("dve", lambda e, a=a, sq=sq: e.tensor_tensor(out=sq, in0=a, in1=a, op=ALU.mult), reads=keys, writes=sqk)
            self.flush_pending()
            self.stats_mm(3, sq, sqk, c == 0, c == KC - 1)
        self.flush_pending()
        rs, rsk = self.R3.view(4096, F32, (1024,))
        s.op("act", lambda e: e.activation(out=rs, in_=self.P[3][:], func=AF.Ln, scale=1.0 / D, bias=self.epsc[:]),
             reads=self.pk(3) + ["epsc"], writes=rsk)
        s.op("act", lambda e: e.activation(out=rs, in_=rs, func=AF.Exp, scale=-0.5), reads=rsk, writes=rsk)
        for c in range(KC):
            a, keys = xs[c]
            tm, tmk = self.R3.view(8192 + (c % 2) * 4096, F32, (1024,))
            s.op("dve", lambda e, a=a, tm=tm: e.tensor_tensor(out=tm, in0=a, in1=rs, op=ALU.mult),
                 reads=keys + rsk, writes=tmk)
            h, hk = self.hT(c)
            s.op("act", lambda e, tm=tm, h=h, c=c: e.activation(
                out=h, in_=tm, func=AF.Identity, scale=self.lcol(layer, ka, c), bias=self.lcol(layer, kb, c)),
                reads=tmk + [("lay", layer)], writes=hk)

    def postnorm(self, yview, xin, xin_name, xout, xout_name, hf, layer, kg, stg_region, stg_off):
        s = self.s
        rs, rsk = self.R3.view(4096, F32, (1024,))
        s.op("act", lambda e: e.activation(out=rs, in_=self.P[3][:], func=AF.Ln, scale=1.0 / D, bias=self.epsc[:]),
             reads=self.pk(3) + ["epsc"], writes=rsk)
        s.op("act", lambda e: e.activation(out=rs, in_=rs, func=AF.Exp, scale=-0.5), reads=rsk, writes=rsk)
        toks = []
        for c in range(KC):
            xa, xk = stg_region.view(stg_off + c * 4096, F32, (1024,))
            s.dma(lambda q, xa=xa, c=c: q.dma_start(out=xa, in_=xin[c, :, hf * TH:(hf + 1) * TH]),
                  reads=[(xin_name, c, hf)], writes=xk, key=("xst2", c))
            y, yk = yview(c)
            s.op("dve", lambda e, y=y, c=c: e.scalar_tensor_tensor(
                out=y, in0=y, scalar=self.lcol(layer, kg, c), in1=rs, op0=ALU.mult, op1=ALU.mult),
                reads=yk + rsk + [("lay", layer)], writes=yk)
            s.op("dve", lambda e, y=y, xa=xa: e.tensor_tensor(out=xa, in0=y, in1=xa, op=ALU.add),
                 reads=yk + xk, writes=xk)
            t = s.dma(lambda q, xa=xa, c=c: q.dma_start(out=xout[c, :, hf * TH:(hf + 1) * TH], in_=xa),
                      reads=xk, writes=[(xout_name, c, hf)], key=("xo", c), eng="act")
            toks.append(t)
        return toks

    def proj(self, wt, n_oc, n_kg, kcb, in_view, psel, epilogue, oc_list=None, tiles=((0, 0), (512, 512)), m=128, after_block=None):
        s = self.s
        for oi, oc in enumerate(oc_list if oc_list is not None else range(n_oc)):
            Pi = psel(oi)
            pkeys = []
            for (_, po) in tiles:
                pkeys += self.pk(Pi, po // 512)
            for kg in range(n_kg):
                t, key = self.load_w(wt[oc, kg], kcb * m)
                ins_ = [in_view(kg * kcb + k) for k in range(kcb)]
                rk = [key]
                for a, k_ in ins_:
                    rk += k_

                def emit(e, t=t, kg=kg, ins_=ins_, Pi=Pi):
                    ins = None
                    for k in range(kcb):
                        for (io, po) in tiles:
                            ins = e.matmul(self.P[Pi][0:m, po:po + 512], lhsT=t[:, k * m:(k + 1) * m],
                                           rhs=ins_[k][0][:, io:io + 512],
                                           start=(kg == 0 and k == 0), stop=(kg == n_kg - 1 and k == kcb - 1))
                    return ins
                s.op("pe", emit, reads=rk, writes=pkeys)
                if after_block is not None:
                    after_block()
            epilogue(oi, oc, Pi)

    def ffn(self, layer, xin, xin_name, xout, xout_name, next_layer=None):
        s = self.s
        dr = self.dr
        toks = []
        units = self.mod_units(next_layer) if next_layer is not None else []

        def inject():
            if units:
                units.pop(0)()
        for hf in range(2):
            self.prenorm(xin, xin_name, hf, layer, 3, 4)
            hid = lambda j: self.R2.view(j * 2048, BF16, (1024,))
            w1 = dr["ffn_w_in_%d" % layer]
            for j in range(JC):
                Pg, Pu = (0, 1) if j % 2 == 0 else (2, 3)
                self.proj(w1, None, 1, KC, self.hT, lambda oi, Pg=Pg, Pu=Pu: (Pg, Pu)[oi], lambda *a: None,
                          oc_list=[j, JC + j])
                sg, sgk = self.R3.view(8192 + (j % 2) * 4096, F32, (1024,))
                s.op("act", lambda e, sg=sg, Pg=Pg: e.activation(out=sg, in_=self.P[Pg][:], func=AF.Silu),
                     reads=self.pk(Pg), writes=sgk)
                h, hk = hid(j)
                s.op("dve", lambda e, sg=sg, h=h, Pu=Pu: e.tensor_tensor(out=h, in0=sg, in1=self.P[Pu][:], op=ALU.mult),
                     reads=sgk + self.pk(Pu), writes=hk)
            yv = lambda c: self.R1.view(c * 4096, F32, (1024,))

            def epi(oi, oc, Pi):
                y, yk = yv(oc)
                s.op("act", lambda e: e.activation(out=y, in_=self.P[Pi][:], func=AF.Identity),
                     reads=self.pk(Pi), writes=yk)
                sq, sqk = self.R3.view((oc % 2) * 2048, BF16, (1024,))
                s.op("dve", lambda e: e.tensor_tensor(out=sq, in0=self.P[Pi][:], in1=y, op=ALU.mult),
                     reads=self.pk(Pi) + yk, writes=sqk)
                self.flush_pending()
                self.stats_mm(3, sq, sqk, oc == 0, oc == KC - 1)
            self.proj(dr["ffn_w_out_%d" % layer], KC, 4, 11, hid, lambda oi: oi % 2, epi, after_block=inject)
            self.flush_pending()
            toks += self.postnorm(yv, xin, xin_name, xout, xout_name, hf, layer, 5, self.R2, 0)
        while units:
            units.pop(0)()
        return toks


    def allgather(self, src, dst, src_keys, dst_keys):
        self.s.op("pool", lambda e: e.collective_compute(
            "AllGather", ALU.bypass, replica_groups=[[0, 1, 2, 3], [4, 5, 6, 7]],
            ins=[src.opt()], outs=[dst.opt()]), reads=list(src_keys) + ["cc_chain"], writes=list(dst_keys) + ["cc_chain"])

    def conv_a(self, layer, j, xin, xin_name):
        s = self.s
        dr = self.dr
        vo = dr["vec_off"]["b_pw1_%d" % j]
        for hf in range(2):
            self.prenorm(xin, xin_name, hf, layer, 0, 1)
            for c in range(KC):
                Pa, Pg = (0, 1) if c % 2 == 0 else (2, 3)
                self.proj(dr["conv_w_pw1_%d" % j], None, 1, KC, self.hT, lambda oi, Pa=Pa, Pg=Pg: (Pa, Pg)[oi],
                          lambda *a: None, oc_list=[c, KC + c])
                sg, sgk = self.R3.view(8192 + (c % 2) * 4096, F32, (1024,))
                s.op("act", lambda e, sg=sg, Pg=Pg, c=c: e.activation(
                    out=sg, in_=self.P[Pg][:], func=AF.Sigmoid, bias=self.vecs[:, vo + KC + c:vo + KC + c + 1]),
                    reads=self.pk(Pg) + ["vecs"], writes=sgk)
                u, uk = self.R2.view(65536 + (c % 3) * 2048, BF16, (1024,))
                s.op("dve", lambda e, sg=sg, u=u, Pa=Pa, c=c: e.scalar_tensor_tensor(
                    out=u, in0=self.P[Pa][:], scalar=self.vecs[:, vo + c:vo + c + 1], in1=sg, op0=ALU.add, op1=ALU.mult),
                    reads=sgk + self.pk(Pa) + ["vecs"], writes=uk)
                s.dma(lambda q, u=u, c=c, hf=hf: q.dma_start(
                    out=dr["u_s"][c, :, HALO + hf * TH:HALO + (hf + 1) * TH], in_=u),
                    reads=uk, writes=[("u_s", c, hf)], key=("uo", c % 3), eng="act")
                if hf == 1:
                    s.dma(lambda q, u=u, c=c: q.dma_start(out=dr["halo_src"][:, c * HALO:(c + 1) * HALO], in_=u[:, TH - HALO:TH]),
                          reads=uk, writes=[("halo_src", c)], key=("ho", c % 3), eng="act")
        self.allgather(dr["halo_src"], dr["halo_g"], [("halo_src", c) for c in range(KC)], ["halo_g"])

    def conv_b(self, layer, j, xin, xin_name, xout, xout_name):
        s = self.s
        dr = self.dr
        vo = dr["vec_off"]
        toks = []
        G, Gk = self.R3.view(16384, BF16, (4, KC * HALO))
        s.dma(lambda q: q.dma_start(out=G, in_=dr["halo_g"].rearrange("(r p) k -> p r k", p=128)),
              reads=["halo_g"], writes=Gk, key="halo_ld")
        hal, halk = self.R3.view(16384 + 4096, BF16, (KC * HALO,))
        mo = vo["mprev"]
        s.op("dve", lambda e: e.tensor_scalar(out=hal, in0=G[:, 0, :], scalar1=self.vecs[:, mo:mo + 1], scalar2=None,
                                               op0=ALU.mult), reads=Gk + ["vecs"], writes=halk)
        for r in range(1, 4):
            s.op("dve", lambda e, r=r: e.scalar_tensor_tensor(
                out=hal, in0=G[:, r, :], scalar=self.vecs[:, mo + r:mo + r + 1], in1=hal, op0=ALU.mult, op1=ALU.add),
                reads=Gk + halk + ["vecs"], writes=halk)
        UW = HALO + TH
        for hf in range(2):
            Us = []
            for c in range(KC):
                U, Uk = self.R2.view(c * UW * 2, BF16, (UW,))
                Us.append((U, Uk))
                s.dma(lambda q, U=U, c=c, hf=hf: q.dma_start(out=U, in_=dr["u_s"][c, :, hf * TH:hf * TH + UW]),
                      reads=[("u_s", c, 0), ("u_s", c, 1)], writes=Uk, key=("Uld", c))
                if hf == 0:
                    s.op("dve", lambda e, U=U, c=c: e.tensor_copy(out=U[:, 0:HALO], in_=hal[:, c * HALO:(c + 1) * HALO]),
                         reads=halk + Uk, writes=Uk)
            vv = lambda c: self.R1.view(c * 4096, F32, (1024,))
            wo = vo["w_dw_%d" % j]
            bo = vo["b_dw_%d" % j]
            def build_taps(c):
                dg, dgk = self.R2.view(33792 + (c % 2) * 7936, BF16, (CW, 128))
                for k in range(CW):
                    rd = ["ident", "vecs"] + ([("dggate", c % 2)] if k > 0 else [])
                    wr = [("dgtap", c % 2, k)] + (dgk + [("dggate", c % 2)] if k == 0 else [])
                    if k % 2 == 0:
                        s.op("dve", lambda e, dg=dg, k=k, c=c: e.tensor_scalar(
                            out=dg[:, k, :], in0=self.ident[:], scalar1=self.vecs[:, wo + k * KC + c:wo + k * KC + c + 1],
                            scalar2=None, op0=ALU.mult), reads=rd, writes=wr)
                    else:
                        s.op("act", lambda e, dg=dg, k=k, c=c: e.activation(
                            out=dg[:, k, :], in_=self.ident[:], func=AF.Identity,
                            scale=self.vecs[:, wo + k * KC + c:wo + k * KC + c + 1]), reads=rd, writes=wr)
            build_taps(0)
            for c in range(KC):
                dg, dgk = self.R2.view(33792 + (c % 2) * 7936, BF16, (CW, 128))
                Pi = c % 2
                U, Uk = Us[c]

                def emit(e, dg=dg, U=U, Pi=Pi):
                    ins = None
                    for tt in range(2):
                        for k in range(CW):
                            ins = e.matmul(self.P[Pi][:, tt * 512:(tt + 1) * 512], lhsT=dg[:, k, :],
                                           rhs=U[:, tt * 512 + k + 2:tt * 512 + k + 2 + 512],
                                           start=(k == 0), stop=(k == CW - 1))
                    return ins
                s.op("pe", emit, reads=[("dgtap", c % 2, k) for k in range(CW)] + dgk + [("dggate", c % 2)] + Uk,
                     writes=self.pk(Pi))
                if c + 1 < KC:
                    build_taps(c + 1)
                v, vk = vv(c)
                s.op("act", lambda e, v=v, Pi=Pi, c=c: e.activation(
                    out=v, in_=self.P[Pi][:], func=AF.Identity, bias=self.vecs[:, bo + c:bo + c + 1]),
                    reads=self.pk(Pi) + ["vecs"], writes=vk)
                sq, sqk = self.R3.view((c % 2) * 2048, BF16, (1024,))
                vb, vbk = self.R3.view(4096 + (c % 2) * 2048, BF16, (1024,))
                s.op("dve", lambda e, v=v, vb=vb: e.tensor_copy(out=vb, in_=v), reads=vk, writes=vbk)
                s.op("dve", lambda e, v=v, sq=sq: e.tensor_tensor(out=sq, in0=v, in1=v, op=ALU.mult), reads=vk, writes=sqk)
                self.flush_pending()
                self.stats_mm(2, vb, vbk, c == 0, c == KC - 1)
                self.stats_mm(3, sq, sqk, c == 0, c == KC - 1)
            self.flush_pending()
            mean, mk = self.R3.view(8192, F32, (1024,))
            rstd, rk = self.R3.view(12288, F32, (1024,))
            s.op("act", lambda e: e.activation(out=mean, in_=self.P[2][:], func=AF.Identity, scale=1.0 / D),
                 reads=self.pk(2), writes=mk)
            s.op("dve", lambda e: e.tensor_tensor(out=rstd, in0=mean, in1=mean, op=ALU.mult), reads=mk, writes=rk)
            s.op("dve", lambda e: e.scalar_tensor_tensor(out=rstd, in0=self.P[3][:], scalar=1.0 / D, in1=rstd,
                                                          op0=ALU.mult, op1=ALU.subtract), reads=self.pk(3) + rk, writes=rk)
            s.op("act", lambda e: e.activation(out=rstd, in_=rstd, func=AF.Ln, bias=self.epsc[:]),
                 reads=rk + ["epsc"], writes=rk)
            s.op("act", lambda e: e.activation(out=rstd, in_=rstd, func=AF.Exp, scale=-0.5), reads=rk, writes=rk)
            sv = lambda c: self.R2.view(49664 + c * 2048, BF16, (1024,))
            go, lbo = vo["ln_g_%d" % j], vo["ln_b_%d" % j]
            for c in range(KC):
                v, vk = vv(c)
                t1, t1k = self.R3.view(16384 + (c % 2) * 4096, F32, (1024,))
                s.op("dve", lambda e, v=v, t1=t1: e.tensor_tensor(out=t1, in0=v, in1=mean, op=ALU.subtract),
                     reads=vk + mk, writes=t1k)
                s.op("dve", lambda e, t1=t1: e.tensor_tensor(out=t1, in0=t1, in1=rstd, op=ALU.mult),
                     reads=t1k + rk, writes=t1k)
                sc, sck = sv(c)
                s.op("act", lambda e, t1=t1, sc=sc, c=c: e.activation(
                    out=sc, in_=t1, func=AF.Silu, scale=self.vecs[:, go + c:go + c + 1], bias=self.vecs[:, lbo + c:lbo + c + 1]),
                    reads=t1k + ["vecs"], writes=sck)
            yv = vv
            b2 = vo["b_pw2_%d" % j]

            def epi(oi, oc, Pi):
                y, yk = yv(oc)
                s.op("act", lambda e: e.activation(out=y, in_=self.P[Pi][:], func=AF.Identity,
                                                   bias=self.vecs[:, b2 + oc:b2 + oc + 1]),
                     reads=self.pk(Pi) + ["vecs"], writes=yk)
                sq, sqk = self.R3.view((oc % 2) * 2048, BF16, (1024,))
                s.op("dve", lambda e: e.tensor_tensor(out=sq, in0=y, in1=y, op=ALU.mult), reads=yk, writes=sqk)
                self.flush_pending()
                self.stats_mm(3, sq, sqk, oc == 0, oc == KC - 1)
            self.proj(dr["conv_w_pw2_%d" % j], KC, 1, KC, sv, lambda oi: oi % 2, epi)
            self.flush_pending()
            toks += self.postnorm(yv, xin, xin_name, xout, xout_name, hf, layer, 2, self.R2, 0)
        return toks

    def rows(self, region, off, dtype, n, nrows):
        esz = 4 if dtype == F32 else 2
        a = region.t[0:nrows, off // 4:(off + n * esz) // 4]
        if dtype != F32:
            a = a.bitcast(dtype)
        keys = [(region.name, pg) for pg in range(off // 1024, (off + n * esz + 1023) // 1024)]
        return a, keys

    def gla(self, layer, j, xin, xin_name, xout, xout_name, state_only):
        s = self.s
        dr = self.dr
        vo = dr["vec_off"]
        R1, R2, R3 = self.R1, self.R2, self.R3
        toks = []
        B = not state_only
        wup, wupk = self.rows(R3, 16384, BF16, 1024, 17)
        aT, aTk = self.rows(R3, 18944, BF16, 512, 17)
        small, smk = R3.view(18432, F32, (8 * 8,))
        bl, ebl, Bsum, Dj, Dp = [small[:, i * 8:(i + 1) * 8] for i in range(5)]
        blk_, eblk_, Bsk, Djk, Dpk = [[("gsm", i)] for i in range(5)]
        s.dma(lambda q: q.dma_start(out=wup, in_=dr["gla_wup_%d" % j]), writes=wupk, key="wup", eng="pool")
        s.op("dve", lambda e: e.memset(aT, 1.0), writes=aTk)
        S32v = lambda dc: R2.view(65536 + dc * 2048, F32, (512,))
        Sbfv = lambda dc: R2.view(81920 + dc * 1024, BF16, (512,))
        Sall, Sallk = R2.view(65536, F32, (4096,))
        Sball, Sballk = R2.view(81920, BF16, (4096,))
        def init_state():
            s.op("dve", lambda e: e.memset(Sall, 0.0), writes=Sallk)
            if state_only:
                s.op("dve", lambda e: e.memset(Bsum, 0.0), writes=Bsk)
            else:
                mo = vo["mlt"]
                for jr in range(3):
                    s.dma(lambda q, jr=jr: q.dma_start(out=Dj, in_=dr["gd_g"][jr * 128:(jr + 1) * 128, 0:8]),
                          reads=["gd_g"], writes=Djk, key="Dj")
                    s.op("dve", lambda e, jr=jr: e.tensor_scalar(out=Dp, in0=Dj, scalar1=-1.0, scalar2=self.vecs[:, mo + jr:mo + jr + 1],
                                                                  op0=ALU.add, op1=ALU.mult), reads=Djk + ["vecs"], writes=Dpk)
                    s.op("dve", lambda e: e.tensor_scalar(out=Dp, in0=Dp, scalar1=1.0, scalar2=None, op0=ALU.add),
                         reads=Dpk, writes=Dpk)
                    for dc in range(8):
                        stg, stgk = R3.view(20480 + (dc % 2) * 2048, F32, (512,))
                        s.dma(lambda q, jr=jr, dc=dc, stg=stg: q.dma_start(
                            out=stg, in_=dr["gsA_g" if dc < 4 else "gsB_g"][jr * 128:(jr + 1) * 128, (dc % 4) * 512:(dc % 4 + 1) * 512]),
                            reads=["gsA_g" if dc < 4 else "gsB_g"], writes=stgk, key=("stg", dc % 2))
                        s.op("dve", lambda e, stg=stg, jr=jr: e.tensor_scalar(
                            out=stg, in0=stg, scalar1=self.vecs[:, mo + jr:mo + jr + 1], scalar2=None, op0=ALU.mult),
                            reads=stgk + ["vecs"], writes=stgk)
                        S, Sk = S32v(dc)
                        s.op("dve", lambda e, S=S, stg=stg, dc=dc: e.scalar_tensor_tensor(
                            out=S, in0=S, scalar=Dp[:, dc:dc + 1], in1=stg, op0=ALU.mult, op1=ALU.add),
                            reads=Sk + stgk + Dpk, writes=Sk)
                s.op("act", lambda e: e.activation(out=Sball, in_=Sall, func=AF.Identity), reads=Sallk, writes=Sballk)


        init_done = [False]

        qTv = lambda c: R2.view(c * 1024, BF16, (512,))
        kTv = lambda c: R2.view(8192 + c * 1024, BF16, (512,))
        vTv = lambda c: R2.view(16384 + c * 1024, BF16, (512,))
        qT3, _ = R2.view(0, BF16, (8, 512))
        kT3, _ = R2.view(8192, BF16, (8, 512))
        qTk = R2.view(0, BF16, (4096,))[1]
        kTk = R2.view(8192, BF16, (4096,))[1]
        vTk = R2.view(16384, BF16, (8192,))[1]
        gpos, gpk = R2.view(32768, F32, (1024,))
        e1, e1k = R2.view(36864, F32, (1024,))
        tE = [R2.view(40960, F32, (8, 128)), R2.view(45056, F32, (8, 128))]
        tEf = [R2.view(40960, F32, (1024,)), R2.view(45056, F32, (1024,))]
        qd, qdk = R2.view(49152, BF16, (8, 128))
        kd, kdk = R2.view(51200, BF16, (8, 128))
        kh, khk = R2.view(53248, BF16, (8, 128))
        vtok, vtk = R2.view(55296, BF16, (2048,))
        ktok, ktk = R2.view(59392, BF16, (1024,))
        sc, sck = R2.view(61440, BF16, (512,))
        osq, osqk = R3.view(0, BF16, (2048,))
        rh, rhk = R3.view(4096, F32, (512,))
        onT = lambda c: R1.view(32768 + c * 2048, BF16, (1024,))
        P = self.P
        P1v = P[1][:].rearrange("p (a b) -> p a b", a=8)
        P2bf = P[2][:].bitcast(BF16)
        P3bf = P[3][:].bitcast(BF16)
        ngo = vo["gla_ng_%d" % j]
        cnt = [0]

        def alt():
            cnt[0] += 1
            return "act" if cnt[0] % 2 else "dve"

        def copy_op(eng, out, in_, reads, writes, scale=None):
            if eng == "act":
                if scale is None:
                    s.op("act", lambda e: e.activation(out=out, in_=in_, func=AF.Identity), reads=reads, writes=writes)
                else:
                    s.op("act", lambda e: e.activation(out=out, in_=in_, func=AF.Identity, scale=scale), reads=reads, writes=writes)
            else:
                if scale is None:
                    s.op("dve", lambda e: e.tensor_copy(out=out, in_=in_), reads=reads, writes=writes)
                else:
                    s.op("dve", lambda e: e.tensor_scalar(out=out, in0=in_, scalar1=scale, scalar2=None, op0=ALU.mult),
                         reads=reads, writes=writes)

        for hf in range(2):
            self.prenorm(xin, xin_name, hf, layer, 0, 1)
            for qt in range(2):
                tl = ((qt * 512, 0),)
                def epi_qkv(oi, oc, Pi):
                    if oc < 8:
                        o_, k_ = qTv(oc)
                        copy_op(alt(), o_, P[Pi][:, 0:512], self.pk(Pi, 0), k_, scale=0.0625)
                    elif oc < 16:
                        o_, k_ = kTv(oc - 8)
                        copy_op(alt(), o_, P[Pi][:, 0:512], self.pk(Pi, 0), k_)
                    else:
                        o_, k_ = vTv(oc - 16)
                        copy_op(alt(), o_, P[Pi][:, 0:512], self.pk(Pi, 0), k_)
                qi = hf * 2 + qt
                kv3, _ = R2.view(8192, BF16, (24, 512))
                if state_only:
                    def epi_a(oi, oc, Pi):
                        s.op("act", lambda e: e.activation(out=aT[0:16, :], in_=P[Pi][0:16, 0:512], func=AF.Identity),
                             reads=self.pk(Pi, 0), writes=aTk)
                    self.proj(dr["gla_wa_%d" % j], 1, 1, KC, self.hT, lambda oi: 0, epi_a, tiles=tl, m=16)
                    self.proj(dr["gla_w_in_%d" % j], None, 1, KC, self.hT, lambda oi: 1 + oi % 3, epi_qkv,
                              oc_list=list(range(8, 32)), tiles=tl)
                    s.dma(lambda q, qi=qi: q.dma_start(out=dr["kv_s"][qi], in_=kv3), reads=kTk + vTk,
                          writes=[("kv_s", qi)], key="kvo", eng="act")
                    s.dma(lambda q, qi=qi: q.dma_start(out=dr["a_s"][qi], in_=aT[0:16, :]), reads=aTk,
                          writes=[("a_s", qi)], key="ao", eng="act")
                else:
                    s.dma(lambda q, qi=qi: q.dma_start(out=kv3, in_=dr["kv_s"][qi]), reads=[("kv_s", qi)],
                          writes=kTk + vTk, key="kvi")
                    s.dma(lambda q, qi=qi: q.dma_start(out=aT[0:16, :], in_=dr["a_s"][qi]), reads=[("a_s", qi)],
                          writes=aTk, key="ai")
                    self.proj(dr["gla_w_in_%d" % j], None, 1, KC, self.hT, lambda oi: 1 + oi % 3, epi_qkv,
                              oc_list=list(range(0, 8)), tiles=tl)
                if not init_done[0]:
                    init_done[0] = True
                    init_state()
                for ch in range(4):
                    c0 = ch * 128
                    tokoff = qt * 512 + c0

                    def state_mm():
                        for dc in range(8):
                            h = dc // 2
                            s.op("pe", lambda e, dc=dc, h=h: e.matmul(P[3][:, (dc % 2) * 512:(dc % 2 + 1) * 512],
                                                                     lhsT=ktok[:, dc * 128:(dc + 1) * 128],
                                                                     rhs=vtok[:, h * 512:(h + 1) * 512], start=True, stop=True),
                                 reads=ktk + vtk, writes=self.pk(3, dc % 2))
                            S, Sk = S32v(dc)
                            s.op("dve", lambda e, S=S, dc=dc: e.scalar_tensor_tensor(
                                out=S, in0=S, scalar=ebl[:, dc:dc + 1], in1=P[3][:, (dc % 2) * 512:(dc % 2 + 1) * 512],
                                op0=ALU.mult, op1=ALU.add), reads=Sk + eblk_ + self.pk(3, dc % 2), writes=Sk)

                    def state_cast():
                        for dc in range(8):
                            S, Sk = S32v(dc)
                            Sb, Sbk = Sbfv(dc)
                            s.op("act", lambda e, S=S, Sb=Sb: e.activation(out=Sb, in_=S, func=AF.Identity), reads=Sk, writes=Sbk)

                    def emit_z(e, c0=c0):
                        ins = None
                        for hh in range(2):
                            ins = e.matmul(P[0][:, hh * 512:(hh + 1) * 512], lhsT=aT[:, c0:c0 + 128],
                                           rhs=wup[:, hh * 512:(hh + 1) * 512], start=True, stop=True)
                        return ins
                    s.op("pe", emit_z, reads=aTk + wupk, writes=self.pk(0))
                    s.op("act", lambda e: e.activation(out=e1, in_=P[0][:], func=AF.Exp, scale=-1.0), reads=self.pk(0), writes=e1k)
                    s.op("act", lambda e: e.activation(out=gpos, in_=e1, func=AF.Ln, bias=self.one32[:]),
                         reads=e1k + ["one32"], writes=gpk)

                    ghi, ghk = R2.view(36864, BF16, (1024,))
                    glo, glk = R2.view(38912, BF16, (1024,))
                    s.op("dve", lambda e: e.tensor_copy(out=ghi, in_=gpos), reads=gpk, writes=ghk)
                    s.op("dve", lambda e: e.tensor_tensor(out=glo, in0=gpos, in1=ghi, op=ALU.subtract), reads=gpk + ghk, writes=glk)

                    def emit_cs(e):
                        ins = None
                        for dc in range(8):
                            e.matmul(P[1][:, dc * 128:(dc + 1) * 128], lhsT=ghi[:, dc * 128:(dc + 1) * 128],
                                     rhs=self.tribf[:], start=True, stop=False)
                            ins = e.matmul(P[1][:, dc * 128:(dc + 1) * 128], lhsT=glo[:, dc * 128:(dc + 1) * 128],
                                           rhs=self.tribf[:], start=False, stop=True)
                        return ins
                    s.op("pe", emit_cs, reads=ghk + glk + ["tribf"], writes=self.pk(1))
                    s.op("dve", lambda e: e.tensor_scalar(out=bl, in0=P1v[:, :, 127], scalar1=-0.0625, scalar2=None, op0=ALU.mult),
                         reads=self.pk(1), writes=blk_)
                    s.op("act", lambda e: e.activation(out=ebl, in_=bl, func=AF.Exp), reads=blk_, writes=eblk_)
                    if state_only:
                        s.op("dve", lambda e: e.tensor_tensor(out=Bsum, in0=Bsum, in1=bl, op=ALU.add), reads=Bsk + blk_, writes=Bsk)
                    (EK, EKk) = tE[0]

                    def emit_ek(e, EK=EK):
                        ins = None
                        for dc in range(8):
                            ins = e.activation(out=EK[:, dc, :], in_=P1v[:, dc, :], func=AF.Exp, scale=0.0625, bias=bl[:, dc:dc + 1])
                        return ins
                    s.op("act", emit_ek, reads=self.pk(1) + blk_, writes=EKk)
                    s.op("dve", lambda e, c0=c0, EK=EK: e.tensor_tensor(out=kh, in0=kT3[:, :, c0:c0 + 128], in1=EK, op=ALU.mult),
                         reads=kTk + EKk, writes=khk)
                    if B:
                        EBf, EBk = tEf[1]
                        s.op("act", lambda e, EBf=EBf: e.activation(out=EBf, in_=P[1][:], func=AF.Exp, scale=-0.0625),
                             reads=self.pk(1), writes=EBk)
                        s.op("dve", lambda e, c0=c0: e.tensor_tensor(out=qd, in0=qT3[:, :, c0:c0 + 128], in1=tE[1][0], op=ALU.mult),
                             reads=qTk + EBk, writes=qdk)
                        ENf, ENk = tEf[0]
                        s.op("act", lambda e, ENf=ENf: e.activation(out=ENf, in_=P[1][:], func=AF.Exp, scale=0.0625),
                             reads=self.pk(1), writes=ENk)
                        s.op("dve", lambda e, c0=c0: e.tensor_tensor(out=kd, in0=kT3[:, :, c0:c0 + 128], in1=tE[0][0], op=ALU.mult),
                             reads=kTk + ENk, writes=kdk)

                    def emit_tk(e):
                        ins = None
                        for dc in range(8):
                            ins = e.transpose(out=P2bf[:, dc * 128:(dc + 1) * 128], in_=kh[:, dc, :], identity=self.ident[:])
                        return ins
                    s.op("pe", emit_tk, reads=khk + ["ident"], writes=self.pk(2, 0))

                    def emit_tv(e, c0=c0):
                        ins = None
                        for c in range(16):
                            ins = e.transpose(out=P3bf[:, c * 128:(c + 1) * 128], in_=vTv(c)[0][:, c0:c0 + 128], identity=self.ident[:])
                        return ins
                    s.op("pe", emit_tv, reads=vTk + ["ident"], writes=self.pk(3))
                    s.op("act", lambda e: e.activation(out=ktok, in_=P2bf[:, 0:1024], func=AF.Identity), reads=self.pk(2, 0), writes=ktk)
                    s.op("dve", lambda e: e.tensor_copy(out=vtok, in_=P3bf[:, 0:2048]), reads=self.pk(3), writes=vtk)
                    if B:
                        def emit_sc(e):
                            ins = None
                            for h in range(4):
                                for d2 in range(2):
                                    dc = 2 * h + d2
                                    ins = e.matmul(P[2][:, 512 + h * 128:512 + (h + 1) * 128], lhsT=kd[:, dc, :], rhs=qd[:, dc, :],
                                                   start=(d2 == 0), stop=(d2 == 1))
                            return ins
                        s.op("pe", emit_sc, reads=kdk + qdk, writes=self.pk(2, 1))
                        s.op("dve", lambda e: e.tensor_tensor(out=sc, in0=P[2][:, 512:1024], in1=self.tri4[:], op=ALU.mult),
                             reads=self.pk(2, 1) + ["tri4"], writes=sck)

                        def emit_o(e):
                            ins = None
                            for h in range(4):
                                for es in range(4):
                                    blk = h * 4 + es
                                    out = P[blk // 8][:, (blk % 8) * 128:(blk % 8 + 1) * 128]
                                    e.matmul(out, lhsT=Sbfv(2 * h)[0][:, es * 128:(es + 1) * 128], rhs=qd[:, 2 * h, :], start=True, stop=False)
                                    e.matmul(out, lhsT=Sbfv(2 * h + 1)[0][:, es * 128:(es + 1) * 128], rhs=qd[:, 2 * h + 1, :], start=False, stop=False)
                                    ins = e.matmul(out, lhsT=vtok[:, h * 512 + es * 128:h * 512 + (es + 1) * 128],
                                                   rhs=sc[:, h * 128:(h + 1) * 128], start=False, stop=True)
                            return ins
                        s.op("pe", emit_o, reads=Sballk + qdk + vtk + sck, writes=self.pk(0) + self.pk(1))
                        state_mm()
                        s.op("act", lambda e: e.activation(out=osq[:, 0:1024], in_=P[0][:], func=AF.Square), reads=self.pk(0), writes=osqk)
                        s.op("act", lambda e: e.activation(out=osq[:, 1024:2048], in_=P[1][:], func=AF.Square), reads=self.pk(1), writes=osqk)

                        def emit_hs(e):
                            ins = None
                            for h in range(4):
                                for es in range(4):
                                    ins = e.matmul(P[2][:, h * 128:(h + 1) * 128], lhsT=self.ones[:],
                                                   rhs=osq[:, (h * 4 + es) * 128:(h * 4 + es + 1) * 128], start=(es == 0), stop=(es == 3))
                            return ins
                        s.op("pe", emit_hs, reads=osqk + ["ones"], writes=self.pk(2, 0))
                        s.op("act", lambda e: e.activation(out=rh, in_=P[2][:, 0:512], func=AF.Ln, scale=1.0 / 512, bias=self.epsc[:]),
                             reads=self.pk(2, 0) + ["epsc"], writes=rhk)
                        s.op("act", lambda e: e.activation(out=rh, in_=rh, func=AF.Exp, scale=-0.5), reads=rhk, writes=rhk)
                        for blk in range(16):
                            h = blk // 4
                            o_, ok_ = onT(blk)
                            s.op("dve", lambda e, blk=blk, h=h, o_=o_, tokoff=tokoff: e.scalar_tensor_tensor(
                                out=o_[:, tokoff:tokoff + 128], in0=P[blk // 8][:, (blk % 8) * 128:(blk % 8 + 1) * 128],
                                scalar=self.vecs[:, ngo + blk:ngo + blk + 1], in1=rh[:, h * 128:(h + 1) * 128],
                                op0=ALU.mult, op1=ALU.mult), reads=self.pk(blk // 8) + rhk + ["vecs"], writes=ok_)
                    if not B:
                        state_mm()
                    else:
                        state_cast()
            if B:
                def epi_r(oi, oc, Pi):
                    c = oc - 32
                    sr, srk = R3.view(8192 + (c % 2) * 4096, F32, (1024,))
                    s.op("act", lambda e: e.activation(out=sr, in_=P[Pi][:], func=AF.Silu), reads=self.pk(Pi), writes=srk)
                    o_, ok_ = onT(c)
                    s.op("dve", lambda e: e.tensor_tensor(out=o_, in0=o_, in1=sr, op=ALU.mult), reads=ok_ + srk, writes=ok_)
                self.proj(dr["gla_w_in_%d" % j], None, 1, KC, self.hT, lambda oi: oi % 2, epi_r, oc_list=list(range(32, 48)))
                yv = lambda c: R2.view(c * 4096, F32, (1024,))

                def epi_o(oi, oc, Pi):
                    y, yk = yv(oc)
                    s.op("act", lambda e: e.activation(out=y, in_=P[Pi][:], func=AF.Identity), reads=self.pk(Pi), writes=yk)
                    sq, sqk = R3.view((oc % 2) * 2048, BF16, (1024,))
                    s.op("dve", lambda e: e.tensor_tensor(out=sq, in0=P[Pi][:], in1=y, op=ALU.mult), reads=self.pk(Pi) + yk, writes=sqk)
                    self.flush_pending()
                    self.stats_mm(3, sq, sqk, oc == 0, oc == KC - 1)
                self.proj(dr["gla_w_out_%d" % j], KC, 1, KC, onT, lambda oi: oi % 2, epi_o)
                self.flush_pending()
                toks += self.postnorm(yv, xin, xin_name, xout, xout_name, hf, layer, 2, R1, 0)
        if state_only:
            s.op("act", lambda e: e.activation(out=Dj, in_=Bsum, func=AF.Exp), reads=Bsk, writes=Djk)
            s.dma(lambda q: q.dma_start(out=dr["gd_src"][:, 0:8], in_=Dj), reads=Djk, writes=["gd_src"], key="gdo", eng="act")
            s.dma(lambda q: q.dma_start(out=dr["gsA_src"], in_=Sall[:, 0:2048]), reads=Sallk, writes=["gsA_src"], key="gsoA", eng="act")
            s.dma(lambda q: q.dma_start(out=dr["gsB_src"], in_=Sall[:, 2048:4096]), reads=Sallk, writes=["gsB_src"], key="gsoB", eng="act")
            self.allgather(dr["gd_src"], dr["gd_g"], ["gd_src"], ["gd_g"])
            self.allgather(dr["gsA_src"], dr["gsA_g"], ["gsA_src"], ["gsA_g"])
            self.allgather(dr["gsB_src"], dr["gsB_g"], ["gsB_src"], ["gsB_g"])
        return toks


def tile_w(W, kcb, m=128):
    K, N = W.shape
    n_kg = K // (128 * kcb)
    a = W.reshape(n_kg, kcb, 128, N // m, m).transpose(3, 0, 2, 1, 4)
    return np.ascontiguousarray(a).reshape(N // m, n_kg, 128, kcb * m)


def colvec(v):
    return np.ascontiguousarray(v.reshape(-1, 128).T)


def build_vecs(inp, b, seg):
    cols = []
    off = {}

    def add(name, arr):
        off[name] = sum(a.shape[1] for a in cols)
        cols.append(np.asarray(arr, dtype=np.float32))
    add("c", colvec(inp["c"][b]))
    for i in range(DEPTH):
        add("b_ada%d" % i, colvec(inp["b_ada"][i]))
        for nm in ["pre_mix_g", "post_mix_g", "pre_ffn_g", "post_ffn_g"]:
            add(nm + "%d" % i, colvec(inp[nm][i]))
    for j in range(2):
        add("b_pw1_%d" % j, colvec(inp["conv_b_pw1"][j]))
        add("b_dw_%d" % j, colvec(inp["conv_b_dw"][j]))
        add("ln_g_%d" % j, colvec(inp["conv_ln_g"][j]))
        add("ln_b_%d" % j, colvec(inp["conv_ln_b"][j]))
        add("b_pw2_%d" % j, colvec(inp["conv_b_pw2"][j]))
        add("w_dw_%d" % j, colvec(inp["conv_w_dw"][j].reshape(-1)))
        add("gla_ng_%d" % j, colvec(inp["gla_norm_g"][j]))
    mprev = np.zeros((128, 4), np.float32)
    if seg > 0:
        mprev[:, seg - 1] = 1.0
    mlt = np.zeros((128, 4), np.float32)
    mlt[:, :seg] = 1.0
    add("mprev", mprev)
    add("mlt", mlt)
    return np.concatenate(cols, axis=1), off


FULL = [("convA", 0), ("convB", 0), ("ffn", 0), ("glaA", 1), ("glaB", 1), ("ffn", 1),
        ("convA", 2), ("convB", 2), ("ffn", 2), ("glaA", 3), ("glaB", 3), ("ffn", 3)]
MODES = {"full": FULL, "ffn0": [("ffn", 0)], "conv0": [("convA", 0), ("convB", 0)],
         "gla1": [("glaA", 1), ("glaB", 1)], "gla1A": [("glaA", 1)], "f0g1": [("ffn", 0), ("glaA", 1), ("glaB", 1)], "l0": FULL[:3], "l01": FULL[:6]}


def weight_arrays(inp, steps):
    w = {}
    for kind, L in steps:
        j = L // 2
        w["w_ada%d" % L] = lambda L=L: np.ascontiguousarray(inp["w_ada"][L])
        if kind == "ffn":
            w["ffn_w_in_%d" % L] = lambda L=L: tile_w(inp["ffn_w_in"][L], KC)
            w["ffn_w_out_%d" % L] = lambda L=L: tile_w(inp["ffn_w_out"][L], 11)
        elif kind == "convA":
            w["conv_w_pw1_%d" % j] = lambda j=j: tile_w(inp["conv_w_pw1"][j], KC)
        elif kind == "convB":
            w["conv_w_pw2_%d" % j] = lambda j=j: tile_w(inp["conv_w_pw2"][j], KC)
        elif kind in ("glaA", "glaB"):
            w["gla_w_in_%d" % j] = lambda j=j: tile_w(inp["gla_w_in"][j][:, :6144], KC)
            w["gla_wa_%d" % j] = lambda j=j: tile_w(inp["gla_w_in"][j][:, 6144:6160], KC, m=16)
            w["gla_wup_%d" % j] = lambda j=j: np.concatenate(
                [inp["gla_w_gate_up"][j], inp["gla_b_gate"][j][None, :]], axis=0).astype(np.float32)
            if kind == "glaB":
                w["gla_w_out_%d" % j] = lambda j=j: tile_w(inp["gla_w_out"][j], KC)
    return {k: f() for k, f in w.items()}


def host_consts():
    c = np.zeros((128, 640), np.float32)
    c[:, 0:128] = np.eye(128, dtype=np.float32)
    tri = np.triu(np.ones((128, 128), np.float32))
    c[:, 128:640] = np.tile(tri, (1, 4))
    return c


def build_program(vec_off, nvec, steps, wshapes):
    nc = bass.Bass("TRN2", target_bir_lowering=False)
    dr = {"vec_off": vec_off}
    dr["xT"] = nc.dram_tensor("xT", [KC, 128, TOK], F32, kind="ExternalInput").ap()
    dr["vecs"] = nc.dram_tensor("vecs", [128, nvec], F32, kind="ExternalInput").ap()
    dr["consts"] = nc.dram_tensor("consts", [128, 640], F32, kind="ExternalInput").ap()
    for k, shp in wshapes.items():
        dr[k] = nc.dram_tensor(k, list(shp), F32, kind="ExternalInput").ap()
    dr["out"] = nc.dram_tensor("out", [KC, 128, TOK], F32, kind="ExternalOutput").ap()
    dr["xs"] = nc.dram_tensor("xs", [KC, 128, TOK], F32).ap()
    dr["u_s"] = nc.dram_tensor("u_s", [KC, 128, HALO + TOK], BF16).ap()
    dr["halo_src"] = nc.dram_tensor("halo_src", [128, KC * HALO], BF16).ap()
    dr["halo_g"] = nc.dram_tensor("halo_g", [4 * 128, KC * HALO], BF16).ap()
    dr["kv_s"] = nc.dram_tensor("kv_s", [4, 128, 24, 512], BF16).ap()
    dr["a_s"] = nc.dram_tensor("a_s", [4, 16, 512], BF16).ap()
    dr["gd_src"] = nc.dram_tensor("gd_src", [128, 64], F32).ap()
    dr["gd_g"] = nc.dram_tensor("gd_g", [4 * 128, 64], F32).ap()
    dr["gsA_src"] = nc.dram_tensor("gsA_src", [128, 2048], F32).ap()
    dr["gsA_g"] = nc.dram_tensor("gsA_g", [4 * 128, 2048], F32).ap()
    dr["gsB_src"] = nc.dram_tensor("gsB_src", [128, 2048], F32).ap()
    dr["gsB_g"] = nc.dram_tensor("gsB_g", [4 * 128, 2048], F32).ap()
    with contextlib.ExitStack() as es:
        b = Builder(nc, es, dr)
        b.epsc = es.enter_context(nc.sbuf_tensor("epsc", [128, 1], F32))
        b.s.op("dve", lambda e: e.memset(b.epsc[:], EPS), writes=["epsc"])
        layers = sorted(set(L for _, L in steps))
        b.prologue_mod(layers[:1])
        resid = [i for i, (k, _) in enumerate(steps) if k in ("convB", "glaB", "ffn")]
        cur, cur_name = dr["xT"], "xT"
        toks = []
        for i, (kind, L) in enumerate(steps):
            j = L // 2
            if resid and i == resid[-1]:
                nxt, nxt_name = dr["out"], "out"
            else:
                nxt, nxt_name = dr["xs"], "xs"
            if kind == "ffn":
                nl = layers[layers.index(L) + 1] if layers.index(L) + 1 < len(layers) else None
                toks = b.ffn(L, cur, cur_name, nxt, nxt_name, next_layer=nl)
            elif kind == "convA":
                b.conv_a(L, j, cur, cur_name)
            elif kind == "convB":
                toks = b.conv_b(L, j, cur, cur_name, nxt, nxt_name)
            elif kind == "glaA":
                b.gla(L, j, cur, cur_name, None, None, True)
            elif kind == "glaB":
                toks = b.gla(L, j, cur, cur_name, nxt, nxt_name, False)
            if i in resid:
                cur, cur_name = nxt, nxt_name
        b.s.emit(final_wait_tokens=toks)
    return nc


def run(inp, mode, trace=False):
    inp = {k: np.asarray(v) for k, v in inp.items()}
    steps = MODES[mode]
    W = weight_arrays(inp, steps)
    consts = host_consts()
    maps = []
    vec_off = None
    for core in range(NCORE):
        b, seg = core // 4, core % 4
        xT = np.ascontiguousarray(inp["x"][b, seg * TOK:(seg + 1) * TOK, :].T).reshape(KC, 128, TOK)
        vecs, vec_off = build_vecs(inp, b, seg)
        m = {"xT": xT, "vecs": vecs, "consts": consts}
        m.update(W)
        maps.append(m)
    nc = build_program(vec_off, maps[0]["vecs"].shape[1], steps, {k: v.shape for k, v in W.items()})
    res = run_bass_kernel_spmd(nc, maps, core_ids=list(range(NCORE)), trace=trace)
    out = np.empty((2, SEQ, D), np.float32)
    for core in range(NCORE):
        b, seg = core // 4, core % 4
        o = res.results[core]["out"].reshape(D, TOK)
        out[b, seg * TOK:(seg + 1) * TOK, :] = o.T
    return out, res


def kernel(**inputs):
    out, _ = run(inputs, "full")
    return out
```

```python
import contextlib
import numpy as np
import concourse.bass as bass
import concourse.mybir as mybir
from concourse.bass_utils import run_bass_kernel_spmd

F32 = mybir.dt.float32
BF16 = mybir.dt.bfloat16
AF = mybir.ActivationFunctionType
ALU = mybir.AluOpType

D = 2048
KC = 16
SEQ = 8192
NCORE = 8
TOK = 2048
TH = 1024
DFF = 5632
JC = 44
DEPTH = 4
CW = 31
HALO = 32
EPS = 1e-6
DK = 1024
DV = 2048
NH = 4
GIN = 6160
NWB = 4


class Sched:
    ENG = ["pe", "act", "dve", "pool", "sp"]

    def __init__(self, nc):
        self.nc = nc
        self.ops = {e: [] for e in self.ENG}
        self.cnt = {e: 0 for e in self.ENG}
        self.seen = {e: {} for e in self.ENG}
        self.last_w = {}
        self.readers = {}
        self.dma_cnt = {}
        self.dma_keys = []

    def _add(self, eng, fn, reads, writes, tok_kind, dma_key=None):
        deps = []
        for r in reads:
            t = self.last_w.get(r)
            if t is not None:
                deps.append(t)
        for w in writes:
            t = self.last_w.get(w)
            if t is not None:
                deps.append(t)
            deps.extend(self.readers.get(w, ()))
        if tok_kind == "eng":
            self.cnt[eng] += 1
            tok = ("eng", eng, self.cnt[eng])
        else:
            if dma_key not in self.dma_cnt:
                self.dma_cnt[dma_key] = 0
                self.dma_keys.append(dma_key)
            self.dma_cnt[dma_key] += 16
            tok = ("dma", dma_key, self.dma_cnt[dma_key])
        need = {}
        for t in deps:
            if t[0] == "eng" and t[1] == eng and tok_kind == "eng" and eng == "pe":
                continue
            k = (t[0], t[1])
            if t[2] > need.get(k, 0):
                need[k] = t[2]
        waits = []
        seen = self.seen[eng]
        for k, v in need.items():
            if seen.get(k, 0) >= v:
                continue
            seen[k] = v
            waits.append((k, v))
        self.ops[eng].append((waits, fn, tok))
        for r in reads:
            self.readers.setdefault(r, []).append(tok)
        for w in writes:
            self.last_w[w] = tok
            self.readers[w] = []
        return tok

    def op(self, eng, fn, reads=(), writes=()):
        return self._add(eng, fn, reads, writes, "eng")

    def dma(self, fn, reads=(), writes=(), key=None, eng="sp"):
        return self._add(eng, fn, reads, writes, "dma", dma_key=key)

    def emit(self, final_wait_tokens=()):
        nc = self.nc
        with contextlib.ExitStack() as es:
            sems = {}
            for e in self.ENG:
                sems[("eng", e)] = es.enter_context(nc.semaphore("s_" + e))
            for i, k in enumerate(self.dma_keys):
                sems[("dma", k)] = es.enter_context(nc.semaphore("d%d" % i))
            block = es.enter_context(nc.Block())
            eng_map = {"pe": block.tensor, "act": block.scalar, "dve": block.vector,
                       "pool": block.gpsimd, "sp": block.sync}
            for e in self.ENG:
                ops = self.ops[e]
                extra = final_wait_tokens if e == "sp" else ()

                def body(eh, ops=ops, e=e, extra=extra):
                    for waits, fn, tok in ops:
                        for k, v in waits:
                            eh.wait_ge(sems[k], v)
                        ins = fn(eh)
                        if tok[0] == "eng":
                            ins.then_inc(sems[("eng", e)], 1)
                        else:
                            ins.then_inc(sems[("dma", tok[1])], 16)
                    for t in extra:
                        eh.wait_ge(sems[(t[0], t[1])], t[2])
                eng_map[e](body)


class Region:
    def __init__(self, nc, es, name, nbytes):
        self.name = name
        self.nbytes = nbytes
        self.t = es.enter_context(nc.sbuf_tensor(name, [128, nbytes // 4], F32))

    def view(self, off, dtype, shape):
        esz = 4 if dtype == F32 else 2
        n = 1
        for x in shape:
            n *= x
        assert off % 4 == 0 and (n * esz) % 4 == 0 and off + n * esz <= self.nbytes, (self.name, off, n, esz)
        a = self.t[:, off // 4:(off + n * esz) // 4]
        if dtype != F32:
            a = a.bitcast(dtype)
        if len(shape) == 2:
            a = a.rearrange("p (a b) -> p a b", a=shape[0])
        keys = [(self.name, pg) for pg in range(off // 1024, (off + n * esz + 1023) // 1024)]
        return a, keys


class Builder:
    def __init__(self, nc, es, dram):
        self.nc = nc
        self.es = es
        self.dr = dram
        self.s = Sched(nc)
        s = self.s
        self.R1 = Region(nc, es, "R1", 64 * 1024)
        self.R2 = Region(nc, es, "R2", 88 * 1024)
        self.R3 = Region(nc, es, "R3", 24 * 1024)
        self.wb = [es.enter_context(nc.sbuf_tensor("wb%d" % i, [128, 2048], BF16)) for i in range(NWB)]
        self.wi = 0
        self.P = [es.enter_context(nc.psum_tensor("P%d" % i, [128, 1024], F32)) for i in range(4)]
        self.ones = es.enter_context(nc.sbuf_tensor("ones", [128, 128], BF16))
        self.one32 = es.enter_context(nc.sbuf_tensor("one32", [128, 1], F32))
        self.vecs = es.enter_context(nc.sbuf_tensor("vecs_sb", [128, self.dr["vecs"].shape[1]], F32))
        self.cact = es.enter_context(nc.sbuf_tensor("cact", [128, 16], BF16))
        self.modrow = self.R3.t[0:1, 4096:6144]
        self.modc = es.enter_context(nc.sbuf_tensor("modc", [128, DEPTH * 96], F32))
        self.lay = es.enter_context(nc.sbuf_tensor("lay", [128, DEPTH * 96], F32))
        s.op("dve", lambda e: e.memset(self.ones[:], 1.0), writes=["ones"])
        s.op("dve", lambda e: e.memset(self.one32[:], 1.0), writes=["one32"])
        s.dma(lambda q: q.dma_start(out=self.vecs[:], in_=self.dr["vecs"]), writes=["vecs"], key="vecs")
        self.pending = []
        self.ident = es.enter_context(nc.sbuf_tensor("ident", [128, 128], BF16))
        self.tri4 = es.enter_context(nc.sbuf_tensor("tri4", [128, 512], F32))
        s.dma(lambda q: q.dma_start(out=self.ident[:], in_=self.dr["consts"][:, 0:128]), writes=["ident"],
              key="ident", eng="pool")
        s.dma(lambda q: q.dma_start(out=self.tri4[:], in_=self.dr["consts"][:, 128:640]), writes=["tri4"], key="tri4")
        self.tribf = es.enter_context(nc.sbuf_tensor("tribf", [128, 128], BF16))
        s.dma(lambda q: q.dma_start(out=self.tribf[:], in_=self.dr["consts"][:, 128:256]), writes=["tribf"],
              key="tribf", eng="pool")

    def pk(self, i, tt=None):
        if tt is None:
            return [("P", i, 0), ("P", i, 1)]
        return [("P", i, tt)]

    def load_w(self, src_ap, ncols):
        i = self.wi % NWB
        self.wi += 1
        t = self.wb[i]
        key = ("wb", i)
        self.s.dma(lambda q: q.dma_start(out=t[:, 0:ncols], in_=src_ap), writes=[key], key=key, eng="pool")
        return t, key

    def flush_pending(self):
        p = self.pending
        self.pending = []
        for f in p:
            f()

    def stats_mm(self, Pi, src_ap, src_keys, first, last):
        def f():
            def emit(e):
                ins = None
                for tt in range(2):
                    ins = e.matmul(self.P[Pi][:, tt * 512:(tt + 1) * 512], lhsT=self.ones[:],
                                   rhs=src_ap[:, tt * 512:(tt + 1) * 512], start=first, stop=last)
                return ins
            self.s.op("pe", emit, reads=list(src_keys) + ["ones"], writes=self.pk(Pi))
        self.pending.append(f)

    def vcol(self, name, c):
        o = self.dr["vec_off"][name]
        return self.vecs[:, o + c:o + c + 1]

    def prologue_mod(self, layers):
        s = self.s
        nc = self.nc
        dr = self.dr
        co = dr["vec_off"]["c"]
        s.op("act", lambda e: e.activation(out=self.cact[:], in_=self.vecs[:, co:co + 16], func=AF.Silu),
             reads=["vecs"], writes=["cact"])
        for i in layers:
            for nb in range(6):
                for kc in range(KC):
                    t, key = self.load_w(dr["w_ada%d" % i][kc * 128:(kc + 1) * 128, nb * 2048:(nb + 1) * 2048], 2048)

                    def emit(e, t=t, kc=kc):
                        ins = None
                        for q in range(4):
                            ins = e.matmul(self.P[q // 2][0:1, (q % 2) * 512:(q % 2 + 1) * 512],
                                           lhsT=self.cact[:, kc:kc + 1], rhs=t[:, q * 512:(q + 1) * 512],
                                           start=(kc == 0), stop=(kc == KC - 1))
                        return ins
                    s.op("pe", emit, reads=[key, "cact"], writes=self.pk(0) + self.pk(1))
                s.op("act", lambda e: e.activation(out=self.modrow[0:1, 0:1024], in_=self.P[0][0:1, :], func=AF.Identity),
                     reads=self.pk(0), writes=[("R3", pg) for pg in range(16, 20)])
                s.op("act", lambda e: e.activation(out=self.modrow[0:1, 1024:2048], in_=self.P[1][0:1, :], func=AF.Identity),
                     reads=self.pk(1), writes=[("R3", pg) for pg in range(20, 24)])

                def emit_t(e):
                    ins = None
                    for j in range(16):
                        ins = e.matmul(self.P[2][:, j:j + 1], lhsT=self.modrow[0:1, j * 128:(j + 1) * 128],
                                       rhs=self.one32[0:1, 0:1], start=True, stop=True)
                    return ins
                s.op("pe", emit_t, reads=[("R3", pg) for pg in range(16, 24)] + ["one32"], writes=self.pk(2))
                bo = dr["vec_off"]["b_ada%d" % i] + nb * 16
                s.op("dve", lambda e, i=i, nb=nb, bo=bo: e.tensor_tensor(
                    out=self.modc[:, i * 96 + nb * 16:i * 96 + nb * 16 + 16], in0=self.P[2][:, 0:16],
                    in1=self.vecs[:, bo:bo + 16], op=ALU.add),
                    reads=self.pk(2) + ["vecs"], writes=[("modc", i, nb)])
            m = lambda nb, i=i: self.modc[:, i * 96 + nb * 16:i * 96 + nb * 16 + 16]
            L = lambda k, i=i: self.lay[:, i * 96 + k * 16:i * 96 + k * 16 + 16]
            vo = dr["vec_off"]
            g = lambda nm, i=i, vo=vo: self.vecs[:, vo[nm % i]:vo[nm % i] + 16]
            rd = [("modc", i, nb) for nb in range(6)] + ["vecs"]
            wr = [("lay", i)]
            s.op("dve", lambda e, m=m, L=L, g=g: e.scalar_tensor_tensor(
                out=L(0), in0=m(1), scalar=1.0, in1=g("pre_mix_g%d"), op0=ALU.add, op1=ALU.mult), reads=rd, writes=wr)
            s.op("dve", lambda e, m=m, L=L: e.tensor_copy(out=L(1), in_=m(0)), reads=rd, writes=wr)
            s.op("dve", lambda e, m=m, L=L, g=g: e.tensor_tensor(
                out=L(2), in0=m(2), in1=g("post_mix_g%d"), op=ALU.mult), reads=rd, writes=wr)
            s.op("dve", lambda e, m=m, L=L, g=g: e.scalar_tensor_tensor(
                out=L(3), in0=m(4), scalar=1.0, in1=g("pre_ffn_g%d"), op0=ALU.add, op1=ALU.mult), reads=rd, writes=wr)
            s.op("dve", lambda e, m=m, L=L: e.tensor_copy(out=L(4), in_=m(3)), reads=rd, writes=wr)
            s.op("dve", lambda e, m=m, L=L, g=g: e.tensor_tensor(
                out=L(5), in0=m(5), in1=g("post_ffn_g%d"), op=ALU.mult), reads=rd, writes=wr)

    def mod_units(self, i):
        s = self.s
        dr = self.dr
        units = []
        for nb in range(6):
            for kc in range(KC):
                def unit(nb=nb, kc=kc):
                    t, key = self.load_w(dr["w_ada%d" % i][kc * 128:(kc + 1) * 128, nb * 2048:(nb + 1) * 2048], 2048)

                    def emit(e):
                        ins = None
                        for si in range(16):
                            ins = e.matmul(self.P[2][:, si:si + 1], lhsT=t[:, si * 128:(si + 1) * 128],
                                           rhs=self.cact[:, kc:kc + 1], start=(kc == 0 and si == 0),
                                           stop=(kc == KC - 1), skip_group_check=True)
                        return ins
                    s.op("pe", emit, reads=[key, "cact"], writes=self.pk(2, 0))
                    if kc == KC - 1:
                        bo = dr["vec_off"]["b_ada%d" % i] + nb * 16
                        s.op("dve", lambda e: e.tensor_tensor(
                            out=self.modc[:, i * 96 + nb * 16:i * 96 + nb * 16 + 16], in0=self.P[2][:, 0:16],
                            in1=self.vecs[:, bo:bo + 16], op=ALU.add),
                            reads=self.pk(2, 0) + ["vecs"], writes=[("modc", i, nb)])
                units.append(unit)
        units.append(lambda: self.mod_derive(i))
        return units

    def mod_derive(self, i):
        s = self.s
        dr = self.dr
        m = lambda nb, i=i: self.modc[:, i * 96 + nb * 16:i * 96 + nb * 16 + 16]
        L = lambda k, i=i: self.lay[:, i * 96 + k * 16:i * 96 + k * 16 + 16]
        vo = dr["vec_off"]
        g = lambda nm, i=i, vo=vo: self.vecs[:, vo[nm % i]:vo[nm % i] + 16]
        rd = [("modc", i, nb) for nb in range(6)] + ["vecs"]
        wr = [("lay", i)]
        s.op("dve", lambda e: e.scalar_tensor_tensor(
            out=L(0), in0=m(1), scalar=1.0, in1=g("pre_mix_g%d"), op0=ALU.add, op1=ALU.mult), reads=rd, writes=wr)
        s.op("dve", lambda e: e.tensor_copy(out=L(1), in_=m(0)), reads=rd, writes=wr)
        s.op("dve", lambda e: e.tensor_tensor(out=L(2), in0=m(2), in1=g("post_mix_g%d"), op=ALU.mult), reads=rd, writes=wr)
        s.op("dve", lambda e: e.scalar_tensor_tensor(
            out=L(3), in0=m(4), scalar=1.0, in1=g("pre_ffn_g%d"), op0=ALU.add, op1=ALU.mult), reads=rd, writes=wr)
        s.op("dve", lambda e: e.tensor_copy(out=L(4), in_=m(3)), reads=rd, writes=wr)
        s.op("dve", lambda e: e.tensor_tensor(out=L(5), in0=m(5), in1=g("post_ffn_g%d"), op=ALU.mult), reads=rd, writes=wr)

    def lcol(self, i, k, c):
        o = i * 96 + k * 16 + c
        return self.lay[:, o:o + 1]

    def hT(self, c, tt=None):
        if tt is None:
            return self.R1.view(c * 2048, BF16, (1024,))
        return self.R1.view(c * 2048 + tt * 1024, BF16, (512,))

    def prenorm(self, xin, xin_name, hf, layer, ka, kb):
        s = self.s
        xs = []
        for c in range(KC):
            a, keys = self.R2.view(c * 4096, F32, (1024,))
            xs.append((a, keys))
            s.dma(lambda q, a=a, c=c: q.dma_start(out=a, in_=xin[c, :, hf * TH:(hf + 1) * TH]),
                  reads=[(xin_name, c, hf)], writes=keys, key=("xst", c))
        for c in range(KC):
            a, keys = xs[c]
            sq, sqk = self.R3.view((c % 2) * 2048, BF16, (1024,))
            if c % 2 == 0:
                s.op("act", lambda e, a=a, sq=sq: e.activation(out=sq, in_=a, func=AF.Square), reads=keys, writes=sqk)
            else:
                s.op("dve", lambda e, a=a, sq=sq: e.tensor_tensor(out=sq, in0=a, in1=a, op=ALU.mult), reads=keys, writes=sqk)
            self.flush_pending()
            self.stats_mm(3, sq, sqk, c == 0, c == KC - 1)
        self.flush_pending()
        rs, rsk = self.R3.view(4096, F32, (1024,))
        s.op("act", lambda e: e.activation(out=rs, in_=self.P[3][:], func=AF.Ln, scale=1.0 / D, bias=self.epsc[:]),
             reads=self.pk(3) + ["epsc"], writes=rsk)
        s.op("act", lambda e: e.activation(out=rs, in_=rs, func=AF.Exp, scale=-0.5), reads=rsk, writes=rsk)
        for c in range(KC):
            a, keys = xs[c]
            tm, tmk = self.R3.view(8192 + (c % 2) * 4096, F32, (1024,))
            s.op("dve", lambda e, a=a, tm=tm: e.tensor_tensor(out=tm, in0=a, in1=rs, op=ALU.mult),
                 reads=keys + rsk, writes=tmk)
            h, hk = self.hT(c)
            s.op("act", lambda e, tm=tm, h=h, c=c: e.activation(
                out=h, in_=tm, func=AF.Identity, scale=self.lcol(layer, ka, c), bias=self.lcol(layer, kb, c)),
                reads=tmk + [("lay", layer)], writes=hk)

    def postnorm(self, yview, xin, xin_name, xout, xout_name, hf, layer, kg, stg_region, stg_off):
        s = self.s
        rs, rsk = self.R3.view(4096, F32, (1024,))
        s.op("act", lambda e: e.activation(out=rs, in_=self.P[3][:], func=AF.Ln, scale=1.0 / D, bias=self.epsc[:]),
             reads=self.pk(3) + ["epsc"], writes=rsk)
        s.op("act", lambda e: e.activation(out=rs, in_=rs, func=AF.Exp, scale=-0.5), reads=rsk, writes=rsk)
        toks = []
        for c in range(KC):
            xa, xk = stg_region.view(stg_off + c * 4096, F32, (1024,))
            s.dma(lambda q, xa=xa, c=c: q.dma_start(out=xa, in_=xin[c, :, hf * TH:(hf + 1) * TH]),
                  reads=[(xin_name, c, hf)], writes=xk, key=("xst2", c))
            y, yk = yview(c)
            s.op("dve", lambda e, y=y, c=c: e.scalar_tensor_tensor(
                out=y, in0=y, scalar=self.lcol(layer, kg, c), in1=rs, op0=ALU.mult, op1=ALU.mult),
                reads=yk + rsk + [("lay", layer)], writes=yk)
            s.op("dve", lambda e, y=y, xa=xa: e.tensor_tensor(out=xa, in0=y, in1=xa, op=ALU.add),
                 reads=yk + xk, writes=xk)
            t = s.dma(lambda q, xa=xa, c=c: q.dma_start(out=xout[c, :, hf * TH:(hf + 1) * TH], in_=xa),
                      reads=xk, writes=[(xout_name, c, hf)], key=("xo", c), eng="act")
            toks.append(t)
        return toks

    def proj(self, wt, n_oc, n_kg, kcb, in_view, psel, epilogue, oc_list=None, tiles=((0, 0), (512, 512)), m=128, after_block=None):
        s = self.s
        for oi, oc in enumerate(oc_list if oc_list is not None else range(n_oc)):
            Pi = psel(oi)
            pkeys = []
            for (_, po) in tiles:
                pkeys += self.pk(Pi, po // 512)
            for kg in range(n_kg):
                t, key = self.load_w(wt[oc, kg], kcb * m)
                ins_ = [in_view(kg * kcb + k) for k in range(kcb)]
                rk = [key]
                for a, k_ in ins_:
                    rk += k_

                def emit(e, t=t, kg=kg, ins_=ins_, Pi=Pi):
                    ins = None
                    for k in range(kcb):
                        for (io, po) in tiles:
                            ins = e.matmul(self.P[Pi][0:m, po:po + 512], lhsT=t[:, k * m:(k + 1) * m],
                                           rhs=ins_[k][0][:, io:io + 512],
                                           start=(kg == 0 and k == 0), stop=(kg == n_kg - 1 and k == kcb - 1))
                    return ins
                if oi == 0 and kg == 0 and kcb > 1:
                    for k in range(kcb):
                        def emit1(e, t=t, k=k, ins_=ins_, Pi=Pi):
                            ins = None
                            for (io, po) in tiles:
                                ins = e.matmul(self.P[Pi][0:m, po:po + 512], lhsT=t[:, k * m:(k + 1) * m],
                                               rhs=ins_[k][0][:, io:io + 512],
                                               start=(k == 0), stop=(n_kg == 1 and k == kcb - 1))
                            return ins
                        s.op("pe", emit1, reads=[key] + ins_[k][1], writes=pkeys)
                else:
                    s.op("pe", emit, reads=rk, writes=pkeys)
                if after_block is not None:
                    after_block()
            epilogue(oi, oc, Pi)

    def ffn(self, layer, xin, xin_name, xout, xout_name, next_layer=None):
        s = self.s
        dr = self.dr
        toks = []
        units = self.mod_units(next_layer) if next_layer is not None else []

        def inject():
            if units:
                units.pop(0)()
        for hf in range(2):
            self.prenorm(xin, xin_name, hf, layer, 3, 4)
            hid = lambda j: self.R2.view(j * 2048, BF16, (1024,))
            w1 = dr["ffn_w_in_%d" % layer]
            for j in range(JC):
                Pg, Pu = (0, 1) if j % 2 == 0 else (2, 3)
                self.proj(w1, None, 1, KC, self.hT, lambda oi, Pg=Pg, Pu=Pu: (Pg, Pu)[oi], lambda *a: None,
                          oc_list=[j, JC + j])
                sg, sgk = self.R3.view(8192 + (j % 2) * 4096, F32, (1024,))
                s.op("act", lambda e, sg=sg, Pg=Pg: e.activation(out=sg, in_=self.P[Pg][:], func=AF.Silu),
                     reads=self.pk(Pg), writes=sgk)
                h, hk = hid(j)
                s.op("dve", lambda e, sg=sg, h=h, Pu=Pu: e.tensor_tensor(out=h, in0=sg, in1=self.P[Pu][:], op=ALU.mult),
                     reads=sgk + self.pk(Pu), writes=hk)
            yv = lambda c: self.R1.view(c * 4096, F32, (1024,))

            def epi(oi, oc, Pi):
                y, yk = yv(oc)
                s.op("act", lambda e: e.activation(out=y, in_=self.P[Pi][:], func=AF.Identity),
                     reads=self.pk(Pi), writes=yk)
                sq, sqk = self.R3.view((oc % 2) * 2048, BF16, (1024,))
                s.op("dve", lambda e: e.tensor_tensor(out=sq, in0=self.P[Pi][:], in1=y, op=ALU.mult),
                     reads=self.pk(Pi) + yk, writes=sqk)
                self.flush_pending()
                self.stats_mm(3, sq, sqk, oc == 0, oc == KC - 1)
            self.proj(dr["ffn_w_out_%d" % layer], KC, 4, 11, hid, lambda oi: oi % 2, epi, after_block=inject)
            self.flush_pending()
            toks += self.postnorm(yv, xin, xin_name, xout, xout_name, hf, layer, 5, self.R2, 0)
        while units:
            units.pop(0)()
        return toks


    def allgather(self, src, dst, src_keys, dst_keys):
        self.s.op("pool", lambda e: e.collective_compute(
            "AllGather", ALU.bypass, replica_groups=[[0, 1, 2, 3], [4, 5, 6, 7]],
            ins=[src.opt()], outs=[dst.opt()]), reads=list(src_keys) + ["cc_chain"], writes=list(dst_keys) + ["cc_chain"])

    def conv_a(self, layer, j, xin, xin_name):
        s = self.s
        dr = self.dr
        vo = dr["vec_off"]["b_pw1_%d" % j]
        for hf in range(2):
            self.prenorm(xin, xin_name, hf, layer, 0, 1)
            for c in range(KC):
                Pa, Pg = (0, 1) if c % 2 == 0 else (2, 3)
                self.proj(dr["conv_w_pw1_%d" % j], None, 1, KC, self.hT, lambda oi, Pa=Pa, Pg=Pg: (Pa, Pg)[oi],
                          lambda *a: None, oc_list=[c, KC + c])
                sg, sgk = self.R3.view(8192 + (c % 2) * 4096, F32, (1024,))
                s.op("act", lambda e, sg=sg, Pg=Pg, c=c: e.activation(
                    out=sg, in_=self.P[Pg][:], func=AF.Sigmoid, bias=self.vecs[:, vo + KC + c:vo + KC + c + 1]),
                    reads=self.pk(Pg) + ["vecs"], writes=sgk)
                u, uk = self.R2.view(65536 + (c % 3) * 2048, BF16, (1024,))
                s.op("dve", lambda e, sg=sg, u=u, Pa=Pa, c=c: e.scalar_tensor_tensor(
                    out=u, in0=self.P[Pa][:], scalar=self.vecs[:, vo + c:vo + c + 1], in1=sg, op0=ALU.add, op1=ALU.mult),
                    reads=sgk + self.pk(Pa) + ["vecs"], writes=uk)
                s.dma(lambda q, u=u, c=c, hf=hf: q.dma_start(
                    out=dr["u_s"][c, :, HALO + hf * TH:HALO + (hf + 1) * TH], in_=u),
                    reads=uk, writes=[("u_s", c, hf)], key=("uo", c % 3), eng="act")
                if hf == 1:
                    s.dma(lambda q, u=u, c=c: q.dma_start(out=dr["halo_src"][:, c * HALO:(c + 1) * HALO], in_=u[:, TH - HALO:TH]),
                          reads=uk, writes=[("halo_src", c)], key=("ho", c % 3), eng="act")
        self.allgather(dr["halo_src"], dr["halo_g"], [("halo_src", c) for c in range(KC)], ["halo_g"])

    def conv_b(self, layer, j, xin, xin_name, xout, xout_name):
        s = self.s
        dr = self.dr
        vo = dr["vec_off"]
        toks = []
        G, Gk = self.R3.view(16384, BF16, (4, KC * HALO))
        s.dma(lambda q: q.dma_start(out=G, in_=dr["halo_g"].rearrange("(r p) k -> p r k", p=128)),
              reads=["halo_g"], writes=Gk, key="halo_ld")
        hal, halk = self.R3.view(16384 + 4096, BF16, (KC * HALO,))
        mo = vo["mprev"]
        s.op("dve", lambda e: e.tensor_scalar(out=hal, in0=G[:, 0, :], scalar1=self.vecs[:, mo:mo + 1], scalar2=None,
                                               op0=ALU.mult), reads=Gk + ["vecs"], writes=halk)
        for r in range(1, 4):
            s.op("dve", lambda e, r=r: e.scalar_tensor_tensor(
                out=hal, in0=G[:, r, :], scalar=self.vecs[:, mo + r:mo + r + 1], in1=hal, op0=ALU.mult, op1=ALU.add),
                reads=Gk + halk + ["vecs"], writes=halk)
        UW = HALO + TH
        for hf in range(2):
            Us = []
            for c in range(KC):
                U, Uk = self.R2.view(c * UW * 2, BF16, (UW,))
                Us.append((U, Uk))
                s.dma(lambda q, U=U, c=c, hf=hf: q.dma_start(out=U, in_=dr["u_s"][c, :, hf * TH:hf * TH + UW]),
                      reads=[("u_s", c, 0), ("u_s", c, 1)], writes=Uk, key=("Uld", c))
                if hf == 0:
                    s.op("dve", lambda e, U=U, c=c: e.tensor_copy(out=U[:, 0:HALO], in_=hal[:, c * HALO:(c + 1) * HALO]),
                         reads=halk + Uk, writes=Uk)
            vv = lambda c: self.R1.view(c * 4096, F32, (1024,))
            wo = vo["w_dw_%d" % j]
            bo = vo["b_dw_%d" % j]
            def build_taps(c):
                dg, dgk = self.R2.view(33792 + (c % 2) * 7936, BF16, (CW, 128))
                for k in range(CW):
                    rd = ["ident", "vecs"] + ([("dggate", c % 2)] if k > 0 else [])
                    wr = [("dgtap", c % 2, k)] + (dgk + [("dggate", c % 2)] if k == 0 else [])
                    if k % 2 == 0:
                        s.op("dve", lambda e, dg=dg, k=k, c=c: e.tensor_scalar(
                            out=dg[:, k, :], in0=self.ident[:], scalar1=self.vecs[:, wo + k * KC + c:wo + k * KC + c + 1],
                            scalar2=None, op0=ALU.mult), reads=rd, writes=wr)
                    else:
                        s.op("act", lambda e, dg=dg, k=k, c=c: e.activation(
                            out=dg[:, k, :], in_=self.ident[:], func=AF.Identity,
                            scale=self.vecs[:, wo + k * KC + c:wo + k * KC + c + 1]), reads=rd, writes=wr)
            build_taps(0)
            for c in range(KC):
                dg, dgk = self.R2.view(33792 + (c % 2) * 7936, BF16, (CW, 128))
                Pi = c % 2
                U, Uk = Us[c]

                def emit(e, dg=dg, U=U, Pi=Pi):
                    ins = None
                    for tt in range(2):
                        for k in range(CW):
                            ins = e.matmul(self.P[Pi][:, tt * 512:(tt + 1) * 512], lhsT=dg[:, k, :],
                                           rhs=U[:, tt * 512 + k + 2:tt * 512 + k + 2 + 512],
                                           start=(k == 0), stop=(k == CW - 1))
                    return ins
                s.op("pe", emit, reads=[("dgtap", c % 2, k) for k in range(CW)] + dgk + [("dggate", c % 2)] + Uk,
                     writes=self.pk(Pi))
                if c + 1 < KC:
                    build_taps(c + 1)
                v, vk = vv(c)
                s.op("act", lambda e, v=v, Pi=Pi, c=c: e.activation(
                    out=v, in_=self.P[Pi][:], func=AF.Identity, bias=self.vecs[:, bo + c:bo + c + 1]),
                    reads=self.pk(Pi) + ["vecs"], writes=vk)
                sq, sqk = self.R3.view((c % 2) * 2048, BF16, (1024,))
                vb, vbk = self.R3.view(4096 + (c % 2) * 2048, BF16, (1024,))
                s.op("dve", lambda e, v=v, vb=vb: e.tensor_copy(out=vb, in_=v), reads=vk, writes=vbk)
                s.op("dve", lambda e, v=v, sq=sq: e.tensor_tensor(out=sq, in0=v, in1=v, op=ALU.mult), reads=vk, writes=sqk)
                self.flush_pending()
                self.stats_mm(2, vb, vbk, c == 0, c == KC - 1)
                self.stats_mm(3, sq, sqk, c == 0, c == KC - 1)
            self.flush_pending()
            mean, mk = self.R3.view(8192, F32, (1024,))
            rstd, rk = self.R3.view(12288, F32, (1024,))
            s.op("act", lambda e: e.activation(out=mean, in_=self.P[2][:], func=AF.Identity, scale=1.0 / D),
                 reads=self.pk(2), writes=mk)
            s.op("dve", lambda e: e.tensor_tensor(out=rstd, in0=mean, in1=mean, op=ALU.mult), reads=mk, writes=rk)
            s.op("dve", lambda e: e.scalar_tensor_tensor(out=rstd, in0=self.P[3][:], scalar=1.0 / D, in1=rstd,
                                                          op0=ALU.mult, op1=ALU.subtract), reads=self.pk(3) + rk, writes=rk)
            s.op("act", lambda e: e.activation(out=rstd, in_=rstd, func=AF.Ln, bias=self.epsc[:]),
                 reads=rk + ["epsc"], writes=rk)
            s.op("act", lambda e: e.activation(out=rstd, in_=rstd, func=AF.Exp, scale=-0.5), reads=rk, writes=rk)
            sv = lambda c: self.R2.view(49664 + c * 2048, BF16, (1024,))
            go, lbo = vo["ln_g_%d" % j], vo["ln_b_%d" % j]
            for c in range(KC):
                v, vk = vv(c)
                t1, t1k = self.R3.view(16384 + (c % 2) * 4096, F32, (1024,))
                s.op("dve", lambda e, v=v, t1=t1: e.tensor_tensor(out=t1, in0=v, in1=mean, op=ALU.subtract),
                     reads=vk + mk, writes=t1k)
                s.op("dve", lambda e, t1=t1: e.tensor_tensor(out=t1, in0=t1, in1=rstd, op=ALU.mult),
                     reads=t1k + rk, writes=t1k)
                sc, sck = sv(c)
                s.op("act", lambda e, t1=t1, sc=sc, c=c: e.activation(
                    out=sc, in_=t1, func=AF.Silu, scale=self.vecs[:, go + c:go + c + 1], bias=self.vecs[:, lbo + c:lbo + c + 1]),
                    reads=t1k + ["vecs"], writes=sck)
            yv = vv
            b2 = vo["b_pw2_%d" % j]

            def epi(oi, oc, Pi):
                y, yk = yv(oc)
                s.op("act", lambda e: e.activation(out=y, in_=self.P[Pi][:], func=AF.Identity,
                                                   bias=self.vecs[:, b2 + oc:b2 + oc + 1]),
                     reads=self.pk(Pi) + ["vecs"], writes=yk)
                sq, sqk = self.R3.view((oc % 2) * 2048, BF16, (1024,))
                s.op("dve", lambda e: e.tensor_tensor(out=sq, in0=y, in1=y, op=ALU.mult), reads=yk, writes=sqk)
                self.flush_pending()
                self.stats_mm(3, sq, sqk, oc == 0, oc == KC - 1)
            self.proj(dr["conv_w_pw2_%d" % j], KC, 1, KC, sv, lambda oi: oi % 2, epi)
            self.flush_pending()
            toks += self.postnorm(yv, xin, xin_name, xout, xout_name, hf, layer, 2, self.R2, 0)
        return toks

    def rows(self, region, off, dtype, n, nrows):
        esz = 4 if dtype == F32 else 2
        a = region.t[0:nrows, off // 4:(off + n * esz) // 4]
        if dtype != F32:
            a = a.bitcast(dtype)
        keys = [(region.name, pg) for pg in range(off // 1024, (off + n * esz + 1023) // 1024)]
        return a, keys

    def gla(self, layer, j, xin, xin_name, xout, xout_name, state_only):
        s = self.s
        dr = self.dr
        vo = dr["vec_off"]
        R1, R2, R3 = self.R1, self.R2, self.R3
        toks = []
        B = not state_only
        wup, wupk = self.rows(R3, 16384, BF16, 1024, 17)
        aT, aTk = self.rows(R3, 18944, BF16, 512, 17)
        small, smk = R3.view(18432, F32, (8 * 8,))
        bl, ebl, Bsum, Dj, Dp = [small[:, i * 8:(i + 1) * 8] for i in range(5)]
        blk_, eblk_, Bsk, Djk, Dpk = [[("gsm", i)] for i in range(5)]
        s.dma(lambda q: q.dma_start(out=wup, in_=dr["gla_wup_%d" % j]), writes=wupk, key="wup", eng="pool")
        s.op("dve", lambda e: e.memset(aT, 1.0), writes=aTk)
        S32v = lambda dc: R2.view(65536 + dc * 2048, F32, (512,))
        Sbfv = lambda dc: R2.view(81920 + dc * 1024, BF16, (512,))
        Sall, Sallk = R2.view(65536, F32, (4096,))
        Sball, Sballk = R2.view(81920, BF16, (4096,))
        def init_state():
            s.op("dve", lambda e: e.memset(Sall, 0.0), writes=Sallk)
            if state_only:
                s.op("dve", lambda e: e.memset(Bsum, 0.0), writes=Bsk)
            else:
                mo = vo["mlt"]
                for jr in range(3):
                    s.dma(lambda q, jr=jr: q.dma_start(out=Dj, in_=dr["gd_g"][jr * 128:(jr + 1) * 128, 0:8]),
                          reads=["gd_g"], writes=Djk, key="Dj")
                    s.op("dve", lambda e, jr=jr: e.tensor_scalar(out=Dp, in0=Dj, scalar1=-1.0, scalar2=self.vecs[:, mo + jr:mo + jr + 1],
                                                                  op0=ALU.add, op1=ALU.mult), reads=Djk + ["vecs"], writes=Dpk)
                    s.op("dve", lambda e: e.tensor_scalar(out=Dp, in0=Dp, scalar1=1.0, scalar2=None, op0=ALU.add),
                         reads=Dpk, writes=Dpk)
                    for dc in range(8):
                        stg, stgk = R3.view(20480 + (dc % 2) * 2048, F32, (512,))
                        s.dma(lambda q, jr=jr, dc=dc, stg=stg: q.dma_start(
                            out=stg, in_=dr["gsA_g" if dc < 4 else "gsB_g"][jr * 128:(jr + 1) * 128, (dc % 4) * 512:(dc % 4 + 1) * 512]),
                            reads=["gsA_g" if dc < 4 else "gsB_g"], writes=stgk, key=("stg", dc % 2))
                        s.op("dve", lambda e, stg=stg, jr=jr: e.tensor_scalar(
                            out=stg, in0=stg, scalar1=self.vecs[:, mo + jr:mo + jr + 1], scalar2=None, op0=ALU.mult),
                            reads=stgk + ["vecs"], writes=stgk)
                        S, Sk = S32v(dc)
                        s.op("dve", lambda e, S=S, stg=stg, dc=dc: e.scalar_tensor_tensor(
                            out=S, in0=S, scalar=Dp[:, dc:dc + 1], in1=stg, op0=ALU.mult, op1=ALU.add),
                            reads=Sk + stgk + Dpk, writes=Sk)
                s.op("act", lambda e: e.activation(out=Sball, in_=Sall, func=AF.Identity), reads=Sallk, writes=Sballk)


        init_done = [False]

        qTv = lambda c: R2.view(c * 1024, BF16, (512,))
        kTv = lambda c: R2.view(8192 + c * 1024, BF16, (512,))
        vTv = lambda c: R2.view(16384 + c * 1024, BF16, (512,))
        qT3, _ = R2.view(0, BF16, (8, 512))
        kT3, _ = R2.view(8192, BF16, (8, 512))
        qTk = R2.view(0, BF16, (4096,))[1]
        kTk = R2.view(8192, BF16, (4096,))[1]
        vTk = R2.view(16384, BF16, (8192,))[1]
        gpos, gpk = R2.view(32768, F32, (1024,))
        e1, e1k = R2.view(36864, F32, (1024,))
        tE = [R2.view(40960, F32, (8, 128)), R2.view(45056, F32, (8, 128))]
        tEf = [R2.view(40960, F32, (1024,)), R2.view(45056, F32, (1024,))]
        qd, qdk = R2.view(49152, BF16, (8, 128))
        kd, kdk = R2.view(51200, BF16, (8, 128))
        kh, khk = R2.view(53248, BF16, (8, 128))
        vtok, vtk = R2.view(55296, BF16, (2048,))
        ktok, ktk = R2.view(59392, BF16, (1024,))
        sc, sck = R2.view(61440, BF16, (512,))
        osq, osqk = R3.view(0, BF16, (2048,))
        rh, rhk = R3.view(4096, F32, (512,))
        onT = lambda c: R1.view(32768 + c * 2048, BF16, (1024,))
        P = self.P
        P1v = P[1][:].rearrange("p (a b) -> p a b", a=8)
        P2bf = P[2][:].bitcast(BF16)
        P3bf = P[3][:].bitcast(BF16)
        ngo = vo["gla_ng_%d" % j]
        cnt = [0]

        def alt():
            cnt[0] += 1
            return "act" if cnt[0] % 2 else "dve"

        def copy_op(eng, out, in_, reads, writes, scale=None):
            if eng == "act":
                if scale is None:
                    s.op("act", lambda e: e.activation(out=out, in_=in_, func=AF.Identity), reads=reads, writes=writes)
                else:
                    s.op("act", lambda e: e.activation(out=out, in_=in_, func=AF.Identity, scale=scale), reads=reads, writes=writes)
            else:
                if scale is None:
                    s.op("dve", lambda e: e.tensor_copy(out=out, in_=in_), reads=reads, writes=writes)
                else:
                    s.op("dve", lambda e: e.tensor_scalar(out=out, in0=in_, scalar1=scale, scalar2=None, op0=ALU.mult),
                         reads=reads, writes=writes)

        for hf in range(2):
            hall, hallk = R1.view(0, BF16, (KC * TH,))
            if state_only:
                self.prenorm(xin, xin_name, hf, layer, 0, 1)
                s.dma(lambda q, hf=hf: q.dma_start(out=dr["h_s"][hf], in_=hall), reads=hallk, writes=[("h_s", hf)],
                      key="hso", eng="act")
            else:
                s.dma(lambda q, hf=hf: q.dma_start(out=hall, in_=dr["h_s"][hf]), reads=[("h_s", hf)], writes=hallk, key="hsi")
            for qt in range(2):
                tl = ((qt * 512, 0),)
                def epi_qkv(oi, oc, Pi):
                    if oc < 8:
                        o_, k_ = qTv(oc)
                        copy_op(alt(), o_, P[Pi][:, 0:512], self.pk(Pi, 0), k_, scale=0.0625)
                    elif oc < 16:
                        o_, k_ = kTv(oc - 8)
                        copy_op(alt(), o_, P[Pi][:, 0:512], self.pk(Pi, 0), k_)
                    else:
                        o_, k_ = vTv(oc - 16)
                        copy_op(alt(), o_, P[Pi][:, 0:512], self.pk(Pi, 0), k_)
                qi = hf * 2 + qt
                kv3, _ = R2.view(8192, BF16, (24, 512))
                if state_only:
                    def epi_a(oi, oc, Pi):
                        s.op("act", lambda e: e.activation(out=aT[0:16, :], in_=P[Pi][0:16, 0:512], func=AF.Identity),
                             reads=self.pk(Pi, 0), writes=aTk)
                    self.proj(dr["gla_wa_%d" % j], 1, 1, KC, self.hT, lambda oi: 0, epi_a, tiles=tl, m=16)
                    self.proj(dr["gla_w_in_%d" % j], None, 1, KC, self.hT, lambda oi: 1 + oi % 3, epi_qkv,
                              oc_list=list(range(8, 32)), tiles=tl)
                    s.dma(lambda q, qi=qi: q.dma_start(out=dr["kv_s"][qi], in_=kv3), reads=kTk + vTk,
                          writes=[("kv_s", qi)], key="kvo", eng="act")
                    s.dma(lambda q, qi=qi: q.dma_start(out=dr["a_s"][qi], in_=aT[0:16, :]), reads=aTk,
                          writes=[("a_s", qi)], key="ao", eng="act")
                else:
                    s.dma(lambda q, qi=qi: q.dma_start(out=kv3, in_=dr["kv_s"][qi]), reads=[("kv_s", qi)],
                          writes=kTk + vTk, key="kvi")
                    s.dma(lambda q, qi=qi: q.dma_start(out=aT[0:16, :], in_=dr["a_s"][qi]), reads=[("a_s", qi)],
                          writes=aTk, key="ai")
                    self.proj(dr["gla_w_in_%d" % j], None, 1, KC, self.hT, lambda oi: 1 + oi % 3, epi_qkv,
                              oc_list=list(range(0, 8)), tiles=tl)
                if not init_done[0]:
                    init_done[0] = True
                    init_state()
                for ch in range(4):
                    c0 = ch * 128
                    tokoff = qt * 512 + c0

                    def state_mm():
                        for dc in range(8):
                            h = dc // 2
                            s.op("pe", lambda e, dc=dc, h=h: e.matmul(P[3][:, (dc % 2) * 512:(dc % 2 + 1) * 512],
                                                                     lhsT=ktok[:, dc * 128:(dc + 1) * 128],
                                                                     rhs=vtok[:, h * 512:(h + 1) * 512], start=True, stop=True),
                                 reads=ktk + vtk, writes=self.pk(3, dc % 2))
                            S, Sk = S32v(dc)
                            s.op("dve", lambda e, S=S, dc=dc: e.scalar_tensor_tensor(
                                out=S, in0=S, scalar=ebl[:, dc:dc + 1], in1=P[3][:, (dc % 2) * 512:(dc % 2 + 1) * 512],
                                op0=ALU.mult, op1=ALU.add), reads=Sk + eblk_ + self.pk(3, dc % 2), writes=Sk)

                    def state_cast():
                        for dc in range(8):
                            S, Sk = S32v(dc)
                            Sb, Sbk = Sbfv(dc)
                            s.op("act", lambda e, S=S, Sb=Sb: e.activation(out=Sb, in_=S, func=AF.Identity), reads=Sk, writes=Sbk)

                    def emit_z(e, c0=c0):
                        ins = None
                        for hh in range(2):
                            ins = e.matmul(P[0][:, hh * 512:(hh + 1) * 512], lhsT=aT[:, c0:c0 + 128],
                                           rhs=wup[:, hh * 512:(hh + 1) * 512], start=True, stop=True)
                        return ins
                    s.op("pe", emit_z, reads=aTk + wupk, writes=self.pk(0))
                    s.op("act", lambda e: e.activation(out=e1, in_=P[0][:], func=AF.Exp, scale=-1.0), reads=self.pk(0), writes=e1k)
                    s.op("act", lambda e: e.activation(out=gpos, in_=e1, func=AF.Ln, bias=self.one32[:]),
                         reads=e1k + ["one32"], writes=gpk)

                    ghi, ghk = R2.view(36864, BF16, (1024,))
                    glo, glk = R2.view(38912, BF16, (1024,))
                    s.op("dve", lambda e: e.tensor_copy(out=ghi, in_=gpos), reads=gpk, writes=ghk)
                    s.op("dve", lambda e: e.tensor_tensor(out=glo, in0=gpos, in1=ghi, op=ALU.subtract), reads=gpk + ghk, writes=glk)

                    def emit_cs(e):
                        ins = None
                        for dc in range(8):
                            e.matmul(P[1][:, dc * 128:(dc + 1) * 128], lhsT=ghi[:, dc * 128:(dc + 1) * 128],
                                     rhs=self.tribf[:], start=True, stop=False)
                            ins = e.matmul(P[1][:, dc * 128:(dc + 1) * 128], lhsT=glo[:, dc * 128:(dc + 1) * 128],
                                           rhs=self.tribf[:], start=False, stop=True)
                        return ins
                    s.op("pe", emit_cs, reads=ghk + glk + ["tribf"], writes=self.pk(1))
                    s.op("dve", lambda e: e.tensor_scalar(out=bl, in0=P1v[:, :, 127], scalar1=-0.0625, scalar2=None, op0=ALU.mult),
                         reads=self.pk(1), writes=blk_)
                    s.op("act", lambda e: e.activation(out=ebl, in_=bl, func=AF.Exp), reads=blk_, writes=eblk_)
                    if state_only:
                        s.op("dve", lambda e: e.tensor_tensor(out=Bsum, in0=Bsum, in1=bl, op=ALU.add), reads=Bsk + blk_, writes=Bsk)
                    (EK, EKk) = tE[0]

                    def emit_ek(e, EK=EK):
                        ins = None
                        for dc in range(8):
                            ins = e.activation(out=EK[:, dc, :], in_=P1v[:, dc, :], func=AF.Exp, scale=0.0625, bias=bl[:, dc:dc + 1])
                        return ins
                    s.op("act", emit_ek, reads=self.pk(1) + blk_, writes=EKk)
                    s.op("dve", lambda e, c0=c0, EK=EK: e.tensor_tensor(out=kh, in0=kT3[:, :, c0:c0 + 128], in1=EK, op=ALU.mult),
                         reads=kTk + EKk, writes=khk)
                    if B:
                        EBf, EBk = tEf[1]
                        s.op("act", lambda e, EBf=EBf: e.activation(out=EBf, in_=P[1][:], func=AF.Exp, scale=-0.0625),
                             reads=self.pk(1), writes=EBk)
                        s.op("dve", lambda e, c0=c0: e.tensor_tensor(out=qd, in0=qT3[:, :, c0:c0 + 128], in1=tE[1][0], op=ALU.mult),
                             reads=qTk + EBk, writes=qdk)
                        ENf, ENk = tEf[0]
                        s.op("act", lambda e, ENf=ENf: e.activation(out=ENf, in_=P[1][:], func=AF.Exp, scale=0.0625),
                             reads=self.pk(1), writes=ENk)
                        s.op("dve", lambda e, c0=c0: e.tensor_tensor(out=kd, in0=kT3[:, :, c0:c0 + 128], in1=tE[0][0], op=ALU.mult),
                             reads=kTk + ENk, writes=kdk)

                    def emit_tk(e):
                        ins = None
                        for dc in range(8):
                            ins = e.transpose(out=P2bf[:, dc * 128:(dc + 1) * 128], in_=kh[:, dc, :], identity=self.ident[:])
                        return ins
                    s.op("pe", emit_tk, reads=khk + ["ident"], writes=self.pk(2, 0))

                    def emit_tv(e, c0=c0):
                        ins = None
                        for c in range(16):
                            ins = e.transpose(out=P3bf[:, c * 128:(c + 1) * 128], in_=vTv(c)[0][:, c0:c0 + 128], identity=self.ident[:])
                        return ins
                    s.op("pe", emit_tv, reads=vTk + ["ident"], writes=self.pk(3))
                    s.op("act", lambda e: e.activation(out=ktok, in_=P2bf[:, 0:1024], func=AF.Identity), reads=self.pk(2, 0), writes=ktk)
                    s.op("dve", lambda e: e.tensor_copy(out=vtok, in_=P3bf[:, 0:2048]), reads=self.pk(3), writes=vtk)
                    if B:
                        def emit_sc(e):
                            ins = None
                            for h in range(4):
                                for d2 in range(2):
                                    dc = 2 * h + d2
                                    ins = e.matmul(P[2][:, 512 + h * 128:512 + (h + 1) * 128], lhsT=kd[:, dc, :], rhs=qd[:, dc, :],
                                                   start=(d2 == 0), stop=(d2 == 1))
                            return ins
                        s.op("pe", emit_sc, reads=kdk + qdk, writes=self.pk(2, 1))
                        s.op("dve", lambda e: e.tensor_tensor(out=sc, in0=P[2][:, 512:1024], in1=self.tri4[:], op=ALU.mult),
                             reads=self.pk(2, 1) + ["tri4"], writes=sck)

                        def emit_o(e):
                            ins = None
                            for h in range(4):
                                for es in range(4):
                                    blk = h * 4 + es
                                    out = P[blk // 8][:, (blk % 8) * 128:(blk % 8 + 1) * 128]
                                    e.matmul(out, lhsT=Sbfv(2 * h)[0][:, es * 128:(es + 1) * 128], rhs=qd[:, 2 * h, :], start=True, stop=False)
                                    e.matmul(out, lhsT=Sbfv(2 * h + 1)[0][:, es * 128:(es + 1) * 128], rhs=qd[:, 2 * h + 1, :], start=False, stop=False)
                                    ins = e.matmul(out, lhsT=vtok[:, h * 512 + es * 128:h * 512 + (es + 1) * 128],
                                                   rhs=sc[:, h * 128:(h + 1) * 128], start=False, stop=True)
                            return ins
                        s.op("pe", emit_o, reads=Sballk + qdk + vtk + sck, writes=self.pk(0) + self.pk(1))
                        state_mm()
                        s.op("act", lambda e: e.activation(out=osq[:, 0:1024], in_=P[0][:], func=AF.Square), reads=self.pk(0), writes=osqk)
                        s.op("act", lambda e: e.activation(out=osq[:, 1024:2048], in_=P[1][:], func=AF.Square), reads=self.pk(1), writes=osqk)

                        def emit_hs(e):
                            ins = None
                            for h in range(4):
                                for es in range(4):
                                    ins = e.matmul(P[2][:, h * 128:(h + 1) * 128], lhsT=self.ones[:],
                                                   rhs=osq[:, (h * 4 + es) * 128:(h * 4 + es + 1) * 128], start=(es == 0), stop=(es == 3))
                            return ins
                        s.op("pe", emit_hs, reads=osqk + ["ones"], writes=self.pk(2, 0))
                        s.op("act", lambda e: e.activation(out=rh, in_=P[2][:, 0:512], func=AF.Ln, scale=1.0 / 512, bias=self.epsc[:]),
                             reads=self.pk(2, 0) + ["epsc"], writes=rhk)
                        s.op("act", lambda e: e.activation(out=rh, in_=rh, func=AF.Exp, scale=-0.5), reads=rhk, writes=rhk)
                        for blk in range(16):
                            h = blk // 4
                            o_, ok_ = onT(blk)
                            s.op("dve", lambda e, blk=blk, h=h, o_=o_, tokoff=tokoff: e.scalar_tensor_tensor(
                                out=o_[:, tokoff:tokoff + 128], in0=P[blk // 8][:, (blk % 8) * 128:(blk % 8 + 1) * 128],
                                scalar=self.vecs[:, ngo + blk:ngo + blk + 1], in1=rh[:, h * 128:(h + 1) * 128],
                                op0=ALU.mult, op1=ALU.mult), reads=self.pk(blk // 8) + rhk + ["vecs"], writes=ok_)
                    if not B:
                        state_mm()
                    else:
                        state_cast()
            if B:
                def epi_r(oi, oc, Pi):
                    c = oc - 32
                    sr, srk = R3.view(8192 + (c % 2) * 4096, F32, (1024,))
                    s.op("act", lambda e: e.activation(out=sr, in_=P[Pi][:], func=AF.Silu), reads=self.pk(Pi), writes=srk)
                    o_, ok_ = onT(c)
                    s.op("dve", lambda e: e.tensor_tensor(out=o_, in0=o_, in1=sr, op=ALU.mult), reads=ok_ + srk, writes=ok_)
                self.proj(dr["gla_w_in_%d" % j], None, 1, KC, self.hT, lambda oi: oi % 2, epi_r, oc_list=list(range(32, 48)))
                yv = lambda c: R2.view(c * 4096, F32, (1024,))

                def epi_o(oi, oc, Pi):
                    y, yk = yv(oc)
                    s.op("act", lambda e: e.activation(out=y, in_=P[Pi][:], func=AF.Identity), reads=self.pk(Pi), writes=yk)
                    sq, sqk = R3.view((oc % 2) * 2048, BF16, (1024,))
                    s.op("dve", lambda e: e.tensor_tensor(out=sq, in0=P[Pi][:], in1=y, op=ALU.mult), reads=self.pk(Pi) + yk, writes=sqk)
                    self.flush_pending()
                    self.stats_mm(3, sq, sqk, oc == 0, oc == KC - 1)
                self.proj(dr["gla_w_out_%d" % j], KC, 1, KC, onT, lambda oi: oi % 2, epi_o)
                self.flush_pending()
                toks += self.postnorm(yv, xin, xin_name, xout, xout_name, hf, layer, 2, R1, 0)
        if state_only:
            s.op("act", lambda e: e.activation(out=Dj, in_=Bsum, func=AF.Exp), reads=Bsk, writes=Djk)
            s.dma(lambda q: q.dma_start(out=dr["gd_src"][:, 0:8], in_=Dj), reads=Djk, writes=["gd_src"], key="gdo", eng="act")
            s.dma(lambda q: q.dma_start(out=dr["gsA_src"], in_=Sall[:, 0:2048]), reads=Sallk, writes=["gsA_src"], key="gsoA", eng="act")
            s.dma(lambda q: q.dma_start(out=dr["gsB_src"], in_=Sall[:, 2048:4096]), reads=Sallk, writes=["gsB_src"], key="gsoB", eng="act")
            self.allgather(dr["gd_src"], dr["gd_g"], ["gd_src"], ["gd_g"])
            self.allgather(dr["gsA_src"], dr["gsA_g"], ["gsA_src"], ["gsA_g"])
            self.allgather(dr["gsB_src"], dr["gsB_g"], ["gsB_src"], ["gsB_g"])
        return toks


def tile_w(W, kcb, m=128):
    K, N = W.shape
    n_kg = K // (128 * kcb)
    a = W.reshape(n_kg, kcb, 128, N // m, m).transpose(3, 0, 2, 1, 4)
    return np.ascontiguousarray(a).reshape(N // m, n_kg, 128, kcb * m)


def colvec(v):
    return np.ascontiguousarray(v.reshape(-1, 128).T)


def build_vecs(inp, b, seg):
    cols = []
    off = {}

    def add(name, arr):
        off[name] = sum(a.shape[1] for a in cols)
        cols.append(np.asarray(arr, dtype=np.float32))
    add("c", colvec(inp["c"][b]))
    for i in range(DEPTH):
        add("b_ada%d" % i, colvec(inp["b_ada"][i]))
        for nm in ["pre_mix_g", "post_mix_g", "pre_ffn_g", "post_ffn_g"]:
            add(nm + "%d" % i, colvec(inp[nm][i]))
    for j in range(2):
        add("b_pw1_%d" % j, colvec(inp["conv_b_pw1"][j]))
        add("b_dw_%d" % j, colvec(inp["conv_b_dw"][j]))
        add("ln_g_%d" % j, colvec(inp["conv_ln_g"][j]))
        add("ln_b_%d" % j, colvec(inp["conv_ln_b"][j]))
        add("b_pw2_%d" % j, colvec(inp["conv_b_pw2"][j]))
        add("w_dw_%d" % j, colvec(inp["conv_w_dw"][j].reshape(-1)))
        add("gla_ng_%d" % j, colvec(inp["gla_norm_g"][j]))
    mprev = np.zeros((128, 4), np.float32)
    if seg > 0:
        mprev[:, seg - 1] = 1.0
    mlt = np.zeros((128, 4), np.float32)
    mlt[:, :seg] = 1.0
    add("mprev", mprev)
    add("mlt", mlt)
    return np.concatenate(cols, axis=1), off


FULL = [("convA", 0), ("convB", 0), ("ffn", 0), ("glaA", 1), ("glaB", 1), ("ffn", 1),
        ("convA", 2), ("convB", 2), ("ffn", 2), ("glaA", 3), ("glaB", 3), ("ffn", 3)]
MODES = {"full": FULL, "ffn0": [("ffn", 0)], "conv0": [("convA", 0), ("convB", 0)],
         "gla1": [("glaA", 1), ("glaB", 1)], "gla1A": [("glaA", 1)], "f0g1": [("ffn", 0), ("glaA", 1), ("glaB", 1)], "l0": FULL[:3], "l01": FULL[:6]}


def weight_arrays(inp, steps):
    w = {}
    for kind, L in steps:
        j = L // 2
        w["w_ada%d" % L] = lambda L=L: np.ascontiguousarray(inp["w_ada"][L])
        if kind == "ffn":
            w["ffn_w_in_%d" % L] = lambda L=L: tile_w(inp["ffn_w_in"][L], KC)
            w["ffn_w_out_%d" % L] = lambda L=L: tile_w(inp["ffn_w_out"][L], 11)
        elif kind == "convA":
            w["conv_w_pw1_%d" % j] = lambda j=j: tile_w(inp["conv_w_pw1"][j], KC)
        elif kind == "convB":
            w["conv_w_pw2_%d" % j] = lambda j=j: tile_w(inp["conv_w_pw2"][j], KC)
        elif kind in ("glaA", "glaB"):
            w["gla_w_in_%d" % j] = lambda j=j: tile_w(inp["gla_w_in"][j][:, :6144], KC)
            w["gla_wa_%d" % j] = lambda j=j: tile_w(inp["gla_w_in"][j][:, 6144:6160], KC, m=16)
            w["gla_wup_%d" % j] = lambda j=j: np.concatenate(
                [inp["gla_w_gate_up"][j], inp["gla_b_gate"][j][None, :]], axis=0).astype(np.float32)
            if kind == "glaB":
                w["gla_w_out_%d" % j] = lambda j=j: tile_w(inp["gla_w_out"][j], KC)
    return {k: f() for k, f in w.items()}


def host_consts():
    c = np.zeros((128, 640), np.float32)
    c[:, 0:128] = np.eye(128, dtype=np.float32)
    tri = np.triu(np.ones((128, 128), np.float32))
    c[:, 128:640] = np.tile(tri, (1, 4))
    return c


def build_program(vec_off, nvec, steps, wshapes):
    nc = bass.Bass("TRN2", target_bir_lowering=False)
    dr = {"vec_off": vec_off}
    dr["xT"] = nc.dram_tensor("xT", [KC, 128, TOK], F32, kind="ExternalInput").ap()
    dr["vecs"] = nc.dram_tensor("vecs", [128, nvec], F32, kind="ExternalInput").ap()
    dr["consts"] = nc.dram_tensor("consts", [128, 640], F32, kind="ExternalInput").ap()
    for k, shp in wshapes.items():
        dr[k] = nc.dram_tensor(k, list(shp), F32, kind="ExternalInput").ap()
    dr["out"] = nc.dram_tensor("out", [KC, 128, TOK], F32, kind="ExternalOutput").ap()
    dr["xs"] = nc.dram_tensor("xs", [KC, 128, TOK], F32).ap()
    dr["u_s"] = nc.dram_tensor("u_s", [KC, 128, HALO + TOK], BF16).ap()
    dr["halo_src"] = nc.dram_tensor("halo_src", [128, KC * HALO], BF16).ap()
    dr["halo_g"] = nc.dram_tensor("halo_g", [4 * 128, KC * HALO], BF16).ap()
    dr["kv_s"] = nc.dram_tensor("kv_s", [4, 128, 24, 512], BF16).ap()
    dr["a_s"] = nc.dram_tensor("a_s", [4, 16, 512], BF16).ap()
    dr["h_s"] = nc.dram_tensor("h_s", [2, 128, KC * TH], BF16).ap()
    dr["gd_src"] = nc.dram_tensor("gd_src", [128, 64], F32).ap()
    dr["gd_g"] = nc.dram_tensor("gd_g", [4 * 128, 64], F32).ap()
    dr["gsA_src"] = nc.dram_tensor("gsA_src", [128, 2048], F32).ap()
    dr["gsA_g"] = nc.dram_tensor("gsA_g", [4 * 128, 2048], F32).ap()
    dr["gsB_src"] = nc.dram_tensor("gsB_src", [128, 2048], F32).ap()
    dr["gsB_g"] = nc.dram_tensor("gsB_g", [4 * 128, 2048], F32).ap()
    with contextlib.ExitStack() as es:
        b = Builder(nc, es, dr)
        b.epsc = es.enter_context(nc.sbuf_tensor("epsc", [128, 1], F32))
        b.s.op("dve", lambda e: e.memset(b.epsc[:], EPS), writes=["epsc"])
        layers = sorted(set(L for _, L in steps))
        b.prologue_mod(layers[:1])
        resid = [i for i, (k, _) in enumerate(steps) if k in ("convB", "glaB", "ffn")]
        cur, cur_name = dr["xT"], "xT"
        toks = []
        for i, (kind, L) in enumerate(steps):
            j = L // 2
            if resid and i == resid[-1]:
                nxt, nxt_name = dr["out"], "out"
            else:
                nxt, nxt_name = dr["xs"], "xs"
            if kind == "ffn":
                nl = layers[layers.index(L) + 1] if layers.index(L) + 1 < len(layers) else None
                toks = b.ffn(L, cur, cur_name, nxt, nxt_name, next_layer=nl)
            elif kind == "convA":
                b.conv_a(L, j, cur, cur_name)
            elif kind == "convB":
                toks = b.conv_b(L, j, cur, cur_name, nxt, nxt_name)
            elif kind == "glaA":
                b.gla(L, j, cur, cur_name, None, None, True)
            elif kind == "glaB":
                toks = b.gla(L, j, cur, cur_name, nxt, nxt_name, False)
            if i in resid:
                cur, cur_name = nxt, nxt_name
        b.s.emit(final_wait_tokens=toks)
    return nc


def run(inp, mode, trace=False):
    inp = {k: np.asarray(v) for k, v in inp.items()}
    steps = MODES[mode]
    W = weight_arrays(inp, steps)
    consts = host_consts()
    maps = []
    vec_off = None
    for core in range(NCORE):
        b, seg = core // 4, core % 4
        xT = np.ascontiguousarray(inp["x"][b, seg * TOK:(seg + 1) * TOK, :].T).reshape(KC, 128, TOK)
        vecs, vec_off = build_vecs(inp, b, seg)
        m = {"xT": xT, "vecs": vecs, "consts": consts}
        m.update(W)
        maps.append(m)
    nc = build_program(vec_off, maps[0]["vecs"].shape[1], steps, {k: v.shape for k, v in W.items()})
    res = run_bass_kernel_spmd(nc, maps, core_ids=list(range(NCORE)), trace=trace)
    out = np.empty((2, SEQ, D), np.float32)
    for core in range(NCORE):
        b, seg = core // 4, core % 4
        o = res.results[core]["out"].reshape(D, TOK)
        out[b, seg * TOK:(seg + 1) * TOK, :] = o.T
    return out, res


def kernel(**inputs):
    out, _ = run(inputs, "full")
    return out
```

```python
import contextlib
import numpy as np
import concourse.bass as bass
import concourse.mybir as mybir
from concourse.bass_utils import run_bass_kernel_spmd

F32 = mybir.dt.float32
BF16 = mybir.dt.bfloat16
AF = mybir.ActivationFunctionType
ALU = mybir.AluOpType

D = 2048
KC = 16
SEQ = 8192
NCORE = 8
TOK = 2048
TH = 1024
DFF = 5632
JC = 44
DEPTH = 4
CW = 31
HALO = 32
EPS = 1e-6
DK = 1024
DV = 2048
NH = 4
GIN = 6160
NWB = 4


class Sched:
    ENG = ["pe", "act", "dve", "pool", "sp"]

    def __init__(self, nc):
        self.nc = nc
        self.ops = {e: [] for e in self.ENG}
        self.cnt = {e: 0 for e in self.ENG}
        self.seen = {e: {} for e in self.ENG}
        self.last_w = {}
        self.readers = {}
        self.dma_cnt = {}
        self.dma_keys = []

    def _add(self, eng, fn, reads, writes, tok_kind, dma_key=None):
        deps = []
        for r in reads:
            t = self.last_w.get(r)
            if t is not None:
                deps.append(t)
        for w in writes:
            t = self.last_w.get(w)
            if t is not None:
                deps.append(t)
            deps.extend(self.readers.get(w, ()))
        if tok_kind == "eng":
            self.cnt[eng] += 1
            tok = ("eng", eng, self.cnt[eng])
        else:
            if dma_key not in self.dma_cnt:
                self.dma_cnt[dma_key] = 0
                self.dma_keys.append(dma_key)
            self.dma_cnt[dma_key] += 16
            tok = ("dma", dma_key, self.dma_cnt[dma_key])
        need = {}
        for t in deps:
            if t[0] == "eng" and t[1] == eng and tok_kind == "eng" and eng == "pe":
                continue
            k = (t[0], t[1])
            if t[2] > need.get(k, 0):
                need[k] = t[2]
        waits = []
        seen = self.seen[eng]
        for k, v in need.items():
            if seen.get(k, 0) >= v:
                continue
            seen[k] = v
            waits.append((k, v))
        self.ops[eng].append((waits, fn, tok))
        for r in reads:
            self.readers.setdefault(r, []).append(tok)
        for w in writes:
            self.last_w[w] = tok
            self.readers[w] = []
        return tok

    def op(self, eng, fn, reads=(), writes=()):
        return self._add(eng, fn, reads, writes, "eng")

    def dma(self, fn, reads=(), writes=(), key=None, eng="sp"):
        return self._add(eng, fn, reads, writes, "dma", dma_key=key)

    def emit(self, final_wait_tokens=()):
        nc = self.nc
        with contextlib.ExitStack() as es:
            sems = {}
            for e in self.ENG:
                sems[("eng", e)] = es.enter_context(nc.semaphore("s_" + e))
            for i, k in enumerate(self.dma_keys):
                sems[("dma", k)] = es.enter_context(nc.semaphore("d%d" % i))
            block = es.enter_context(nc.Block())
            eng_map = {"pe": block.tensor, "act": block.scalar, "dve": block.vector,
                       "pool": block.gpsimd, "sp": block.sync}
            for e in self.ENG:
                ops = self.ops[e]
                extra = final_wait_tokens if e == "sp" else ()

                def body(eh, ops=ops, e=e, extra=extra):
                    for waits, fn, tok in ops:
                        for k, v in waits:
                            eh.wait_ge(sems[k], v)
                        ins = fn(eh)
                        if tok[0] == "eng":
                            ins.then_inc(sems[("eng", e)], 1)
                        else:
                            ins.then_inc(sems[("dma", tok[1])], 16)
                    for t in extra:
                        eh.wait_ge(sems[(t[0], t[1])], t[2])
                eng_map[e](body)


class Region:
    def __init__(self, nc, es, name, nbytes):
        self.name = name
        self.nbytes = nbytes
        self.t = es.enter_context(nc.sbuf_tensor(name, [128, nbytes // 4], F32))

    def view(self, off, dtype, shape):
        esz = 4 if dtype == F32 else 2
        n = 1
        for x in shape:
            n *= x
        assert off % 4 == 0 and (n * esz) % 4 == 0 and off + n * esz <= self.nbytes, (self.name, off, n, esz)
        a = self.t[:, off // 4:(off + n * esz) // 4]
        if dtype != F32:
            a = a.bitcast(dtype)
        if len(shape) == 2:
            a = a.rearrange("p (a b) -> p a b", a=shape[0])
        keys = [(self.name, pg) for pg in range(off // 1024, (off + n * esz + 1023) // 1024)]
        return a, keys


class Builder:
    def __init__(self, nc, es, dram):
        self.nc = nc
        self.es = es
        self.dr = dram
        self.s = Sched(nc)
        s = self.s
        self.R1 = Region(nc, es, "R1", 64 * 1024)
        self.R2 = Region(nc, es, "R2", 88 * 1024)
        self.R3 = Region(nc, es, "R3", 24 * 1024)
        self.wb = [es.enter_context(nc.sbuf_tensor("wb%d" % i, [128, 2048], BF16)) for i in range(NWB)]
        self.wi = 0
        self.P = [es.enter_context(nc.psum_tensor("P%d" % i, [128, 1024], F32)) for i in range(4)]
        self.ones = es.enter_context(nc.sbuf_tensor("ones", [128, 128], BF16))
        self.one32 = es.enter_context(nc.sbuf_tensor("one32", [128, 1], F32))
        self.vecs = es.enter_context(nc.sbuf_tensor("vecs_sb", [128, self.dr["vecs"].shape[1]], F32))
        self.cact = es.enter_context(nc.sbuf_tensor("cact", [128, 16], BF16))
        self.modrow = self.R3.t[0:1, 4096:6144]
        self.modc = es.enter_context(nc.sbuf_tensor("modc", [128, DEPTH * 96], F32))
        self.lay = es.enter_context(nc.sbuf_tensor("lay", [128, DEPTH * 96], F32))
        s.op("dve", lambda e: e.memset(self.ones[:], 1.0), writes=["ones"])
        s.op("dve", lambda e: e.memset(self.one32[:], 1.0), writes=["one32"])
        s.dma(lambda q: q.dma_start(out=self.vecs[:], in_=self.dr["vecs"]), writes=["vecs"], key="vecs")
        self.pending = []
        self.ident = es.enter_context(nc.sbuf_tensor("ident", [128, 128], BF16))
        self.tri4 = es.enter_context(nc.sbuf_tensor("tri4", [128, 512], F32))
        s.dma(lambda q: q.dma_start(out=self.ident[:], in_=self.dr["consts"][:, 0:128]), writes=["ident"],
              key="ident", eng="pool")
        s.dma(lambda q: q.dma_start(out=self.tri4[:], in_=self.dr["consts"][:, 128:640]), writes=["tri4"], key="tri4")
        self.tribf = es.enter_context(nc.sbuf_tensor("tribf", [128, 128], BF16))
        s.dma(lambda q: q.dma_start(out=self.tribf[:], in_=self.dr["consts"][:, 128:256]), writes=["tribf"],
              key="tribf", eng="pool")

    def pk(self, i, tt=None):
        if tt is None:
            return [("P", i, 0), ("P", i, 1)]
        return [("P", i, tt)]

    def load_w(self, src_ap, ncols):
        i = self.wi % NWB
        self.wi += 1
        t = self.wb[i]
        key = ("wb", i)
        self.s.dma(lambda q: q.dma_start(out=t[:, 0:ncols], in_=src_ap), writes=[key], key=key, eng="pool")
        return t, key

    def flush_pending(self):
        p = self.pending
        self.pending = []
        for f in p:
            f()

    def stats_mm(self, Pi, src_ap, src_keys, first, last):
        def f():
            def emit(e):
                ins = None
                for tt in range(2):
                    ins = e.matmul(self.P[Pi][:, tt * 512:(tt + 1) * 512], lhsT=self.ones[:],
                                   rhs=src_ap[:, tt * 512:(tt + 1) * 512], start=first, stop=last)
                return ins
            self.s.op("pe", emit, reads=list(src_keys) + ["ones"], writes=self.pk(Pi))
        self.pending.append(f)

    def vcol(self, name, c):
        o = self.dr["vec_off"][name]
        return self.vecs[:, o + c:o + c + 1]

    def prologue_mod(self, layers):
        s = self.s
        nc = self.nc
        dr = self.dr
        co = dr["vec_off"]["c"]
        s.op("act", lambda e: e.activation(out=self.cact[:], in_=self.vecs[:, co:co + 16], func=AF.Silu),
             reads=["vecs"], writes=["cact"])
        for i in layers:
            for nb in range(6):
                for kc in range(KC):
                    t, key = self.load_w(dr["w_ada%d" % i][kc * 128:(kc + 1) * 128, nb * 2048:(nb + 1) * 2048], 2048)

                    def emit(e, t=t, kc=kc):
                        ins = None
                        for q in range(4):
                            ins = e.matmul(self.P[q // 2][0:1, (q % 2) * 512:(q % 2 + 1) * 512],
                                           lhsT=self.cact[:, kc:kc + 1], rhs=t[:, q * 512:(q + 1) * 512],
                                           start=(kc == 0), stop=(kc == KC - 1))
                        return ins
                    s.op("pe", emit, reads=[key, "cact"], writes=self.pk(0) + self.pk(1))
                s.op("act", lambda e: e.activation(out=self.modrow[0:1, 0:1024], in_=self.P[0][0:1, :], func=AF.Identity),
                     reads=self.pk(0), writes=[("R3", pg) for pg in range(16, 20)])
                s.op("act", lambda e: e.activation(out=self.modrow[0:1, 1024:2048], in_=self.P[1][0:1, :], func=AF.Identity),
                     reads=self.pk(1), writes=[("R3", pg) for pg in range(20, 24)])

                def emit_t(e):
                    ins = None
                    for j in range(16):
                        ins = e.matmul(self.P[2][:, j:j + 1], lhsT=self.modrow[0:1, j * 128:(j + 1) * 128],
                                       rhs=self.one32[0:1, 0:1], start=True, stop=True)
                    return ins
                s.op("pe", emit_t, reads=[("R3", pg) for pg in range(16, 24)] + ["one32"], writes=self.pk(2))
                bo = dr["vec_off"]["b_ada%d" % i] + nb * 16
                s.op("dve", lambda e, i=i, nb=nb, bo=bo: e.tensor_tensor(
                    out=self.modc[:, i * 96 + nb * 16:i * 96 + nb * 16 + 16], in0=self.P[2][:, 0:16],
                    in1=self.vecs[:, bo:bo + 16], op=ALU.add),
                    reads=self.pk(2) + ["vecs"], writes=[("modc", i, nb)])
            m = lambda nb, i=i: self.modc[:, i * 96 + nb * 16:i * 96 + nb * 16 + 16]
            L = lambda k, i=i: self.lay[:, i * 96 + k * 16:i * 96 + k * 16 + 16]
            vo = dr["vec_off"]
            g = lambda nm, i=i, vo=vo: self.vecs[:, vo[nm % i]:vo[nm % i] + 16]
            rd = [("modc", i, nb) for nb in range(6)] + ["vecs"]
            wr = [("lay", i)]
            s.op("dve", lambda e, m=m, L=L, g=g: e.scalar_tensor_tensor(
                out=L(0), in0=m(1), scalar=1.0, in1=g("pre_mix_g%d"), op0=ALU.add, op1=ALU.mult), reads=rd, writes=wr)
            s.op("dve", lambda e, m=m, L=L: e.tensor_copy(out=L(1), in_=m(0)), reads=rd, writes=wr)
            s.op("dve", lambda e, m=m, L=L, g=g: e.tensor_tensor(
                out=L(2), in0=m(2), in1=g("post_mix_g%d"), op=ALU.mult), reads=rd, writes=wr)
            s.op("dve", lambda e, m=m, L=L, g=g: e.scalar_tensor_tensor(
                out=L(3), in0=m(4), scalar=1.0, in1=g("pre_ffn_g%d"), op0=ALU.add, op1=ALU.mult), reads=rd, writes=wr)
            s.op("dve", lambda e, m=m, L=L: e.tensor_copy(out=L(4), in_=m(3)), reads=rd, writes=wr)
            s.op("dve", lambda e, m=m, L=L, g=g: e.tensor_tensor(
                out=L(5), in0=m(5), in1=g("post_ffn_g%d"), op=ALU.mult), reads=rd, writes=wr)

    def mod_units(self, i):
        s = self.s
        dr = self.dr
        units = []
        for nb in range(6):
            for kc in range(KC):
                def unit(nb=nb, kc=kc):
                    t, key = self.load_w(dr["w_ada%d" % i][kc * 128:(kc + 1) * 128, nb * 2048:(nb + 1) * 2048], 2048)

                    def emit(e):
                        ins = None
                        for si in range(16):
                            ins = e.matmul(self.P[2][:, si:si + 1], lhsT=t[:, si * 128:(si + 1) * 128],
                                           rhs=self.cact[:, kc:kc + 1], start=(kc == 0 and si == 0),
                                           stop=(kc == KC - 1), skip_group_check=True)
                        return ins
                    s.op("pe", emit, reads=[key, "cact"], writes=self.pk(2, 0))
                    if kc == KC - 1:
                        bo = dr["vec_off"]["b_ada%d" % i] + nb * 16
                        s.op("dve", lambda e: e.tensor_tensor(
                            out=self.modc[:, i * 96 + nb * 16:i * 96 + nb * 16 + 16], in0=self.P[2][:, 0:16],
                            in1=self.vecs[:, bo:bo + 16], op=ALU.add),
                            reads=self.pk(2, 0) + ["vecs"], writes=[("modc", i, nb)])
                units.append(unit)
        units.append(lambda: self.mod_derive(i))
        return units

    def mod_derive(self, i):
        s = self.s
        dr = self.dr
        m = lambda nb, i=i: self.modc[:, i * 96 + nb * 16:i * 96 + nb * 16 + 16]
        L = lambda k, i=i: self.lay[:, i * 96 + k * 16:i * 96 + k * 16 + 16]
        vo = dr["vec_off"]
        g = lambda nm, i=i, vo=vo: self.vecs[:, vo[nm % i]:vo[nm % i] + 16]
        rd = [("modc", i, nb) for nb in range(6)] + ["vecs"]
        wr = [("lay", i)]
        s.op("dve", lambda e: e.scalar_tensor_tensor(
            out=L(0), in0=m(1), scalar=1.0, in1=g("pre_mix_g%d"), op0=ALU.add, op1=ALU.mult), reads=rd, writes=wr)
        s.op("dve", lambda e: e.tensor_copy(out=L(1), in_=m(0)), reads=rd, writes=wr)
        s.op("dve", lambda e: e.tensor_tensor(out=L(2), in0=m(2), in1=g("post_mix_g%d"), op=ALU.mult), reads=rd, writes=wr)
        s.op("dve", lambda e: e.scalar_tensor_tensor(
            out=L(3), in0=m(4), scalar=1.0, in1=g("pre_ffn_g%d"), op0=ALU.add, op1=ALU.mult), reads=rd, writes=wr)
        s.op("dve", lambda e: e.tensor_copy(out=L(4), in_=m(3)), reads=rd, writes=wr)
        s.op("dve", lambda e: e.tensor_tensor(out=L(5), in0=m(5), in1=g("post_ffn_g%d"), op=ALU.mult), reads=rd, writes=wr)

    def lcol(self, i, k, c):
        o = i * 96 + k * 16 + c
        return self.lay[:, o:o + 1]

    def hT(self, c, tt=None):
        if tt is None:
            return self.R1.view(c * 2048, BF16, (1024,))
        return self.R1.view(c * 2048 + tt * 1024, BF16, (512,))

    def prenorm(self, xin, xin_name, hf, layer, ka, kb):
        s = self.s
        xs = []
        for c in range(KC):
            a, keys = self.R2.view(c * 4096, F32, (1024,))
            xs.append((a, keys))
            s.dma(lambda q, a=a, c=c: q.dma_start(out=a, in_=xin[c, :, hf * TH:(hf + 1) * TH]),
                  reads=[(xin_name, c, hf)], writes=keys, key=("xst", c))
        for c in range(KC):
            a, keys = xs[c]
            sq, sqk = self.R3.view((c % 2) * 2048, BF16, (1024,))
            if c % 2 == 0:
                s.op("act", lambda e, a=a, sq=sq: e.activation(out=sq, in_=a, func=AF.Square), reads=keys, writes=sqk)
            else:
                s.op("dve", lambda e, a=a, sq=sq: e.tensor_tensor(out=sq, in0=a, in1=a, op=ALU.mult), reads=keys, writes=sqk)
            self.flush_pending()
            self.stats_mm(3, sq, sqk, c == 0, c == KC - 1)
        self.flush_pending()
        rs, rsk = self.R3.view(4096, F32, (1024,))
        s.op("act", lambda e: e.activation(out=rs, in_=self.P[3][:], func=AF.Ln, scale=1.0 / D, bias=self.epsc[:]),
             reads=self.pk(3) + ["epsc"], writes=rsk)
        s.op("act", lambda e: e.activation(out=rs, in_=rs, func=AF.Exp, scale=-0.5), reads=rsk, writes=rsk)
        for c in range(KC):
            a, keys = xs[c]
            tm, tmk = self.R3.view(8192 + (c % 2) * 4096, F32, (1024,))
            s.op("dve", lambda e, a=a, tm=tm: e.tensor_tensor(out=tm, in0=a, in1=rs, op=ALU.mult),
                 reads=keys + rsk, writes=tmk)
            h, hk = self.hT(c)
            s.op("act", lambda e, tm=tm, h=h, c=c: e.activation(
                out=h, in_=tm, func=AF.Identity, scale=self.lcol(layer, ka, c), bias=self.lcol(layer, kb, c)),
                reads=tmk + [("lay", layer)], writes=hk)

    def postnorm(self, yview, xin, xin_name, xout, xout_name, hf, layer, kg, stg_region, stg_off):
        s = self.s
        rs, rsk = self.R3.view(4096, F32, (1024,))
        s.op("act", lambda e: e.activation(out=rs, in_=self.P[3][:], func=AF.Ln, scale=1.0 / D, bias=self.epsc[:]),
             reads=self.pk(3) + ["epsc"], writes=rsk)
        s.op("act", lambda e: e.activation(out=rs, in_=rs, func=AF.Exp, scale=-0.5), reads=rsk, writes=rsk)
        toks = []
        for c in range(KC):
            xa, xk = stg_region.view(stg_off + (c % 8) * 4096, F32, (1024,))
            s.dma(lambda q, xa=xa, c=c: q.dma_start(out=xa, in_=xin[c, :, hf * TH:(hf + 1) * TH]),
                  reads=[(xin_name, c, hf)], writes=xk, key=("xst2", c % 8))
            y, yk = yview(c)
            s.op("dve", lambda e, y=y, c=c: e.scalar_tensor_tensor(
                out=y, in0=y, scalar=self.lcol(layer, kg, c), in1=rs, op0=ALU.mult, op1=ALU.mult),
                reads=yk + rsk + [("lay", layer)], writes=yk)
            s.op("dve", lambda e, y=y, xa=xa: e.tensor_tensor(out=xa, in0=y, in1=xa, op=ALU.add),
                 reads=yk + xk, writes=xk)
            t = s.dma(lambda q, xa=xa, c=c: q.dma_start(out=xout[c, :, hf * TH:(hf + 1) * TH], in_=xa),
                      reads=xk, writes=[(xout_name, c, hf)], key=("xo", c % 8), eng="act")
            toks.append(t)
        return toks

    def proj(self, wt, n_oc, n_kg, kcb, in_view, psel, epilogue, oc_list=None, tiles=((0, 0), (512, 512)), m=128, after_block=None):
        s = self.s
        for oi, oc in enumerate(oc_list if oc_list is not None else range(n_oc)):
            Pi = psel(oi)
            pkeys = []
            for (_, po) in tiles:
                pkeys += self.pk(Pi, po // 512)
            for kg in range(n_kg):
                t, key = self.load_w(wt[oc, kg], kcb * m)
                ins_ = [in_view(kg * kcb + k) for k in range(kcb)]
                rk = [key]
                for a, k_ in ins_:
                    rk += k_

                def emit(e, t=t, kg=kg, ins_=ins_, Pi=Pi):
                    ins = None
                    for k in range(kcb):
                        for (io, po) in tiles:
                            ins = e.matmul(self.P[Pi][0:m, po:po + 512], lhsT=t[:, k * m:(k + 1) * m],
                                           rhs=ins_[k][0][:, io:io + 512],
                                           start=(kg == 0 and k == 0), stop=(kg == n_kg - 1 and k == kcb - 1))
                    return ins
                if oi == 0 and kg == 0 and kcb > 1:
                    for k in range(kcb):
                        def emit1(e, t=t, k=k, ins_=ins_, Pi=Pi):
                            ins = None
                            for (io, po) in tiles:
                                ins = e.matmul(self.P[Pi][0:m, po:po + 512], lhsT=t[:, k * m:(k + 1) * m],
                                               rhs=ins_[k][0][:, io:io + 512],
                                               start=(k == 0), stop=(n_kg == 1 and k == kcb - 1))
                            return ins
                        s.op("pe", emit1, reads=[key] + ins_[k][1], writes=pkeys)
                else:
                    s.op("pe", emit, reads=rk, writes=pkeys)
                if after_block is not None:
                    after_block()
            epilogue(oi, oc, Pi)

    def ffn(self, layer, xin, xin_name, xout, xout_name, next_layer=None):
        s = self.s
        dr = self.dr
        toks = []
        units = self.mod_units(next_layer) if next_layer is not None else []

        def inject():
            if units:
                units.pop(0)()
        for hf in range(2):
            self.prenorm(xin, xin_name, hf, layer, 3, 4)
            hid = lambda j: self.R2.view(j * 2048, BF16, (1024,))
            w1 = dr["ffn_w_in_%d" % layer]
            for j in range(JC):
                Pg, Pu = (0, 1) if j % 2 == 0 else (2, 3)
                self.proj(w1, None, 1, KC, self.hT, lambda oi, Pg=Pg, Pu=Pu: (Pg, Pu)[oi], lambda *a: None,
                          oc_list=[j, JC + j])
                sg, sgk = self.R3.view(8192 + (j % 2) * 4096, F32, (1024,))
                s.op("act", lambda e, sg=sg, Pg=Pg: e.activation(out=sg, in_=self.P[Pg][:], func=AF.Silu),
                     reads=self.pk(Pg), writes=sgk)
                h, hk = hid(j)
                s.op("dve", lambda e, sg=sg, h=h, Pu=Pu: e.tensor_tensor(out=h, in0=sg, in1=self.P[Pu][:], op=ALU.mult),
                     reads=sgk + self.pk(Pu), writes=hk)
            yv = lambda c: self.R1.view(c * 4096, F32, (1024,))

            def epi(oi, oc, Pi):
                y, yk = yv(oc)
                s.op("act", lambda e: e.activation(out=y, in_=self.P[Pi][:], func=AF.Identity),
                     reads=self.pk(Pi), writes=yk)
                sq, sqk = self.R3.view((oc % 2) * 2048, BF16, (1024,))
                s.op("dve", lambda e: e.tensor_tensor(out=sq, in0=self.P[Pi][:], in1=y, op=ALU.mult),
                     reads=self.pk(Pi) + yk, writes=sqk)
                self.flush_pending()
                self.stats_mm(3, sq, sqk, oc == 0, oc == KC - 1)
            self.proj(dr["ffn_w_out_%d" % layer], KC, 4, 11, hid, lambda oi: oi % 2, epi, after_block=inject)
            self.flush_pending()
            toks += self.postnorm(yv, xin, xin_name, xout, xout_name, hf, layer, 5, self.R2, 0)
        while units:
            units.pop(0)()
        return toks


    def allgather(self, src, dst, src_keys, dst_keys):
        self.s.op("pool", lambda e: e.collective_compute(
            "AllGather", ALU.bypass, replica_groups=[[0, 1, 2, 3], [4, 5, 6, 7]],
            ins=[src.opt()], outs=[dst.opt()]), reads=list(src_keys) + ["cc_chain"], writes=list(dst_keys) + ["cc_chain"])

    def conv_a(self, layer, j, xin, xin_name):
        s = self.s
        dr = self.dr
        vo = dr["vec_off"]["b_pw1_%d" % j]
        for hf in range(2):
            self.prenorm(xin, xin_name, hf, layer, 0, 1)
            for c in range(KC):
                Pa, Pg = (0, 1) if c % 2 == 0 else (2, 3)
                self.proj(dr["conv_w_pw1_%d" % j], None, 1, KC, self.hT, lambda oi, Pa=Pa, Pg=Pg: (Pa, Pg)[oi],
                          lambda *a: None, oc_list=[c, KC + c])
                sg, sgk = self.R3.view(8192 + (c % 2) * 4096, F32, (1024,))
                s.op("act", lambda e, sg=sg, Pg=Pg, c=c: e.activation(
                    out=sg, in_=self.P[Pg][:], func=AF.Sigmoid, bias=self.vecs[:, vo + KC + c:vo + KC + c + 1]),
                    reads=self.pk(Pg) + ["vecs"], writes=sgk)
                u, uk = self.R2.view(65536 + (c % 3) * 2048, BF16, (1024,))
                s.op("dve", lambda e, sg=sg, u=u, Pa=Pa, c=c: e.scalar_tensor_tensor(
                    out=u, in0=self.P[Pa][:], scalar=self.vecs[:, vo + c:vo + c + 1], in1=sg, op0=ALU.add, op1=ALU.mult),
                    reads=sgk + self.pk(Pa) + ["vecs"], writes=uk)
                s.dma(lambda q, u=u, c=c, hf=hf: q.dma_start(
                    out=dr["u_s"][c, :, HALO + hf * TH:HALO + (hf + 1) * TH], in_=u),
                    reads=uk, writes=[("u_s", c, hf)], key=("uo", c % 3), eng="act")
                if hf == 1:
                    s.dma(lambda q, u=u, c=c: q.dma_start(out=dr["halo_src"][:, c * HALO:(c + 1) * HALO], in_=u[:, TH - HALO:TH]),
                          reads=uk, writes=[("halo_src", c)], key=("ho", c % 3), eng="act")
        self.allgather(dr["halo_src"], dr["halo_g"], [("halo_src", c) for c in range(KC)], ["halo_g"])

    def conv_b(self, layer, j, xin, xin_name, xout, xout_name):
        s = self.s
        dr = self.dr
        vo = dr["vec_off"]
        toks = []
        G, Gk = self.R3.view(16384, BF16, (4, KC * HALO))
        s.dma(lambda q: q.dma_start(out=G, in_=dr["halo_g"].rearrange("(r p) k -> p r k", p=128)),
              reads=["halo_g"], writes=Gk, key="halo_ld")
        hal, halk = self.R3.view(16384 + 4096, BF16, (KC * HALO,))
        mo = vo["mprev"]
        s.op("dve", lambda e: e.tensor_scalar(out=hal, in0=G[:, 0, :], scalar1=self.vecs[:, mo:mo + 1], scalar2=None,
                                               op0=ALU.mult), reads=Gk + ["vecs"], writes=halk)
        for r in range(1, 4):
            s.op("dve", lambda e, r=r: e.scalar_tensor_tensor(
                out=hal, in0=G[:, r, :], scalar=self.vecs[:, mo + r:mo + r + 1], in1=hal, op0=ALU.mult, op1=ALU.add),
                reads=Gk + halk + ["vecs"], writes=halk)
        UW = HALO + TH
        for hf in range(2):
            Us = []
            for c in range(KC):
                U, Uk = self.R2.view(c * UW * 2, BF16, (UW,))
                Us.append((U, Uk))
                s.dma(lambda q, U=U, c=c, hf=hf: q.dma_start(out=U, in_=dr["u_s"][c, :, hf * TH:hf * TH + UW]),
                      reads=[("u_s", c, 0), ("u_s", c, 1)], writes=Uk, key=("Uld", c))
                if hf == 0:
                    s.op("dve", lambda e, U=U, c=c: e.tensor_copy(out=U[:, 0:HALO], in_=hal[:, c * HALO:(c + 1) * HALO]),
                         reads=halk + Uk, writes=Uk)
            vv = lambda c: self.R1.view(c * 4096, F32, (1024,))
            wo = vo["w_dw_%d" % j]
            bo = vo["b_dw_%d" % j]
            def build_taps(c):
                dg, dgk = self.R2.view(33792 + (c % 2) * 7936, BF16, (CW, 128))
                for k in range(CW):
                    rd = ["ident", "vecs"] + ([("dggate", c % 2)] if k > 0 else [])
                    wr = [("dgtap", c % 2, k)] + (dgk + [("dggate", c % 2)] if k == 0 else [])
                    if k % 2 == 0:
                        s.op("dve", lambda e, dg=dg, k=k, c=c: e.tensor_scalar(
                            out=dg[:, k, :], in0=self.ident[:], scalar1=self.vecs[:, wo + k * KC + c:wo + k * KC + c + 1],
                            scalar2=None, op0=ALU.mult), reads=rd, writes=wr)
                    else:
                        s.op("act", lambda e, dg=dg, k=k, c=c: e.activation(
                            out=dg[:, k, :], in_=self.ident[:], func=AF.Identity,
                            scale=self.vecs[:, wo + k * KC + c:wo + k * KC + c + 1]), reads=rd, writes=wr)
            build_taps(0)
            for c in range(KC):
                dg, dgk = self.R2.view(33792 + (c % 2) * 7936, BF16, (CW, 128))
                Pi = c % 2
                U, Uk = Us[c]

                def emit(e, dg=dg, U=U, Pi=Pi):
                    ins = None
                    for tt in range(2):
                        for k in range(CW):
                            ins = e.matmul(self.P[Pi][:, tt * 512:(tt + 1) * 512], lhsT=dg[:, k, :],
                                           rhs=U[:, tt * 512 + k + 2:tt * 512 + k + 2 + 512],
                                           start=(k == 0), stop=(k == CW - 1))
                    return ins
                s.op("pe", emit, reads=[("dgtap", c % 2, k) for k in range(CW)] + dgk + [("dggate", c % 2)] + Uk,
                     writes=self.pk(Pi))
                if c + 1 < KC:
                    build_taps(c + 1)
                v, vk = vv(c)
                s.op("act", lambda e, v=v, Pi=Pi, c=c: e.activation(
                    out=v, in_=self.P[Pi][:], func=AF.Identity, bias=self.vecs[:, bo + c:bo + c + 1]),
                    reads=self.pk(Pi) + ["vecs"], writes=vk)
                sq, sqk = self.R3.view((c % 2) * 2048, BF16, (1024,))
                vb, vbk = self.R3.view(4096 + (c % 2) * 2048, BF16, (1024,))
                s.op("dve", lambda e, v=v, vb=vb: e.tensor_copy(out=vb, in_=v), reads=vk, writes=vbk)
                s.op("dve", lambda e, v=v, sq=sq: e.tensor_tensor(out=sq, in0=v, in1=v, op=ALU.mult), reads=vk, writes=sqk)
                self.flush_pending()
                self.stats_mm(2, vb, vbk, c == 0, c == KC - 1)
                self.stats_mm(3, sq, sqk, c == 0, c == KC - 1)
            self.flush_pending()
            mean, mk = self.R3.view(8192, F32, (1024,))
            rstd, rk = self.R3.view(12288, F32, (1024,))
            s.op("act", lambda e: e.activation(out=mean, in_=self.P[2][:], func=AF.Identity, scale=1.0 / D),
                 reads=self.pk(2), writes=mk)
            s.op("dve", lambda e: e.tensor_tensor(out=rstd, in0=mean, in1=mean, op=ALU.mult), reads=mk, writes=rk)
            s.op("dve", lambda e: e.scalar_tensor_tensor(out=rstd, in0=self.P[3][:], scalar=1.0 / D, in1=rstd,
                                                          op0=ALU.mult, op1=ALU.subtract), reads=self.pk(3) + rk, writes=rk)
            s.op("act", lambda e: e.activation(out=rstd, in_=rstd, func=AF.Ln, bias=self.epsc[:]),
                 reads=rk + ["epsc"], writes=rk)
            s.op("act", lambda e: e.activation(out=rstd, in_=rstd, func=AF.Exp, scale=-0.5), reads=rk, writes=rk)
            sv = lambda c: self.R2.view(49664 + c * 2048, BF16, (1024,))
            go, lbo = vo["ln_g_%d" % j], vo["ln_b_%d" % j]
            for c in range(KC):
                v, vk = vv(c)
                t1, t1k = self.R3.view(16384 + (c % 2) * 4096, F32, (1024,))
                s.op("dve", lambda e, v=v, t1=t1: e.tensor_tensor(out=t1, in0=v, in1=mean, op=ALU.subtract),
                     reads=vk + mk, writes=t1k)
                s.op("dve", lambda e, t1=t1: e.tensor_tensor(out=t1, in0=t1, in1=rstd, op=ALU.mult),
                     reads=t1k + rk, writes=t1k)
                sc, sck = sv(c)
                s.op("act", lambda e, t1=t1, sc=sc, c=c: e.activation(
                    out=sc, in_=t1, func=AF.Silu, scale=self.vecs[:, go + c:go + c + 1], bias=self.vecs[:, lbo + c:lbo + c + 1]),
                    reads=t1k + ["vecs"], writes=sck)
            yv = vv
            b2 = vo["b_pw2_%d" % j]

            def epi(oi, oc, Pi):
                y, yk = yv(oc)
                s.op("act", lambda e: e.activation(out=y, in_=self.P[Pi][:], func=AF.Identity,
                                                   bias=self.vecs[:, b2 + oc:b2 + oc + 1]),
                     reads=self.pk(Pi) + ["vecs"], writes=yk)
                sq, sqk = self.R3.view((oc % 2) * 2048, BF16, (1024,))
                s.op("dve", lambda e: e.tensor_tensor(out=sq, in0=y, in1=y, op=ALU.mult), reads=yk, writes=sqk)
                self.flush_pending()
                self.stats_mm(3, sq, sqk, oc == 0, oc == KC - 1)
            self.proj(dr["conv_w_pw2_%d" % j], KC, 1, KC, sv, lambda oi: oi % 2, epi)
            self.flush_pending()
            toks += self.postnorm(yv, xin, xin_name, xout, xout_name, hf, layer, 2, self.R2, 0)
        return toks

    def rows(self, region, off, dtype, n, nrows):
        esz = 4 if dtype == F32 else 2
        a = region.t[0:nrows, off // 4:(off + n * esz) // 4]
        if dtype != F32:
            a = a.bitcast(dtype)
        keys = [(region.name, pg) for pg in range(off // 1024, (off + n * esz + 1023) // 1024)]
        return a, keys

    def gla(self, layer, j, xin, xin_name, xout, xout_name, state_only):
        s = self.s
        dr = self.dr
        vo = dr["vec_off"]
        R1, R2, R3 = self.R1, self.R2, self.R3
        toks = []
        B = not state_only
        wup, wupk = self.rows(R3, 16384, BF16, 1024, 17)
        aT, aTk = self.rows(R3, 18944, BF16, 512, 17)
        small, smk = R3.view(18432, F32, (8 * 8,))
        bl, ebl, Bsum, Dj, Dp = [small[:, i * 8:(i + 1) * 8] for i in range(5)]
        blk_, eblk_, Bsk, Djk, Dpk = [[("gsm", i)] for i in range(5)]
        s.dma(lambda q: q.dma_start(out=wup, in_=dr["gla_wup_%d" % j]), writes=wupk, key="wup", eng="pool")
        s.op("dve", lambda e: e.memset(aT, 1.0), writes=aTk)
        S32v = lambda dc: R2.view(65536 + dc * 2048, F32, (512,))
        Sbfv = lambda dc: R2.view(81920 + dc * 1024, BF16, (512,))
        Sall, Sallk = R2.view(65536, F32, (4096,))
        Sball, Sballk = R2.view(81920, BF16, (4096,))
        def init_state():
            s.op("dve", lambda e: e.memset(Sall, 0.0), writes=Sallk)
            if state_only:
                s.op("dve", lambda e: e.memset(Bsum, 0.0), writes=Bsk)
            else:
                mo = vo["mlt"]
                for jr in range(3):
                    s.dma(lambda q, jr=jr: q.dma_start(out=Dj, in_=dr["gd_g"][jr * 128:(jr + 1) * 128, 0:8]),
                          reads=["gd_g"], writes=Djk, key="Dj")
                    s.op("dve", lambda e, jr=jr: e.tensor_scalar(out=Dp, in0=Dj, scalar1=-1.0, scalar2=self.vecs[:, mo + jr:mo + jr + 1],
                                                                  op0=ALU.add, op1=ALU.mult), reads=Djk + ["vecs"], writes=Dpk)
                    s.op("dve", lambda e: e.tensor_scalar(out=Dp, in0=Dp, scalar1=1.0, scalar2=None, op0=ALU.add),
                         reads=Dpk, writes=Dpk)
                    for dc in range(8):
                        stg, stgk = R3.view(20480 + (dc % 2) * 2048, F32, (512,))
                        s.dma(lambda q, jr=jr, dc=dc, stg=stg: q.dma_start(
                            out=stg, in_=dr["gsA_g" if dc < 4 else "gsB_g"][jr * 128:(jr + 1) * 128, (dc % 4) * 512:(dc % 4 + 1) * 512]),
                            reads=["gsA_g" if dc < 4 else "gsB_g"], writes=stgk, key=("stg", dc % 2))
                        s.op("dve", lambda e, stg=stg, jr=jr: e.tensor_scalar(
                            out=stg, in0=stg, scalar1=self.vecs[:, mo + jr:mo + jr + 1], scalar2=None, op0=ALU.mult),
                            reads=stgk + ["vecs"], writes=stgk)
                        S, Sk = S32v(dc)
                        s.op("dve", lambda e, S=S, stg=stg, dc=dc: e.scalar_tensor_tensor(
                            out=S, in0=S, scalar=Dp[:, dc:dc + 1], in1=stg, op0=ALU.mult, op1=ALU.add),
                            reads=Sk + stgk + Dpk, writes=Sk)
                s.op("act", lambda e: e.activation(out=Sball, in_=Sall, func=AF.Identity), reads=Sallk, writes=Sballk)


        init_done = [False]

        qTv = lambda c: R2.view(c * 1024, BF16, (512,))
        kTv = lambda c: R2.view(8192 + c * 1024, BF16, (512,))
        vTv = lambda c: R2.view(16384 + c * 1024, BF16, (512,))
        qT3, _ = R2.view(0, BF16, (8, 512))
        kT3, _ = R2.view(8192, BF16, (8, 512))
        qTk = R2.view(0, BF16, (4096,))[1]
        kTk = R2.view(8192, BF16, (4096,))[1]
        vTk = R2.view(16384, BF16, (8192,))[1]
        gpos, gpk = R2.view(32768, F32, (1024,))
        e1, e1k = R2.view(36864, F32, (1024,))
        tE = [R2.view(40960, F32, (8, 128)), R2.view(45056, F32, (8, 128))]
        tEf = [R2.view(40960, F32, (1024,)), R2.view(45056, F32, (1024,))]
        qd, qdk = R2.view(49152, BF16, (8, 128))
        kd, kdk = R2.view(51200, BF16, (8, 128))
        kh, khk = R2.view(53248, BF16, (8, 128))
        vtok, vtk = R2.view(55296, BF16, (2048,))
        ktok, ktk = R2.view(59392, BF16, (1024,))
        sc, sck = R2.view(61440, BF16, (512,))
        osq, osqk = R3.view(0, BF16, (2048,))
        rh, rhk = R3.view(4096, F32, (512,))
        onT = lambda c: R1.view(32768 + c * 2048, BF16, (1024,))
        P = self.P
        P1v = P[1][:].rearrange("p (a b) -> p a b", a=8)
        P2bf = P[2][:].bitcast(BF16)
        P3bf = P[3][:].bitcast(BF16)
        ngo = vo["gla_ng_%d" % j]
        cnt = [0]

        def alt():
            cnt[0] += 1
            return "act" if cnt[0] % 2 else "dve"

        def copy_op(eng, out, in_, reads, writes, scale=None):
            if eng == "act":
                if scale is None:
                    s.op("act", lambda e: e.activation(out=out, in_=in_, func=AF.Identity), reads=reads, writes=writes)
                else:
                    s.op("act", lambda e: e.activation(out=out, in_=in_, func=AF.Identity, scale=scale), reads=reads, writes=writes)
            else:
                if scale is None:
                    s.op("dve", lambda e: e.tensor_copy(out=out, in_=in_), reads=reads, writes=writes)
                else:
                    s.op("dve", lambda e: e.tensor_scalar(out=out, in0=in_, scalar1=scale, scalar2=None, op0=ALU.mult),
                         reads=reads, writes=writes)

        for hf in range(2):
            hall, hallk = R1.view(0, BF16, (KC * TH,))
            if state_only:
                self.prenorm(xin, xin_name, hf, layer, 0, 1)
                s.dma(lambda q, hf=hf: q.dma_start(out=dr["h_s"][hf], in_=hall), reads=hallk, writes=[("h_s", hf)],
                      key="hso", eng="act")
            else:
                s.dma(lambda q, hf=hf: q.dma_start(out=hall, in_=dr["h_s"][hf]), reads=[("h_s", hf)], writes=hallk, key="hsi")
            for qt in range(2):
                tl = ((qt * 512, 0),)
                def epi_qkv(oi, oc, Pi):
                    if oc < 8:
                        o_, k_ = qTv(oc)
                        copy_op(alt(), o_, P[Pi][:, 0:512], self.pk(Pi, 0), k_, scale=0.0625)
                    elif oc < 16:
                        o_, k_ = kTv(oc - 8)
                        copy_op(alt(), o_, P[Pi][:, 0:512], self.pk(Pi, 0), k_)
                    else:
                        o_, k_ = vTv(oc - 16)
                        copy_op(alt(), o_, P[Pi][:, 0:512], self.pk(Pi, 0), k_)
                qi = hf * 2 + qt
                kv3, _ = R2.view(8192, BF16, (24, 512))
                if state_only:
                    def epi_a(oi, oc, Pi):
                        s.op("act", lambda e: e.activation(out=aT[0:16, :], in_=P[Pi][0:16, 0:512], func=AF.Identity),
                             reads=self.pk(Pi, 0), writes=aTk)
                    self.proj(dr["gla_wa_%d" % j], 1, 1, KC, self.hT, lambda oi: 0, epi_a, tiles=tl, m=16)
                    self.proj(dr["gla_w_in_%d" % j], None, 1, KC, self.hT, lambda oi: 1 + oi % 3, epi_qkv,
                              oc_list=list(range(8, 32)), tiles=tl)
                    s.dma(lambda q, qi=qi: q.dma_start(out=dr["kv_s"][qi], in_=kv3), reads=kTk + vTk,
                          writes=[("kv_s", qi)], key="kvo", eng="act")
                    s.dma(lambda q, qi=qi: q.dma_start(out=dr["a_s"][qi], in_=aT[0:16, :]), reads=aTk,
                          writes=[("a_s", qi)], key="ao", eng="act")
                else:
                    s.dma(lambda q, qi=qi: q.dma_start(out=kv3[:, 0:8, :], in_=dr["kv_s"][qi][:, 0:8, :]), reads=[("kv_s", qi)],
                          writes=kTk, key="kvi")
                    self.proj(dr["gla_w_in_%d" % j], None, 1, KC, self.hT, lambda oi: 1 + oi % 3, epi_qkv,
                              oc_list=list(range(0, 8)), tiles=tl)
                if not init_done[0]:
                    init_done[0] = True
                    init_state()
                for ch in range(4):
                    c0 = ch * 128
                    tokoff = qt * 512 + c0

                    def state_mm():
                        for dc in range(8):
                            h = dc // 2
                            s.op("pe", lambda e, dc=dc, h=h: e.matmul(P[3][:, (dc % 2) * 512:(dc % 2 + 1) * 512],
                                                                     lhsT=ktok[:, dc * 128:(dc + 1) * 128],
                                                                     rhs=vtok[:, h * 512:(h + 1) * 512], start=True, stop=True),
                                 reads=ktk + vtk, writes=self.pk(3, dc % 2))
                            S, Sk = S32v(dc)
                            s.op("dve", lambda e, S=S, dc=dc: e.scalar_tensor_tensor(
                                out=S, in0=S, scalar=ebl[:, dc:dc + 1], in1=P[3][:, (dc % 2) * 512:(dc % 2 + 1) * 512],
                                op0=ALU.mult, op1=ALU.add), reads=Sk + eblk_ + self.pk(3, dc % 2), writes=Sk)

                    def state_cast():
                        for dc in range(8):
                            S, Sk = S32v(dc)
                            Sb, Sbk = Sbfv(dc)
                            s.op("act", lambda e, S=S, Sb=Sb: e.activation(out=Sb, in_=S, func=AF.Identity), reads=Sk, writes=Sbk)

                    ci = (hf * 2 + qt) * 4 + ch
                    if state_only:
                        def emit_z(e, c0=c0):
                            ins = None
                            for hh in range(2):
                                ins = e.matmul(P[0][:, hh * 512:(hh + 1) * 512], lhsT=aT[:, c0:c0 + 128],
                                               rhs=wup[:, hh * 512:(hh + 1) * 512], start=True, stop=True)
                            return ins
                        s.op("pe", emit_z, reads=aTk + wupk, writes=self.pk(0))
                        s.op("act", lambda e: e.activation(out=e1, in_=P[0][:], func=AF.Exp, scale=-1.0), reads=self.pk(0), writes=e1k)
                        s.op("act", lambda e: e.activation(out=gpos, in_=e1, func=AF.Ln, bias=self.one32[:]),
                             reads=e1k + ["one32"], writes=gpk)

                        ghi, ghk = R2.view(36864, BF16, (1024,))
                        glo, glk = R2.view(38912, BF16, (1024,))
                        s.op("dve", lambda e: e.tensor_copy(out=ghi, in_=gpos), reads=gpk, writes=ghk)
                        s.op("dve", lambda e: e.tensor_tensor(out=glo, in0=gpos, in1=ghi, op=ALU.subtract), reads=gpk + ghk, writes=glk)

                        def emit_cs(e):
                            ins = None
                            for dc in range(8):
                                e.matmul(P[1][:, dc * 128:(dc + 1) * 128], lhsT=ghi[:, dc * 128:(dc + 1) * 128],
                                         rhs=self.tribf[:], start=True, stop=False)
                                ins = e.matmul(P[1][:, dc * 128:(dc + 1) * 128], lhsT=glo[:, dc * 128:(dc + 1) * 128],
                                               rhs=self.tribf[:], start=False, stop=True)
                            return ins
                        s.op("pe", emit_cs, reads=ghk + glk + ["tribf"], writes=self.pk(1))
                        s.op("dve", lambda e: e.tensor_scalar(out=bl, in0=P1v[:, :, 127], scalar1=-0.0625, scalar2=None, op0=ALU.mult),
                             reads=self.pk(1), writes=blk_)
                        s.op("act", lambda e: e.activation(out=ebl, in_=bl, func=AF.Exp), reads=blk_, writes=eblk_)
                        if state_only:
                            s.op("dve", lambda e: e.tensor_tensor(out=Bsum, in0=Bsum, in1=bl, op=ALU.add), reads=Bsk + blk_, writes=Bsk)
                        (EK, EKk) = tE[0]

                        def emit_ek(e, EK=EK):
                            ins = None
                            for dc in range(8):
                                ins = e.activation(out=EK[:, dc, :], in_=P1v[:, dc, :], func=AF.Exp, scale=0.0625, bias=bl[:, dc:dc + 1])
                            return ins
                        s.op("act", emit_ek, reads=self.pk(1) + blk_, writes=EKk)
                        s.op("dve", lambda e, c0=c0, EK=EK: e.tensor_tensor(out=kh, in0=kT3[:, :, c0:c0 + 128], in1=EK, op=ALU.mult),
                             reads=kTk + EKk, writes=khk)
                        def emit_tk(e):
                            ins = None
                            for dc in range(8):
                                ins = e.transpose(out=P2bf[:, dc * 128:(dc + 1) * 128], in_=kh[:, dc, :], identity=self.ident[:])
                            return ins
                        s.op("pe", emit_tk, reads=khk + ["ident"], writes=self.pk(2, 0))

                        def emit_tv(e, c0=c0):
                            ins = None
                            for c in range(16):
                                ins = e.transpose(out=P3bf[:, c * 128:(c + 1) * 128], in_=vTv(c)[0][:, c0:c0 + 128], identity=self.ident[:])
                            return ins
                        s.op("pe", emit_tv, reads=vTk + ["ident"], writes=self.pk(3))
                        s.op("act", lambda e: e.activation(out=ktok, in_=P2bf[:, 0:1024], func=AF.Identity), reads=self.pk(2, 0), writes=ktk)
                        s.op("dve", lambda e: e.tensor_copy(out=vtok, in_=P3bf[:, 0:2048]), reads=self.pk(3), writes=vtk)
                        csc, csck = tEf[1]
                        s.op("act", lambda e, csc=csc: e.activation(out=csc, in_=P[1][:], func=AF.Identity), reads=self.pk(1), writes=csck)
                        s.dma(lambda q, ci=ci: q.dma_start(out=dr["ck_s"][ci], in_=ktok), reads=ktk, writes=[("ck_s", ci)], key="cko", eng="act")
                        s.dma(lambda q, ci=ci: q.dma_start(out=dr["cv_s"][ci], in_=vtok), reads=vtk, writes=[("cv_s", ci)], key="cvo", eng="act")
                        s.dma(lambda q, ci=ci, csc=csc: q.dma_start(out=dr["cs_s"][ci], in_=csc), reads=csck, writes=[("cs_s", ci)], key="cso", eng="act")
                    else:
                        csb, csbk = R2.view(32768, F32, (1024,))
                        csb3, _ = R2.view(32768, F32, (8, 128))
                        s.dma(lambda q, ci=ci: q.dma_start(out=ktok, in_=dr["ck_s"][ci]), reads=[("ck_s", ci)], writes=ktk, key="ckl")
                        s.dma(lambda q, ci=ci: q.dma_start(out=vtok, in_=dr["cv_s"][ci]), reads=[("cv_s", ci)], writes=vtk, key="cvl")
                        s.dma(lambda q, ci=ci: q.dma_start(out=csb, in_=dr["cs_s"][ci]), reads=[("cs_s", ci)], writes=csbk, key="csl")
                        s.op("dve", lambda e: e.tensor_scalar(out=bl, in0=csb3[:, :, 127], scalar1=-0.0625, scalar2=None, op0=ALU.mult),
                             reads=csbk, writes=blk_)
                        s.op("act", lambda e: e.activation(out=ebl, in_=bl, func=AF.Exp), reads=blk_, writes=eblk_)
                        EBf, EBk = tEf[1]
                        s.op("act", lambda e, EBf=EBf: e.activation(out=EBf, in_=csb, func=AF.Exp, scale=-0.0625),
                             reads=csbk, writes=EBk)
                        s.op("dve", lambda e, c0=c0: e.tensor_tensor(out=qd, in0=qT3[:, :, c0:c0 + 128], in1=tE[1][0], op=ALU.mult),
                             reads=qTk + EBk, writes=qdk)
                        ENf, ENk = tEf[0]
                        s.op("act", lambda e, ENf=ENf: e.activation(out=ENf, in_=csb, func=AF.Exp, scale=0.0625),
                             reads=csbk, writes=ENk)
                        s.op("dve", lambda e, c0=c0: e.tensor_tensor(out=kd, in0=kT3[:, :, c0:c0 + 128], in1=tE[0][0], op=ALU.mult),
                             reads=kTk + ENk, writes=kdk)
                    if B:
                        def emit_sc(e):
                            ins = None
                            for h in range(4):
                                for d2 in range(2):
                                    dc = 2 * h + d2
                                    ins = e.matmul(P[2][:, 512 + h * 128:512 + (h + 1) * 128], lhsT=kd[:, dc, :], rhs=qd[:, dc, :],
                                                   start=(d2 == 0), stop=(d2 == 1))
                            return ins
                        s.op("pe", emit_sc, reads=kdk + qdk, writes=self.pk(2, 1))
                        s.op("dve", lambda e: e.tensor_tensor(out=sc, in0=P[2][:, 512:1024], in1=self.tri4[:], op=ALU.mult),
                             reads=self.pk(2, 1) + ["tri4"], writes=sck)

                        def emit_o(e):
                            ins = None
                            for h in range(4):
                                for es in range(4):
                                    blk = h * 4 + es
                                    out = P[blk // 8][:, (blk % 8) * 128:(blk % 8 + 1) * 128]
                                    e.matmul(out, lhsT=Sbfv(2 * h)[0][:, es * 128:(es + 1) * 128], rhs=qd[:, 2 * h, :], start=True, stop=False)
                                    e.matmul(out, lhsT=Sbfv(2 * h + 1)[0][:, es * 128:(es + 1) * 128], rhs=qd[:, 2 * h + 1, :], start=False, stop=False)
                                    ins = e.matmul(out, lhsT=vtok[:, h * 512 + es * 128:h * 512 + (es + 1) * 128],
                                                   rhs=sc[:, h * 128:(h + 1) * 128], start=False, stop=True)
                            return ins
                        s.op("pe", emit_o, reads=Sballk + qdk + vtk + sck, writes=self.pk(0) + self.pk(1))
                        state_mm()
                        s.op("act", lambda e: e.activation(out=osq[:, 0:1024], in_=P[0][:], func=AF.Square), reads=self.pk(0), writes=osqk)
                        s.op("act", lambda e: e.activation(out=osq[:, 1024:2048], in_=P[1][:], func=AF.Square), reads=self.pk(1), writes=osqk)

                        def emit_hs(e):
                            ins = None
                            for h in range(4):
                                for es in range(4):
                                    ins = e.matmul(P[2][:, h * 128:(h + 1) * 128], lhsT=self.ones[:],
                                                   rhs=osq[:, (h * 4 + es) * 128:(h * 4 + es + 1) * 128], start=(es == 0), stop=(es == 3))
                            return ins
                        s.op("pe", emit_hs, reads=osqk + ["ones"], writes=self.pk(2, 0))
                        s.op("act", lambda e: e.activation(out=rh, in_=P[2][:, 0:512], func=AF.Ln, scale=1.0 / 512, bias=self.epsc[:]),
                             reads=self.pk(2, 0) + ["epsc"], writes=rhk)
                        s.op("act", lambda e: e.activation(out=rh, in_=rh, func=AF.Exp, scale=-0.5), reads=rhk, writes=rhk)
                        for blk in range(16):
                            h = blk // 4
                            o_, ok_ = onT(blk)
                            s.op("dve", lambda e, blk=blk, h=h, o_=o_, tokoff=tokoff: e.scalar_tensor_tensor(
                                out=o_[:, tokoff:tokoff + 128], in0=P[blk // 8][:, (blk % 8) * 128:(blk % 8 + 1) * 128],
                                scalar=self.vecs[:, ngo + blk:ngo + blk + 1], in1=rh[:, h * 128:(h + 1) * 128],
                                op0=ALU.mult, op1=ALU.mult), reads=self.pk(blk // 8) + rhk + ["vecs"], writes=ok_)
                    if not B:
                        state_mm()
                    else:
                        state_cast()
            if B:
                def epi_r(oi, oc, Pi):
                    c = oc - 32
                    sr, srk = R3.view(8192 + (c % 2) * 4096, F32, (1024,))
                    s.op("act", lambda e: e.activation(out=sr, in_=P[Pi][:], func=AF.Silu), reads=self.pk(Pi), writes=srk)
                    o_, ok_ = onT(c)
                    s.op("dve", lambda e: e.tensor_tensor(out=o_, in0=o_, in1=sr, op=ALU.mult), reads=ok_ + srk, writes=ok_)
                self.proj(dr["gla_w_in_%d" % j], None, 1, KC, self.hT, lambda oi: oi % 2, epi_r, oc_list=list(range(32, 48)))
                yv = lambda c: R2.view(c * 4096, F32, (1024,))

                def epi_o(oi, oc, Pi):
                    y, yk = yv(oc)
                    s.op("act", lambda e: e.activation(out=y, in_=P[Pi][:], func=AF.Identity), reads=self.pk(Pi), writes=yk)
                    sq, sqk = R3.view((oc % 2) * 2048, BF16, (1024,))
                    s.op("dve", lambda e: e.tensor_tensor(out=sq, in0=P[Pi][:], in1=y, op=ALU.mult), reads=self.pk(Pi) + yk, writes=sqk)
                    self.flush_pending()
                    self.stats_mm(3, sq, sqk, oc == 0, oc == KC - 1)
                self.proj(dr["gla_w_out_%d" % j], KC, 1, KC, onT, lambda oi: oi % 2, epi_o)
                self.flush_pending()
                toks += self.postnorm(yv, xin, xin_name, xout, xout_name, hf, layer, 2, R1, 0)
        if state_only:
            s.op("act", lambda e: e.activation(out=Dj, in_=Bsum, func=AF.Exp), reads=Bsk, writes=Djk)
            s.dma(lambda q: q.dma_start(out=dr["gd_src"][:, 0:8], in_=Dj), reads=Djk, writes=["gd_src"], key="gdo", eng="act")
            s.dma(lambda q: q.dma_start(out=dr["gsA_src"], in_=Sall[:, 0:2048]), reads=Sallk, writes=["gsA_src"], key="gsoA", eng="act")
            s.dma(lambda q: q.dma_start(out=dr["gsB_src"], in_=Sall[:, 2048:4096]), reads=Sallk, writes=["gsB_src"], key="gsoB", eng="act")
            self.allgather(dr["gd_src"], dr["gd_g"], ["gd_src"], ["gd_g"])
            self.allgather(dr["gsA_src"], dr["gsA_g"], ["gsA_src"], ["gsA_g"])
            self.allgather(dr["gsB_src"], dr["gsB_g"], ["gsB_src"], ["gsB_g"])
        return toks


def tile_w(W, kcb, m=128):
    K, N = W.shape
    n_kg = K // (128 * kcb)
    a = W.reshape(n_kg, kcb, 128, N // m, m).transpose(3, 0, 2, 1, 4)
    return np.ascontiguousarray(a).reshape(N // m, n_kg, 128, kcb * m)


def colvec(v):
    return np.ascontiguousarray(v.reshape(-1, 128).T)


def build_vecs(inp, b, seg):
    cols = []
    off = {}

    def add(name, arr):
        off[name] = sum(a.shape[1] for a in cols)
        cols.append(np.asarray(arr, dtype=np.float32))
    add("c", colvec(inp["c"][b]))
    for i in range(DEPTH):
        add("b_ada%d" % i, colvec(inp["b_ada"][i]))
        for nm in ["pre_mix_g", "post_mix_g", "pre_ffn_g", "post_ffn_g"]:
            add(nm + "%d" % i, colvec(inp[nm][i]))
    for j in range(2):
        add("b_pw1_%d" % j, colvec(inp["conv_b_pw1"][j]))
        add("b_dw_%d" % j, colvec(inp["conv_b_dw"][j]))
        add("ln_g_%d" % j, colvec(inp["conv_ln_g"][j]))
        add("ln_b_%d" % j, colvec(inp["conv_ln_b"][j]))
        add("b_pw2_%d" % j, colvec(inp["conv_b_pw2"][j]))
        add("w_dw_%d" % j, colvec(inp["conv_w_dw"][j].reshape(-1)))
        add("gla_ng_%d" % j, colvec(inp["gla_norm_g"][j]))
    mprev = np.zeros((128, 4), np.float32)
    if seg > 0:
        mprev[:, seg - 1] = 1.0
    mlt = np.zeros((128, 4), np.float32)
    mlt[:, :seg] = 1.0
    add("mprev", mprev)
    add("mlt", mlt)
    return np.concatenate(cols, axis=1), off


FULL = [("convA", 0), ("convB", 0), ("ffn", 0), ("glaA", 1), ("glaB", 1), ("ffn", 1),
        ("convA", 2), ("convB", 2), ("ffn", 2), ("glaA", 3), ("glaB", 3), ("ffn", 3)]
MODES = {"full": FULL, "ffn0": [("ffn", 0)], "conv0": [("convA", 0), ("convB", 0)],
         "gla1": [("glaA", 1), ("glaB", 1)], "gla1A": [("glaA", 1)], "f0g1": [("ffn", 0), ("glaA", 1), ("glaB", 1)], "l0": FULL[:3], "l01": FULL[:6]}


def weight_arrays(inp, steps):
    w = {}
    for kind, L in steps:
        j = L // 2
        w["w_ada%d" % L] = lambda L=L: np.ascontiguousarray(inp["w_ada"][L])
        if kind == "ffn":
            w["ffn_w_in_%d" % L] = lambda L=L: tile_w(inp["ffn_w_in"][L], KC)
            w["ffn_w_out_%d" % L] = lambda L=L: tile_w(inp["ffn_w_out"][L], 11)
        elif kind == "convA":
            w["conv_w_pw1_%d" % j] = lambda j=j: tile_w(inp["conv_w_pw1"][j], KC)
        elif kind == "convB":
            w["conv_w_pw2_%d" % j] = lambda j=j: tile_w(inp["conv_w_pw2"][j], KC)
        elif kind in ("glaA", "glaB"):
            w["gla_w_in_%d" % j] = lambda j=j: tile_w(inp["gla_w_in"][j][:, :6144], KC)
            w["gla_wa_%d" % j] = lambda j=j: tile_w(inp["gla_w_in"][j][:, 6144:6160], KC, m=16)
            w["gla_wup_%d" % j] = lambda j=j: np.concatenate(
                [inp["gla_w_gate_up"][j], inp["gla_b_gate"][j][None, :]], axis=0).astype(np.float32)
            if kind == "glaB":
                w["gla_w_out_%d" % j] = lambda j=j: tile_w(inp["gla_w_out"][j], KC)
    return {k: f() for k, f in w.items()}


def host_consts():
    c = np.zeros((128, 640), np.float32)
    c[:, 0:128] = np.eye(128, dtype=np.float32)
    tri = np.triu(np.ones((128, 128), np.float32))
    c[:, 128:640] = np.tile(tri, (1, 4))
    return c


def build_program(vec_off, nvec, steps, wshapes):
    nc = bass.Bass("TRN2", target_bir_lowering=False)
    dr = {"vec_off": vec_off}
    dr["xT"] = nc.dram_tensor("xT", [KC, 128, TOK], F32, kind="ExternalInput").ap()
    dr["vecs"] = nc.dram_tensor("vecs", [128, nvec], F32, kind="ExternalInput").ap()
    dr["consts"] = nc.dram_tensor("consts", [128, 640], F32, kind="ExternalInput").ap()
    for k, shp in wshapes.items():
        dr[k] = nc.dram_tensor(k, list(shp), F32, kind="ExternalInput").ap()
    dr["out"] = nc.dram_tensor("out", [KC, 128, TOK], F32, kind="ExternalOutput").ap()
    dr["xs"] = nc.dram_tensor("xs", [KC, 128, TOK], F32).ap()
    dr["u_s"] = nc.dram_tensor("u_s", [KC, 128, HALO + TOK], BF16).ap()
    dr["halo_src"] = nc.dram_tensor("halo_src", [128, KC * HALO], BF16).ap()
    dr["halo_g"] = nc.dram_tensor("halo_g", [4 * 128, KC * HALO], BF16).ap()
    dr["kv_s"] = nc.dram_tensor("kv_s", [4, 128, 24, 512], BF16).ap()
    dr["a_s"] = nc.dram_tensor("a_s", [4, 16, 512], BF16).ap()
    dr["h_s"] = nc.dram_tensor("h_s", [2, 128, KC * TH], BF16).ap()
    dr["ck_s"] = nc.dram_tensor("ck_s", [16, 128, 1024], BF16).ap()
    dr["cv_s"] = nc.dram_tensor("cv_s", [16, 128, 2048], BF16).ap()
    dr["cs_s"] = nc.dram_tensor("cs_s", [16, 128, 1024], F32).ap()
    dr["gd_src"] = nc.dram_tensor("gd_src", [128, 64], F32).ap()
    dr["gd_g"] = nc.dram_tensor("gd_g", [4 * 128, 64], F32).ap()
    dr["gsA_src"] = nc.dram_tensor("gsA_src", [128, 2048], F32).ap()
    dr["gsA_g"] = nc.dram_tensor("gsA_g", [4 * 128, 2048], F32).ap()
    dr["gsB_src"] = nc.dram_tensor("gsB_src", [128, 2048], F32).ap()
    dr["gsB_g"] = nc.dram_tensor("gsB_g", [4 * 128, 2048], F32).ap()
    with contextlib.ExitStack() as es:
        b = Builder(nc, es, dr)
        b.epsc = es.enter_context(nc.sbuf_tensor("epsc", [128, 1], F32))
        b.s.op("dve", lambda e: e.memset(b.epsc[:], EPS), writes=["epsc"])
        layers = sorted(set(L for _, L in steps))
        b.prologue_mod(layers[:1])
        resid = [i for i, (k, _) in enumerate(steps) if k in ("convB", "glaB", "ffn")]
        cur, cur_name = dr["xT"], "xT"
        toks = []
        for i, (kind, L) in enumerate(steps):
            j = L // 2
            if resid and i == resid[-1]:
                nxt, nxt_name = dr["out"], "out"
            else:
                nxt, nxt_name = dr["xs"], "xs"
            if kind == "ffn":
                nl = layers[layers.index(L) + 1] if layers.index(L) + 1 < len(layers) else None
                toks = b.ffn(L, cur, cur_name, nxt, nxt_name, next_layer=nl)
            elif kind == "convA":
                b.conv_a(L, j, cur, cur_name)
            elif kind == "convB":
                toks = b.conv_b(L, j, cur, cur_name, nxt, nxt_name)
            elif kind == "glaA":
                b.gla(L, j, cur, cur_name, None, None, True)
            elif kind == "glaB":
                toks = b.gla(L, j, cur, cur_name, nxt, nxt_name, False)
            if i in resid:
                cur, cur_name = nxt, nxt_name
        b.s.emit(final_wait_tokens=toks)
    return nc


def run(inp, mode, trace=False):
    inp = {k: np.asarray(v) for k, v in inp.items()}
    steps = MODES[mode]
    W = weight_arrays(inp, steps)
    consts = host_consts()
    maps = []
    vec_off = None
    for core in range(NCORE):
        b, seg = core // 4, core % 4
        xT = np.ascontiguousarray(inp["x"][b, seg * TOK:(seg + 1) * TOK, :].T).reshape(KC, 128, TOK)
        vecs, vec_off = build_vecs(inp, b, seg)
        m = {"xT": xT, "vecs": vecs, "consts": consts}
        m.update(W)
        maps.append(m)
    nc = build_program(vec_off, maps[0]["vecs"].shape[1], steps, {k: v.shape for k, v in W.items()})
    res = run_bass_kernel_spmd(nc, maps, core_ids=list(range(NCORE)), trace=trace)
    out = np.empty((2, SEQ, D), np.float32)
    for core in range(NCORE):
        b, seg = core // 4, core % 4
        o = res.results[core]["out"].reshape(D, TOK)
        out[b, seg * TOK:(seg + 1) * TOK, :] = o.T
    return out, res


def kernel(**inputs):
    out, _ = run(inputs, "full")
    return out
```

```python
import contextlib
import numpy as np
import concourse.bass as bass
import concourse.mybir as mybir
from concourse.bass_utils import run_bass_kernel_spmd

F32 = mybir.dt.float32
BF16 = mybir.dt.bfloat16
AF = mybir.ActivationFunctionType
ALU = mybir.AluOpType

D = 2048
KC = 16
SEQ = 8192
NCORE = 8
TOK = 2048
TH = 1024
DFF = 5632
JC = 44
DEPTH = 4
CW = 31
HALO = 32
EPS = 1e-6
DK = 1024
DV = 2048
NH = 4
GIN = 6160
NWB = 4


class Sched:
    ENG = ["pe", "act", "dve", "pool", "sp"]

    def __init__(self, nc):
        self.nc = nc
        self.ops = {e: [] for e in self.ENG}
        self.cnt = {e: 0 for e in self.ENG}
        self.seen = {e: {} for e in self.ENG}
        self.last_w = {}
        self.readers = {}
        self.dma_cnt = {}
        self.dma_keys = []

    def _add(self, eng, fn, reads, writes, tok_kind, dma_key=None):
        deps = []
        for r in reads:
            t = self.last_w.get(r)
            if t is not None:
                deps.append(t)
        for w in writes:
            t = self.last_w.get(w)
            if t is not None:
                deps.append(t)
            deps.extend(self.readers.get(w, ()))
        if tok_kind == "eng":
            self.cnt[eng] += 1
            tok = ("eng", eng, self.cnt[eng])
        else:
            if dma_key not in self.dma_cnt:
                self.dma_cnt[dma_key] = 0
                self.dma_keys.append(dma_key)
            self.dma_cnt[dma_key] += 16
            tok = ("dma", dma_key, self.dma_cnt[dma_key])
        need = {}
        for t in deps:
            if t[0] == "eng" and t[1] == eng and tok_kind == "eng" and eng == "pe":
                continue
            k = (t[0], t[1])
            if t[2] > need.get(k, 0):
                need[k] = t[2]
        waits = []
        seen = self.seen[eng]
        for k, v in need.items():
            if seen.get(k, 0) >= v:
                continue
            seen[k] = v
            waits.append((k, v))
        self.ops[eng].append((waits, fn, tok))
        for r in reads:
            self.readers.setdefault(r, []).append(tok)
        for w in writes:
            self.last_w[w] = tok
            self.readers[w] = []
        return tok

    def op(self, eng, fn, reads=(), writes=()):
        return self._add(eng, fn, reads, writes, "eng")

    def dma(self, fn, reads=(), writes=(), key=None, eng="sp"):
        return self._add(eng, fn, reads, writes, "dma", dma_key=key)

    def emit(self, final_wait_tokens=()):
        nc = self.nc
        with contextlib.ExitStack() as es:
            sems = {}
            for e in self.ENG:
                sems[("eng", e)] = es.enter_context(nc.semaphore("s_" + e))
            for i, k in enumerate(self.dma_keys):
                sems[("dma", k)] = es.enter_context(nc.semaphore("d%d" % i))
            block = es.enter_context(nc.Block())
            eng_map = {"pe": block.tensor, "act": block.scalar, "dve": block.vector,
                       "pool": block.gpsimd, "sp": block.sync}
            for e in self.ENG:
                ops = self.ops[e]
                extra = final_wait_tokens if e == "sp" else ()

                def body(eh, ops=ops, e=e, extra=extra):
                    for waits, fn, tok in ops:
                        for k, v in waits:
                            eh.wait_ge(sems[k], v)
                        ins = fn(eh)
                        if tok[0] == "eng":
                            ins.then_inc(sems[("eng", e)], 1)
                        else:
                            ins.then_inc(sems[("dma", tok[1])], 16)
                    for t in extra:
                        eh.wait_ge(sems[(t[0], t[1])], t[2])
                eng_map[e](body)


class Region:
    def __init__(self, nc, es, name, nbytes):
        self.name = name
        self.nbytes = nbytes
        self.t = es.enter_context(nc.sbuf_tensor(name, [128, nbytes // 4], F32))

    def view(self, off, dtype, shape):
        esz = 4 if dtype == F32 else 2
        n = 1
        for x in shape:
            n *= x
        assert off % 4 == 0 and (n * esz) % 4 == 0 and off + n * esz <= self.nbytes, (self.name, off, n, esz)
        a = self.t[:, off // 4:(off + n * esz) // 4]
        if dtype != F32:
            a = a.bitcast(dtype)
        if len(shape) == 2:
            a = a.rearrange("p (a b) -> p a b", a=shape[0])
        keys = [(self.name, pg) for pg in range(off // 1024, (off + n * esz + 1023) // 1024)]
        return a, keys


class Builder:
    def __init__(self, nc, es, dram):
        self.nc = nc
        self.es = es
        self.dr = dram
        self.s = Sched(nc)
        s = self.s
        self.R1 = Region(nc, es, "R1", 64 * 1024)
        self.R2 = Region(nc, es, "R2", 88 * 1024)
        self.R3 = Region(nc, es, "R3", 24 * 1024)
        self.wb = [es.enter_context(nc.sbuf_tensor("wb%d" % i, [128, 2048], BF16)) for i in range(NWB)]
        self.wi = 0
        self.P = [es.enter_context(nc.psum_tensor("P%d" % i, [128, 1024], F32)) for i in range(4)]
        self.ones = es.enter_context(nc.sbuf_tensor("ones", [128, 128], BF16))
        self.one32 = es.enter_context(nc.sbuf_tensor("one32", [128, 1], F32))
        self.vecs = es.enter_context(nc.sbuf_tensor("vecs_sb", [128, self.dr["vecs"].shape[1]], F32))
        self.cact = es.enter_context(nc.sbuf_tensor("cact", [128, 16], BF16))
        self.modrow = self.R3.t[0:1, 4096:6144]
        self.modc = es.enter_context(nc.sbuf_tensor("modc", [128, DEPTH * 96], F32))
        self.lay = es.enter_context(nc.sbuf_tensor("lay", [128, DEPTH * 96], F32))
        s.op("dve", lambda e: e.memset(self.ones[:], 1.0), writes=["ones"])
        s.op("dve", lambda e: e.memset(self.one32[:], 1.0), writes=["one32"])
        s.dma(lambda q: q.dma_start(out=self.vecs[:], in_=self.dr["vecs"]), writes=["vecs"], key="vecs")
        self.pending = []
        self.ident = es.enter_context(nc.sbuf_tensor("ident", [128, 128], BF16))
        self.tri4 = es.enter_context(nc.sbuf_tensor("tri4", [128, 512], F32))
        s.dma(lambda q: q.dma_start(out=self.ident[:], in_=self.dr["consts"][:, 0:128]), writes=["ident"],
              key="ident", eng="pool")
        s.dma(lambda q: q.dma_start(out=self.tri4[:], in_=self.dr["consts"][:, 128:640]), writes=["tri4"], key="tri4")
        self.tribf = es.enter_context(nc.sbuf_tensor("tribf", [128, 128], BF16))
        s.dma(lambda q: q.dma_start(out=self.tribf[:], in_=self.dr["consts"][:, 128:256]), writes=["tribf"],
              key="tribf", eng="pool")

    def pk(self, i, tt=None):
        if tt is None:
            return [("P", i, 0), ("P", i, 1)]
        return [("P", i, tt)]

    def load_w(self, src_ap, ncols):
        i = self.wi % NWB
        self.wi += 1
        t = self.wb[i]
        key = ("wb", i)
        self.s.dma(lambda q: q.dma_start(out=t[:, 0:ncols], in_=src_ap), writes=[key], key=key, eng="pool")
        return t, key

    def flush_pending(self):
        p = self.pending
        self.pending = []
        for f in p:
            f()

    def stats_mm(self, Pi, src_ap, src_keys, first, last):
        def f():
            def emit(e):
                ins = None
                for tt in range(2):
                    ins = e.matmul(self.P[Pi][:, tt * 512:(tt + 1) * 512], lhsT=self.ones[:],
                                   rhs=src_ap[:, tt * 512:(tt + 1) * 512], start=first, stop=last)
                return ins
            self.s.op("pe", emit, reads=list(src_keys) + ["ones"], writes=self.pk(Pi))
        self.pending.append(f)

    def vcol(self, name, c):
        o = self.dr["vec_off"][name]
        return self.vecs[:, o + c:o + c + 1]

    def prologue_mod(self, layers):
        s = self.s
        dr = self.dr
        co = dr["vec_off"]["c"]
        s.op("act", lambda e: e.activation(out=self.cact[:], in_=self.vecs[:, co:co + 16], func=AF.Silu),
             reads=["vecs"], writes=["cact"])
        self.pro_units = []
        for i in layers:
            units = self.mod_units(i, split=True)
            for _ in range(2 * KC + 1):
                units.pop(0)()
            self.pro_units = units

    def mod_units(self, i, split=False):
        s = self.s
        dr = self.dr
        units = []
        for nb in range(6):
            for kc in range(KC):
                def unit(nb=nb, kc=kc):
                    t, key = self.load_w(dr["w_ada%d" % i][kc * 128:(kc + 1) * 128, nb * 2048:(nb + 1) * 2048], 2048)

                    def emit(e):
                        ins = None
                        for si in range(16):
                            ins = e.matmul(self.P[2][:, si:si + 1], lhsT=t[:, si * 128:(si + 1) * 128],
                                           rhs=self.cact[:, kc:kc + 1], start=(kc == 0 and si == 0),
                                           stop=(kc == KC - 1), skip_group_check=True)
                        return ins
                    s.op("pe", emit, reads=[key, "cact"], writes=self.pk(2, 0))
                    if kc == KC - 1:
                        bo = dr["vec_off"]["b_ada%d" % i] + nb * 16
                        s.op("dve", lambda e: e.tensor_tensor(
                            out=self.modc[:, i * 96 + nb * 16:i * 96 + nb * 16 + 16], in0=self.P[2][:, 0:16],
                            in1=self.vecs[:, bo:bo + 16], op=ALU.add),
                            reads=self.pk(2, 0) + ["vecs"], writes=[("modc", i, nb)])
                units.append(unit)
            if split and nb == 1:
                units.append(lambda: self.mod_derive(i, which=(0, 1)))
        units.append(lambda: self.mod_derive(i, which=((2, 3, 4, 5) if split else (0, 1, 2, 3, 4, 5))))
        return units

    def mod_derive(self, i, which=(0, 1, 2, 3, 4, 5)):
        s = self.s
        dr = self.dr
        m = lambda nb, i=i: self.modc[:, i * 96 + nb * 16:i * 96 + nb * 16 + 16]
        L = lambda k, i=i: self.lay[:, i * 96 + k * 16:i * 96 + k * 16 + 16]
        vo = dr["vec_off"]
        g = lambda nm, i=i, vo=vo: self.vecs[:, vo[nm % i]:vo[nm % i] + 16]
        rd = [("modc", i, nb) for nb in range(6)] + ["vecs"]
        wr = [("lay", i)]
        if 0 in which:
            s.op("dve", lambda e: e.scalar_tensor_tensor(
                out=L(0), in0=m(1), scalar=1.0, in1=g("pre_mix_g%d"), op0=ALU.add, op1=ALU.mult), reads=rd, writes=wr)
        if 1 in which:
            s.op("dve", lambda e: e.tensor_copy(out=L(1), in_=m(0)), reads=rd, writes=wr)
        if 2 in which:
            s.op("dve", lambda e: e.tensor_tensor(out=L(2), in0=m(2), in1=g("post_mix_g%d"), op=ALU.mult), reads=rd, writes=wr)
        if 3 in which:
            s.op("dve", lambda e: e.scalar_tensor_tensor(
                out=L(3), in0=m(4), scalar=1.0, in1=g("pre_ffn_g%d"), op0=ALU.add, op1=ALU.mult), reads=rd, writes=wr)
        if 4 in which:
            s.op("dve", lambda e: e.tensor_copy(out=L(4), in_=m(3)), reads=rd, writes=wr)
        if 5 in which:
            s.op("dve", lambda e: e.tensor_tensor(out=L(5), in0=m(5), in1=g("post_ffn_g%d"), op=ALU.mult), reads=rd, writes=wr)

    def lcol(self, i, k, c):
        o = i * 96 + k * 16 + c
        return self.lay[:, o:o + 1]

    def hT(self, c, tt=None):
        if tt is None:
            return self.R1.view(c * 2048, BF16, (1024,))
        return self.R1.view(c * 2048 + tt * 1024, BF16, (512,))

    def prenorm(self, xin, xin_name, hf, layer, ka, kb):
        s = self.s
        xs = []
        for c in range(KC):
            a, keys = self.R2.view(c * 4096, F32, (1024,))
            xs.append((a, keys))
            s.dma(lambda q, a=a, c=c: q.dma_start(out=a, in_=xin[c, :, hf * TH:(hf + 1) * TH]),
                  reads=[(xin_name, c, hf)], writes=keys, key=("xst", c))
        for c in range(KC):
            a, keys = xs[c]
            sq, sqk = self.R3.view((c % 2) * 2048, BF16, (1024,))
            if c % 2 == 0:
                s.op("act", lambda e, a=a, sq=sq: e.activation(out=sq, in_=a, func=AF.Square), reads=keys, writes=sqk)
            else:
                s.op("dve", lambda e, a=a, sq=sq: e.tensor_tensor(out=sq, in0=a, in1=a, op=ALU.mult), reads=keys, writes=sqk)
            self.flush_pending()
            self.stats_mm(3, sq, sqk, c == 0, c == KC - 1)
        self.flush_pending()
        rs, rsk = self.R3.view(4096, F32, (1024,))
        s.op("act", lambda e: e.activation(out=rs, in_=self.P[3][:], func=AF.Ln, scale=1.0 / D, bias=self.epsc[:]),
             reads=self.pk(3) + ["epsc"], writes=rsk)
        s.op("act", lambda e: e.activation(out=rs, in_=rs, func=AF.Exp, scale=-0.5), reads=rsk, writes=rsk)
        for c in range(KC):
            a, keys = xs[c]
            tm, tmk = self.R3.view(8192 + (c % 2) * 4096, F32, (1024,))
            s.op("dve", lambda e, a=a, tm=tm: e.tensor_tensor(out=tm, in0=a, in1=rs, op=ALU.mult),
                 reads=keys + rsk, writes=tmk)
            h, hk = self.hT(c)
            s.op("act", lambda e, tm=tm, h=h, c=c: e.activation(
                out=h, in_=tm, func=AF.Identity, scale=self.lcol(layer, ka, c), bias=self.lcol(layer, kb, c)),
                reads=tmk + [("lay", layer)], writes=hk)

    def postnorm(self, yview, xin, xin_name, xout, xout_name, hf, layer, kg, stg_region, stg_off):
        s = self.s
        rs, rsk = self.R3.view(4096, F32, (1024,))
        s.op("act", lambda e: e.activation(out=rs, in_=self.P[3][:], func=AF.Ln, scale=1.0 / D, bias=self.epsc[:]),
             reads=self.pk(3) + ["epsc"], writes=rsk)
        s.op("act", lambda e: e.activation(out=rs, in_=rs, func=AF.Exp, scale=-0.5), reads=rsk, writes=rsk)
        toks = []
        for c in range(KC):
            xa, xk = stg_region.view(stg_off + (c % 8) * 4096, F32, (1024,))
            s.dma(lambda q, xa=xa, c=c: q.dma_start(out=xa, in_=xin[c, :, hf * TH:(hf + 1) * TH]),
                  reads=[(xin_name, c, hf)], writes=xk, key=("xst2", c % 8))
            y, yk = yview(c)
            s.op("dve", lambda e, y=y, c=c: e.scalar_tensor_tensor(
                out=y, in0=y, scalar=self.lcol(layer, kg, c), in1=rs, op0=ALU.mult, op1=ALU.mult),
                reads=yk + rsk + [("lay", layer)], writes=yk)
            s.op("dve", lambda e, y=y, xa=xa: e.tensor_tensor(out=xa, in0=y, in1=xa, op=ALU.add),
                 reads=yk + xk, writes=xk)
            t = s.dma(lambda q, xa=xa, c=c: q.dma_start(out=xout[c, :, hf * TH:(hf + 1) * TH], in_=xa),
                      reads=xk, writes=[(xout_name, c, hf)], key=("xo", c % 8), eng="act")
            toks.append(t)
        return toks

    def proj(self, wt, n_oc, n_kg, kcb, in_view, psel, epilogue, oc_list=None, tiles=((0, 0), (512, 512)), m=128, after_block=None):
        s = self.s
        for oi, oc in enumerate(oc_list if oc_list is not None else range(n_oc)):
            Pi = psel(oi)
            pkeys = []
            for (_, po) in tiles:
                pkeys += self.pk(Pi, po // 512)
            for kg in range(n_kg):
                t, key = self.load_w(wt[oc, kg], kcb * m)
                ins_ = [in_view(kg * kcb + k) for k in range(kcb)]
                rk = [key]
                for a, k_ in ins_:
                    rk += k_

                def emit(e, t=t, kg=kg, ins_=ins_, Pi=Pi):
                    ins = None
                    for k in range(kcb):
                        for (io, po) in tiles:
                            ins = e.matmul(self.P[Pi][0:m, po:po + 512], lhsT=t[:, k * m:(k + 1) * m],
                                           rhs=ins_[k][0][:, io:io + 512],
                                           start=(kg == 0 and k == 0), stop=(kg == n_kg - 1 and k == kcb - 1))
                    return ins
                if oi == 0 and kg == 0 and kcb > 1:
                    for k in range(kcb):
                        def emit1(e, t=t, k=k, ins_=ins_, Pi=Pi):
                            ins = None
                            for (io, po) in tiles:
                                ins = e.matmul(self.P[Pi][0:m, po:po + 512], lhsT=t[:, k * m:(k + 1) * m],
                                               rhs=ins_[k][0][:, io:io + 512],
                                               start=(k == 0), stop=(n_kg == 1 and k == kcb - 1))
                            return ins
                        s.op("pe", emit1, reads=[key] + ins_[k][1], writes=pkeys)
                else:
                    s.op("pe", emit, reads=rk, writes=pkeys)
                if after_block is not None:
                    after_block()
            epilogue(oi, oc, Pi)

    def ffn(self, layer, xin, xin_name, xout, xout_name, next_layer=None):
        s = self.s
        dr = self.dr
        toks = []
        units = self.mod_units(next_layer) if next_layer is not None else []

        def inject():
            if units:
                units.pop(0)()
        for hf in range(2):
            self.prenorm(xin, xin_name, hf, layer, 3, 4)
            hid = lambda j: self.R2.view(j * 2048, BF16, (1024,))
            w1 = dr["ffn_w_in_%d" % layer]
            for j in range(JC):
                Pg, Pu = (0, 1) if j % 2 == 0 else (2, 3)
                self.proj(w1, None, 1, KC, self.hT, lambda oi, Pg=Pg, Pu=Pu: (Pg, Pu)[oi], lambda *a: None,
                          oc_list=[j, JC + j])
                sg, sgk = self.R3.view(8192 + (j % 2) * 4096, F32, (1024,))
                s.op("act", lambda e, sg=sg, Pg=Pg: e.activation(out=sg, in_=self.P[Pg][:], func=AF.Silu),
                     reads=self.pk(Pg), writes=sgk)
                h, hk = hid(j)
                s.op("dve", lambda e, sg=sg, h=h, Pu=Pu: e.tensor_tensor(out=h, in0=sg, in1=self.P[Pu][:], op=ALU.mult),
                     reads=sgk + self.pk(Pu), writes=hk)
            yv = lambda c: self.R1.view(c * 4096, F32, (1024,))

            def epi(oi, oc, Pi):
                y, yk = yv(oc)
                s.op("act", lambda e: e.activation(out=y, in_=self.P[Pi][:], func=AF.Identity),
                     reads=self.pk(Pi), writes=yk)
                sq, sqk = self.R3.view((oc % 2) * 2048, BF16, (1024,))
                s.op("dve", lambda e: e.tensor_tensor(out=sq, in0=self.P[Pi][:], in1=y, op=ALU.mult),
                     reads=self.pk(Pi) + yk, writes=sqk)
                self.flush_pending()
                self.stats_mm(3, sq, sqk, oc == 0, oc == KC - 1)
            self.proj(dr["ffn_w_out_%d" % layer], KC, 4, 11, hid, lambda oi: oi % 2, epi, after_block=inject)
            self.flush_pending()
            toks += self.postnorm(yv, xin, xin_name, xout, xout_name, hf, layer, 5, self.R2, 0)
        while units:
            units.pop(0)()
        return toks


    def allgather(self, src, dst, src_keys, dst_keys):
        self.s.op("pool", lambda e: e.collective_compute(
            "AllGather", ALU.bypass, replica_groups=[[0, 1, 2, 3], [4, 5, 6, 7]],
            ins=[src.opt()], outs=[dst.opt()]), reads=list(src_keys) + ["cc_chain"], writes=list(dst_keys) + ["cc_chain"])

    def conv_a(self, layer, j, xin, xin_name):
        s = self.s
        dr = self.dr
        vo = dr["vec_off"]["b_pw1_%d" % j]
        for hf in range(2):
            self.prenorm(xin, xin_name, hf, layer, 0, 1)
            for c in range(KC):
                inj = bool(self.pro_units)
                Pa, Pg = (0, 1) if (c % 2 == 0 or inj) else (2, 3)

                def inject():
                    if self.pro_units:
                        self.pro_units.pop(0)()
                self.proj(dr["conv_w_pw1_%d" % j], None, 1, KC, self.hT, lambda oi, Pa=Pa, Pg=Pg: (Pa, Pg)[oi],
                          lambda *a: None, oc_list=[c, KC + c], after_block=(inject if inj else None))
                sg, sgk = self.R3.view(8192 + (c % 2) * 4096, F32, (1024,))
                s.op("act", lambda e, sg=sg, Pg=Pg, c=c: e.activation(
                    out=sg, in_=self.P[Pg][:], func=AF.Sigmoid, bias=self.vecs[:, vo + KC + c:vo + KC + c + 1]),
                    reads=self.pk(Pg) + ["vecs"], writes=sgk)
                u, uk = self.R2.view(65536 + (c % 3) * 2048, BF16, (1024,))
                s.op("dve", lambda e, sg=sg, u=u, Pa=Pa, c=c: e.scalar_tensor_tensor(
                    out=u, in0=self.P[Pa][:], scalar=self.vecs[:, vo + c:vo + c + 1], in1=sg, op0=ALU.add, op1=ALU.mult),
                    reads=sgk + self.pk(Pa) + ["vecs"], writes=uk)
                s.dma(lambda q, u=u, c=c, hf=hf: q.dma_start(
                    out=dr["u_s"][c, :, HALO + hf * TH:HALO + (hf + 1) * TH], in_=u),
                    reads=uk, writes=[("u_s", c, hf)], key=("uo", c % 3), eng="act")
                if hf == 1:
                    s.dma(lambda q, u=u, c=c: q.dma_start(out=dr["halo_src"][:, c * HALO:(c + 1) * HALO], in_=u[:, TH - HALO:TH]),
                          reads=uk, writes=[("halo_src", c)], key=("ho", c % 3), eng="act")
        while self.pro_units:
            self.pro_units.pop(0)()
        self.allgather(dr["halo_src"], dr["halo_g"], [("halo_src", c) for c in range(KC)], ["halo_g"])

    def conv_b(self, layer, j, xin, xin_name, xout, xout_name):
        s = self.s
        dr = self.dr
        vo = dr["vec_off"]
        toks = []
        G, Gk = self.R3.view(16384, BF16, (4, KC * HALO))
        s.dma(lambda q: q.dma_start(out=G, in_=dr["halo_g"].rearrange("(r p) k -> p r k", p=128)),
              reads=["halo_g"], writes=Gk, key="halo_ld")
        hal, halk = self.R3.view(16384 + 4096, BF16, (KC * HALO,))
        mo = vo["mprev"]
        s.op("dve", lambda e: e.tensor_scalar(out=hal, in0=G[:, 0, :], scalar1=self.vecs[:, mo:mo + 1], scalar2=None,
                                               op0=ALU.mult), reads=Gk + ["vecs"], writes=halk)
        for r in range(1, 4):
            s.op("dve", lambda e, r=r: e.scalar_tensor_tensor(
                out=hal, in0=G[:, r, :], scalar=self.vecs[:, mo + r:mo + r + 1], in1=hal, op0=ALU.mult, op1=ALU.add),
                reads=Gk + halk + ["vecs"], writes=halk)
        UW = HALO + TH
        for hf in range(2):
            Us = []
            for c in range(KC):
                U, Uk = self.R2.view(c * UW * 2, BF16, (UW,))
                Us.append((U, Uk))
                s.dma(lambda q, U=U, c=c, hf=hf: q.dma_start(out=U, in_=dr["u_s"][c, :, hf * TH:hf * TH + UW]),
                      reads=[("u_s", c, 0), ("u_s", c, 1)], writes=Uk, key=("Uld", c))
                if hf == 0:
                    s.op("dve", lambda e, U=U, c=c: e.tensor_copy(out=U[:, 0:HALO], in_=hal[:, c * HALO:(c + 1) * HALO]),
                         reads=halk + Uk, writes=Uk)
            vv = lambda c: self.R1.view(c * 4096, F32, (1024,))
            wo = vo["w_dw_%d" % j]
            bo = vo["b_dw_%d" % j]
            def build_taps(c):
                dg, dgk = self.R2.view(33792 + (c % 2) * 7936, BF16, (CW, 128))
                for k in range(CW):
                    rd = ["ident", "vecs"] + ([("dggate", c % 2)] if k > 0 else [])
                    wr = [("dgtap", c % 2, k)] + (dgk + [("dggate", c % 2)] if k == 0 else [])
                    if k % 2 == 0:
                        s.op("dve", lambda e, dg=dg, k=k, c=c: e.tensor_scalar(
                            out=dg[:, k, :], in0=self.ident[:], scalar1=self.vecs[:, wo + k * KC + c:wo + k * KC + c + 1],
                            scalar2=None, op0=ALU.mult), reads=rd, writes=wr)
                    else:
                        s.op("act", lambda e, dg=dg, k=k, c=c: e.activation(
                            out=dg[:, k, :], in_=self.ident[:], func=AF.Identity,
                            scale=self.vecs[:, wo + k * KC + c:wo + k * KC + c + 1]), reads=rd, writes=wr)
            build_taps(0)
            for c in range(KC):
                dg, dgk = self.R2.view(33792 + (c % 2) * 7936, BF16, (CW, 128))
                Pi = c % 2
                U, Uk = Us[c]

                def emit(e, dg=dg, U=U, Pi=Pi):
                    ins = None
                    for tt in range(2):
                        for k in range(CW):
                            ins = e.matmul(self.P[Pi][:, tt * 512:(tt + 1) * 512], lhsT=dg[:, k, :],
                                           rhs=U[:, tt * 512 + k + 2:tt * 512 + k + 2 + 512],
                                           start=(k == 0), stop=(k == CW - 1))
                    return ins
                s.op("pe", emit, reads=[("dgtap", c % 2, k) for k in range(CW)] + dgk + [("dggate", c % 2)] + Uk,
                     writes=self.pk(Pi))
                if c + 1 < KC:
                    build_taps(c + 1)
                v, vk = vv(c)
                s.op("act", lambda e, v=v, Pi=Pi, c=c: e.activation(
                    out=v, in_=self.P[Pi][:], func=AF.Identity, bias=self.vecs[:, bo + c:bo + c + 1]),
                    reads=self.pk(Pi) + ["vecs"], writes=vk)
                sq, sqk = self.R3.view((c % 2) * 2048, BF16, (1024,))
                vb, vbk = self.R3.view(4096 + (c % 2) * 2048, BF16, (1024,))
                s.op("dve", lambda e, v=v, vb=vb: e.tensor_copy(out=vb, in_=v), reads=vk, writes=vbk)
                s.op("dve", lambda e, v=v, sq=sq: e.tensor_tensor(out=sq, in0=v, in1=v, op=ALU.mult), reads=vk, writes=sqk)
                self.flush_pending()
                self.stats_mm(2, vb, vbk, c == 0, c == KC - 1)
                self.stats_mm(3, sq, sqk, c == 0, c == KC - 1)
            self.flush_pending()
            mean, mk = self.R3.view(8192, F32, (1024,))
            rstd, rk = self.R3.view(12288, F32, (1024,))
            s.op("act", lambda e: e.activation(out=mean, in_=self.P[2][:], func=AF.Identity, scale=1.0 / D),
                 reads=self.pk(2), writes=mk)
            s.op("dve", lambda e: e.tensor_tensor(out=rstd, in0=mean, in1=mean, op=ALU.mult), reads=mk, writes=rk)
            s.op("dve", lambda e: e.scalar_tensor_tensor(out=rstd, in0=self.P[3][:], scalar=1.0 / D, in1=rstd,
                                                          op0=ALU.mult, op1=ALU.subtract), reads=self.pk(3) + rk, writes=rk)
            s.op("act", lambda e: e.activation(out=rstd, in_=rstd, func=AF.Ln, bias=self.epsc[:]),
                 reads=rk + ["epsc"], writes=rk)
            s.op("act", lambda e: e.activation(out=rstd, in_=rstd, func=AF.Exp, scale=-0.5), reads=rk, writes=rk)
            sv = lambda c: self.R2.view(49664 + c * 2048, BF16, (1024,))
            go, lbo = vo["ln_g_%d" % j], vo["ln_b_%d" % j]
            for c in range(KC):
                v, vk = vv(c)
                t1, t1k = self.R3.view(16384 + (c % 2) * 4096, F32, (1024,))
                s.op("dve", lambda e, v=v, t1=t1: e.tensor_tensor(out=t1, in0=v, in1=mean, op=ALU.subtract),
                     reads=vk + mk, writes=t1k)
                s.op("dve", lambda e, t1=t1: e.tensor_tensor(out=t1, in0=t1, in1=rstd, op=ALU.mult),
                     reads=t1k + rk, writes=t1k)
                sc, sck = sv(c)
                s.op("act", lambda e, t1=t1, sc=sc, c=c: e.activation(
                    out=sc, in_=t1, func=AF.Silu, scale=self.vecs[:, go + c:go + c + 1], bias=self.vecs[:, lbo + c:lbo + c + 1]),
                    reads=t1k + ["vecs"], writes=sck)
            yv = vv
            b2 = vo["b_pw2_%d" % j]

            def epi(oi, oc, Pi):
                y, yk = yv(oc)
                s.op("act", lambda e: e.activation(out=y, in_=self.P[Pi][:], func=AF.Identity,
                                                   bias=self.vecs[:, b2 + oc:b2 + oc + 1]),
                     reads=self.pk(Pi) + ["vecs"], writes=yk)
                sq, sqk = self.R3.view((oc % 2) * 2048, BF16, (1024,))
                s.op("dve", lambda e: e.tensor_tensor(out=sq, in0=y, in1=y, op=ALU.mult), reads=yk, writes=sqk)
                self.flush_pending()
                self.stats_mm(3, sq, sqk, oc == 0, oc == KC - 1)
            self.proj(dr["conv_w_pw2_%d" % j], KC, 1, KC, sv, lambda oi: oi % 2, epi)
            self.flush_pending()
            toks += self.postnorm(yv, xin, xin_name, xout, xout_name, hf, layer, 2, self.R2, 0)
        return toks

    def rows(self, region, off, dtype, n, nrows):
        esz = 4 if dtype == F32 else 2
        a = region.t[0:nrows, off // 4:(off + n * esz) // 4]
        if dtype != F32:
            a = a.bitcast(dtype)
        keys = [(region.name, pg) for pg in range(off // 1024, (off + n * esz + 1023) // 1024)]
        return a, keys

    def gla(self, layer, j, xin, xin_name, xout, xout_name, state_only):
        s = self.s
        dr = self.dr
        vo = dr["vec_off"]
        R1, R2, R3 = self.R1, self.R2, self.R3
        toks = []
        B = not state_only
        wup, wupk = self.rows(R3, 16384, BF16, 1024, 17)
        aT, aTk = self.rows(R3, 18944, BF16, 512, 17)
        small, smk = R3.view(18432, F32, (8 * 8,))
        bl, ebl, Bsum, Dj, Dp = [small[:, i * 8:(i + 1) * 8] for i in range(5)]
        blk_, eblk_, Bsk, Djk, Dpk = [[("gsm", i)] for i in range(5)]
        s.dma(lambda q: q.dma_start(out=wup, in_=dr["gla_wup_%d" % j]), writes=wupk, key="wup", eng="pool")
        s.op("dve", lambda e: e.memset(aT, 1.0), writes=aTk)
        S32v = lambda dc: R2.view(65536 + dc * 2048, F32, (512,))
        Sbfv = lambda dc: R2.view(81920 + dc * 1024, BF16, (512,))
        Sall, Sallk = R2.view(65536, F32, (4096,))
        Sball, Sballk = R2.view(81920, BF16, (4096,))
        def init_state():
            s.op("dve", lambda e: e.memset(Sall, 0.0), writes=Sallk)
            if state_only:
                s.op("dve", lambda e: e.memset(Bsum, 0.0), writes=Bsk)
            else:
                mo = vo["mlt"]
                for jr in range(3):
                    s.dma(lambda q, jr=jr: q.dma_start(out=Dj, in_=dr["gd_g"][jr * 128:(jr + 1) * 128, 0:8]),
                          reads=["gd_g"], writes=Djk, key="Dj")
                    s.op("dve", lambda e, jr=jr: e.tensor_scalar(out=Dp, in0=Dj, scalar1=-1.0, scalar2=self.vecs[:, mo + jr:mo + jr + 1],
                                                                  op0=ALU.add, op1=ALU.mult), reads=Djk + ["vecs"], writes=Dpk)
                    s.op("dve", lambda e: e.tensor_scalar(out=Dp, in0=Dp, scalar1=1.0, scalar2=None, op0=ALU.add),
                         reads=Dpk, writes=Dpk)
                    for dc in range(8):
                        stg, stgk = R3.view(20480 + (dc % 2) * 2048, F32, (512,))
                        s.dma(lambda q, jr=jr, dc=dc, stg=stg: q.dma_start(
                            out=stg, in_=dr["gsA_g" if dc < 4 else "gsB_g"][jr * 128:(jr + 1) * 128, (dc % 4) * 512:(dc % 4 + 1) * 512]),
                            reads=["gsA_g" if dc < 4 else "gsB_g"], writes=stgk, key=("stg", dc % 2))
                        s.op("dve", lambda e, stg=stg, jr=jr: e.tensor_scalar(
                            out=stg, in0=stg, scalar1=self.vecs[:, mo + jr:mo + jr + 1], scalar2=None, op0=ALU.mult),
                            reads=stgk + ["vecs"], writes=stgk)
                        S, Sk = S32v(dc)
                        s.op("dve", lambda e, S=S, stg=stg, dc=dc: e.scalar_tensor_tensor(
                            out=S, in0=S, scalar=Dp[:, dc:dc + 1], in1=stg, op0=ALU.mult, op1=ALU.add),
                            reads=Sk + stgk + Dpk, writes=Sk)
                s.op("act", lambda e: e.activation(out=Sball, in_=Sall, func=AF.Identity), reads=Sallk, writes=Sballk)


        init_done = [False]

        qTv = lambda c: R2.view(c * 1024, BF16, (512,))
        kTv = lambda c: R2.view(8192 + c * 1024, BF16, (512,))
        vTv = lambda c: R2.view(16384 + c * 1024, BF16, (512,))
        qT3, _ = R2.view(0, BF16, (8, 512))
        kT3, _ = R2.view(8192, BF16, (8, 512))
        qTk = R2.view(0, BF16, (4096,))[1]
        kTk = R2.view(8192, BF16, (4096,))[1]
        vTk = R2.view(16384, BF16, (8192,))[1]
        gpos, gpk = R2.view(32768, F32, (1024,))
        e1, e1k = R2.view(36864, F32, (1024,))
        tE = [R2.view(40960, F32, (8, 128)), R2.view(45056, F32, (8, 128))]
        tEf = [R2.view(40960, F32, (1024,)), R2.view(45056, F32, (1024,))]
        qd, qdk = R2.view(49152, BF16, (8, 128))
        kd, kdk = R2.view(51200, BF16, (8, 128))
        kh, khk = R2.view(53248, BF16, (8, 128))
        vtok, vtk = R2.view(55296, BF16, (2048,))
        ktok, ktk = R2.view(59392, BF16, (1024,))
        sc, sck = R2.view(61440, BF16, (512,))
        osq, osqk = R3.view(0, BF16, (2048,))
        rh, rhk = R3.view(4096, F32, (512,))
        onT = lambda c: R1.view(32768 + c * 2048, BF16, (1024,))
        P = self.P
        P1v = P[1][:].rearrange("p (a b) -> p a b", a=8)
        P2bf = P[2][:].bitcast(BF16)
        P3bf = P[3][:].bitcast(BF16)
        ngo = vo["gla_ng_%d" % j]
        cnt = [0]

        def alt():
            cnt[0] += 1
            return "act" if cnt[0] % 2 else "dve"

        def copy_op(eng, out, in_, reads, writes, scale=None):
            if eng == "act":
                if scale is None:
                    s.op("act", lambda e: e.activation(out=out, in_=in_, func=AF.Identity), reads=reads, writes=writes)
                else:
                    s.op("act", lambda e: e.activation(out=out, in_=in_, func=AF.Identity, scale=scale), reads=reads, writes=writes)
            else:
                if scale is None:
                    s.op("dve", lambda e: e.tensor_copy(out=out, in_=in_), reads=reads, writes=writes)
                else:
                    s.op("dve", lambda e: e.tensor_scalar(out=out, in0=in_, scalar1=scale, scalar2=None, op0=ALU.mult),
                         reads=reads, writes=writes)

        for hf in range(2):
            hall, hallk = R1.view(0, BF16, (KC * TH,))
            if state_only:
                self.prenorm(xin, xin_name, hf, layer, 0, 1)
                s.dma(lambda q, hf=hf: q.dma_start(out=dr["h_s"][hf], in_=hall), reads=hallk, writes=[("h_s", hf)],
                      key="hso", eng="act")
            else:
                s.dma(lambda q, hf=hf: q.dma_start(out=hall, in_=dr["h_s"][hf]), reads=[("h_s", hf)], writes=hallk, key="hsi")
            for qt in range(2):
                tl = ((qt * 512, 0),)
                def epi_qkv(oi, oc, Pi):
                    if oc < 8:
                        o_, k_ = qTv(oc)
                        copy_op(alt(), o_, P[Pi][:, 0:512], self.pk(Pi, 0), k_, scale=0.0625)
                    elif oc < 16:
                        o_, k_ = kTv(oc - 8)
                        copy_op(alt(), o_, P[Pi][:, 0:512], self.pk(Pi, 0), k_)
                    else:
                        o_, k_ = vTv(oc - 16)
                        copy_op(alt(), o_, P[Pi][:, 0:512], self.pk(Pi, 0), k_)
                qi = hf * 2 + qt
                kv3, _ = R2.view(8192, BF16, (24, 512))
                if state_only:
                    def epi_a(oi, oc, Pi):
                        s.op("act", lambda e: e.activation(out=aT[0:16, :], in_=P[Pi][0:16, 0:512], func=AF.Identity),
                             reads=self.pk(Pi, 0), writes=aTk)
                    self.proj(dr["gla_wa_%d" % j], 1, 1, KC, self.hT, lambda oi: 0, epi_a, tiles=tl, m=16)
                    self.proj(dr["gla_w_in_%d" % j], None, 1, KC, self.hT, lambda oi: 1 + oi % 3, epi_qkv,
                              oc_list=list(range(8, 32)), tiles=tl)
                    s.dma(lambda q, qi=qi: q.dma_start(out=dr["kv_s"][qi], in_=kv3), reads=kTk + vTk,
                          writes=[("kv_s", qi)], key="kvo", eng="act")
                    s.dma(lambda q, qi=qi: q.dma_start(out=dr["a_s"][qi], in_=aT[0:16, :]), reads=aTk,
                          writes=[("a_s", qi)], key="ao", eng="act")
                else:
                    s.dma(lambda q, qi=qi: q.dma_start(out=kv3[:, 0:8, :], in_=dr["kv_s"][qi][:, 0:8, :]), reads=[("kv_s", qi)],
                          writes=kTk, key="kvi")
                    self.proj(dr["gla_w_in_%d" % j], None, 1, KC, self.hT, lambda oi: 1 + oi % 3, epi_qkv,
                              oc_list=list(range(0, 8)), tiles=tl)
                if not init_done[0]:
                    init_done[0] = True
                    init_state()
                for ch in range(4):
                    c0 = ch * 128
                    tokoff = qt * 512 + c0

                    def state_mm():
                        for dc in range(8):
                            h = dc // 2
                            s.op("pe", lambda e, dc=dc, h=h: e.matmul(P[3][:, (dc % 2) * 512:(dc % 2 + 1) * 512],
                                                                     lhsT=ktok[:, dc * 128:(dc + 1) * 128],
                                                                     rhs=vtok[:, h * 512:(h + 1) * 512], start=True, stop=True),
                                 reads=ktk + vtk, writes=self.pk(3, dc % 2))
                            S, Sk = S32v(dc)
                            s.op("dve", lambda e, S=S, dc=dc: e.scalar_tensor_tensor(
                                out=S, in0=S, scalar=ebl[:, dc:dc + 1], in1=P[3][:, (dc % 2) * 512:(dc % 2 + 1) * 512],
                                op0=ALU.mult, op1=ALU.add), reads=Sk + eblk_ + self.pk(3, dc % 2), writes=Sk)

                    def state_cast():
                        for dc in range(8):
                            S, Sk = S32v(dc)
                            Sb, Sbk = Sbfv(dc)
                            s.op("act", lambda e, S=S, Sb=Sb: e.activation(out=Sb, in_=S, func=AF.Identity), reads=Sk, writes=Sbk)

                    ci = (hf * 2 + qt) * 4 + ch
                    if state_only:
                        def emit_z(e, c0=c0):
                            ins = None
                            for hh in range(2):
                                ins = e.matmul(P[0][:, hh * 512:(hh + 1) * 512], lhsT=aT[:, c0:c0 + 128],
                                               rhs=wup[:, hh * 512:(hh + 1) * 512], start=True, stop=True)
                            return ins
                        s.op("pe", emit_z, reads=aTk + wupk, writes=self.pk(0))
                        s.op("act", lambda e: e.activation(out=e1, in_=P[0][:], func=AF.Exp, scale=-1.0), reads=self.pk(0), writes=e1k)
                        s.op("act", lambda e: e.activation(out=gpos, in_=e1, func=AF.Ln, bias=self.one32[:]),
                             reads=e1k + ["one32"], writes=gpk)

                        ghi, ghk = R2.view(36864, BF16, (1024,))
                        glo, glk = R2.view(38912, BF16, (1024,))
                        s.op("dve", lambda e: e.tensor_copy(out=ghi, in_=gpos), reads=gpk, writes=ghk)
                        s.op("dve", lambda e: e.tensor_tensor(out=glo, in0=gpos, in1=ghi, op=ALU.subtract), reads=gpk + ghk, writes=glk)

                        def emit_cs(e):
                            ins = None
                            for dc in range(8):
                                e.matmul(P[1][:, dc * 128:(dc + 1) * 128], lhsT=ghi[:, dc * 128:(dc + 1) * 128],
                                         rhs=self.tribf[:], start=True, stop=False)
                                ins = e.matmul(P[1][:, dc * 128:(dc + 1) * 128], lhsT=glo[:, dc * 128:(dc + 1) * 128],
                                               rhs=self.tribf[:], start=False, stop=True)
                            return ins
                        s.op("pe", emit_cs, reads=ghk + glk + ["tribf"], writes=self.pk(1))
                        s.op("dve", lambda e: e.tensor_scalar(out=bl, in0=P1v[:, :, 127], scalar1=-0.0625, scalar2=None, op0=ALU.mult),
                             reads=self.pk(1), writes=blk_)
                        s.op("act", lambda e: e.activation(out=ebl, in_=bl, func=AF.Exp), reads=blk_, writes=eblk_)
                        if state_only:
                            s.op("dve", lambda e: e.tensor_tensor(out=Bsum, in0=Bsum, in1=bl, op=ALU.add), reads=Bsk + blk_, writes=Bsk)
                        (EK, EKk) = tE[0]

                        def emit_ek(e, EK=EK):
                            ins = None
                            for dc in range(8):
                                ins = e.activation(out=EK[:, dc, :], in_=P1v[:, dc, :], func=AF.Exp, scale=0.0625, bias=bl[:, dc:dc + 1])
                            return ins
                        s.op("act", emit_ek, reads=self.pk(1) + blk_, writes=EKk)
                        s.op("dve", lambda e, c0=c0, EK=EK: e.tensor_tensor(out=kh, in0=kT3[:, :, c0:c0 + 128], in1=EK, op=ALU.mult),
                             reads=kTk + EKk, writes=khk)
                        def emit_tk(e):
                            ins = None
                            for dc in range(8):
                                ins = e.transpose(out=P2bf[:, dc * 128:(dc + 1) * 128], in_=kh[:, dc, :], identity=self.ident[:])
                            return ins
                        s.op("pe", emit_tk, reads=khk + ["ident"], writes=self.pk(2, 0))

                        def emit_tv(e, c0=c0):
                            ins = None
                            for c in range(16):
                                ins = e.transpose(out=P3bf[:, c * 128:(c + 1) * 128], in_=vTv(c)[0][:, c0:c0 + 128], identity=self.ident[:])
                            return ins
                        s.op("pe", emit_tv, reads=vTk + ["ident"], writes=self.pk(3))
                        s.op("act", lambda e: e.activation(out=ktok, in_=P2bf[:, 0:1024], func=AF.Identity), reads=self.pk(2, 0), writes=ktk)
                        s.op("dve", lambda e: e.tensor_copy(out=vtok, in_=P3bf[:, 0:2048]), reads=self.pk(3), writes=vtk)
                        csc, csck = tEf[1]
                        s.op("act", lambda e, csc=csc: e.activation(out=csc, in_=P[1][:], func=AF.Identity), reads=self.pk(1), writes=csck)
                        s.dma(lambda q, ci=ci: q.dma_start(out=dr["ck_s"][ci], in_=ktok), reads=ktk, writes=[("ck_s", ci)], key="cko", eng="act")
                        s.dma(lambda q, ci=ci: q.dma_start(out=dr["cv_s"][ci], in_=vtok), reads=vtk, writes=[("cv_s", ci)], key="cvo", eng="act")
                        s.dma(lambda q, ci=ci, csc=csc: q.dma_start(out=dr["cs_s"][ci], in_=csc), reads=csck, writes=[("cs_s", ci)], key="cso", eng="act")
                    else:
                        csb, csbk = R2.view(32768, F32, (1024,))
                        csb3, _ = R2.view(32768, F32, (8, 128))
                        s.dma(lambda q, ci=ci: q.dma_start(out=ktok, in_=dr["ck_s"][ci]), reads=[("ck_s", ci)], writes=ktk, key="ckl")
                        s.dma(lambda q, ci=ci: q.dma_start(out=vtok, in_=dr["cv_s"][ci]), reads=[("cv_s", ci)], writes=vtk, key="cvl")
                        s.dma(lambda q, ci=ci: q.dma_start(out=csb, in_=dr["cs_s"][ci]), reads=[("cs_s", ci)], writes=csbk, key="csl")
                        s.op("dve", lambda e: e.tensor_scalar(out=bl, in0=csb3[:, :, 127], scalar1=-0.0625, scalar2=None, op0=ALU.mult),
                             reads=csbk, writes=blk_)
                        s.op("act", lambda e: e.activation(out=ebl, in_=bl, func=AF.Exp), reads=blk_, writes=eblk_)
                        EBf, EBk = tEf[1]
                        s.op("act", lambda e, EBf=EBf: e.activation(out=EBf, in_=csb, func=AF.Exp, scale=-0.0625),
                             reads=csbk, writes=EBk)
                        s.op("dve", lambda e, c0=c0: e.tensor_tensor(out=qd, in0=qT3[:, :, c0:c0 + 128], in1=tE[1][0], op=ALU.mult),
                             reads=qTk + EBk, writes=qdk)
                        ENf, ENk = tEf[0]
                        s.op("act", lambda e, ENf=ENf: e.activation(out=ENf, in_=csb, func=AF.Exp, scale=0.0625),
                             reads=csbk, writes=ENk)
                        s.op("dve", lambda e, c0=c0: e.tensor_tensor(out=kd, in0=kT3[:, :, c0:c0 + 128], in1=tE[0][0], op=ALU.mult),
                             reads=kTk + ENk, writes=kdk)
                    if B:
                        def emit_sc(e):
                            ins = None
                            for h in range(4):
                                for d2 in range(2):
                                    dc = 2 * h + d2
                                    ins = e.matmul(P[2][:, 512 + h * 128:512 + (h + 1) * 128], lhsT=kd[:, dc, :], rhs=qd[:, dc, :],
                                                   start=(d2 == 0), stop=(d2 == 1))
                            return ins
                        s.op("pe", emit_sc, reads=kdk + qdk, writes=self.pk(2, 1))
                        s.op("dve", lambda e: e.tensor_tensor(out=sc, in0=P[2][:, 512:1024], in1=self.tri4[:], op=ALU.mult),
                             reads=self.pk(2, 1) + ["tri4"], writes=sck)

                        def emit_o(e):
                            ins = None
                            for h in range(4):
                                for es in range(4):
                                    blk = h * 4 + es
                                    out = P[blk // 8][:, (blk % 8) * 128:(blk % 8 + 1) * 128]
                                    e.matmul(out, lhsT=Sbfv(2 * h)[0][:, es * 128:(es + 1) * 128], rhs=qd[:, 2 * h, :], start=True, stop=False)
                                    e.matmul(out, lhsT=Sbfv(2 * h + 1)[0][:, es * 128:(es + 1) * 128], rhs=qd[:, 2 * h + 1, :], start=False, stop=False)
                                    ins = e.matmul(out, lhsT=vtok[:, h * 512 + es * 128:h * 512 + (es + 1) * 128],
                                                   rhs=sc[:, h * 128:(h + 1) * 128], start=False, stop=True)
                            return ins
                        s.op("pe", emit_o, reads=Sballk + qdk + vtk + sck, writes=self.pk(0) + self.pk(1))
                        state_mm()
                        s.op("act", lambda e: e.activation(out=osq[:, 0:1024], in_=P[0][:], func=AF.Square), reads=self.pk(0), writes=osqk)
                        s.op("act", lambda e: e.activation(out=osq[:, 1024:2048], in_=P[1][:], func=AF.Square), reads=self.pk(1), writes=osqk)

                        def emit_hs(e):
                            ins = None
                            for h in range(4):
                                for es in range(4):
                                    ins = e.matmul(P[2][:, h * 128:(h + 1) * 128], lhsT=self.ones[:],
                                                   rhs=osq[:, (h * 4 + es) * 128:(h * 4 + es + 1) * 128], start=(es == 0), stop=(es == 3))
                            return ins
                        s.op("pe", emit_hs, reads=osqk + ["ones"], writes=self.pk(2, 0))
                        s.op("act", lambda e: e.activation(out=rh, in_=P[2][:, 0:512], func=AF.Ln, scale=1.0 / 512, bias=self.epsc[:]),
                             reads=self.pk(2, 0) + ["epsc"], writes=rhk)
                        s.op("act", lambda e: e.activation(out=rh, in_=rh, func=AF.Exp, scale=-0.5), reads=rhk, writes=rhk)
                        for blk in range(16):
                            h = blk // 4
                            o_, ok_ = onT(blk)
                            s.op("dve", lambda e, blk=blk, h=h, o_=o_, tokoff=tokoff: e.scalar_tensor_tensor(
                                out=o_[:, tokoff:tokoff + 128], in0=P[blk // 8][:, (blk % 8) * 128:(blk % 8 + 1) * 128],
                                scalar=self.vecs[:, ngo + blk:ngo + blk + 1], in1=rh[:, h * 128:(h + 1) * 128],
                                op0=ALU.mult, op1=ALU.mult), reads=self.pk(blk // 8) + rhk + ["vecs"], writes=ok_)
                    if not B:
                        state_mm()
                    else:
                        state_cast()
            if B:
                def epi_r(oi, oc, Pi):
                    c = oc - 32
                    sr, srk = R3.view(8192 + (c % 2) * 4096, F32, (1024,))
                    s.op("act", lambda e: e.activation(out=sr, in_=P[Pi][:], func=AF.Silu), reads=self.pk(Pi), writes=srk)
                    o_, ok_ = onT(c)
                    s.op("dve", lambda e: e.tensor_tensor(out=o_, in0=o_, in1=sr, op=ALU.mult), reads=ok_ + srk, writes=ok_)
                self.proj(dr["gla_w_in_%d" % j], None, 1, KC, self.hT, lambda oi: oi % 2, epi_r, oc_list=list(range(32, 48)))
                yv = lambda c: R2.view(c * 4096, F32, (1024,))

                def epi_o(oi, oc, Pi):
                    y, yk = yv(oc)
                    s.op("act", lambda e: e.activation(out=y, in_=P[Pi][:], func=AF.Identity), reads=self.pk(Pi), writes=yk)
                    sq, sqk = R3.view((oc % 2) * 2048, BF16, (1024,))
                    s.op("dve", lambda e: e.tensor_tensor(out=sq, in0=P[Pi][:], in1=y, op=ALU.mult), reads=self.pk(Pi) + yk, writes=sqk)
                    self.flush_pending()
                    self.stats_mm(3, sq, sqk, oc == 0, oc == KC - 1)
                self.proj(dr["gla_w_out_%d" % j], KC, 1, KC, onT, lambda oi: oi % 2, epi_o)
                self.flush_pending()
                toks += self.postnorm(yv, xin, xin_name, xout, xout_name, hf, layer, 2, R1, 0)
        if state_only:
            s.op("act", lambda e: e.activation(out=Dj, in_=Bsum, func=AF.Exp), reads=Bsk, writes=Djk)
            s.dma(lambda q: q.dma_start(out=dr["gd_src"][:, 0:8], in_=Dj), reads=Djk, writes=["gd_src"], key="gdo", eng="act")
            s.dma(lambda q: q.dma_start(out=dr["gsA_src"], in_=Sall[:, 0:2048]), reads=Sallk, writes=["gsA_src"], key="gsoA", eng="act")
            s.dma(lambda q: q.dma_start(out=dr["gsB_src"], in_=Sall[:, 2048:4096]), reads=Sallk, writes=["gsB_src"], key="gsoB", eng="act")
            self.allgather(dr["gd_src"], dr["gd_g"], ["gd_src"], ["gd_g"])
            self.allgather(dr["gsA_src"], dr["gsA_g"], ["gsA_src"], ["gsA_g"])
            self.allgather(dr["gsB_src"], dr["gsB_g"], ["gsB_src"], ["gsB_g"])
        return toks


def tile_w(W, kcb, m=128):
    K, N = W.shape
    n_kg = K // (128 * kcb)
    a = W.reshape(n_kg, kcb, 128, N // m, m).transpose(3, 0, 2, 1, 4)
    return np.ascontiguousarray(a).reshape(N // m, n_kg, 128, kcb * m)


def colvec(v):
    return np.ascontiguousarray(v.reshape(-1, 128).T)


def build_vecs(inp, b, seg):
    cols = []
    off = {}

    def add(name, arr):
        off[name] = sum(a.shape[1] for a in cols)
        cols.append(np.asarray(arr, dtype=np.float32))
    add("c", colvec(inp["c"][b]))
    for i in range(DEPTH):
        add("b_ada%d" % i, colvec(inp["b_ada"][i]))
        for nm in ["pre_mix_g", "post_mix_g", "pre_ffn_g", "post_ffn_g"]:
            add(nm + "%d" % i, colvec(inp[nm][i]))
    for j in range(2):
        add("b_pw1_%d" % j, colvec(inp["conv_b_pw1"][j]))
        add("b_dw_%d" % j, colvec(inp["conv_b_dw"][j]))
        add("ln_g_%d" % j, colvec(inp["conv_ln_g"][j]))
        add("ln_b_%d" % j, colvec(inp["conv_ln_b"][j]))
        add("b_pw2_%d" % j, colvec(inp["conv_b_pw2"][j]))
        add("w_dw_%d" % j, colvec(inp["conv_w_dw"][j].reshape(-1)))
        add("gla_ng_%d" % j, colvec(inp["gla_norm_g"][j]))
    mprev = np.zeros((128, 4), np.float32)
    if seg > 0:
        mprev[:, seg - 1] = 1.0
    mlt = np.zeros((128, 4), np.float32)
    mlt[:, :seg] = 1.0
    add("mprev", mprev)
    add("mlt", mlt)
    return np.concatenate(cols, axis=1), off


FULL = [("convA", 0), ("convB", 0), ("ffn", 0), ("glaA", 1), ("glaB", 1), ("ffn", 1),
        ("convA", 2), ("convB", 2), ("ffn", 2), ("glaA", 3), ("glaB", 3), ("ffn", 3)]
MODES = {"full": FULL, "ffn0": [("ffn", 0)], "conv0": [("convA", 0), ("convB", 0)],
         "gla1": [("glaA", 1), ("glaB", 1)], "gla1A": [("glaA", 1)], "f0g1": [("ffn", 0), ("glaA", 1), ("glaB", 1)], "l0": FULL[:3], "l01": FULL[:6]}


def weight_arrays(inp, steps):
    w = {}
    for kind, L in steps:
        j = L // 2
        w["w_ada%d" % L] = lambda L=L: np.ascontiguousarray(inp["w_ada"][L])
        if kind == "ffn":
            w["ffn_w_in_%d" % L] = lambda L=L: tile_w(inp["ffn_w_in"][L], KC)
            w["ffn_w_out_%d" % L] = lambda L=L: tile_w(inp["ffn_w_out"][L], 11)
        elif kind == "convA":
            w["conv_w_pw1_%d" % j] = lambda j=j: tile_w(inp["conv_w_pw1"][j], KC)
        elif kind == "convB":
            w["conv_w_pw2_%d" % j] = lambda j=j: tile_w(inp["conv_w_pw2"][j], KC)
        elif kind in ("glaA", "glaB"):
            w["gla_w_in_%d" % j] = lambda j=j: tile_w(inp["gla_w_in"][j][:, :6144], KC)
            w["gla_wa_%d" % j] = lambda j=j: tile_w(inp["gla_w_in"][j][:, 6144:6160], KC, m=16)
            w["gla_wup_%d" % j] = lambda j=j: np.concatenate(
                [inp["gla_w_gate_up"][j], inp["gla_b_gate"][j][None, :]], axis=0).astype(np.float32)
            if kind == "glaB":
                w["gla_w_out_%d" % j] = lambda j=j: tile_w(inp["gla_w_out"][j], KC)
    return {k: f() for k, f in w.items()}


def host_consts():
    c = np.zeros((128, 640), np.float32)
    c[:, 0:128] = np.eye(128, dtype=np.float32)
    tri = np.triu(np.ones((128, 128), np.float32))
    c[:, 128:640] = np.tile(tri, (1, 4))
    return c


def build_program(vec_off, nvec, steps, wshapes):
    nc = bass.Bass("TRN2", target_bir_lowering=False)
    dr = {"vec_off": vec_off}
    dr["xT"] = nc.dram_tensor("xT", [KC, 128, TOK], F32, kind="ExternalInput").ap()
    dr["vecs"] = nc.dram_tensor("vecs", [128, nvec], F32, kind="ExternalInput").ap()
    dr["consts"] = nc.dram_tensor("consts", [128, 640], F32, kind="ExternalInput").ap()
    for k, shp in wshapes.items():
        dr[k] = nc.dram_tensor(k, list(shp), F32, kind="ExternalInput").ap()
    dr["out"] = nc.dram_tensor("out", [KC, 128, TOK], F32, kind="ExternalOutput").ap()
    dr["xs"] = nc.dram_tensor("xs", [KC, 128, TOK], F32).ap()
    dr["u_s"] = nc.dram_tensor("u_s", [KC, 128, HALO + TOK], BF16).ap()
    dr["halo_src"] = nc.dram_tensor("halo_src", [128, KC * HALO], BF16).ap()
    dr["halo_g"] = nc.dram_tensor("halo_g", [4 * 128, KC * HALO], BF16).ap()
    dr["kv_s"] = nc.dram_tensor("kv_s", [4, 128, 24, 512], BF16).ap()
    dr["a_s"] = nc.dram_tensor("a_s", [4, 16, 512], BF16).ap()
    dr["h_s"] = nc.dram_tensor("h_s", [2, 128, KC * TH], BF16).ap()
    dr["ck_s"] = nc.dram_tensor("ck_s", [16, 128, 1024], BF16).ap()
    dr["cv_s"] = nc.dram_tensor("cv_s", [16, 128, 2048], BF16).ap()
    dr["cs_s"] = nc.dram_tensor("cs_s", [16, 128, 1024], F32).ap()
    dr["gd_src"] = nc.dram_tensor("gd_src", [128, 64], F32).ap()
    dr["gd_g"] = nc.dram_tensor("gd_g", [4 * 128, 64], F32).ap()
    dr["gsA_src"] = nc.dram_tensor("gsA_src", [128, 2048], F32).ap()
    dr["gsA_g"] = nc.dram_tensor("gsA_g", [4 * 128, 2048], F32).ap()
    dr["gsB_src"] = nc.dram_tensor("gsB_src", [128, 2048], F32).ap()
    dr["gsB_g"] = nc.dram_tensor("gsB_g", [4 * 128, 2048], F32).ap()
    with contextlib.ExitStack() as es:
        b = Builder(nc, es, dr)
        b.epsc = es.enter_context(nc.sbuf_tensor("epsc", [128, 1], F32))
        b.s.op("dve", lambda e: e.memset(b.epsc[:], EPS), writes=["epsc"])
        layers = sorted(set(L for _, L in steps))
        b.prologue_mod(layers[:1])
        if steps[0][0] != "convA":
            while b.pro_units:
                b.pro_units.pop(0)()
        resid = [i for i, (k, _) in enumerate(steps) if k in ("convB", "glaB", "ffn")]
        cur, cur_name = dr["xT"], "xT"
        toks = []
        for i, (kind, L) in enumerate(steps):
            j = L // 2
            if resid and i == resid[-1]:
                nxt, nxt_name = dr["out"], "out"
            else:
                nxt, nxt_name = dr["xs"], "xs"
            if kind == "ffn":
                nl = layers[layers.index(L) + 1] if layers.index(L) + 1 < len(layers) else None
                toks = b.ffn(L, cur, cur_name, nxt, nxt_name, next_layer=nl)
            elif kind == "convA":
                b.conv_a(L, j, cur, cur_name)
            elif kind == "convB":
                toks = b.conv_b(L, j, cur, cur_name, nxt, nxt_name)
            elif kind == "glaA":
                b.gla(L, j, cur, cur_name, None, None, True)
            elif kind == "glaB":
                toks = b.gla(L, j, cur, cur_name, nxt, nxt_name, False)
            if i in resid:
                cur, cur_name = nxt, nxt_name
        b.s.emit(final_wait_tokens=toks)
    return nc


def run(inp, mode, trace=False):
    inp = {k: np.asarray(v) for k, v in inp.items()}
    steps = MODES[mode]
    W = weight_arrays(inp, steps)
    consts = host_consts()
    maps = []
    vec_off = None
    for core in range(NCORE):
        b, seg = core // 4, core % 4
        xT = np.ascontiguousarray(inp["x"][b, seg * TOK:(seg + 1) * TOK, :].T).reshape(KC, 128, TOK)
        vecs, vec_off = build_vecs(inp, b, seg)
        m = {"xT": xT, "vecs": vecs, "consts": consts}
        m.update(W)
        maps.append(m)
    nc = build_program(vec_off, maps[0]["vecs"].shape[1], steps, {k: v.shape for k, v in W.items()})
    res = run_bass_kernel_spmd(nc, maps, core_ids=list(range(NCORE)), trace=trace)
    out = np.empty((2, SEQ, D), np.float32)
    for core in range(NCORE):
        b, seg = core // 4, core % 4
        o = res.results[core]["out"].reshape(D, TOK)
        out[b, seg * TOK:(seg + 1) * TOK, :] = o.T
    return out, res


def kernel(**inputs):
    out, _ = run(inputs, "full")
    return out
```

```python
import contextlib
import numpy as np
import concourse.bass as bass
import concourse.mybir as mybir
from concourse.bass_utils import run_bass_kernel_spmd

F32 = mybir.dt.float32
BF16 = mybir.dt.bfloat16
AF = mybir.ActivationFunctionType
ALU = mybir.AluOpType

D = 2048
KC = 16
SEQ = 8192
NCORE = 8
TOK = 2048
TH = 1024
DFF = 5632
JC = 44
DEPTH = 4
CW = 31
HALO = 32
EPS = 1e-6
DK = 1024
DV = 2048
NH = 4
GIN = 6160
NWB = 4


class Sched:
    ENG = ["pe", "act", "dve", "pool", "sp"]

    def __init__(self, nc):
        self.nc = nc
        self.ops = {e: [] for e in self.ENG}
        self.cnt = {e: 0 for e in self.ENG}
        self.seen = {e: {} for e in self.ENG}
        self.last_w = {}
        self.readers = {}
        self.dma_cnt = {}
        self.dma_keys = []

    def _add(self, eng, fn, reads, writes, tok_kind, dma_key=None):
        deps = []
        for r in reads:
            t = self.last_w.get(r)
            if t is not None:
                deps.append(t)
        for w in writes:
            t = self.last_w.get(w)
            if t is not None:
                deps.append(t)
            deps.extend(self.readers.get(w, ()))
        if tok_kind == "eng":
            self.cnt[eng] += 1
            tok = ("eng", eng, self.cnt[eng])
        else:
            if dma_key not in self.dma_cnt:
                self.dma_cnt[dma_key] = 0
                self.dma_keys.append(dma_key)
            self.dma_cnt[dma_key] += 16
            tok = ("dma", dma_key, self.dma_cnt[dma_key])
        need = {}
        for t in deps:
            if t[0] == "eng" and t[1] == eng and tok_kind == "eng" and eng == "pe":
                continue
            k = (t[0], t[1])
            if t[2] > need.get(k, 0):
                need[k] = t[2]
        waits = []
        seen = self.seen[eng]
        for k, v in need.items():
            if seen.get(k, 0) >= v:
                continue
            seen[k] = v
            waits.append((k, v))
        self.ops[eng].append((waits, fn, tok))
        for r in reads:
            self.readers.setdefault(r, []).append(tok)
        for w in writes:
            self.last_w[w] = tok
            self.readers[w] = []
        return tok

    def op(self, eng, fn, reads=(), writes=()):
        return self._add(eng, fn, reads, writes, "eng")

    def dma(self, fn, reads=(), writes=(), key=None, eng="sp"):
        return self._add(eng, fn, reads, writes, "dma", dma_key=key)

    def emit(self, final_wait_tokens=()):
        nc = self.nc
        with contextlib.ExitStack() as es:
            sems = {}
            for e in self.ENG:
                sems[("eng", e)] = es.enter_context(nc.semaphore("s_" + e))
            for i, k in enumerate(self.dma_keys):
                sems[("dma", k)] = es.enter_context(nc.semaphore("d%d" % i))
            block = es.enter_context(nc.Block())
            eng_map = {"pe": block.tensor, "act": block.scalar, "dve": block.vector,
                       "pool": block.gpsimd, "sp": block.sync}
            for e in self.ENG:
                ops = self.ops[e]
                extra = final_wait_tokens if e == "sp" else ()

                def body(eh, ops=ops, e=e, extra=extra):
                    for waits, fn, tok in ops:
                        for k, v in waits:
                            eh.wait_ge(sems[k], v)
                        ins = fn(eh)
                        if tok[0] == "eng":
                            ins.then_inc(sems[("eng", e)], 1)
                        else:
                            ins.then_inc(sems[("dma", tok[1])], 16)
                    for t in extra:
                        eh.wait_ge(sems[(t[0], t[1])], t[2])
                eng_map[e](body)


class Region:
    def __init__(self, nc, es, name, nbytes):
        self.name = name
        self.nbytes = nbytes
        self.t = es.enter_context(nc.sbuf_tensor(name, [128, nbytes // 4], F32))

    def view(self, off, dtype, shape):
        esz = 4 if dtype == F32 else 2
        n = 1
        for x in shape:
            n *= x
        assert off % 4 == 0 and (n * esz) % 4 == 0 and off + n * esz <= self.nbytes, (self.name, off, n, esz)
        a = self.t[:, off // 4:(off + n * esz) // 4]
        if dtype != F32:
            a = a.bitcast(dtype)
        if len(shape) == 2:
            a = a.rearrange("p (a b) -> p a b", a=shape[0])
        keys = [(self.name, pg) for pg in range(off // 1024, (off + n * esz + 1023) // 1024)]
        return a, keys


class Builder:
    def __init__(self, nc, es, dram):
        self.nc = nc
        self.es = es
        self.dr = dram
        self.s = Sched(nc)
        s = self.s
        self.R1 = Region(nc, es, "R1", 64 * 1024)
        self.R2 = Region(nc, es, "R2", 88 * 1024)
        self.R3 = Region(nc, es, "R3", 24 * 1024)
        self.wb = [es.enter_context(nc.sbuf_tensor("wb%d" % i, [128, 2048], BF16)) for i in range(NWB)]
        self.wi = 0
        self.P = [es.enter_context(nc.psum_tensor("P%d" % i, [128, 1024], F32)) for i in range(4)]
        self.ones = es.enter_context(nc.sbuf_tensor("ones", [128, 128], BF16))
        self.one32 = es.enter_context(nc.sbuf_tensor("one32", [128, 1], F32))
        self.vecs = es.enter_context(nc.sbuf_tensor("vecs_sb", [128, self.dr["vecs"].shape[1]], F32))
        self.cact = es.enter_context(nc.sbuf_tensor("cact", [128, 16], BF16))
        self.modrow = self.R3.t[0:1, 4096:6144]
        self.modc = es.enter_context(nc.sbuf_tensor("modc", [128, DEPTH * 96], F32))
        self.lay = es.enter_context(nc.sbuf_tensor("lay", [128, DEPTH * 96], F32))
        s.op("dve", lambda e: e.memset(self.ones[:], 1.0), writes=["ones"])
        s.op("dve", lambda e: e.memset(self.one32[:], 1.0), writes=["one32"])
        s.dma(lambda q: q.dma_start(out=self.vecs[:], in_=self.dr["vecs"]), writes=["vecs"], key="vecs")
        self.pending = []
        self.ident = es.enter_context(nc.sbuf_tensor("ident", [128, 128], BF16))
        self.tri4 = es.enter_context(nc.sbuf_tensor("tri4", [128, 512], F32))
        s.dma(lambda q: q.dma_start(out=self.ident[:], in_=self.dr["consts"][:, 0:128]), writes=["ident"],
              key="ident", eng="pool")
        s.dma(lambda q: q.dma_start(out=self.tri4[:], in_=self.dr["consts"][:, 128:640]), writes=["tri4"], key="tri4")
        self.tribf = es.enter_context(nc.sbuf_tensor("tribf", [128, 128], BF16))
        s.dma(lambda q: q.dma_start(out=self.tribf[:], in_=self.dr["consts"][:, 128:256]), writes=["tribf"],
              key="tribf", eng="pool")

    def pk(self, i, tt=None):
        if tt is None:
            return [("P", i, 0), ("P", i, 1)]
        return [("P", i, tt)]

    def load_w(self, src_ap, ncols):
        i = self.wi % NWB
        self.wi += 1
        t = self.wb[i]
        key = ("wb", i)
        self.s.dma(lambda q: q.dma_start(out=t[:, 0:ncols], in_=src_ap), writes=[key], key=key, eng="pool")
        return t, key

    def flush_pending(self):
        p = self.pending
        self.pending = []
        for f in p:
            f()

    def stats_mm(self, Pi, src_ap, src_keys, first, last):
        def f():
            def emit(e):
                ins = None
                for tt in range(2):
                    ins = e.matmul(self.P[Pi][:, tt * 512:(tt + 1) * 512], lhsT=self.ones[:],
                                   rhs=src_ap[:, tt * 512:(tt + 1) * 512], start=first, stop=last)
                return ins
            self.s.op("pe", emit, reads=list(src_keys) + ["ones"], writes=self.pk(Pi))
        self.pending.append(f)

    def vcol(self, name, c):
        o = self.dr["vec_off"][name]
        return self.vecs[:, o + c:o + c + 1]

    def prologue_mod(self, layers):
        s = self.s
        dr = self.dr
        co = dr["vec_off"]["c"]
        s.op("act", lambda e: e.activation(out=self.cact[:], in_=self.vecs[:, co:co + 16], func=AF.Silu),
             reads=["vecs"], writes=["cact"])
        self.pro_units = []
        for i in layers:
            units = self.mod_units(i, split=True)
            for _ in range(2 * KC + 1):
                units.pop(0)()
            self.pro_units = units

    def mod_units(self, i, split=False):
        s = self.s
        dr = self.dr
        units = []
        for nb in range(6):
            for kc in range(KC):
                def unit(nb=nb, kc=kc):
                    t, key = self.load_w(dr["w_ada%d" % i][kc * 128:(kc + 1) * 128, nb * 2048:(nb + 1) * 2048], 2048)

                    def emit(e):
                        ins = None
                        for si in range(16):
                            ins = e.matmul(self.P[2][:, si:si + 1], lhsT=t[:, si * 128:(si + 1) * 128],
                                           rhs=self.cact[:, kc:kc + 1], start=(kc == 0 and si == 0),
                                           stop=(kc == KC - 1), skip_group_check=True)
                        return ins
                    s.op("pe", emit, reads=[key, "cact"], writes=self.pk(2, 0))
                    if kc == KC - 1:
                        bo = dr["vec_off"]["b_ada%d" % i] + nb * 16
                        s.op("dve", lambda e: e.tensor_tensor(
                            out=self.modc[:, i * 96 + nb * 16:i * 96 + nb * 16 + 16], in0=self.P[2][:, 0:16],
                            in1=self.vecs[:, bo:bo + 16], op=ALU.add),
                            reads=self.pk(2, 0) + ["vecs"], writes=[("modc", i, nb)])
                units.append(unit)
            if split and nb == 1:
                units.append(lambda: self.mod_derive(i, which=(0, 1)))
        units.append(lambda: self.mod_derive(i, which=((2, 3, 4, 5) if split else (0, 1, 2, 3, 4, 5))))
        return units

    def mod_derive(self, i, which=(0, 1, 2, 3, 4, 5)):
        s = self.s
        dr = self.dr
        m = lambda nb, i=i: self.modc[:, i * 96 + nb * 16:i * 96 + nb * 16 + 16]
        L = lambda k, i=i: self.lay[:, i * 96 + k * 16:i * 96 + k * 16 + 16]
        vo = dr["vec_off"]
        g = lambda nm, i=i, vo=vo: self.vecs[:, vo[nm % i]:vo[nm % i] + 16]
        rd = [("modc", i, nb) for nb in range(6)] + ["vecs"]
        wr = [("lay", i)]
        if 0 in which:
            s.op("dve", lambda e: e.scalar_tensor_tensor(
                out=L(0), in0=m(1), scalar=1.0, in1=g("pre_mix_g%d"), op0=ALU.add, op1=ALU.mult), reads=rd, writes=wr)
        if 1 in which:
            s.op("dve", lambda e: e.tensor_copy(out=L(1), in_=m(0)), reads=rd, writes=wr)
        if 2 in which:
            s.op("dve", lambda e: e.tensor_tensor(out=L(2), in0=m(2), in1=g("post_mix_g%d"), op=ALU.mult), reads=rd, writes=wr)
        if 3 in which:
            s.op("dve", lambda e: e.scalar_tensor_tensor(
                out=L(3), in0=m(4), scalar=1.0, in1=g("pre_ffn_g%d"), op0=ALU.add, op1=ALU.mult), reads=rd, writes=wr)
        if 4 in which:
            s.op("dve", lambda e: e.tensor_copy(out=L(4), in_=m(3)), reads=rd, writes=wr)
        if 5 in which:
            s.op("dve", lambda e: e.tensor_tensor(out=L(5), in0=m(5), in1=g("post_ffn_g%d"), op=ALU.mult), reads=rd, writes=wr)

    def lcol(self, i, k, c):
        o = i * 96 + k * 16 + c
        return self.lay[:, o:o + 1]

    def hT(self, c, tt=None):
        if tt is None:
            return self.R1.view(c * 2048, BF16, (1024,))
        return self.R1.view(c * 2048 + tt * 1024, BF16, (512,))

    def prenorm(self, xin, xin_name, hf, layer, ka, kb):
        s = self.s
        xs = []
        for c in range(KC):
            a, keys = self.R2.view(c * 4096, F32, (1024,))
            xs.append((a, keys))
            s.dma(lambda q, a=a, c=c: q.dma_start(out=a, in_=xin[c, :, hf * TH:(hf + 1) * TH]),
                  reads=[(xin_name, c, hf)], writes=keys, key=("xst", c))
        for c in range(KC):
            a, keys = xs[c]
            sq, sqk = self.R3.view((c % 2) * 2048, BF16, (1024,))
            if c % 2 == 0:
                s.op("act", lambda e, a=a, sq=sq: e.activation(out=sq, in_=a, func=AF.Square), reads=keys, writes=sqk)
            else:
                s.op("dve", lambda e, a=a, sq=sq: e.tensor_tensor(out=sq, in0=a, in1=a, op=ALU.mult), reads=keys, writes=sqk)
            self.flush_pending()
            self.stats_mm(3, sq, sqk, c == 0, c == KC - 1)
        self.flush_pending()
        rs, rsk = self.R3.view(4096, F32, (1024,))
        s.op("act", lambda e: e.activation(out=rs, in_=self.P[3][:], func=AF.Ln, scale=1.0 / D, bias=self.epsc[:]),
             reads=self.pk(3) + ["epsc"], writes=rsk)
        s.op("act", lambda e: e.activation(out=rs, in_=rs, func=AF.Exp, scale=-0.5), reads=rsk, writes=rsk)
        for c in range(KC):
            a, keys = xs[c]
            tm, tmk = self.R3.view(8192 + (c % 2) * 4096, F32, (1024,))
            s.op("dve", lambda e, a=a, tm=tm: e.tensor_tensor(out=tm, in0=a, in1=rs, op=ALU.mult),
                 reads=keys + rsk, writes=tmk)
            h, hk = self.hT(c)
            s.op("act", lambda e, tm=tm, h=h, c=c: e.activation(
                out=h, in_=tm, func=AF.Identity, scale=self.lcol(layer, ka, c), bias=self.lcol(layer, kb, c)),
                reads=tmk + [("lay", layer)], writes=hk)

    def postnorm(self, yview, xin, xin_name, xout, xout_name, hf, layer, kg, stg_region, stg_off):
        s = self.s
        rs, rsk = self.R3.view(4096, F32, (1024,))
        s.op("act", lambda e: e.activation(out=rs, in_=self.P[3][:], func=AF.Ln, scale=1.0 / D, bias=self.epsc[:]),
             reads=self.pk(3) + ["epsc"], writes=rsk)
        s.op("act", lambda e: e.activation(out=rs, in_=rs, func=AF.Exp, scale=-0.5), reads=rsk, writes=rsk)
        toks = []
        for c in range(KC):
            xa, xk = stg_region.view(stg_off + (c % 8) * 4096, F32, (1024,))
            s.dma(lambda q, xa=xa, c=c: q.dma_start(out=xa, in_=xin[c, :, hf * TH:(hf + 1) * TH]),
                  reads=[(xin_name, c, hf)], writes=xk, key=("xst2", c % 8))
            y, yk = yview(c)
            s.op("dve", lambda e, y=y, c=c: e.scalar_tensor_tensor(
                out=y, in0=y, scalar=self.lcol(layer, kg, c), in1=rs, op0=ALU.mult, op1=ALU.mult),
                reads=yk + rsk + [("lay", layer)], writes=yk)
            s.op("dve", lambda e, y=y, xa=xa: e.tensor_tensor(out=xa, in0=y, in1=xa, op=ALU.add),
                 reads=yk + xk, writes=xk)
            t = s.dma(lambda q, xa=xa, c=c: q.dma_start(out=xout[c, :, hf * TH:(hf + 1) * TH], in_=xa),
                      reads=xk, writes=[(xout_name, c, hf)], key=("xo", c % 8), eng="act")
            toks.append(t)
        return toks

    def proj(self, wt, n_oc, n_kg, kcb, in_view, psel, epilogue, oc_list=None, tiles=((0, 0), (512, 512)), m=128, after_block=None):
        s = self.s
        for oi, oc in enumerate(oc_list if oc_list is not None else range(n_oc)):
            Pi = psel(oi)
            pkeys = []
            for (_, po) in tiles:
                pkeys += self.pk(Pi, po // 512)
            for kg in range(n_kg):
                t, key = self.load_w(wt[oc, kg], kcb * m)
                ins_ = [in_view(kg * kcb + k) for k in range(kcb)]
                rk = [key]
                for a, k_ in ins_:
                    rk += k_

                def emit(e, t=t, kg=kg, ins_=ins_, Pi=Pi):
                    ins = None
                    for k in range(kcb):
                        for (io, po) in tiles:
                            ins = e.matmul(self.P[Pi][0:m, po:po + 512], lhsT=t[:, k * m:(k + 1) * m],
                                           rhs=ins_[k][0][:, io:io + 512],
                                           start=(kg == 0 and k == 0), stop=(kg == n_kg - 1 and k == kcb - 1))
                    return ins
                if oi == 0 and kg == 0 and kcb > 1:
                    for k in range(kcb):
                        def emit1(e, t=t, k=k, ins_=ins_, Pi=Pi):
                            ins = None
                            for (io, po) in tiles:
                                ins = e.matmul(self.P[Pi][0:m, po:po + 512], lhsT=t[:, k * m:(k + 1) * m],
                                               rhs=ins_[k][0][:, io:io + 512],
                                               start=(k == 0), stop=(n_kg == 1 and k == kcb - 1))
                            return ins
                        s.op("pe", emit1, reads=[key] + ins_[k][1], writes=pkeys)
                else:
                    s.op("pe", emit, reads=rk, writes=pkeys)
                if after_block is not None:
                    after_block()
            epilogue(oi, oc, Pi)

    def ffn(self, layer, xin, xin_name, xout, xout_name, next_layer=None):
        s = self.s
        dr = self.dr
        toks = []
        units = self.mod_units(next_layer) if next_layer is not None else []

        def inject():
            if units:
                units.pop(0)()
        for hf in range(2):
            self.prenorm(xin, xin_name, hf, layer, 3, 4)
            hid = lambda j: self.R2.view(j * 2048, BF16, (1024,))
            w1 = dr["ffn_w_in_%d" % layer]
            for j in range(JC):
                Pg, Pu = (0, 1) if j % 2 == 0 else (2, 3)
                self.proj(w1, None, 1, KC, self.hT, lambda oi, Pg=Pg, Pu=Pu: (Pg, Pu)[oi], lambda *a: None,
                          oc_list=[j, JC + j])
                sg, sgk = self.R3.view(8192 + (j % 2) * 4096, F32, (1024,))
                s.op("act", lambda e, sg=sg, Pg=Pg: e.activation(out=sg, in_=self.P[Pg][:], func=AF.Silu),
                     reads=self.pk(Pg), writes=sgk)
                h, hk = hid(j)
                s.op("dve", lambda e, sg=sg, h=h, Pu=Pu: e.tensor_tensor(out=h, in0=sg, in1=self.P[Pu][:], op=ALU.mult),
                     reads=sgk + self.pk(Pu), writes=hk)
            yv = lambda c: self.R1.view(c * 4096, F32, (1024,))

            def epi(oi, oc, Pi):
                y, yk = yv(oc)
                s.op("act", lambda e: e.activation(out=y, in_=self.P[Pi][:], func=AF.Identity),
                     reads=self.pk(Pi), writes=yk)
                sq, sqk = self.R3.view((oc % 2) * 2048, BF16, (1024,))
                s.op("dve", lambda e: e.tensor_tensor(out=sq, in0=self.P[Pi][:], in1=y, op=ALU.mult),
                     reads=self.pk(Pi) + yk, writes=sqk)
                self.flush_pending()
                self.stats_mm(3, sq, sqk, oc == 0, oc == KC - 1)
            self.proj(dr["ffn_w_out_%d" % layer], KC, 4, 11, hid, lambda oi: oi % 2, epi, after_block=inject)
            self.flush_pending()
            toks += self.postnorm(yv, xin, xin_name, xout, xout_name, hf, layer, 5, self.R2, 0)
        while units:
            units.pop(0)()
        return toks


    def allgather(self, src, dst, src_keys, dst_keys):
        self.s.op("pool", lambda e: e.collective_compute(
            "AllGather", ALU.bypass, replica_groups=[[0, 1, 2, 3], [4, 5, 6, 7]],
            ins=[src.opt()], outs=[dst.opt()]), reads=list(src_keys) + ["cc_chain"], writes=list(dst_keys) + ["cc_chain"])

    def conv_a(self, layer, j, xin, xin_name):
        s = self.s
        dr = self.dr
        vo = dr["vec_off"]["b_pw1_%d" % j]
        for hf in range(2):
            self.prenorm(xin, xin_name, hf, layer, 0, 1)
            for c in range(KC):
                inj = bool(self.pro_units)
                Pa, Pg = (0, 1) if (c % 2 == 0 or inj) else (2, 3)

                def inject():
                    if self.pro_units:
                        self.pro_units.pop(0)()
                self.proj(dr["conv_w_pw1_%d" % j], None, 1, KC, self.hT, lambda oi, Pa=Pa, Pg=Pg: (Pa, Pg)[oi],
                          lambda *a: None, oc_list=[c, KC + c], after_block=(inject if inj else None))
                sg, sgk = self.R3.view(8192 + (c % 2) * 4096, F32, (1024,))
                s.op("act", lambda e, sg=sg, Pg=Pg, c=c: e.activation(
                    out=sg, in_=self.P[Pg][:], func=AF.Sigmoid, bias=self.vecs[:, vo + KC + c:vo + KC + c + 1]),
                    reads=self.pk(Pg) + ["vecs"], writes=sgk)
                u, uk = self.R2.view(65536 + (c % 3) * 2048, BF16, (1024,))
                s.op("dve", lambda e, sg=sg, u=u, Pa=Pa, c=c: e.scalar_tensor_tensor(
                    out=u, in0=self.P[Pa][:], scalar=self.vecs[:, vo + c:vo + c + 1], in1=sg, op0=ALU.add, op1=ALU.mult),
                    reads=sgk + self.pk(Pa) + ["vecs"], writes=uk)
                s.dma(lambda q, u=u, c=c, hf=hf: q.dma_start(
                    out=dr["u_s"][c, :, HALO + hf * TH:HALO + (hf + 1) * TH], in_=u),
                    reads=uk, writes=[("u_s", c, hf)], key=("uo", c % 3), eng="act")
                if hf == 1:
                    s.dma(lambda q, u=u, c=c: q.dma_start(out=dr["halo_src"][:, c * HALO:(c + 1) * HALO], in_=u[:, TH - HALO:TH]),
                          reads=uk, writes=[("halo_src", c)], key=("ho", c % 3), eng="act")
        while self.pro_units:
            self.pro_units.pop(0)()
        self.allgather(dr["halo_src"], dr["halo_g"], [("halo_src", c) for c in range(KC)], ["halo_g"])

    def conv_b(self, layer, j, xin, xin_name, xout, xout_name):
        s = self.s
        dr = self.dr
        vo = dr["vec_off"]
        toks = []
        G, Gk = self.R3.view(16384, BF16, (4, KC * HALO))
        s.dma(lambda q: q.dma_start(out=G, in_=dr["halo_g"].rearrange("(r p) k -> p r k", p=128)),
              reads=["halo_g"], writes=Gk, key="halo_ld")
        hal, halk = self.R3.view(16384 + 4096, BF16, (KC * HALO,))
        mo = vo["mprev"]
        s.op("dve", lambda e: e.tensor_scalar(out=hal, in0=G[:, 0, :], scalar1=self.vecs[:, mo:mo + 1], scalar2=None,
                                               op0=ALU.mult), reads=Gk + ["vecs"], writes=halk)
        for r in range(1, 4):
            s.op("dve", lambda e, r=r: e.scalar_tensor_tensor(
                out=hal, in0=G[:, r, :], scalar=self.vecs[:, mo + r:mo + r + 1], in1=hal, op0=ALU.mult, op1=ALU.add),
                reads=Gk + halk + ["vecs"], writes=halk)
        UW = HALO + TH
        for hf in range(2):
            Us = []
            for c in range(KC):
                U, Uk = self.R2.view(c * UW * 2, BF16, (UW,))
                Us.append((U, Uk))
                s.dma(lambda q, U=U, c=c, hf=hf: q.dma_start(out=U, in_=dr["u_s"][c, :, hf * TH:hf * TH + UW]),
                      reads=[("u_s", c, 0), ("u_s", c, 1)], writes=Uk, key=("Uld", c))
                if hf == 0:
                    s.op("dve", lambda e, U=U, c=c: e.tensor_copy(out=U[:, 0:HALO], in_=hal[:, c * HALO:(c + 1) * HALO]),
                         reads=halk + Uk, writes=Uk)
            vv = lambda c: self.R1.view(c * 4096, F32, (1024,))
            wo = vo["w_dw_%d" % j]
            bo = vo["b_dw_%d" % j]
            def build_taps(c):
                dg, dgk = self.R2.view(33792 + (c % 2) * 8192, BF16, (CW, 128))
                for k in range(CW):
                    rd = ["ident", "vecs"] + ([("dggate", c % 2)] if k > 0 else [])
                    wr = [("dgtap", c % 2, k)] + (dgk + [("dggate", c % 2)] if k == 0 else [])
                    if k % 2 == 0:
                        s.op("dve", lambda e, dg=dg, k=k, c=c: e.tensor_scalar(
                            out=dg[:, k, :], in0=self.ident[:], scalar1=self.vecs[:, wo + k * KC + c:wo + k * KC + c + 1],
                            scalar2=None, op0=ALU.mult), reads=rd, writes=wr)
                    else:
                        s.op("act", lambda e, dg=dg, k=k, c=c: e.activation(
                            out=dg[:, k, :], in_=self.ident[:], func=AF.Identity,
                            scale=self.vecs[:, wo + k * KC + c:wo + k * KC + c + 1]), reads=rd, writes=wr)
            build_taps(0)
            for c in range(KC):
                dg, dgk = self.R2.view(33792 + (c % 2) * 8192, BF16, (CW, 128))
                Pi = c % 2
                U, Uk = Us[c]

                def emit(e, dg=dg, U=U, Pi=Pi):
                    ins = None
                    for tt in range(2):
                        for k in range(CW):
                            ins = e.matmul(self.P[Pi][:, tt * 512:(tt + 1) * 512], lhsT=dg[:, k, :],
                                           rhs=U[:, tt * 512 + k + 2:tt * 512 + k + 2 + 512],
                                           start=(k == 0), stop=(k == CW - 1))
                    return ins
                s.op("pe", emit, reads=[("dgtap", c % 2, k) for k in range(CW)] + dgk + [("dggate", c % 2)] + Uk,
                     writes=self.pk(Pi))
                if c + 1 < KC:
                    build_taps(c + 1)
                v, vk = vv(c)
                s.op("act", lambda e, v=v, Pi=Pi, c=c: e.activation(
                    out=v, in_=self.P[Pi][:], func=AF.Identity, bias=self.vecs[:, bo + c:bo + c + 1]),
                    reads=self.pk(Pi) + ["vecs"], writes=vk)
                sq, sqk = self.R3.view((c % 2) * 2048, BF16, (1024,))
                vb, vbk = self.R3.view(4096 + (c % 2) * 2048, BF16, (1024,))
                s.op("dve", lambda e, v=v, vb=vb: e.tensor_copy(out=vb, in_=v), reads=vk, writes=vbk)
                s.op("dve", lambda e, v=v, sq=sq: e.tensor_tensor(out=sq, in0=v, in1=v, op=ALU.mult), reads=vk, writes=sqk)
                self.flush_pending()
                self.stats_mm(2, vb, vbk, c == 0, c == KC - 1)
                self.stats_mm(3, sq, sqk, c == 0, c == KC - 1)
            self.flush_pending()
            mean, mk = self.R3.view(8192, F32, (1024,))
            rstd, rk = self.R3.view(12288, F32, (1024,))
            s.op("act", lambda e: e.activation(out=mean, in_=self.P[2][:], func=AF.Identity, scale=1.0 / D),
                 reads=self.pk(2), writes=mk)
            s.op("dve", lambda e: e.tensor_tensor(out=rstd, in0=mean, in1=mean, op=ALU.mult), reads=mk, writes=rk)
            s.op("dve", lambda e: e.scalar_tensor_tensor(out=rstd, in0=self.P[3][:], scalar=1.0 / D, in1=rstd,
                                                          op0=ALU.mult, op1=ALU.subtract), reads=self.pk(3) + rk, writes=rk)
            s.op("act", lambda e: e.activation(out=rstd, in_=rstd, func=AF.Ln, bias=self.epsc[:]),
                 reads=rk + ["epsc"], writes=rk)
            s.op("act", lambda e: e.activation(out=rstd, in_=rstd, func=AF.Exp, scale=-0.5), reads=rk, writes=rk)
            sv = lambda c: self.R2.view(50176 + c * 2048, BF16, (1024,))
            go, lbo = vo["ln_g_%d" % j], vo["ln_b_%d" % j]
            for c in range(KC):
                v, vk = vv(c)
                t1, t1k = self.R3.view(16384 + (c % 2) * 4096, F32, (1024,))
                s.op("dve", lambda e, v=v, t1=t1: e.tensor_tensor(out=t1, in0=v, in1=mean, op=ALU.subtract),
                     reads=vk + mk, writes=t1k)
                s.op("dve", lambda e, t1=t1: e.tensor_tensor(out=t1, in0=t1, in1=rstd, op=ALU.mult),
                     reads=t1k + rk, writes=t1k)
                sc, sck = sv(c)
                s.op("act", lambda e, t1=t1, sc=sc, c=c: e.activation(
                    out=sc, in_=t1, func=AF.Silu, scale=self.vecs[:, go + c:go + c + 1], bias=self.vecs[:, lbo + c:lbo + c + 1]),
                    reads=t1k + ["vecs"], writes=sck)
            yv = vv
            b2 = vo["b_pw2_%d" % j]

            def epi(oi, oc, Pi):
                y, yk = yv(oc)
                s.op("act", lambda e: e.activation(out=y, in_=self.P[Pi][:], func=AF.Identity,
                                                   bias=self.vecs[:, b2 + oc:b2 + oc + 1]),
                     reads=self.pk(Pi) + ["vecs"], writes=yk)
                sq, sqk = self.R3.view((oc % 2) * 2048, BF16, (1024,))
                s.op("dve", lambda e: e.tensor_tensor(out=sq, in0=y, in1=y, op=ALU.mult), reads=yk, writes=sqk)
                self.flush_pending()
                self.stats_mm(3, sq, sqk, oc == 0, oc == KC - 1)
            self.proj(dr["conv_w_pw2_%d" % j], KC, 1, KC, sv, lambda oi: oi % 2, epi)
            self.flush_pending()
            toks += self.postnorm(yv, xin, xin_name, xout, xout_name, hf, layer, 2, self.R2, 0)
        return toks

    def rows(self, region, off, dtype, n, nrows):
        esz = 4 if dtype == F32 else 2
        a = region.t[0:nrows, off // 4:(off + n * esz) // 4]
        if dtype != F32:
            a = a.bitcast(dtype)
        keys = [(region.name, pg) for pg in range(off // 1024, (off + n * esz + 1023) // 1024)]
        return a, keys

    def gla(self, layer, j, xin, xin_name, xout, xout_name, state_only):
        s = self.s
        dr = self.dr
        vo = dr["vec_off"]
        R1, R2, R3 = self.R1, self.R2, self.R3
        toks = []
        B = not state_only
        wup, wupk = self.rows(R3, 16384, BF16, 1024, 17)
        aT, aTk = self.rows(R3, 18944, BF16, 512, 17)
        small, smk = R3.view(18432, F32, (8 * 8,))
        bl, ebl, Bsum, Dj, Dp = [small[:, i * 8:(i + 1) * 8] for i in range(5)]
        blk_, eblk_, Bsk, Djk, Dpk = [[("gsm", i)] for i in range(5)]
        s.dma(lambda q: q.dma_start(out=wup, in_=dr["gla_wup_%d" % j]), writes=wupk, key="wup", eng="pool")
        s.op("dve", lambda e: e.memset(aT, 1.0), writes=aTk)
        S32v = lambda dc: R2.view(65536 + dc * 2048, F32, (512,))
        Sbfv = lambda dc: R2.view(81920 + dc * 1024, BF16, (512,))
        Sall, Sallk = R2.view(65536, F32, (4096,))
        Sball, Sballk = R2.view(81920, BF16, (4096,))
        def init_state():
            s.op("dve", lambda e: e.memset(Sall, 0.0), writes=Sallk)
            if state_only:
                s.op("dve", lambda e: e.memset(Bsum, 0.0), writes=Bsk)
            else:
                mo = vo["mlt"]
                for jr in range(3):
                    s.dma(lambda q, jr=jr: q.dma_start(out=Dj, in_=dr["gd_g"][jr * 128:(jr + 1) * 128, 0:8]),
                          reads=["gd_g"], writes=Djk, key="Dj")
                    s.op("dve", lambda e, jr=jr: e.tensor_scalar(out=Dp, in0=Dj, scalar1=-1.0, scalar2=self.vecs[:, mo + jr:mo + jr + 1],
                                                                  op0=ALU.add, op1=ALU.mult), reads=Djk + ["vecs"], writes=Dpk)
                    s.op("dve", lambda e: e.tensor_scalar(out=Dp, in0=Dp, scalar1=1.0, scalar2=None, op0=ALU.add),
                         reads=Dpk, writes=Dpk)
                    for dc in range(8):
                        stg, stgk = R3.view(20480 + (dc % 2) * 2048, F32, (512,))
                        s.dma(lambda q, jr=jr, dc=dc, stg=stg: q.dma_start(
                            out=stg, in_=dr["gsA_g" if dc < 4 else "gsB_g"][jr * 128:(jr + 1) * 128, (dc % 4) * 512:(dc % 4 + 1) * 512]),
                            reads=["gsA_g" if dc < 4 else "gsB_g"], writes=stgk, key=("stg", dc % 2))
                        s.op("dve", lambda e, stg=stg, jr=jr: e.tensor_scalar(
                            out=stg, in0=stg, scalar1=self.vecs[:, mo + jr:mo + jr + 1], scalar2=None, op0=ALU.mult),
                            reads=stgk + ["vecs"], writes=stgk)
                        S, Sk = S32v(dc)
                        s.op("dve", lambda e, S=S, stg=stg, dc=dc: e.scalar_tensor_tensor(
                            out=S, in0=S, scalar=Dp[:, dc:dc + 1], in1=stg, op0=ALU.mult, op1=ALU.add),
                            reads=Sk + stgk + Dpk, writes=Sk)
                s.op("act", lambda e: e.activation(out=Sball, in_=Sall, func=AF.Identity), reads=Sallk, writes=Sballk)


        init_done = [False]

        qTv = lambda c: R2.view(c * 1024, BF16, (512,))
        kTv = lambda c: R2.view(8192 + c * 1024, BF16, (512,))
        vTv = lambda c: R2.view(16384 + c * 1024, BF16, (512,))
        qT3, _ = R2.view(0, BF16, (8, 512))
        kT3, _ = R2.view(8192, BF16, (8, 512))
        qTk = R2.view(0, BF16, (4096,))[1]
        kTk = R2.view(8192, BF16, (4096,))[1]
        vTk = R2.view(16384, BF16, (8192,))[1]
        gpos, gpk = R2.view(32768, F32, (1024,))
        e1, e1k = R2.view(36864, F32, (1024,))
        tE = [R2.view(40960, F32, (8, 128)), R2.view(45056, F32, (8, 128))]
        tEf = [R2.view(40960, F32, (1024,)), R2.view(45056, F32, (1024,))]
        qd, qdk = R2.view(49152, BF16, (8, 128))
        kd, kdk = R2.view(51200, BF16, (8, 128))
        kh, khk = R2.view(53248, BF16, (8, 128))
        vtok, vtk = R2.view(55296, BF16, (2048,))
        ktok, ktk = R2.view(59392, BF16, (1024,))
        sc, sck = R2.view(61440, BF16, (512,))
        osq, osqk = R3.view(0, BF16, (2048,))
        rh, rhk = R3.view(4096, F32, (512,))
        onT = lambda c: R1.view(32768 + c * 2048, BF16, (1024,))
        P = self.P
        P1v = P[1][:].rearrange("p (a b) -> p a b", a=8)
        P2bf = P[2][:].bitcast(BF16)
        P3bf = P[3][:].bitcast(BF16)
        ngo = vo["gla_ng_%d" % j]
        cnt = [0]

        def alt():
            cnt[0] += 1
            return "act" if cnt[0] % 2 else "dve"

        def copy_op(eng, out, in_, reads, writes, scale=None):
            if eng == "act":
                if scale is None:
                    s.op("act", lambda e: e.activation(out=out, in_=in_, func=AF.Identity), reads=reads, writes=writes)
                else:
                    s.op("act", lambda e: e.activation(out=out, in_=in_, func=AF.Identity, scale=scale), reads=reads, writes=writes)
            else:
                if scale is None:
                    s.op("dve", lambda e: e.tensor_copy(out=out, in_=in_), reads=reads, writes=writes)
                else:
                    s.op("dve", lambda e: e.tensor_scalar(out=out, in0=in_, scalar1=scale, scalar2=None, op0=ALU.mult),
                         reads=reads, writes=writes)

        for hf in range(2):
            hall, hallk = R1.view(0, BF16, (KC * TH,))
            if state_only:
                self.prenorm(xin, xin_name, hf, layer, 0, 1)
                s.dma(lambda q, hf=hf: q.dma_start(out=dr["h_s"][hf], in_=hall), reads=hallk, writes=[("h_s", hf)],
                      key="hso", eng="act")
            else:
                s.dma(lambda q, hf=hf: q.dma_start(out=hall, in_=dr["h_s"][hf]), reads=[("h_s", hf)], writes=hallk, key="hsi")
            for qt in range(2):
                tl = ((qt * 512, 0),)
                def epi_qkv(oi, oc, Pi):
                    if oc < 8:
                        o_, k_ = qTv(oc)
                        copy_op(alt(), o_, P[Pi][:, 0:512], self.pk(Pi, 0), k_, scale=0.0625)
                    elif oc < 16:
                        o_, k_ = kTv(oc - 8)
                        copy_op(alt(), o_, P[Pi][:, 0:512], self.pk(Pi, 0), k_)
                    else:
                        o_, k_ = vTv(oc - 16)
                        copy_op(alt(), o_, P[Pi][:, 0:512], self.pk(Pi, 0), k_)
                qi = hf * 2 + qt
                kv3, _ = R2.view(8192, BF16, (24, 512))
                if state_only:
                    def epi_a(oi, oc, Pi):
                        s.op("act", lambda e: e.activation(out=aT[0:16, :], in_=P[Pi][0:16, 0:512], func=AF.Identity),
                             reads=self.pk(Pi, 0), writes=aTk)
                    self.proj(dr["gla_wa_%d" % j], 1, 1, KC, self.hT, lambda oi: 0, epi_a, tiles=tl, m=16)
                    self.proj(dr["gla_w_in_%d" % j], None, 1, KC, self.hT, lambda oi: 1 + oi % 3, epi_qkv,
                              oc_list=list(range(8, 32)), tiles=tl)
                    s.dma(lambda q, qi=qi: q.dma_start(out=dr["kv_s"][qi], in_=kv3), reads=kTk + vTk,
                          writes=[("kv_s", qi)], key="kvo", eng="act")
                    s.dma(lambda q, qi=qi: q.dma_start(out=dr["a_s"][qi], in_=aT[0:16, :]), reads=aTk,
                          writes=[("a_s", qi)], key="ao", eng="act")
                else:
                    s.dma(lambda q, qi=qi: q.dma_start(out=kv3[:, 0:8, :], in_=dr["kv_s"][qi][:, 0:8, :]), reads=[("kv_s", qi)],
                          writes=kTk, key="kvi")
                    self.proj(dr["gla_w_in_%d" % j], None, 1, KC, self.hT, lambda oi: 1 + oi % 3, epi_qkv,
                              oc_list=list(range(0, 8)), tiles=tl)
                if not init_done[0]:
                    init_done[0] = True
                    init_state()
                for ch in range(4):
                    c0 = ch * 128
                    tokoff = qt * 512 + c0

                    def state_mm():
                        for dc in range(8):
                            h = dc // 2
                            s.op("pe", lambda e, dc=dc, h=h: e.matmul(P[3][:, (dc % 2) * 512:(dc % 2 + 1) * 512],
                                                                     lhsT=ktok[:, dc * 128:(dc + 1) * 128],
                                                                     rhs=vtok[:, h * 512:(h + 1) * 512], start=True, stop=True),
                                 reads=ktk + vtk, writes=self.pk(3, dc % 2))
                            S, Sk = S32v(dc)
                            s.op("dve", lambda e, S=S, dc=dc: e.scalar_tensor_tensor(
                                out=S, in0=S, scalar=ebl[:, dc:dc + 1], in1=P[3][:, (dc % 2) * 512:(dc % 2 + 1) * 512],
                                op0=ALU.mult, op1=ALU.add), reads=Sk + eblk_ + self.pk(3, dc % 2), writes=Sk)

                    def state_cast():
                        for dc in range(8):
                            S, Sk = S32v(dc)
                            Sb, Sbk = Sbfv(dc)
                            s.op("act", lambda e, S=S, Sb=Sb: e.activation(out=Sb, in_=S, func=AF.Identity), reads=Sk, writes=Sbk)

                    ci = (hf * 2 + qt) * 4 + ch
                    if state_only:
                        def emit_z(e, c0=c0):
                            ins = None
                            for hh in range(2):
                                ins = e.matmul(P[0][:, hh * 512:(hh + 1) * 512], lhsT=aT[:, c0:c0 + 128],
                                               rhs=wup[:, hh * 512:(hh + 1) * 512], start=True, stop=True)
                            return ins
                        s.op("pe", emit_z, reads=aTk + wupk, writes=self.pk(0))
                        s.op("act", lambda e: e.activation(out=e1, in_=P[0][:], func=AF.Exp, scale=-1.0), reads=self.pk(0), writes=e1k)
                        s.op("act", lambda e: e.activation(out=gpos, in_=e1, func=AF.Ln, bias=self.one32[:]),
                             reads=e1k + ["one32"], writes=gpk)

                        ghi, ghk = R2.view(36864, BF16, (1024,))
                        glo, glk = R2.view(38912, BF16, (1024,))
                        s.op("dve", lambda e: e.tensor_copy(out=ghi, in_=gpos), reads=gpk, writes=ghk)
                        s.op("dve", lambda e: e.tensor_tensor(out=glo, in0=gpos, in1=ghi, op=ALU.subtract), reads=gpk + ghk, writes=glk)

                        def emit_cs(e):
                            ins = None
                            for dc in range(8):
                                e.matmul(P[1][:, dc * 128:(dc + 1) * 128], lhsT=ghi[:, dc * 128:(dc + 1) * 128],
                                         rhs=self.tribf[:], start=True, stop=False)
                                ins = e.matmul(P[1][:, dc * 128:(dc + 1) * 128], lhsT=glo[:, dc * 128:(dc + 1) * 128],
                                               rhs=self.tribf[:], start=False, stop=True)
                            return ins
                        s.op("pe", emit_cs, reads=ghk + glk + ["tribf"], writes=self.pk(1))
                        s.op("dve", lambda e: e.tensor_scalar(out=bl, in0=P1v[:, :, 127], scalar1=-0.0625, scalar2=None, op0=ALU.mult),
                             reads=self.pk(1), writes=blk_)
                        s.op("act", lambda e: e.activation(out=ebl, in_=bl, func=AF.Exp), reads=blk_, writes=eblk_)
                        if state_only:
                            s.op("dve", lambda e: e.tensor_tensor(out=Bsum, in0=Bsum, in1=bl, op=ALU.add), reads=Bsk + blk_, writes=Bsk)
                        (EK, EKk) = tE[0]

                        def emit_ek(e, EK=EK):
                            ins = None
                            for dc in range(8):
                                ins = e.activation(out=EK[:, dc, :], in_=P1v[:, dc, :], func=AF.Exp, scale=0.0625, bias=bl[:, dc:dc + 1])
                            return ins
                        s.op("act", emit_ek, reads=self.pk(1) + blk_, writes=EKk)
                        s.op("dve", lambda e, c0=c0, EK=EK: e.tensor_tensor(out=kh, in0=kT3[:, :, c0:c0 + 128], in1=EK, op=ALU.mult),
                             reads=kTk + EKk, writes=khk)
                        def emit_tk(e):
                            ins = None
                            for dc in range(8):
                                ins = e.transpose(out=P2bf[:, dc * 128:(dc + 1) * 128], in_=kh[:, dc, :], identity=self.ident[:])
                            return ins
                        s.op("pe", emit_tk, reads=khk + ["ident"], writes=self.pk(2, 0))

                        def emit_tv(e, c0=c0):
                            ins = None
                            for c in range(16):
                                ins = e.transpose(out=P3bf[:, c * 128:(c + 1) * 128], in_=vTv(c)[0][:, c0:c0 + 128], identity=self.ident[:])
                            return ins
                        s.op("pe", emit_tv, reads=vTk + ["ident"], writes=self.pk(3))
                        s.op("act", lambda e: e.activation(out=ktok, in_=P2bf[:, 0:1024], func=AF.Identity), reads=self.pk(2, 0), writes=ktk)
                        s.op("dve", lambda e: e.tensor_copy(out=vtok, in_=P3bf[:, 0:2048]), reads=self.pk(3), writes=vtk)
                        csc, csck = tEf[1]
                        s.op("act", lambda e, csc=csc: e.activation(out=csc, in_=P[1][:], func=AF.Identity), reads=self.pk(1), writes=csck)
                        s.dma(lambda q, ci=ci: q.dma_start(out=dr["ck_s"][ci], in_=ktok), reads=ktk, writes=[("ck_s", ci)], key="cko", eng="act")
                        s.dma(lambda q, ci=ci: q.dma_start(out=dr["cv_s"][ci], in_=vtok), reads=vtk, writes=[("cv_s", ci)], key="cvo", eng="act")
                        s.dma(lambda q, ci=ci, csc=csc: q.dma_start(out=dr["cs_s"][ci], in_=csc), reads=csck, writes=[("cs_s", ci)], key="cso", eng="act")
                    else:
                        csb, csbk = R2.view(32768, F32, (1024,))
                        csb3, _ = R2.view(32768, F32, (8, 128))
                        s.dma(lambda q, ci=ci: q.dma_start(out=ktok, in_=dr["ck_s"][ci]), reads=[("ck_s", ci)], writes=ktk, key="ckl")
                        s.dma(lambda q, ci=ci: q.dma_start(out=vtok, in_=dr["cv_s"][ci]), reads=[("cv_s", ci)], writes=vtk, key="cvl")
                        s.dma(lambda q, ci=ci: q.dma_start(out=csb, in_=dr["cs_s"][ci]), reads=[("cs_s", ci)], writes=csbk, key="csl")
                        s.op("dve", lambda e: e.tensor_scalar(out=bl, in0=csb3[:, :, 127], scalar1=-0.0625, scalar2=None, op0=ALU.mult),
                             reads=csbk, writes=blk_)
                        s.op("act", lambda e: e.activation(out=ebl, in_=bl, func=AF.Exp), reads=blk_, writes=eblk_)
                        EBf, EBk = tEf[1]
                        s.op("act", lambda e, EBf=EBf: e.activation(out=EBf, in_=csb, func=AF.Exp, scale=-0.0625),
                             reads=csbk, writes=EBk)
                        s.op("dve", lambda e, c0=c0: e.tensor_tensor(out=qd, in0=qT3[:, :, c0:c0 + 128], in1=tE[1][0], op=ALU.mult),
                             reads=qTk + EBk, writes=qdk)
                        ENf, ENk = tEf[0]
                        s.op("act", lambda e, ENf=ENf: e.activation(out=ENf, in_=csb, func=AF.Exp, scale=0.0625),
                             reads=csbk, writes=ENk)
                        s.op("dve", lambda e, c0=c0: e.tensor_tensor(out=kd, in0=kT3[:, :, c0:c0 + 128], in1=tE[0][0], op=ALU.mult),
                             reads=kTk + ENk, writes=kdk)
                    if B:
                        def emit_sc(e):
                            ins = None
                            for h in range(4):
                                for d2 in range(2):
                                    dc = 2 * h + d2
                                    ins = e.matmul(P[2][:, 512 + h * 128:512 + (h + 1) * 128], lhsT=kd[:, dc, :], rhs=qd[:, dc, :],
                                                   start=(d2 == 0), stop=(d2 == 1))
                            return ins
                        s.op("pe", emit_sc, reads=kdk + qdk, writes=self.pk(2, 1))
                        s.op("dve", lambda e: e.tensor_tensor(out=sc, in0=P[2][:, 512:1024], in1=self.tri4[:], op=ALU.mult),
                             reads=self.pk(2, 1) + ["tri4"], writes=sck)

                        def emit_o(e):
                            ins = None
                            for h in range(4):
                                for es in range(4):
                                    blk = h * 4 + es
                                    out = P[blk // 8][:, (blk % 8) * 128:(blk % 8 + 1) * 128]
                                    e.matmul(out, lhsT=Sbfv(2 * h)[0][:, es * 128:(es + 1) * 128], rhs=qd[:, 2 * h, :], start=True, stop=False)
                                    e.matmul(out, lhsT=Sbfv(2 * h + 1)[0][:, es * 128:(es + 1) * 128], rhs=qd[:, 2 * h + 1, :], start=False, stop=False)
                                    ins = e.matmul(out, lhsT=vtok[:, h * 512 + es * 128:h * 512 + (es + 1) * 128],
                                                   rhs=sc[:, h * 128:(h + 1) * 128], start=False, stop=True)
                            return ins
                        s.op("pe", emit_o, reads=Sballk + qdk + vtk + sck, writes=self.pk(0) + self.pk(1))
                        state_mm()
                        s.op("act", lambda e: e.activation(out=osq[:, 0:1024], in_=P[0][:], func=AF.Square), reads=self.pk(0), writes=osqk)
                        s.op("act", lambda e: e.activation(out=osq[:, 1024:2048], in_=P[1][:], func=AF.Square), reads=self.pk(1), writes=osqk)

                        def emit_hs(e):
                            ins = None
                            for h in range(4):
                                for es in range(4):
                                    ins = e.matmul(P[2][:, h * 128:(h + 1) * 128], lhsT=self.ones[:],
                                                   rhs=osq[:, (h * 4 + es) * 128:(h * 4 + es + 1) * 128], start=(es == 0), stop=(es == 3))
                            return ins
                        s.op("pe", emit_hs, reads=osqk + ["ones"], writes=self.pk(2, 0))
                        s.op("act", lambda e: e.activation(out=rh, in_=P[2][:, 0:512], func=AF.Ln, scale=1.0 / 512, bias=self.epsc[:]),
                             reads=self.pk(2, 0) + ["epsc"], writes=rhk)
                        s.op("act", lambda e: e.activation(out=rh, in_=rh, func=AF.Exp, scale=-0.5), reads=rhk, writes=rhk)
                        for blk in range(16):
                            h = blk // 4
                            o_, ok_ = onT(blk)
                            s.op("dve", lambda e, blk=blk, h=h, o_=o_, tokoff=tokoff: e.scalar_tensor_tensor(
                                out=o_[:, tokoff:tokoff + 128], in0=P[blk // 8][:, (blk % 8) * 128:(blk % 8 + 1) * 128],
                                scalar=self.vecs[:, ngo + blk:ngo + blk + 1], in1=rh[:, h * 128:(h + 1) * 128],
                                op0=ALU.mult, op1=ALU.mult), reads=self.pk(blk // 8) + rhk + ["vecs"], writes=ok_)
                    if not B:
                        state_mm()
                    else:
                        state_cast()
            if B:
                def epi_r(oi, oc, Pi):
                    c = oc - 32
                    sr, srk = R3.view(8192 + (c % 2) * 4096, F32, (1024,))
                    s.op("act", lambda e: e.activation(out=sr, in_=P[Pi][:], func=AF.Silu), reads=self.pk(Pi), writes=srk)
                    o_, ok_ = onT(c)
                    s.op("dve", lambda e: e.tensor_tensor(out=o_, in0=o_, in1=sr, op=ALU.mult), reads=ok_ + srk, writes=ok_)
                self.proj(dr["gla_w_in_%d" % j], None, 1, KC, self.hT, lambda oi: oi % 2, epi_r, oc_list=list(range(32, 48)))
                yv = lambda c: R2.view(c * 4096, F32, (1024,))

                def epi_o(oi, oc, Pi):
                    y, yk = yv(oc)
                    s.op("act", lambda e: e.activation(out=y, in_=P[Pi][:], func=AF.Identity), reads=self.pk(Pi), writes=yk)
                    sq, sqk = R3.view((oc % 2) * 2048, BF16, (1024,))
                    s.op("dve", lambda e: e.tensor_tensor(out=sq, in0=P[Pi][:], in1=y, op=ALU.mult), reads=self.pk(Pi) + yk, writes=sqk)
                    self.flush_pending()
                    self.stats_mm(3, sq, sqk, oc == 0, oc == KC - 1)
                self.proj(dr["gla_w_out_%d" % j], KC, 1, KC, onT, lambda oi: oi % 2, epi_o)
                self.flush_pending()
                toks += self.postnorm(yv, xin, xin_name, xout, xout_name, hf, layer, 2, R1, 0)
        if state_only:
            s.op("act", lambda e: e.activation(out=Dj, in_=Bsum, func=AF.Exp), reads=Bsk, writes=Djk)
            s.dma(lambda q: q.dma_start(out=dr["gd_src"][:, 0:8], in_=Dj), reads=Djk, writes=["gd_src"], key="gdo", eng="act")
            s.dma(lambda q: q.dma_start(out=dr["gsA_src"], in_=Sall[:, 0:2048]), reads=Sallk, writes=["gsA_src"], key="gsoA", eng="act")
            s.dma(lambda q: q.dma_start(out=dr["gsB_src"], in_=Sall[:, 2048:4096]), reads=Sallk, writes=["gsB_src"], key="gsoB", eng="act")
            self.allgather(dr["gd_src"], dr["gd_g"], ["gd_src"], ["gd_g"])
            self.allgather(dr["gsA_src"], dr["gsA_g"], ["gsA_src"], ["gsA_g"])
            self.allgather(dr["gsB_src"], dr["gsB_g"], ["gsB_src"], ["gsB_g"])
        return toks


def tile_w(W, kcb, m=128):
    K, N = W.shape
    n_kg = K // (128 * kcb)
    a = W.reshape(n_kg, kcb, 128, N // m, m).transpose(3, 0, 2, 1, 4)
    return np.ascontiguousarray(a).reshape(N // m, n_kg, 128, kcb * m)


def colvec(v):
    return np.ascontiguousarray(v.reshape(-1, 128).T)


def build_vecs(inp, b, seg):
    cols = []
    off = {}

    def add(name, arr):
        off[name] = sum(a.shape[1] for a in cols)
        cols.append(np.asarray(arr, dtype=np.float32))
    add("c", colvec(inp["c"][b]))
    for i in range(DEPTH):
        add("b_ada%d" % i, colvec(inp["b_ada"][i]))
        for nm in ["pre_mix_g", "post_mix_g", "pre_ffn_g", "post_ffn_g"]:
            add(nm + "%d" % i, colvec(inp[nm][i]))
    for j in range(2):
        add("b_pw1_%d" % j, colvec(inp["conv_b_pw1"][j]))
        add("b_dw_%d" % j, colvec(inp["conv_b_dw"][j]))
        add("ln_g_%d" % j, colvec(inp["conv_ln_g"][j]))
        add("ln_b_%d" % j, colvec(inp["conv_ln_b"][j]))
        add("b_pw2_%d" % j, colvec(inp["conv_b_pw2"][j]))
        add("w_dw_%d" % j, colvec(inp["conv_w_dw"][j].reshape(-1)))
        add("gla_ng_%d" % j, colvec(inp["gla_norm_g"][j]))
    mprev = np.zeros((128, 4), np.float32)
    if seg > 0:
        mprev[:, seg - 1] = 1.0
    mlt = np.zeros((128, 4), np.float32)
    mlt[:, :seg] = 1.0
    add("mprev", mprev)
    add("mlt", mlt)
    return np.concatenate(cols, axis=1), off


FULL = [("convA", 0), ("convB", 0), ("ffn", 0), ("glaA", 1), ("glaB", 1), ("ffn", 1),
        ("convA", 2), ("convB", 2), ("ffn", 2), ("glaA", 3), ("glaB", 3), ("ffn", 3)]
MODES = {"full": FULL, "ffn0": [("ffn", 0)], "conv0": [("convA", 0), ("convB", 0)],
         "gla1": [("glaA", 1), ("glaB", 1)], "gla1A": [("glaA", 1)], "f0g1": [("ffn", 0), ("glaA", 1), ("glaB", 1)], "l0": FULL[:3], "l01": FULL[:6]}


def weight_arrays(inp, steps):
    w = {}
    for kind, L in steps:
        j = L // 2
        w["w_ada%d" % L] = lambda L=L: np.ascontiguousarray(inp["w_ada"][L])
        if kind == "ffn":
            w["ffn_w_in_%d" % L] = lambda L=L: tile_w(inp["ffn_w_in"][L], KC)
            w["ffn_w_out_%d" % L] = lambda L=L: tile_w(inp["ffn_w_out"][L], 11)
        elif kind == "convA":
            w["conv_w_pw1_%d" % j] = lambda j=j: tile_w(inp["conv_w_pw1"][j], KC)
        elif kind == "convB":
            w["conv_w_pw2_%d" % j] = lambda j=j: tile_w(inp["conv_w_pw2"][j], KC)
        elif kind in ("glaA", "glaB"):
            w["gla_w_in_%d" % j] = lambda j=j: tile_w(inp["gla_w_in"][j][:, :6144], KC)
            w["gla_wa_%d" % j] = lambda j=j: tile_w(inp["gla_w_in"][j][:, 6144:6160], KC, m=16)
            w["gla_wup_%d" % j] = lambda j=j: np.concatenate(
                [inp["gla_w_gate_up"][j], inp["gla_b_gate"][j][None, :]], axis=0).astype(np.float32)
            if kind == "glaB":
                w["gla_w_out_%d" % j] = lambda j=j: tile_w(inp["gla_w_out"][j], KC)
    return {k: f() for k, f in w.items()}


def host_consts():
    c = np.zeros((128, 640), np.float32)
    c[:, 0:128] = np.eye(128, dtype=np.float32)
    tri = np.triu(np.ones((128, 128), np.float32))
    c[:, 128:640] = np.tile(tri, (1, 4))
    return c


def build_program(vec_off, nvec, steps, wshapes):
    nc = bass.Bass("TRN2", target_bir_lowering=False)
    dr = {"vec_off": vec_off}
    dr["xT"] = nc.dram_tensor("xT", [KC, 128, TOK], F32, kind="ExternalInput").ap()
    dr["vecs"] = nc.dram_tensor("vecs", [128, nvec], F32, kind="ExternalInput").ap()
    dr["consts"] = nc.dram_tensor("consts", [128, 640], F32, kind="ExternalInput").ap()
    for k, shp in wshapes.items():
        dr[k] = nc.dram_tensor(k, list(shp), F32, kind="ExternalInput").ap()
    dr["out"] = nc.dram_tensor("out", [KC, 128, TOK], F32, kind="ExternalOutput").ap()
    dr["xs"] = nc.dram_tensor("xs", [KC, 128, TOK], F32).ap()
    dr["u_s"] = nc.dram_tensor("u_s", [KC, 128, HALO + TOK], BF16).ap()
    dr["halo_src"] = nc.dram_tensor("halo_src", [128, KC * HALO], BF16).ap()
    dr["halo_g"] = nc.dram_tensor("halo_g", [4 * 128, KC * HALO], BF16).ap()
    dr["kv_s"] = nc.dram_tensor("kv_s", [4, 128, 24, 512], BF16).ap()
    dr["a_s"] = nc.dram_tensor("a_s", [4, 16, 512], BF16).ap()
    dr["h_s"] = nc.dram_tensor("h_s", [2, 128, KC * TH], BF16).ap()
    dr["ck_s"] = nc.dram_tensor("ck_s", [16, 128, 1024], BF16).ap()
    dr["cv_s"] = nc.dram_tensor("cv_s", [16, 128, 2048], BF16).ap()
    dr["cs_s"] = nc.dram_tensor("cs_s", [16, 128, 1024], F32).ap()
    dr["gd_src"] = nc.dram_tensor("gd_src", [128, 64], F32).ap()
    dr["gd_g"] = nc.dram_tensor("gd_g", [4 * 128, 64], F32).ap()
    dr["gsA_src"] = nc.dram_tensor("gsA_src", [128, 2048], F32).ap()
    dr["gsA_g"] = nc.dram_tensor("gsA_g", [4 * 128, 2048], F32).ap()
    dr["gsB_src"] = nc.dram_tensor("gsB_src", [128, 2048], F32).ap()
    dr["gsB_g"] = nc.dram_tensor("gsB_g", [4 * 128, 2048], F32).ap()
    with contextlib.ExitStack() as es:
        b = Builder(nc, es, dr)
        b.epsc = es.enter_context(nc.sbuf_tensor("epsc", [128, 1], F32))
        b.s.op("dve", lambda e: e.memset(b.epsc[:], EPS), writes=["epsc"])
        layers = sorted(set(L for _, L in steps))
        b.prologue_mod(layers[:1])
        if steps[0][0] != "convA":
            while b.pro_units:
                b.pro_units.pop(0)()
        resid = [i for i, (k, _) in enumerate(steps) if k in ("convB", "glaB", "ffn")]
        cur, cur_name = dr["xT"], "xT"
        toks = []
        for i, (kind, L) in enumerate(steps):
            j = L // 2
            if resid and i == resid[-1]:
                nxt, nxt_name = dr["out"], "out"
            else:
                nxt, nxt_name = dr["xs"], "xs"
            if kind == "ffn":
                nl = layers[layers.index(L) + 1] if layers.index(L) + 1 < len(layers) else None
                toks = b.ffn(L, cur, cur_name, nxt, nxt_name, next_layer=nl)
            elif kind == "convA":
                b.conv_a(L, j, cur, cur_name)
            elif kind == "convB":
                toks = b.conv_b(L, j, cur, cur_name, nxt, nxt_name)
            elif kind == "glaA":
                b.gla(L, j, cur, cur_name, None, None, True)
            elif kind == "glaB":
                toks = b.gla(L, j, cur, cur_name, nxt, nxt_name, False)
            if i in resid:
                cur, cur_name = nxt, nxt_name
        b.s.emit(final_wait_tokens=toks)
    return nc


def run(inp, mode, trace=False):
    inp = {k: np.asarray(v) for k, v in inp.items()}
    steps = MODES[mode]
    W = weight_arrays(inp, steps)
    consts = host_consts()
    maps = []
    vec_off = None
    for core in range(NCORE):
        b, seg = core // 4, core % 4
        xT = np.ascontiguousarray(inp["x"][b, seg * TOK:(seg + 1) * TOK, :].T).reshape(KC, 128, TOK)
        vecs, vec_off = build_vecs(inp, b, seg)
        m = {"xT": xT, "vecs": vecs, "consts": consts}
        m.update(W)
        maps.append(m)
    nc = build_program(vec_off, maps[0]["vecs"].shape[1], steps, {k: v.shape for k, v in W.items()})
    res = run_bass_kernel_spmd(nc, maps, core_ids=list(range(NCORE)), trace=trace)
    out = np.empty((2, SEQ, D), np.float32)
    for core in range(NCORE):
        b, seg = core // 4, core % 4
        o = res.results[core]["out"].reshape(D, TOK)
        out[b, seg * TOK:(seg + 1) * TOK, :] = o.T
    return out, res


def kernel(**inputs):
    out, _ = run(inputs, "full")
    return out
```

```python
import contextlib
import numpy as np
import concourse.bass as bass
import concourse.mybir as mybir
from concourse.bass_utils import run_bass_kernel_spmd

F32 = mybir.dt.float32
BF16 = mybir.dt.bfloat16
AF = mybir.ActivationFunctionType
ALU = mybir.AluOpType

D = 2048
KC = 16
SEQ = 8192
NCORE = 8
TOK = 2048
TH = 1024
DFF = 5632
JC = 44
DEPTH = 4
CW = 31
HALO = 32
EPS = 1e-6
DK = 1024
DV = 2048
NH = 4
GIN = 6160
NWB = 4


class Sched:
    ENG = ["pe", "act", "dve", "pool", "sp"]

    def __init__(self, nc):
        self.nc = nc
        self.ops = {e: [] for e in self.ENG}
        self.cnt = {e: 0 for e in self.ENG}
        self.seen = {e: {} for e in self.ENG}
        self.last_w = {}
        self.readers = {}
        self.dma_cnt = {}
        self.dma_keys = []

    def _add(self, eng, fn, reads, writes, tok_kind, dma_key=None):
        deps = []
        for r in reads:
            t = self.last_w.get(r)
            if t is not None:
                deps.append(t)
        for w in writes:
            t = self.last_w.get(w)
            if t is not None:
                deps.append(t)
            deps.extend(self.readers.get(w, ()))
        if tok_kind == "eng":
            self.cnt[eng] += 1
            tok = ("eng", eng, self.cnt[eng])
        else:
            if dma_key not in self.dma_cnt:
                self.dma_cnt[dma_key] = 0
                self.dma_keys.append(dma_key)
            self.dma_cnt[dma_key] += 16
            tok = ("dma", dma_key, self.dma_cnt[dma_key])
        need = {}
        for t in deps:
            if t[0] == "eng" and t[1] == eng and tok_kind == "eng" and eng == "pe":
                continue
            k = (t[0], t[1])
            if t[2] > need.get(k, 0):
                need[k] = t[2]
        waits = []
        seen = self.seen[eng]
        for k, v in need.items():
            if seen.get(k, 0) >= v:
                continue
            seen[k] = v
            waits.append((k, v))
        self.ops[eng].append((waits, fn, tok))
        for r in reads:
            self.readers.setdefault(r, []).append(tok)
        for w in writes:
            self.last_w[w] = tok
            self.readers[w] = []
        return tok

    def op(self, eng, fn, reads=(), writes=()):
        return self._add(eng, fn, reads, writes, "eng")

    def dma(self, fn, reads=(), writes=(), key=None, eng="sp"):
        return self._add(eng, fn, reads, writes, "dma", dma_key=key)

    def emit(self, final_wait_tokens=()):
        nc = self.nc
        with contextlib.ExitStack() as es:
            sems = {}
            for e in self.ENG:
                sems[("eng", e)] = es.enter_context(nc.semaphore("s_" + e))
            for i, k in enumerate(self.dma_keys):
                sems[("dma", k)] = es.enter_context(nc.semaphore("d%d" % i))
            block = es.enter_context(nc.Block())
            eng_map = {"pe": block.tensor, "act": block.scalar, "dve": block.vector,
                       "pool": block.gpsimd, "sp": block.sync}
            for e in self.ENG:
                ops = self.ops[e]
                extra = final_wait_tokens if e == "sp" else ()

                def body(eh, ops=ops, e=e, extra=extra):
                    for waits, fn, tok in ops:
                        for k, v in waits:
                            eh.wait_ge(sems[k], v)
                        ins = fn(eh)
                        if tok[0] == "eng":
                            ins.then_inc(sems[("eng", e)], 1)
                        else:
                            ins.then_inc(sems[("dma", tok[1])], 16)
                    for t in extra:
                        eh.wait_ge(sems[(t[0], t[1])], t[2])
                eng_map[e](body)


class Region:
    def __init__(self, nc, es, name, nbytes):
        self.name = name
        self.nbytes = nbytes
        self.t = es.enter_context(nc.sbuf_tensor(name, [128, nbytes // 4], F32))

    def view(self, off, dtype, shape):
        esz = 4 if dtype == F32 else 2
        n = 1
        for x in shape:
            n *= x
        assert off % 4 == 0 and (n * esz) % 4 == 0 and off + n * esz <= self.nbytes, (self.name, off, n, esz)
        a = self.t[:, off // 4:(off + n * esz) // 4]
        if dtype != F32:
            a = a.bitcast(dtype)
        if len(shape) == 2:
            a = a.rearrange("p (a b) -> p a b", a=shape[0])
        keys = [(self.name, pg) for pg in range(off // 1024, (off + n * esz + 1023) // 1024)]
        return a, keys


class Builder:
    def __init__(self, nc, es, dram):
        self.nc = nc
        self.es = es
        self.dr = dram
        self.s = Sched(nc)
        s = self.s
        self.R1 = Region(nc, es, "R1", 64 * 1024)
        self.R2 = Region(nc, es, "R2", 88 * 1024)
        self.R3 = Region(nc, es, "R3", 24 * 1024)
        self.wb = [es.enter_context(nc.sbuf_tensor("wb%d" % i, [128, 2048], BF16)) for i in range(NWB)]
        self.wi = 0
        self.P = [es.enter_context(nc.psum_tensor("P%d" % i, [128, 1024], F32)) for i in range(4)]
        self.ones = es.enter_context(nc.sbuf_tensor("ones", [128, 128], BF16))
        self.one32 = es.enter_context(nc.sbuf_tensor("one32", [128, 1], F32))
        self.vecs = es.enter_context(nc.sbuf_tensor("vecs_sb", [128, self.dr["vecs"].shape[1]], F32))
        self.cact = es.enter_context(nc.sbuf_tensor("cact", [128, 16], BF16))
        self.modrow = self.R3.t[0:1, 4096:6144]
        self.modc = es.enter_context(nc.sbuf_tensor("modc", [128, DEPTH * 96], F32))
        self.lay = es.enter_context(nc.sbuf_tensor("lay", [128, DEPTH * 96], F32))
        s.op("dve", lambda e: e.memset(self.ones[:], 1.0), writes=["ones"])
        s.op("dve", lambda e: e.memset(self.one32[:], 1.0), writes=["one32"])
        s.dma(lambda q: q.dma_start(out=self.vecs[:], in_=self.dr["vecs"]), writes=["vecs"], key="vecs")
        self.pending = []
        self.ident = es.enter_context(nc.sbuf_tensor("ident", [128, 128], BF16))
        self.tri4 = es.enter_context(nc.sbuf_tensor("tri4", [128, 512], F32))
        s.dma(lambda q: q.dma_start(out=self.ident[:], in_=self.dr["consts"][:, 0:128]), writes=["ident"],
              key="ident", eng="pool")
        s.dma(lambda q: q.dma_start(out=self.tri4[:], in_=self.dr["consts"][:, 128:640]), writes=["tri4"], key="tri4")
        self.tribf = es.enter_context(nc.sbuf_tensor("tribf", [128, 128], BF16))
        s.dma(lambda q: q.dma_start(out=self.tribf[:], in_=self.dr["consts"][:, 128:256]), writes=["tribf"],
              key="tribf", eng="pool")

    def pk(self, i, tt=None):
        if tt is None:
            return [("P", i, 0), ("P", i, 1)]
        return [("P", i, tt)]

    def load_w(self, src_ap, ncols):
        i = self.wi % NWB
        self.wi += 1
        t = self.wb[i]
        key = ("wb", i)
        self.s.dma(lambda q: q.dma_start(out=t[:, 0:ncols], in_=src_ap), writes=[key], key=key, eng="pool")
        return t, key

    def flush_pending(self):
        p = self.pending
        self.pending = []
        for f in p:
            f()

    def stats_mm(self, Pi, src_ap, src_keys, first, last):
        def f():
            def emit(e):
                ins = None
                for tt in range(2):
                    ins = e.matmul(self.P[Pi][:, tt * 512:(tt + 1) * 512], lhsT=self.ones[:],
                                   rhs=src_ap[:, tt * 512:(tt + 1) * 512], start=first, stop=last)
                return ins
            self.s.op("pe", emit, reads=list(src_keys) + ["ones"], writes=self.pk(Pi))
        self.pending.append(f)

    def vcol(self, name, c):
        o = self.dr["vec_off"][name]
        return self.vecs[:, o + c:o + c + 1]

    def prologue_mod(self, layers):
        s = self.s
        dr = self.dr
        co = dr["vec_off"]["c"]
        s.op("act", lambda e: e.activation(out=self.cact[:], in_=self.vecs[:, co:co + 16], func=AF.Silu),
             reads=["vecs"], writes=["cact"])
        self.pro_units = []
        for i in layers:
            units = self.mod_units(i, split=True)
            for _ in range(2 * KC + 1):
                units.pop(0)()
            self.pro_units = units

    def mod_units(self, i, split=False):
        s = self.s
        dr = self.dr
        units = []
        for nb in range(6):
            for kc in range(KC):
                def unit(nb=nb, kc=kc):
                    t, key = self.load_w(dr["w_ada%d" % i][kc * 128:(kc + 1) * 128, nb * 2048:(nb + 1) * 2048], 2048)

                    def emit(e):
                        ins = None
                        for si in range(16):
                            ins = e.matmul(self.P[2][:, si:si + 1], lhsT=t[:, si * 128:(si + 1) * 128],
                                           rhs=self.cact[:, kc:kc + 1], start=(kc == 0 and si == 0),
                                           stop=(kc == KC - 1), skip_group_check=True)
                        return ins
                    s.op("pe", emit, reads=[key, "cact"], writes=self.pk(2, 0))
                    if kc == KC - 1:
                        bo = dr["vec_off"]["b_ada%d" % i] + nb * 16
                        s.op("dve", lambda e: e.tensor_tensor(
                            out=self.modc[:, i * 96 + nb * 16:i * 96 + nb * 16 + 16], in0=self.P[2][:, 0:16],
                            in1=self.vecs[:, bo:bo + 16], op=ALU.add),
                            reads=self.pk(2, 0) + ["vecs"], writes=[("modc", i, nb)])
                units.append(unit)
            if split and nb == 1:
                units.append(lambda: self.mod_derive(i, which=(0, 1)))
        units.append(lambda: self.mod_derive(i, which=((2, 3, 4, 5) if split else (0, 1, 2, 3, 4, 5))))
        return units

    def mod_derive(self, i, which=(0, 1, 2, 3, 4, 5)):
        s = self.s
        dr = self.dr
        m = lambda nb, i=i: self.modc[:, i * 96 + nb * 16:i * 96 + nb * 16 + 16]
        L = lambda k, i=i: self.lay[:, i * 96 + k * 16:i * 96 + k * 16 + 16]
        vo = dr["vec_off"]
        g = lambda nm, i=i, vo=vo: self.vecs[:, vo[nm % i]:vo[nm % i] + 16]
        rd = [("modc", i, nb) for nb in range(6)] + ["vecs"]
        wr = [("lay", i)]
        if 0 in which:
            s.op("dve", lambda e: e.scalar_tensor_tensor(
                out=L(0), in0=m(1), scalar=1.0, in1=g("pre_mix_g%d"), op0=ALU.add, op1=ALU.mult), reads=rd, writes=wr)
        if 1 in which:
            s.op("dve", lambda e: e.tensor_copy(out=L(1), in_=m(0)), reads=rd, writes=wr)
        if 2 in which:
            s.op("dve", lambda e: e.tensor_tensor(out=L(2), in0=m(2), in1=g("post_mix_g%d"), op=ALU.mult), reads=rd, writes=wr)
        if 3 in which:
            s.op("dve", lambda e: e.scalar_tensor_tensor(
                out=L(3), in0=m(4), scalar=1.0, in1=g("pre_ffn_g%d"), op0=ALU.add, op1=ALU.mult), reads=rd, writes=wr)
        if 4 in which:
            s.op("dve", lambda e: e.tensor_copy(out=L(4), in_=m(3)), reads=rd, writes=wr)
        if 5 in which:
            s.op("dve", lambda e: e.tensor_tensor(out=L(5), in0=m(5), in1=g("post_ffn_g%d"), op=ALU.mult), reads=rd, writes=wr)

    def lcol(self, i, k, c):
        o = i * 96 + k * 16 + c
        return self.lay[:, o:o + 1]

    def hT(self, c, tt=None):
        if tt is None:
            return self.R1.view(c * 2048, BF16, (1024,))
        return self.R1.view(c * 2048 + tt * 1024, BF16, (512,))

    def prenorm(self, xin, xin_name, hf, layer, ka, kb):
        s = self.s
        xs = []
        for c in range(KC):
            a, keys = self.R2.view(c * 4096, F32, (1024,))
            xs.append((a, keys))
            s.dma(lambda q, a=a, c=c: q.dma_start(out=a, in_=xin[c, :, hf * TH:(hf + 1) * TH]),
                  reads=[(xin_name, c, hf)], writes=keys, key=("xst", c))
        for c in range(KC):
            a, keys = xs[c]
            sq, sqk = self.R3.view((c % 2) * 2048, BF16, (1024,))
            if c % 2 == 0:
                s.op("act", lambda e, a=a, sq=sq: e.activation(out=sq, in_=a, func=AF.Square), reads=keys, writes=sqk)
            else:
                s.op("dve", lambda e, a=a, sq=sq: e.tensor_tensor(out=sq, in0=a, in1=a, op=ALU.mult), reads=keys, writes=sqk)
            self.flush_pending()
            self.stats_mm(3, sq, sqk, c == 0, c == KC - 1)
        self.flush_pending()
        rs, rsk = self.R3.view(4096, F32, (1024,))
        s.op("act", lambda e: e.activation(out=rs, in_=self.P[3][:], func=AF.Ln, scale=1.0 / D, bias=self.epsc[:]),
             reads=self.pk(3) + ["epsc"], writes=rsk)
        s.op("act", lambda e: e.activation(out=rs, in_=rs, func=AF.Exp, scale=-0.5), reads=rsk, writes=rsk)
        for c in range(KC):
            a, keys = xs[c]
            tm, tmk = self.R3.view(8192 + (c % 2) * 4096, F32, (1024,))
            s.op("dve", lambda e, a=a, tm=tm: e.tensor_tensor(out=tm, in0=a, in1=rs, op=ALU.mult),
                 reads=keys + rsk, writes=tmk)
            h, hk = self.hT(c)
            s.op("act", lambda e, tm=tm, h=h, c=c: e.activation(
                out=h, in_=tm, func=AF.Identity, scale=self.lcol(layer, ka, c), bias=self.lcol(layer, kb, c)),
                reads=tmk + [("lay", layer)], writes=hk)

    def postnorm(self, yview, xin, xin_name, xout, xout_name, hf, layer, kg, stg_region, stg_off, nslots=8):
        s = self.s
        rs, rsk = self.R3.view(4096, F32, (1024,))
        s.op("act", lambda e: e.activation(out=rs, in_=self.P[3][:], func=AF.Ln, scale=1.0 / D, bias=self.epsc[:]),
             reads=self.pk(3) + ["epsc"], writes=rsk)
        s.op("act", lambda e: e.activation(out=rs, in_=rs, func=AF.Exp, scale=-0.5), reads=rsk, writes=rsk)
        toks = []
        for c in range(KC):
            xa, xk = stg_region.view(stg_off + (c % nslots) * 4096, F32, (1024,))
            s.dma(lambda q, xa=xa, c=c: q.dma_start(out=xa, in_=xin[c, :, hf * TH:(hf + 1) * TH]),
                  reads=[(xin_name, c, hf)], writes=xk, key=("xst2", c % nslots))
            y, yk = yview(c)
            s.op("dve", lambda e, y=y, c=c: e.scalar_tensor_tensor(
                out=y, in0=y, scalar=self.lcol(layer, kg, c), in1=rs, op0=ALU.mult, op1=ALU.mult),
                reads=yk + rsk + [("lay", layer)], writes=yk)
            s.op("dve", lambda e, y=y, xa=xa: e.tensor_tensor(out=xa, in0=y, in1=xa, op=ALU.add),
                 reads=yk + xk, writes=xk)
            t = s.dma(lambda q, xa=xa, c=c: q.dma_start(out=xout[c, :, hf * TH:(hf + 1) * TH], in_=xa),
                      reads=xk, writes=[(xout_name, c, hf)], key=("xo", c % nslots), eng="act")
            toks.append(t)
        return toks

    def proj(self, wt, n_oc, n_kg, kcb, in_view, psel, epilogue, oc_list=None, tiles=((0, 0), (512, 512)), m=128, after_block=None):
        s = self.s
        for oi, oc in enumerate(oc_list if oc_list is not None else range(n_oc)):
            Pi = psel(oi)
            pkeys = []
            for (_, po) in tiles:
                pkeys += self.pk(Pi, po // 512)
            for kg in range(n_kg):
                t, key = self.load_w(wt[oc, kg], kcb * m)
                ins_ = [in_view(kg * kcb + k) for k in range(kcb)]
                rk = [key]
                for a, k_ in ins_:
                    rk += k_

                def emit(e, t=t, kg=kg, ins_=ins_, Pi=Pi):
                    ins = None
                    for k in range(kcb):
                        for (io, po) in tiles:
                            ins = e.matmul(self.P[Pi][0:m, po:po + 512], lhsT=t[:, k * m:(k + 1) * m],
                                           rhs=ins_[k][0][:, io:io + 512],
                                           start=(kg == 0 and k == 0), stop=(kg == n_kg - 1 and k == kcb - 1))
                    return ins
                if oi == 0 and kg == 0 and kcb > 1:
                    for k in range(kcb):
                        def emit1(e, t=t, k=k, ins_=ins_, Pi=Pi):
                            ins = None
                            for (io, po) in tiles:
                                ins = e.matmul(self.P[Pi][0:m, po:po + 512], lhsT=t[:, k * m:(k + 1) * m],
                                               rhs=ins_[k][0][:, io:io + 512],
                                               start=(k == 0), stop=(n_kg == 1 and k == kcb - 1))
                            return ins
                        s.op("pe", emit1, reads=[key] + ins_[k][1], writes=pkeys)
                else:
                    s.op("pe", emit, reads=rk, writes=pkeys)
                if after_block is not None:
                    after_block()
            epilogue(oi, oc, Pi)

    def ffn(self, layer, xin, xin_name, xout, xout_name, next_layer=None):
        s = self.s
        dr = self.dr
        toks = []
        units = self.mod_units(next_layer) if next_layer is not None else []

        def inject():
            if units:
                units.pop(0)()
        for hf in range(2):
            self.prenorm(xin, xin_name, hf, layer, 3, 4)
            hid = lambda j: self.R2.view(j * 2048, BF16, (1024,))
            w1 = dr["ffn_w_in_%d" % layer]
            for j in range(JC):
                Pg, Pu = (0, 1) if j % 2 == 0 else (2, 3)
                self.proj(w1, None, 1, KC, self.hT, lambda oi, Pg=Pg, Pu=Pu: (Pg, Pu)[oi], lambda *a: None,
                          oc_list=[j, JC + j])
                sg, sgk = self.R3.view(8192 + (j % 2) * 4096, F32, (1024,))
                s.op("act", lambda e, sg=sg, Pg=Pg: e.activation(out=sg, in_=self.P[Pg][:], func=AF.Silu),
                     reads=self.pk(Pg), writes=sgk)
                h, hk = hid(j)
                s.op("dve", lambda e, sg=sg, h=h, Pu=Pu: e.tensor_tensor(out=h, in0=sg, in1=self.P[Pu][:], op=ALU.mult),
                     reads=sgk + self.pk(Pu), writes=hk)
            yv = lambda c: self.R1.view(c * 4096, F32, (1024,))

            def epi(oi, oc, Pi):
                y, yk = yv(oc)
                s.op("act", lambda e: e.activation(out=y, in_=self.P[Pi][:], func=AF.Identity),
                     reads=self.pk(Pi), writes=yk)
                sq, sqk = self.R3.view((oc % 2) * 2048, BF16, (1024,))
                s.op("dve", lambda e: e.tensor_tensor(out=sq, in0=self.P[Pi][:], in1=y, op=ALU.mult),
                     reads=self.pk(Pi) + yk, writes=sqk)
                self.flush_pending()
                self.stats_mm(3, sq, sqk, oc == 0, oc == KC - 1)
            self.proj(dr["ffn_w_out_%d" % layer], KC, 4, 11, hid, lambda oi: oi % 2, epi, after_block=inject)
            self.flush_pending()
            toks += self.postnorm(yv, xin, xin_name, xout, xout_name, hf, layer, 5, self.R2, 0, nslots=12)
        while units:
            units.pop(0)()
        return toks


    def allgather(self, src, dst, src_keys, dst_keys):
        self.s.op("pool", lambda e: e.collective_compute(
            "AllGather", ALU.bypass, replica_groups=[[0, 1, 2, 3], [4, 5, 6, 7]],
            ins=[src.opt()], outs=[dst.opt()]), reads=list(src_keys) + ["cc_chain"], writes=list(dst_keys) + ["cc_chain"])

    def conv_a(self, layer, j, xin, xin_name):
        s = self.s
        dr = self.dr
        vo = dr["vec_off"]["b_pw1_%d" % j]
        for hf in range(2):
            self.prenorm(xin, xin_name, hf, layer, 0, 1)
            for c in range(KC):
                inj = bool(self.pro_units)
                Pa, Pg = (0, 1) if (c % 2 == 0 or inj) else (2, 3)

                def inject():
                    if self.pro_units:
                        self.pro_units.pop(0)()
                self.proj(dr["conv_w_pw1_%d" % j], None, 1, KC, self.hT, lambda oi, Pa=Pa, Pg=Pg: (Pa, Pg)[oi],
                          lambda *a: None, oc_list=[c, KC + c], after_block=(inject if inj else None))
                sg, sgk = self.R3.view(8192 + (c % 2) * 4096, F32, (1024,))
                s.op("act", lambda e, sg=sg, Pg=Pg, c=c: e.activation(
                    out=sg, in_=self.P[Pg][:], func=AF.Sigmoid, bias=self.vecs[:, vo + KC + c:vo + KC + c + 1]),
                    reads=self.pk(Pg) + ["vecs"], writes=sgk)
                u, uk = self.R2.view(65536 + (c % 3) * 2048, BF16, (1024,))
                s.op("dve", lambda e, sg=sg, u=u, Pa=Pa, c=c: e.scalar_tensor_tensor(
                    out=u, in0=self.P[Pa][:], scalar=self.vecs[:, vo + c:vo + c + 1], in1=sg, op0=ALU.add, op1=ALU.mult),
                    reads=sgk + self.pk(Pa) + ["vecs"], writes=uk)
                s.dma(lambda q, u=u, c=c, hf=hf: q.dma_start(
                    out=dr["u_s"][c, :, HALO + hf * TH:HALO + (hf + 1) * TH], in_=u),
                    reads=uk, writes=[("u_s", c, hf)], key=("uo", c % 3), eng="act")
                if hf == 1:
                    s.dma(lambda q, u=u, c=c: q.dma_start(out=dr["halo_src"][:, c * HALO:(c + 1) * HALO], in_=u[:, TH - HALO:TH]),
                          reads=uk, writes=[("halo_src", c)], key=("ho", c % 3), eng="act")
        while self.pro_units:
            self.pro_units.pop(0)()
        self.allgather(dr["halo_src"], dr["halo_g"], [("halo_src", c) for c in range(KC)], ["halo_g"])

    def conv_b(self, layer, j, xin, xin_name, xout, xout_name):
        s = self.s
        dr = self.dr
        vo = dr["vec_off"]
        toks = []
        G, Gk = self.R3.view(16384, BF16, (4, KC * HALO))
        s.dma(lambda q: q.dma_start(out=G, in_=dr["halo_g"].rearrange("(r p) k -> p r k", p=128)),
              reads=["halo_g"], writes=Gk, key="halo_ld")
        hal, halk = self.R3.view(16384 + 4096, BF16, (KC * HALO,))
        mo = vo["mprev"]
        s.op("dve", lambda e: e.tensor_scalar(out=hal, in0=G[:, 0, :], scalar1=self.vecs[:, mo:mo + 1], scalar2=None,
                                               op0=ALU.mult), reads=Gk + ["vecs"], writes=halk)
        for r in range(1, 4):
            s.op("dve", lambda e, r=r: e.scalar_tensor_tensor(
                out=hal, in0=G[:, r, :], scalar=self.vecs[:, mo + r:mo + r + 1], in1=hal, op0=ALU.mult, op1=ALU.add),
                reads=Gk + halk + ["vecs"], writes=halk)
        UW = HALO + TH
        for hf in range(2):
            Us = []
            for c in range(KC):
                U, Uk = self.R2.view(c * UW * 2, BF16, (UW,))
                Us.append((U, Uk))
                s.dma(lambda q, U=U, c=c, hf=hf: q.dma_start(out=U, in_=dr["u_s"][c, :, hf * TH:hf * TH + UW]),
                      reads=[("u_s", c, 0), ("u_s", c, 1)], writes=Uk, key=("Uld", c))
                if hf == 0:
                    s.op("dve", lambda e, U=U, c=c: e.tensor_copy(out=U[:, 0:HALO], in_=hal[:, c * HALO:(c + 1) * HALO]),
                         reads=halk + Uk, writes=Uk)
            vv = lambda c: self.R1.view(c * 4096, F32, (1024,))
            wo = vo["w_dw_%d" % j]
            bo = vo["b_dw_%d" % j]
            def build_taps(c):
                dg, dgk = self.R2.view(33792 + (c % 2) * 8192, BF16, (CW, 128))
                for k in range(CW):
                    rd = ["ident", "vecs"] + ([("dggate", c % 2)] if k > 0 else [])
                    wr = [("dgtap", c % 2, k)] + (dgk + [("dggate", c % 2)] if k == 0 else [])
                    if k % 2 == 0:
                        s.op("dve", lambda e, dg=dg, k=k, c=c: e.tensor_scalar(
                            out=dg[:, k, :], in0=self.ident[:], scalar1=self.vecs[:, wo + k * KC + c:wo + k * KC + c + 1],
                            scalar2=None, op0=ALU.mult), reads=rd, writes=wr)
                    else:
                        s.op("act", lambda e, dg=dg, k=k, c=c: e.activation(
                            out=dg[:, k, :], in_=self.ident[:], func=AF.Identity,
                            scale=self.vecs[:, wo + k * KC + c:wo + k * KC + c + 1]), reads=rd, writes=wr)
            build_taps(0)
            for c in range(KC):
                dg, dgk = self.R2.view(33792 + (c % 2) * 8192, BF16, (CW, 128))
                Pi = c % 2
                U, Uk = Us[c]

                def emit(e, dg=dg, U=U, Pi=Pi):
                    ins = None
                    for tt in range(2):
                        for k in range(CW):
                            ins = e.matmul(self.P[Pi][:, tt * 512:(tt + 1) * 512], lhsT=dg[:, k, :],
                                           rhs=U[:, tt * 512 + k + 2:tt * 512 + k + 2 + 512],
                                           start=(k == 0), stop=(k == CW - 1))
                    return ins
                s.op("pe", emit, reads=[("dgtap", c % 2, k) for k in range(CW)] + dgk + [("dggate", c % 2)] + Uk,
                     writes=self.pk(Pi))
                if c + 1 < KC:
                    build_taps(c + 1)
                v, vk = vv(c)
                s.op("act", lambda e, v=v, Pi=Pi, c=c: e.activation(
                    out=v, in_=self.P[Pi][:], func=AF.Identity, bias=self.vecs[:, bo + c:bo + c + 1]),
                    reads=self.pk(Pi) + ["vecs"], writes=vk)
                sq, sqk = self.R3.view((c % 2) * 2048, BF16, (1024,))
                vb, vbk = self.R3.view(4096 + (c % 2) * 2048, BF16, (1024,))
                s.op("dve", lambda e, v=v, vb=vb: e.tensor_copy(out=vb, in_=v), reads=vk, writes=vbk)
                s.op("dve", lambda e, v=v, sq=sq: e.tensor_tensor(out=sq, in0=v, in1=v, op=ALU.mult), reads=vk, writes=sqk)
                self.flush_pending()
                self.stats_mm(2, vb, vbk, c == 0, c == KC - 1)
                self.stats_mm(3, sq, sqk, c == 0, c == KC - 1)
            self.flush_pending()
            mean, mk = self.R3.view(8192, F32, (1024,))
            rstd, rk = self.R3.view(12288, F32, (1024,))
            s.op("act", lambda e: e.activation(out=mean, in_=self.P[2][:], func=AF.Identity, scale=1.0 / D),
                 reads=self.pk(2), writes=mk)
            s.op("dve", lambda e: e.tensor_tensor(out=rstd, in0=mean, in1=mean, op=ALU.mult), reads=mk, writes=rk)
            s.op("dve", lambda e: e.scalar_tensor_tensor(out=rstd, in0=self.P[3][:], scalar=1.0 / D, in1=rstd,
                                                          op0=ALU.mult, op1=ALU.subtract), reads=self.pk(3) + rk, writes=rk)
            s.op("act", lambda e: e.activation(out=rstd, in_=rstd, func=AF.Ln, bias=self.epsc[:]),
                 reads=rk + ["epsc"], writes=rk)
            s.op("act", lambda e: e.activation(out=rstd, in_=rstd, func=AF.Exp, scale=-0.5), reads=rk, writes=rk)
            sv = lambda c: self.R2.view(50176 + c * 2048, BF16, (1024,))
            go, lbo = vo["ln_g_%d" % j], vo["ln_b_%d" % j]
            for c in range(KC):
                v, vk = vv(c)
                t1, t1k = self.R3.view(16384 + (c % 2) * 4096, F32, (1024,))
                s.op("dve", lambda e, v=v, t1=t1: e.tensor_tensor(out=t1, in0=v, in1=mean, op=ALU.subtract),
                     reads=vk + mk, writes=t1k)
                s.op("dve", lambda e, t1=t1: e.tensor_tensor(out=t1, in0=t1, in1=rstd, op=ALU.mult),
                     reads=t1k + rk, writes=t1k)
                sc, sck = sv(c)
                s.op("act", lambda e, t1=t1, sc=sc, c=c: e.activation(
                    out=sc, in_=t1, func=AF.Silu, scale=self.vecs[:, go + c:go + c + 1], bias=self.vecs[:, lbo + c:lbo + c + 1]),
                    reads=t1k + ["vecs"], writes=sck)
            yv = vv
            b2 = vo["b_pw2_%d" % j]

            def epi(oi, oc, Pi):
                y, yk = yv(oc)
                s.op("act", lambda e: e.activation(out=y, in_=self.P[Pi][:], func=AF.Identity,
                                                   bias=self.vecs[:, b2 + oc:b2 + oc + 1]),
                     reads=self.pk(Pi) + ["vecs"], writes=yk)
                sq, sqk = self.R3.view((oc % 2) * 2048, BF16, (1024,))
                s.op("dve", lambda e: e.tensor_tensor(out=sq, in0=y, in1=y, op=ALU.mult), reads=yk, writes=sqk)
                self.flush_pending()
                self.stats_mm(3, sq, sqk, oc == 0, oc == KC - 1)
            self.proj(dr["conv_w_pw2_%d" % j], KC, 1, KC, sv, lambda oi: oi % 2, epi)
            self.flush_pending()
            toks += self.postnorm(yv, xin, xin_name, xout, xout_name, hf, layer, 2, self.R2, 50176, nslots=8)
        return toks

    def rows(self, region, off, dtype, n, nrows):
        esz = 4 if dtype == F32 else 2
        a = region.t[0:nrows, off // 4:(off + n * esz) // 4]
        if dtype != F32:
            a = a.bitcast(dtype)
        keys = [(region.name, pg) for pg in range(off // 1024, (off + n * esz + 1023) // 1024)]
        return a, keys

    def gla(self, layer, j, xin, xin_name, xout, xout_name, state_only):
        s = self.s
        dr = self.dr
        vo = dr["vec_off"]
        R1, R2, R3 = self.R1, self.R2, self.R3
        toks = []
        B = not state_only
        wup, wupk = self.rows(R3, 16384, BF16, 1024, 17)
        aT, aTk = self.rows(R3, 18944, BF16, 512, 17)
        small, smk = R3.view(18432, F32, (8 * 8,))
        bl, ebl, Bsum, Dj, Dp = [small[:, i * 8:(i + 1) * 8] for i in range(5)]
        blk_, eblk_, Bsk, Djk, Dpk = [[("gsm", i)] for i in range(5)]
        s.dma(lambda q: q.dma_start(out=wup, in_=dr["gla_wup_%d" % j]), writes=wupk, key="wup", eng="pool")
        s.op("dve", lambda e: e.memset(aT, 1.0), writes=aTk)
        S32v = lambda dc: R2.view(65536 + dc * 2048, F32, (512,))
        Sbfv = lambda dc: R2.view(81920 + dc * 1024, BF16, (512,))
        Sall, Sallk = R2.view(65536, F32, (4096,))
        Sball, Sballk = R2.view(81920, BF16, (4096,))
        def init_state():
            s.op("dve", lambda e: e.memset(Sall, 0.0), writes=Sallk)
            if state_only:
                s.op("dve", lambda e: e.memset(Bsum, 0.0), writes=Bsk)
            else:
                mo = vo["mlt"]
                for jr in range(3):
                    s.dma(lambda q, jr=jr: q.dma_start(out=Dj, in_=dr["gd_g"][jr * 128:(jr + 1) * 128, 0:8]),
                          reads=["gd_g"], writes=Djk, key="Dj")
                    s.op("dve", lambda e, jr=jr: e.tensor_scalar(out=Dp, in0=Dj, scalar1=-1.0, scalar2=self.vecs[:, mo + jr:mo + jr + 1],
                                                                  op0=ALU.add, op1=ALU.mult), reads=Djk + ["vecs"], writes=Dpk)
                    s.op("dve", lambda e: e.tensor_scalar(out=Dp, in0=Dp, scalar1=1.0, scalar2=None, op0=ALU.add),
                         reads=Dpk, writes=Dpk)
                    for dc in range(8):
                        stg, stgk = R3.view(20480 + (dc % 2) * 2048, F32, (512,))
                        s.dma(lambda q, jr=jr, dc=dc, stg=stg: q.dma_start(
                            out=stg, in_=dr["gsA_g" if dc < 4 else "gsB_g"][jr * 128:(jr + 1) * 128, (dc % 4) * 512:(dc % 4 + 1) * 512]),
                            reads=["gsA_g" if dc < 4 else "gsB_g"], writes=stgk, key=("stg", dc % 2))
                        s.op("dve", lambda e, stg=stg, jr=jr: e.tensor_scalar(
                            out=stg, in0=stg, scalar1=self.vecs[:, mo + jr:mo + jr + 1], scalar2=None, op0=ALU.mult),
                            reads=stgk + ["vecs"], writes=stgk)
                        S, Sk = S32v(dc)
                        s.op("dve", lambda e, S=S, stg=stg, dc=dc: e.scalar_tensor_tensor(
                            out=S, in0=S, scalar=Dp[:, dc:dc + 1], in1=stg, op0=ALU.mult, op1=ALU.add),
                            reads=Sk + stgk + Dpk, writes=Sk)
                s.op("act", lambda e: e.activation(out=Sball, in_=Sall, func=AF.Identity), reads=Sallk, writes=Sballk)


        init_done = [False]

        qTv = lambda c: R2.view(c * 1024, BF16, (512,))
        kTv = lambda c: R2.view(8192 + c * 1024, BF16, (512,))
        vTv = lambda c: R2.view(16384 + c * 1024, BF16, (512,))
        qT3, _ = R2.view(0, BF16, (8, 512))
        kT3, _ = R2.view(8192, BF16, (8, 512))
        qTk = R2.view(0, BF16, (4096,))[1]
        kTk = R2.view(8192, BF16, (4096,))[1]
        vTk = R2.view(16384, BF16, (8192,))[1]
        gpos, gpk = R2.view(32768, F32, (1024,))
        e1, e1k = R2.view(36864, F32, (1024,))
        tE = [R2.view(40960, F32, (8, 128)), R2.view(45056, F32, (8, 128))]
        tEf = [R2.view(40960, F32, (1024,)), R2.view(45056, F32, (1024,))]
        qd, qdk = R2.view(49152, BF16, (8, 128))
        kd, kdk = R2.view(51200, BF16, (8, 128))
        kh, khk = R2.view(53248, BF16, (8, 128))
        vtok, vtk = R2.view(55296, BF16, (2048,))
        ktok, ktk = R2.view(59392, BF16, (1024,))
        sc, sck = R2.view(61440, BF16, (512,))
        osq, osqk = R3.view(0, BF16, (2048,))
        rh, rhk = R3.view(4096, F32, (512,))
        onT = lambda c: R1.view(32768 + c * 2048, BF16, (1024,))
        P = self.P
        P1v = P[1][:].rearrange("p (a b) -> p a b", a=8)
        P2bf = P[2][:].bitcast(BF16)
        P3bf = P[3][:].bitcast(BF16)
        ngo = vo["gla_ng_%d" % j]
        cnt = [0]

        def alt():
            cnt[0] += 1
            return "act" if cnt[0] % 2 else "dve"

        def copy_op(eng, out, in_, reads, writes, scale=None):
            if eng == "act":
                if scale is None:
                    s.op("act", lambda e: e.activation(out=out, in_=in_, func=AF.Identity), reads=reads, writes=writes)
                else:
                    s.op("act", lambda e: e.activation(out=out, in_=in_, func=AF.Identity, scale=scale), reads=reads, writes=writes)
            else:
                if scale is None:
                    s.op("dve", lambda e: e.tensor_copy(out=out, in_=in_), reads=reads, writes=writes)
                else:
                    s.op("dve", lambda e: e.tensor_scalar(out=out, in0=in_, scalar1=scale, scalar2=None, op0=ALU.mult),
                         reads=reads, writes=writes)

        for hf in range(2):
            hall, hallk = R1.view(0, BF16, (KC * TH,))
            if state_only:
                self.prenorm(xin, xin_name, hf, layer, 0, 1)
                s.dma(lambda q, hf=hf: q.dma_start(out=dr["h_s"][hf], in_=hall), reads=hallk, writes=[("h_s", hf)],
                      key="hso", eng="act")
            else:
                s.dma(lambda q, hf=hf: q.dma_start(out=hall, in_=dr["h_s"][hf]), reads=[("h_s", hf)], writes=hallk, key="hsi")
            for qt in range(2):
                tl = ((qt * 512, 0),)
                def epi_qkv(oi, oc, Pi):
                    if oc < 8:
                        o_, k_ = qTv(oc)
                        copy_op(alt(), o_, P[Pi][:, 0:512], self.pk(Pi, 0), k_, scale=0.0625)
                    elif oc < 16:
                        o_, k_ = kTv(oc - 8)
                        copy_op(alt(), o_, P[Pi][:, 0:512], self.pk(Pi, 0), k_)
                    else:
                        o_, k_ = vTv(oc - 16)
                        copy_op(alt(), o_, P[Pi][:, 0:512], self.pk(Pi, 0), k_)
                qi = hf * 2 + qt
                kv3, _ = R2.view(8192, BF16, (24, 512))
                if state_only:
                    def epi_a(oi, oc, Pi):
                        s.op("act", lambda e: e.activation(out=aT[0:16, :], in_=P[Pi][0:16, 0:512], func=AF.Identity),
                             reads=self.pk(Pi, 0), writes=aTk)
                    self.proj(dr["gla_wa_%d" % j], 1, 1, KC, self.hT, lambda oi: 0, epi_a, tiles=tl, m=16)
                    self.proj(dr["gla_w_in_%d" % j], None, 1, KC, self.hT, lambda oi: 1 + oi % 3, epi_qkv,
                              oc_list=list(range(8, 32)), tiles=tl)
                    s.dma(lambda q, qi=qi: q.dma_start(out=dr["kv_s"][qi], in_=kv3), reads=kTk + vTk,
                          writes=[("kv_s", qi)], key="kvo", eng="act")
                    s.dma(lambda q, qi=qi: q.dma_start(out=dr["a_s"][qi], in_=aT[0:16, :]), reads=aTk,
                          writes=[("a_s", qi)], key="ao", eng="act")
                else:
                    s.dma(lambda q, qi=qi: q.dma_start(out=kv3[:, 0:8, :], in_=dr["kv_s"][qi][:, 0:8, :]), reads=[("kv_s", qi)],
                          writes=kTk, key="kvi")
                    self.proj(dr["gla_w_in_%d" % j], None, 1, KC, self.hT, lambda oi: 1 + oi % 3, epi_qkv,
                              oc_list=list(range(0, 8)), tiles=tl)
                if not init_done[0]:
                    init_done[0] = True
                    init_state()
                for ch in range(4):
                    c0 = ch * 128
                    tokoff = qt * 512 + c0

                    def state_mm():
                        for dc in range(8):
                            h = dc // 2
                            s.op("pe", lambda e, dc=dc, h=h: e.matmul(P[3][:, (dc % 2) * 512:(dc % 2 + 1) * 512],
                                                                     lhsT=ktok[:, dc * 128:(dc + 1) * 128],
                                                                     rhs=vtok[:, h * 512:(h + 1) * 512], start=True, stop=True),
                                 reads=ktk + vtk, writes=self.pk(3, dc % 2))
                            S, Sk = S32v(dc)
                            s.op("dve", lambda e, S=S, dc=dc: e.scalar_tensor_tensor(
                                out=S, in0=S, scalar=ebl[:, dc:dc + 1], in1=P[3][:, (dc % 2) * 512:(dc % 2 + 1) * 512],
                                op0=ALU.mult, op1=ALU.add), reads=Sk + eblk_ + self.pk(3, dc % 2), writes=Sk)

                    def state_cast():
                        for dc in range(8):
                            S, Sk = S32v(dc)
                            Sb, Sbk = Sbfv(dc)
                            s.op("act", lambda e, S=S, Sb=Sb: e.activation(out=Sb, in_=S, func=AF.Identity), reads=Sk, writes=Sbk)

                    ci = (hf * 2 + qt) * 4 + ch
                    if state_only:
                        def emit_z(e, c0=c0):
                            ins = None
                            for hh in range(2):
                                ins = e.matmul(P[0][:, hh * 512:(hh + 1) * 512], lhsT=aT[:, c0:c0 + 128],
                                               rhs=wup[:, hh * 512:(hh + 1) * 512], start=True, stop=True)
                            return ins
                        s.op("pe", emit_z, reads=aTk + wupk, writes=self.pk(0))
                        s.op("act", lambda e: e.activation(out=e1, in_=P[0][:], func=AF.Exp, scale=-1.0), reads=self.pk(0), writes=e1k)
                        s.op("act", lambda e: e.activation(out=gpos, in_=e1, func=AF.Ln, bias=self.one32[:]),
                             reads=e1k + ["one32"], writes=gpk)

                        ghi, ghk = R2.view(36864, BF16, (1024,))
                        glo, glk = R2.view(38912, BF16, (1024,))
                        s.op("dve", lambda e: e.tensor_copy(out=ghi, in_=gpos), reads=gpk, writes=ghk)
                        s.op("dve", lambda e: e.tensor_tensor(out=glo, in0=gpos, in1=ghi, op=ALU.subtract), reads=gpk + ghk, writes=glk)

                        def emit_cs(e):
                            ins = None
                            for dc in range(8):
                                e.matmul(P[1][:, dc * 128:(dc + 1) * 128], lhsT=ghi[:, dc * 128:(dc + 1) * 128],
                                         rhs=self.tribf[:], start=True, stop=False)
                                ins = e.matmul(P[1][:, dc * 128:(dc + 1) * 128], lhsT=glo[:, dc * 128:(dc + 1) * 128],
                                               rhs=self.tribf[:], start=False, stop=True)
                            return ins
                        s.op("pe", emit_cs, reads=ghk + glk + ["tribf"], writes=self.pk(1))
                        s.op("dve", lambda e: e.tensor_scalar(out=bl, in0=P1v[:, :, 127], scalar1=-0.0625, scalar2=None, op0=ALU.mult),
                             reads=self.pk(1), writes=blk_)
                        s.op("act", lambda e: e.activation(out=ebl, in_=bl, func=AF.Exp), reads=blk_, writes=eblk_)
                        if state_only:
                            s.op("dve", lambda e: e.tensor_tensor(out=Bsum, in0=Bsum, in1=bl, op=ALU.add), reads=Bsk + blk_, writes=Bsk)
                        (EK, EKk) = tE[0]

                        def emit_ek(e, EK=EK):
                            ins = None
                            for dc in range(8):
                                ins = e.activation(out=EK[:, dc, :], in_=P1v[:, dc, :], func=AF.Exp, scale=0.0625, bias=bl[:, dc:dc + 1])
                            return ins
                        s.op("act", emit_ek, reads=self.pk(1) + blk_, writes=EKk)
                        s.op("dve", lambda e, c0=c0, EK=EK: e.tensor_tensor(out=kh, in0=kT3[:, :, c0:c0 + 128], in1=EK, op=ALU.mult),
                             reads=kTk + EKk, writes=khk)
                        def emit_tk(e):
                            ins = None
                            for dc in range(8):
                                ins = e.transpose(out=P2bf[:, dc * 128:(dc + 1) * 128], in_=kh[:, dc, :], identity=self.ident[:])
                            return ins
                        s.op("pe", emit_tk, reads=khk + ["ident"], writes=self.pk(2, 0))

                        def emit_tv(e, c0=c0):
                            ins = None
                            for c in range(16):
                                ins = e.transpose(out=P3bf[:, c * 128:(c + 1) * 128], in_=vTv(c)[0][:, c0:c0 + 128], identity=self.ident[:])
                            return ins
                        s.op("pe", emit_tv, reads=vTk + ["ident"], writes=self.pk(3))
                        s.op("act", lambda e: e.activation(out=ktok, in_=P2bf[:, 0:1024], func=AF.Identity), reads=self.pk(2, 0), writes=ktk)
                        s.op("dve", lambda e: e.tensor_copy(out=vtok, in_=P3bf[:, 0:2048]), reads=self.pk(3), writes=vtk)
                        csc, csck = tEf[1]
                        s.op("act", lambda e, csc=csc: e.activation(out=csc, in_=P[1][:], func=AF.Identity), reads=self.pk(1), writes=csck)
                        s.dma(lambda q, ci=ci: q.dma_start(out=dr["ck_s"][ci], in_=ktok), reads=ktk, writes=[("ck_s", ci)], key="cko", eng="act")
                        s.dma(lambda q, ci=ci: q.dma_start(out=dr["cv_s"][ci], in_=vtok), reads=vtk, writes=[("cv_s", ci)], key="cvo", eng="act")
                        s.dma(lambda q, ci=ci, csc=csc: q.dma_start(out=dr["cs_s"][ci], in_=csc), reads=csck, writes=[("cs_s", ci)], key="cso", eng="act")
                    else:
                        csb, csbk = R2.view(32768, F32, (1024,))
                        csb3, _ = R2.view(32768, F32, (8, 128))
                        s.dma(lambda q, ci=ci: q.dma_start(out=ktok, in_=dr["ck_s"][ci]), reads=[("ck_s", ci)], writes=ktk, key="ckl")
                        s.dma(lambda q, ci=ci: q.dma_start(out=vtok, in_=dr["cv_s"][ci]), reads=[("cv_s", ci)], writes=vtk, key="cvl")
                        s.dma(lambda q, ci=ci: q.dma_start(out=csb, in_=dr["cs_s"][ci]), reads=[("cs_s", ci)], writes=csbk, key="csl")
                        s.op("dve", lambda e: e.tensor_scalar(out=bl, in0=csb3[:, :, 127], scalar1=-0.0625, scalar2=None, op0=ALU.mult),
                             reads=csbk, writes=blk_)
                        s.op("act", lambda e: e.activation(out=ebl, in_=bl, func=AF.Exp), reads=blk_, writes=eblk_)
                        EBf, EBk = tEf[1]
                        s.op("act", lambda e, EBf=EBf: e.activation(out=EBf, in_=csb, func=AF.Exp, scale=-0.0625),
                             reads=csbk, writes=EBk)
                        s.op("dve", lambda e, c0=c0: e.tensor_tensor(out=qd, in0=qT3[:, :, c0:c0 + 128], in1=tE[1][0], op=ALU.mult),
                             reads=qTk + EBk, writes=qdk)
                        ENf, ENk = tEf[0]
                        s.op("act", lambda e, ENf=ENf: e.activation(out=ENf, in_=csb, func=AF.Exp, scale=0.0625),
                             reads=csbk, writes=ENk)
                        s.op("dve", lambda e, c0=c0: e.tensor_tensor(out=kd, in0=kT3[:, :, c0:c0 + 128], in1=tE[0][0], op=ALU.mult),
                             reads=kTk + ENk, writes=kdk)
                    if B:
                        def emit_sc(e):
                            ins = None
                            for h in range(4):
                                for d2 in range(2):
                                    dc = 2 * h + d2
                                    ins = e.matmul(P[2][:, 512 + h * 128:512 + (h + 1) * 128], lhsT=kd[:, dc, :], rhs=qd[:, dc, :],
                                                   start=(d2 == 0), stop=(d2 == 1))
                            return ins
                        s.op("pe", emit_sc, reads=kdk + qdk, writes=self.pk(2, 1))
                        s.op("dve", lambda e: e.tensor_tensor(out=sc, in0=P[2][:, 512:1024], in1=self.tri4[:], op=ALU.mult),
                             reads=self.pk(2, 1) + ["tri4"], writes=sck)

                        def emit_o(e):
                            ins = None
                            for h in range(4):
                                for es in range(4):
                                    blk = h * 4 + es
                                    out = P[blk // 8][:, (blk % 8) * 128:(blk % 8 + 1) * 128]
                                    e.matmul(out, lhsT=Sbfv(2 * h)[0][:, es * 128:(es + 1) * 128], rhs=qd[:, 2 * h, :], start=True, stop=False)
                                    e.matmul(out, lhsT=Sbfv(2 * h + 1)[0][:, es * 128:(es + 1) * 128], rhs=qd[:, 2 * h + 1, :], start=False, stop=False)
                                    ins = e.matmul(out, lhsT=vtok[:, h * 512 + es * 128:h * 512 + (es + 1) * 128],
                                                   rhs=sc[:, h * 128:(h + 1) * 128], start=False, stop=True)
                            return ins
                        s.op("pe", emit_o, reads=Sballk + qdk + vtk + sck, writes=self.pk(0) + self.pk(1))
                        state_mm()
                        s.op("act", lambda e: e.activation(out=osq[:, 0:1024], in_=P[0][:], func=AF.Square), reads=self.pk(0), writes=osqk)
                        s.op("act", lambda e: e.activation(out=osq[:, 1024:2048], in_=P[1][:], func=AF.Square), reads=self.pk(1), writes=osqk)

                        def emit_hs(e):
                            ins = None
                            for h in range(4):
                                for es in range(4):
                                    ins = e.matmul(P[2][:, h * 128:(h + 1) * 128], lhsT=self.ones[:],
                                                   rhs=osq[:, (h * 4 + es) * 128:(h * 4 + es + 1) * 128], start=(es == 0), stop=(es == 3))
                            return ins
                        s.op("pe", emit_hs, reads=osqk + ["ones"], writes=self.pk(2, 0))
                        s.op("act", lambda e: e.activation(out=rh, in_=P[2][:, 0:512], func=AF.Ln, scale=1.0 / 512, bias=self.epsc[:]),
                             reads=self.pk(2, 0) + ["epsc"], writes=rhk)
                        s.op("act", lambda e: e.activation(out=rh, in_=rh, func=AF.Exp, scale=-0.5), reads=rhk, writes=rhk)
                        for blk in range(16):
                            h = blk // 4
                            o_, ok_ = onT(blk)
                            s.op("dve", lambda e, blk=blk, h=h, o_=o_, tokoff=tokoff: e.scalar_tensor_tensor(
                                out=o_[:, tokoff:tokoff + 128], in0=P[blk // 8][:, (blk % 8) * 128:(blk % 8 + 1) * 128],
                                scalar=self.vecs[:, ngo + blk:ngo + blk + 1], in1=rh[:, h * 128:(h + 1) * 128],
                                op0=ALU.mult, op1=ALU.mult), reads=self.pk(blk // 8) + rhk + ["vecs"], writes=ok_)
                    if not B:
                        state_mm()
                    else:
                        state_cast()
            if B:
                def epi_r(oi, oc, Pi):
                    c = oc - 32
                    sr, srk = R3.view(8192 + (c % 2) * 4096, F32, (1024,))
                    s.op("act", lambda e: e.activation(out=sr, in_=P[Pi][:], func=AF.Silu), reads=self.pk(Pi), writes=srk)
                    o_, ok_ = onT(c)
                    s.op("dve", lambda e: e.tensor_tensor(out=o_, in0=o_, in1=sr, op=ALU.mult), reads=ok_ + srk, writes=ok_)
                self.proj(dr["gla_w_in_%d" % j], None, 1, KC, self.hT, lambda oi: oi % 2, epi_r, oc_list=list(range(32, 48)))
                yv = lambda c: R2.view(c * 4096, F32, (1024,))

                def epi_o(oi, oc, Pi):
                    y, yk = yv(oc)
                    s.op("act", lambda e: e.activation(out=y, in_=P[Pi][:], func=AF.Identity), reads=self.pk(Pi), writes=yk)
                    sq, sqk = R3.view((oc % 2) * 2048, BF16, (1024,))
                    s.op("dve", lambda e: e.tensor_tensor(out=sq, in0=P[Pi][:], in1=y, op=ALU.mult), reads=self.pk(Pi) + yk, writes=sqk)
                    self.flush_pending()
                    self.stats_mm(3, sq, sqk, oc == 0, oc == KC - 1)
                self.proj(dr["gla_w_out_%d" % j], KC, 1, KC, onT, lambda oi: oi % 2, epi_o)
                self.flush_pending()
                toks += self.postnorm(yv, xin, xin_name, xout, xout_name, hf, layer, 2, R1, 0, nslots=12)
        if state_only:
            s.op("act", lambda e: e.activation(out=Dj, in_=Bsum, func=AF.Exp), reads=Bsk, writes=Djk)
            s.dma(lambda q: q.dma_start(out=dr["gd_src"][:, 0:8], in_=Dj), reads=Djk, writes=["gd_src"], key="gdo", eng="act")
            s.dma(lambda q: q.dma_start(out=dr["gsA_src"], in_=Sall[:, 0:2048]), reads=Sallk, writes=["gsA_src"], key="gsoA", eng="act")
            s.dma(lambda q: q.dma_start(out=dr["gsB_src"], in_=Sall[:, 2048:4096]), reads=Sallk, writes=["gsB_src"], key="gsoB", eng="act")
            self.allgather(dr["gd_src"], dr["gd_g"], ["gd_src"], ["gd_g"])
            self.allgather(dr["gsA_src"], dr["gsA_g"], ["gsA_src"], ["gsA_g"])
            self.allgather(dr["gsB_src"], dr["gsB_g"], ["gsB_src"], ["gsB_g"])
        return toks


def tile_w(W, kcb, m=128):
    K, N = W.shape
    n_kg = K // (128 * kcb)
    a = W.reshape(n_kg, kcb, 128, N // m, m).transpose(3, 0, 2, 1, 4)
    return np.ascontiguousarray(a).reshape(N // m, n_kg, 128, kcb * m)


def colvec(v):
    return np.ascontiguousarray(v.reshape(-1, 128).T)


def build_vecs(inp, b, seg):
    cols = []
    off = {}

    def add(name, arr):
        off[name] = sum(a.shape[1] for a in cols)
        cols.append(np.asarray(arr, dtype=np.float32))
    add("c", colvec(inp["c"][b]))
    for i in range(DEPTH):
        add("b_ada%d" % i, colvec(inp["b_ada"][i]))
        for nm in ["pre_mix_g", "post_mix_g", "pre_ffn_g", "post_ffn_g"]:
            add(nm + "%d" % i, colvec(inp[nm][i]))
    for j in range(2):
        add("b_pw1_%d" % j, colvec(inp["conv_b_pw1"][j]))
        add("b_dw_%d" % j, colvec(inp["conv_b_dw"][j]))
        add("ln_g_%d" % j, colvec(inp["conv_ln_g"][j]))
        add("ln_b_%d" % j, colvec(inp["conv_ln_b"][j]))
        add("b_pw2_%d" % j, colvec(inp["conv_b_pw2"][j]))
        add("w_dw_%d" % j, colvec(inp["conv_w_dw"][j].reshape(-1)))
        add("gla_ng_%d" % j, colvec(inp["gla_norm_g"][j]))
    mprev = np.zeros((128, 4), np.float32)
    if seg > 0:
        mprev[:, seg - 1] = 1.0
    mlt = np.zeros((128, 4), np.float32)
    mlt[:, :seg] = 1.0
    add("mprev", mprev)
    add("mlt", mlt)
    return np.concatenate(cols, axis=1), off


FULL = [("convA", 0), ("convB", 0), ("ffn", 0), ("glaA", 1), ("glaB", 1), ("ffn", 1),
        ("convA", 2), ("convB", 2), ("ffn", 2), ("glaA", 3), ("glaB", 3), ("ffn", 3)]
MODES = {"full": FULL, "ffn0": [("ffn", 0)], "conv0": [("convA", 0), ("convB", 0)],
         "gla1": [("glaA", 1), ("glaB", 1)], "gla1A": [("glaA", 1)], "f0g1": [("ffn", 0), ("glaA", 1), ("glaB", 1)], "l0": FULL[:3], "l01": FULL[:6]}


def weight_arrays(inp, steps):
    w = {}
    for kind, L in steps:
        j = L // 2
        w["w_ada%d" % L] = lambda L=L: np.ascontiguousarray(inp["w_ada"][L])
        if kind == "ffn":
            w["ffn_w_in_%d" % L] = lambda L=L: tile_w(inp["ffn_w_in"][L], KC)
            w["ffn_w_out_%d" % L] = lambda L=L: tile_w(inp["ffn_w_out"][L], 11)
        elif kind == "convA":
            w["conv_w_pw1_%d" % j] = lambda j=j: tile_w(inp["conv_w_pw1"][j], KC)
        elif kind == "convB":
            w["conv_w_pw2_%d" % j] = lambda j=j: tile_w(inp["conv_w_pw2"][j], KC)
        elif kind in ("glaA", "glaB"):
            w["gla_w_in_%d" % j] = lambda j=j: tile_w(inp["gla_w_in"][j][:, :6144], KC)
            w["gla_wa_%d" % j] = lambda j=j: tile_w(inp["gla_w_in"][j][:, 6144:6160], KC, m=16)
            w["gla_wup_%d" % j] = lambda j=j: np.concatenate(
                [inp["gla_w_gate_up"][j], inp["gla_b_gate"][j][None, :]], axis=0).astype(np.float32)
            if kind == "glaB":
                w["gla_w_out_%d" % j] = lambda j=j: tile_w(inp["gla_w_out"][j], KC)
    return {k: f() for k, f in w.items()}


def host_consts():
    c = np.zeros((128, 640), np.float32)
    c[:, 0:128] = np.eye(128, dtype=np.float32)
    tri = np.triu(np.ones((128, 128), np.float32))
    c[:, 128:640] = np.tile(tri, (1, 4))
    return c


def build_program(vec_off, nvec, steps, wshapes):
    nc = bass.Bass("TRN2", target_bir_lowering=False)
    dr = {"vec_off": vec_off}
    dr["xT"] = nc.dram_tensor("xT", [KC, 128, TOK], F32, kind="ExternalInput").ap()
    dr["vecs"] = nc.dram_tensor("vecs", [128, nvec], F32, kind="ExternalInput").ap()
    dr["consts"] = nc.dram_tensor("consts", [128, 640], F32, kind="ExternalInput").ap()
    for k, shp in wshapes.items():
        dr[k] = nc.dram_tensor(k, list(shp), F32, kind="ExternalInput").ap()
    dr["out"] = nc.dram_tensor("out", [KC, 128, TOK], F32, kind="ExternalOutput").ap()
    dr["xs"] = nc.dram_tensor("xs", [KC, 128, TOK], F32).ap()
    dr["u_s"] = nc.dram_tensor("u_s", [KC, 128, HALO + TOK], BF16).ap()
    dr["halo_src"] = nc.dram_tensor("halo_src", [128, KC * HALO], BF16).ap()
    dr["halo_g"] = nc.dram_tensor("halo_g", [4 * 128, KC * HALO], BF16).ap()
    dr["kv_s"] = nc.dram_tensor("kv_s", [4, 128, 24, 512], BF16).ap()
    dr["a_s"] = nc.dram_tensor("a_s", [4, 16, 512], BF16).ap()
    dr["h_s"] = nc.dram_tensor("h_s", [2, 128, KC * TH], BF16).ap()
    dr["ck_s"] = nc.dram_tensor("ck_s", [16, 128, 1024], BF16).ap()
    dr["cv_s"] = nc.dram_tensor("cv_s", [16, 128, 2048], BF16).ap()
    dr["cs_s"] = nc.dram_tensor("cs_s", [16, 128, 1024], F32).ap()
    dr["gd_src"] = nc.dram_tensor("gd_src", [128, 64], F32).ap()
    dr["gd_g"] = nc.dram_tensor("gd_g", [4 * 128, 64], F32).ap()
    dr["gsA_src"] = nc.dram_tensor("gsA_src", [128, 2048], F32).ap()
    dr["gsA_g"] = nc.dram_tensor("gsA_g", [4 * 128, 2048], F32).ap()
    dr["gsB_src"] = nc.dram_tensor("gsB_src", [128, 2048], F32).ap()
    dr["gsB_g"] = nc.dram_tensor("gsB_g", [4 * 128, 2048], F32).ap()
    with contextlib.ExitStack() as es:
        b = Builder(nc, es, dr)
        b.epsc = es.enter_context(nc.sbuf_tensor("epsc", [128, 1], F32))
        b.s.op("dve", lambda e: e.memset(b.epsc[:], EPS), writes=["epsc"])
        layers = sorted(set(L for _, L in steps))
        b.prologue_mod(layers[:1])
        if steps[0][0] != "convA":
            while b.pro_units:
                b.pro_units.pop(0)()
        resid = [i for i, (k, _) in enumerate(steps) if k in ("convB", "glaB", "ffn")]
        cur, cur_name = dr["xT"], "xT"
        toks = []
        for i, (kind, L) in enumerate(steps):
            j = L // 2
            if resid and i == resid[-1]:
                nxt, nxt_name = dr["out"], "out"
            else:
                nxt, nxt_name = dr["xs"], "xs"
            if kind == "ffn":
                nl = layers[layers.index(L) + 1] if layers.index(L) + 1 < len(layers) else None
                toks = b.ffn(L, cur, cur_name, nxt, nxt_name, next_layer=nl)
            elif kind == "convA":
                b.conv_a(L, j, cur, cur_name)
            elif kind == "convB":
                toks = b.conv_b(L, j, cur, cur_name, nxt, nxt_name)
            elif kind == "glaA":
                b.gla(L, j, cur, cur_name, None, None, True)
            elif kind == "glaB":
                toks = b.gla(L, j, cur, cur_name, nxt, nxt_name, False)
            if i in resid:
                cur, cur_name = nxt, nxt_name
        b.s.emit(final_wait_tokens=toks)
    return nc


def run(inp, mode, trace=False):
    inp = {k: np.asarray(v) for k, v in inp.items()}
    steps = MODES[mode]
    W = weight_arrays(inp, steps)
    consts = host_consts()
    maps = []
    vec_off = None
    for core in range(NCORE):
        b, seg = core // 4, core % 4
        xT = np.ascontiguousarray(inp["x"][b, seg * TOK:(seg + 1) * TOK, :].T).reshape(KC, 128, TOK)
        vecs, vec_off = build_vecs(inp, b, seg)
        m = {"xT": xT, "vecs": vecs, "consts": consts}
        m.update(W)
        maps.append(m)
    nc = build_program(vec_off, maps[0]["vecs"].shape[1], steps, {k: v.shape for k, v in W.items()})
    res = run_bass_kernel_spmd(nc, maps, core_ids=list(range(NCORE)), trace=trace)
    out = np.empty((2, SEQ, D), np.float32)
    for core in range(NCORE):
        b, seg = core // 4, core % 4
        o = res.results[core]["out"].reshape(D, TOK)
        out[b, seg * TOK:(seg + 1) * TOK, :] = o.T
    return out, res


def kernel(**inputs):
    out, _ = run(inputs, "full")
    return out
```

```python
import contextlib
import numpy as np
import concourse.bass as bass
import concourse.mybir as mybir
from concourse.bass_utils import run_bass_kernel_spmd

F32 = mybir.dt.float32
BF16 = mybir.dt.bfloat16
AF = mybir.ActivationFunctionType
ALU = mybir.AluOpType

D = 2048
KC = 16
SEQ = 8192
NCORE = 8
TOK = 2048
TH = 1024
DFF = 5632
JC = 44
DEPTH = 4
CW = 31
HALO = 32
EPS = 1e-6
DK = 1024
DV = 2048
NH = 4
GIN = 6160
NWB = 4


class Sched:
    ENG = ["pe", "act", "dve", "pool", "sp"]

    def __init__(self, nc):
        self.nc = nc
        self.ops = {e: [] for e in self.ENG}
        self.cnt = {e: 0 for e in self.ENG}
        self.seen = {e: {} for e in self.ENG}
        self.last_w = {}
        self.readers = {}
        self.dma_cnt = {}
        self.dma_keys = []

    def _add(self, eng, fn, reads, writes, tok_kind, dma_key=None):
        deps = []
        for r in reads:
            t = self.last_w.get(r)
            if t is not None:
                deps.append(t)
        for w in writes:
            t = self.last_w.get(w)
            if t is not None:
                deps.append(t)
            deps.extend(self.readers.get(w, ()))
        if tok_kind == "eng":
            self.cnt[eng] += 1
            tok = ("eng", eng, self.cnt[eng])
        else:
            if dma_key not in self.dma_cnt:
                self.dma_cnt[dma_key] = 0
                self.dma_keys.append(dma_key)
            self.dma_cnt[dma_key] += 16
            tok = ("dma", dma_key, self.dma_cnt[dma_key])
        need = {}
        for t in deps:
            if t[0] == "eng" and t[1] == eng and tok_kind == "eng" and eng == "pe":
                continue
            k = (t[0], t[1])
            if t[2] > need.get(k, 0):
                need[k] = t[2]
        waits = []
        seen = self.seen[eng]
        for k, v in need.items():
            if seen.get(k, 0) >= v:
                continue
            seen[k] = v
            waits.append((k, v))
        self.ops[eng].append((waits, fn, tok))
        for r in reads:
            self.readers.setdefault(r, []).append(tok)
        for w in writes:
            self.last_w[w] = tok
            self.readers[w] = []
        return tok

    def op(self, eng, fn, reads=(), writes=()):
        return self._add(eng, fn, reads, writes, "eng")

    def dma(self, fn, reads=(), writes=(), key=None, eng="sp"):
        return self._add(eng, fn, reads, writes, "dma", dma_key=key)

    def emit(self, final_wait_tokens=()):
        nc = self.nc
        with contextlib.ExitStack() as es:
            sems = {}
            for e in self.ENG:
                sems[("eng", e)] = es.enter_context(nc.semaphore("s_" + e))
            for i, k in enumerate(self.dma_keys):
                sems[("dma", k)] = es.enter_context(nc.semaphore("d%d" % i))
            block = es.enter_context(nc.Block())
            eng_map = {"pe": block.tensor, "act": block.scalar, "dve": block.vector,
                       "pool": block.gpsimd, "sp": block.sync}
            for e in self.ENG:
                ops = self.ops[e]
                extra = final_wait_tokens if e == "sp" else ()

                def body(eh, ops=ops, e=e, extra=extra):
                    for waits, fn, tok in ops:
                        for k, v in waits:
                            eh.wait_ge(sems[k], v)
                        ins = fn(eh)
                        if tok[0] == "eng":
                            ins.then_inc(sems[("eng", e)], 1)
                        else:
                            ins.then_inc(sems[("dma", tok[1])], 16)
                    for t in extra:
                        eh.wait_ge(sems[(t[0], t[1])], t[2])
                eng_map[e](body)


class Region:
    def __init__(self, nc, es, name, nbytes):
        self.name = name
        self.nbytes = nbytes
        self.t = es.enter_context(nc.sbuf_tensor(name, [128, nbytes // 4], F32))

    def view(self, off, dtype, shape):
        esz = 4 if dtype == F32 else 2
        n = 1
        for x in shape:
            n *= x
        assert off % 4 == 0 and (n * esz) % 4 == 0 and off + n * esz <= self.nbytes, (self.name, off, n, esz)
        a = self.t[:, off // 4:(off + n * esz) // 4]
        if dtype != F32:
            a = a.bitcast(dtype)
        if len(shape) == 2:
            a = a.rearrange("p (a b) -> p a b", a=shape[0])
        keys = [(self.name, pg) for pg in range(off // 1024, (off + n * esz + 1023) // 1024)]
        return a, keys


class Builder:
    def __init__(self, nc, es, dram):
        self.nc = nc
        self.es = es
        self.dr = dram
        self.s = Sched(nc)
        s = self.s
        self.R1 = Region(nc, es, "R1", 64 * 1024)
        self.R2 = Region(nc, es, "R2", 88 * 1024)
        self.R3 = Region(nc, es, "R3", 24 * 1024)
        self.wb = [es.enter_context(nc.sbuf_tensor("wb%d" % i, [128, 2048], BF16)) for i in range(NWB)]
        self.wi = 0
        self.P = [es.enter_context(nc.psum_tensor("P%d" % i, [128, 1024], F32)) for i in range(4)]
        self.ones = es.enter_context(nc.sbuf_tensor("ones", [128, 128], BF16))
        self.one32 = es.enter_context(nc.sbuf_tensor("one32", [128, 1], F32))
        self.vecs = es.enter_context(nc.sbuf_tensor("vecs_sb", [128, self.dr["vecs"].shape[1]], F32))
        self.cact = es.enter_context(nc.sbuf_tensor("cact", [128, 16], BF16))
        self.modrow = self.R3.t[0:1, 4096:6144]
        self.modc = es.enter_context(nc.sbuf_tensor("modc", [128, DEPTH * 96], F32))
        self.lay = es.enter_context(nc.sbuf_tensor("lay", [128, DEPTH * 96], F32))
        s.op("dve", lambda e: e.memset(self.ones[:], 1.0), writes=["ones"])
        s.op("dve", lambda e: e.memset(self.one32[:], 1.0), writes=["one32"])
        s.dma(lambda q: q.dma_start(out=self.vecs[:], in_=self.dr["vecs"]), writes=["vecs"], key="vecs")
        self.pending = []
        self.ident = es.enter_context(nc.sbuf_tensor("ident", [128, 128], BF16))
        self.tri4 = es.enter_context(nc.sbuf_tensor("tri4", [128, 512], F32))
        s.dma(lambda q: q.dma_start(out=self.ident[:], in_=self.dr["consts"][:, 0:128]), writes=["ident"],
              key="ident", eng="pool")
        s.dma(lambda q: q.dma_start(out=self.tri4[:], in_=self.dr["consts"][:, 128:640]), writes=["tri4"], key="tri4")
        self.tribf = es.enter_context(nc.sbuf_tensor("tribf", [128, 128], BF16))
        s.dma(lambda q: q.dma_start(out=self.tribf[:], in_=self.dr["consts"][:, 128:256]), writes=["tribf"],
              key="tribf", eng="pool")

    def pk(self, i, tt=None):
        if tt is None:
            return [("P", i, 0), ("P", i, 1)]
        return [("P", i, tt)]

    def load_w(self, src_ap, ncols):
        i = self.wi % NWB
        self.wi += 1
        t = self.wb[i]
        key = ("wb", i)
        self.s.dma(lambda q: q.dma_start(out=t[:, 0:ncols], in_=src_ap), writes=[key], key=key, eng="pool")
        return t, key

    def flush_pending(self):
        p = self.pending
        self.pending = []
        for f in p:
            f()

    def stats_mm(self, Pi, src_ap, src_keys, first, last):
        def f():
            def emit(e):
                ins = None
                for tt in range(2):
                    ins = e.matmul(self.P[Pi][:, tt * 512:(tt + 1) * 512], lhsT=self.ones[:],
                                   rhs=src_ap[:, tt * 512:(tt + 1) * 512], start=first, stop=last)
                return ins
            self.s.op("pe", emit, reads=list(src_keys) + ["ones"], writes=self.pk(Pi))
        self.pending.append(f)

    def vcol(self, name, c):
        o = self.dr["vec_off"][name]
        return self.vecs[:, o + c:o + c + 1]

    def prologue_mod(self, layers):
        s = self.s
        dr = self.dr
        co = dr["vec_off"]["c"]
        s.op("act", lambda e: e.activation(out=self.cact[:], in_=self.vecs[:, co:co + 16], func=AF.Silu),
             reads=["vecs"], writes=["cact"])
        self.pro_units = []
        for i in layers:
            units = self.mod_units(i, split=True)
            for _ in range(2 * KC + 1):
                units.pop(0)()
            self.pro_units = units

    def mod_units(self, i, split=False):
        s = self.s
        dr = self.dr
        units = []
        for nb in range(6):
            for kc in range(KC):
                def unit(nb=nb, kc=kc):
                    t, key = self.load_w(dr["w_ada%d" % i][kc * 128:(kc + 1) * 128, nb * 2048:(nb + 1) * 2048], 2048)

                    def emit(e):
                        ins = None
                        for si in range(16):
                            ins = e.matmul(self.P[2][:, si:si + 1], lhsT=t[:, si * 128:(si + 1) * 128],
                                           rhs=self.cact[:, kc:kc + 1], start=(kc == 0 and si == 0),
                                           stop=(kc == KC - 1), skip_group_check=True)
                        return ins
                    s.op("pe", emit, reads=[key, "cact"], writes=self.pk(2, 0))
                    if kc == KC - 1:
                        bo = dr["vec_off"]["b_ada%d" % i] + nb * 16
                        s.op("dve", lambda e: e.tensor_tensor(
                            out=self.modc[:, i * 96 + nb * 16:i * 96 + nb * 16 + 16], in0=self.P[2][:, 0:16],
                            in1=self.vecs[:, bo:bo + 16], op=ALU.add),
                            reads=self.pk(2, 0) + ["vecs"], writes=[("modc", i, nb)])
                units.append(unit)
            if split and nb == 1:
                units.append(lambda: self.mod_derive(i, which=(0, 1)))
        units.append(lambda: self.mod_derive(i, which=((2, 3, 4, 5) if split else (0, 1, 2, 3, 4, 5))))
        return units

    def mod_derive(self, i, which=(0, 1, 2, 3, 4, 5)):
        s = self.s
        dr = self.dr
        m = lambda nb, i=i: self.modc[:, i * 96 + nb * 16:i * 96 + nb * 16 + 16]
        L = lambda k, i=i: self.lay[:, i * 96 + k * 16:i * 96 + k * 16 + 16]
        vo = dr["vec_off"]
        g = lambda nm, i=i, vo=vo: self.vecs[:, vo[nm % i]:vo[nm % i] + 16]
        rd = [("modc", i, nb) for nb in range(6)] + ["vecs"]
        wr = [("lay", i)]
        if 0 in which:
            s.op("dve", lambda e: e.scalar_tensor_tensor(
                out=L(0), in0=m(1), scalar=1.0, in1=g("pre_mix_g%d"), op0=ALU.add, op1=ALU.mult), reads=rd, writes=wr)
        if 1 in which:
            s.op("dve", lambda e: e.tensor_copy(out=L(1), in_=m(0)), reads=rd, writes=wr)
        if 2 in which:
            s.op("dve", lambda e: e.tensor_tensor(out=L(2), in0=m(2), in1=g("post_mix_g%d"), op=ALU.mult), reads=rd, writes=wr)
        if 3 in which:
            s.op("dve", lambda e: e.scalar_tensor_tensor(
                out=L(3), in0=m(4), scalar=1.0, in1=g("pre_ffn_g%d"), op0=ALU.add, op1=ALU.mult), reads=rd, writes=wr)
        if 4 in which:
            s.op("dve", lambda e: e.tensor_copy(out=L(4), in_=m(3)), reads=rd, writes=wr)
        if 5 in which:
            s.op("dve", lambda e: e.tensor_tensor(out=L(5), in0=m(5), in1=g("post_ffn_g%d"), op=ALU.mult), reads=rd, writes=wr)

    def lcol(self, i, k, c):
        o = i * 96 + k * 16 + c
        return self.lay[:, o:o + 1]

    def hT(self, c, tt=None):
        if tt is None:
            return self.R1.view(c * 2048, BF16, (1024,))
        return self.R1.view(c * 2048 + tt * 1024, BF16, (512,))

    def prenorm(self, xin, xin_name, hf, layer, ka, kb):
        s = self.s
        xs = []
        for c in range(KC):
            a, keys = self.R2.view(c * 4096, F32, (1024,))
            xs.append((a, keys))
            s.dma(lambda q, a=a, c=c: q.dma_start(out=a, in_=xin[c, :, hf * TH:(hf + 1) * TH]),
                  reads=[(xin_name, c, hf)], writes=keys, key=("xst", c))
        for c in range(KC):
            a, keys = xs[c]
            sq, sqk = self.R3.view((c % 2) * 2048, BF16, (1024,))
            if c % 3 != 2:
                s.op("act", lambda e, a=a, sq=sq: e.activation(out=sq, in_=a, func=AF.Square), reads=keys, writes=sqk)
            else:
                s.op("dve", lambda e, a=a, sq=sq: e.tensor_tensor(out=sq, in0=a, in1=a, op=ALU.mult), reads=keys, writes=sqk)
            self.flush_pending()
            self.stats_mm(3, sq, sqk, c == 0, c == KC - 1)
        self.flush_pending()
        rs, rsk = self.R3.view(4096, F32, (1024,))
        s.op("act", lambda e: e.activation(out=rs, in_=self.P[3][:], func=AF.Ln, scale=1.0 / D, bias=self.epsc[:]),
             reads=self.pk(3) + ["epsc"], writes=rsk)
        s.op("act", lambda e: e.activation(out=rs, in_=rs, func=AF.Exp, scale=-0.5), reads=rsk, writes=rsk)
        for c in range(KC):
            a, keys = xs[c]
            tm, tmk = self.R3.view(8192 + (c % 2) * 4096, F32, (1024,))
            s.op("dve", lambda e, a=a, tm=tm: e.tensor_tensor(out=tm, in0=a, in1=rs, op=ALU.mult),
                 reads=keys + rsk, writes=tmk)
            h, hk = self.hT(c)
            s.op("act", lambda e, tm=tm, h=h, c=c: e.activation(
                out=h, in_=tm, func=AF.Identity, scale=self.lcol(layer, ka, c), bias=self.lcol(layer, kb, c)),
                reads=tmk + [("lay", layer)], writes=hk)

    def postnorm(self, yview, xin, xin_name, xout, xout_name, hf, layer, kg, stg_region, stg_off):
        s = self.s
        rs, rsk = self.R3.view(4096, F32, (1024,))
        s.op("act", lambda e: e.activation(out=rs, in_=self.P[3][:], func=AF.Ln, scale=1.0 / D, bias=self.epsc[:]),
             reads=self.pk(3) + ["epsc"], writes=rsk)
        s.op("act", lambda e: e.activation(out=rs, in_=rs, func=AF.Exp, scale=-0.5), reads=rsk, writes=rsk)
        toks = []
        for c in range(KC):
            xa, xk = stg_region.view(stg_off + (c % 8) * 4096, F32, (1024,))
            s.dma(lambda q, xa=xa, c=c: q.dma_start(out=xa, in_=xin[c, :, hf * TH:(hf + 1) * TH]),
                  reads=[(xin_name, c, hf)], writes=xk, key=("xst2", c % 8))
            y, yk = yview(c)
            s.op("dve", lambda e, y=y, c=c: e.scalar_tensor_tensor(
                out=y, in0=y, scalar=self.lcol(layer, kg, c), in1=rs, op0=ALU.mult, op1=ALU.mult),
                reads=yk + rsk + [("lay", layer)], writes=yk)
            s.op("dve", lambda e, y=y, xa=xa: e.tensor_tensor(out=xa, in0=y, in1=xa, op=ALU.add),
                 reads=yk + xk, writes=xk)
            t = s.dma(lambda q, xa=xa, c=c: q.dma_start(out=xout[c, :, hf * TH:(hf + 1) * TH], in_=xa),
                      reads=xk, writes=[(xout_name, c, hf)], key=("xo", c % 8), eng="act")
            toks.append(t)
        return toks

    def proj(self, wt, n_oc, n_kg, kcb, in_view, psel, epilogue, oc_list=None, tiles=((0, 0), (512, 512)), m=128, after_block=None):
        s = self.s
        for oi, oc in enumerate(oc_list if oc_list is not None else range(n_oc)):
            Pi = psel(oi)
            pkeys = []
            for (_, po) in tiles:
                pkeys += self.pk(Pi, po // 512)
            for kg in range(n_kg):
                t, key = self.load_w(wt[oc, kg], kcb * m)
                ins_ = [in_view(kg * kcb + k) for k in range(kcb)]
                rk = [key]
                for a, k_ in ins_:
                    rk += k_

                def emit(e, t=t, kg=kg, ins_=ins_, Pi=Pi):
                    ins = None
                    for k in range(kcb):
                        for (io, po) in tiles:
                            ins = e.matmul(self.P[Pi][0:m, po:po + 512], lhsT=t[:, k * m:(k + 1) * m],
                                           rhs=ins_[k][0][:, io:io + 512],
                                           start=(kg == 0 and k == 0), stop=(kg == n_kg - 1 and k == kcb - 1))
                    return ins
                if oi == 0 and kg == 0 and kcb > 1:
                    for k in range(kcb):
                        def emit1(e, t=t, k=k, ins_=ins_, Pi=Pi):
                            ins = None
                            for (io, po) in tiles:
                                ins = e.matmul(self.P[Pi][0:m, po:po + 512], lhsT=t[:, k * m:(k + 1) * m],
                                               rhs=ins_[k][0][:, io:io + 512],
                                               start=(k == 0), stop=(n_kg == 1 and k == kcb - 1))
                            return ins
                        s.op("pe", emit1, reads=[key] + ins_[k][1], writes=pkeys)
                else:
                    s.op("pe", emit, reads=rk, writes=pkeys)
                if after_block is not None:
                    after_block()
            epilogue(oi, oc, Pi)

    def ffn(self, layer, xin, xin_name, xout, xout_name, next_layer=None):
        s = self.s
        dr = self.dr
        toks = []
        units = self.mod_units(next_layer) if next_layer is not None else []

        def inject():
            if units:
                units.pop(0)()
        for hf in range(2):
            self.prenorm(xin, xin_name, hf, layer, 3, 4)
            hid = lambda j: self.R2.view(j * 2048, BF16, (1024,))
            w1 = dr["ffn_w_in_%d" % layer]
            for j in range(JC):
                Pg, Pu = (0, 1) if j % 2 == 0 else (2, 3)
                self.proj(w1, None, 1, KC, self.hT, lambda oi, Pg=Pg, Pu=Pu: (Pg, Pu)[oi], lambda *a: None,
                          oc_list=[j, JC + j])
                sg, sgk = self.R3.view(8192 + (j % 2) * 4096, F32, (1024,))
                s.op("act", lambda e, sg=sg, Pg=Pg: e.activation(out=sg, in_=self.P[Pg][:], func=AF.Silu),
                     reads=self.pk(Pg), writes=sgk)
                h, hk = hid(j)
                s.op("dve", lambda e, sg=sg, h=h, Pu=Pu: e.tensor_tensor(out=h, in0=sg, in1=self.P[Pu][:], op=ALU.mult),
                     reads=sgk + self.pk(Pu), writes=hk)
            yv = lambda c: self.R1.view(c * 4096, F32, (1024,))

            def epi(oi, oc, Pi):
                y, yk = yv(oc)
                s.op("act", lambda e: e.activation(out=y, in_=self.P[Pi][:], func=AF.Identity),
                     reads=self.pk(Pi), writes=yk)
                sq, sqk = self.R3.view((oc % 2) * 2048, BF16, (1024,))
                s.op("dve", lambda e: e.tensor_tensor(out=sq, in0=self.P[Pi][:], in1=y, op=ALU.mult),
                     reads=self.pk(Pi) + yk, writes=sqk)
                self.flush_pending()
                self.stats_mm(3, sq, sqk, oc == 0, oc == KC - 1)
            self.proj(dr["ffn_w_out_%d" % layer], KC, 4, 11, hid, lambda oi: oi % 2, epi, after_block=inject)
            self.flush_pending()
            toks += self.postnorm(yv, xin, xin_name, xout, xout_name, hf, layer, 5, self.R2, 0)
        while units:
            units.pop(0)()
        return toks


    def allgather(self, src, dst, src_keys, dst_keys):
        self.s.op("pool", lambda e: e.collective_compute(
            "AllGather", ALU.bypass, replica_groups=[[0, 1, 2, 3], [4, 5, 6, 7]],
            ins=[src.opt()], outs=[dst.opt()]), reads=list(src_keys) + ["cc_chain"], writes=list(dst_keys) + ["cc_chain"])

    def conv_a(self, layer, j, xin, xin_name):
        s = self.s
        dr = self.dr
        vo = dr["vec_off"]["b_pw1_%d" % j]
        for hf in range(2):
            self.prenorm(xin, xin_name, hf, layer, 0, 1)
            for c in range(KC):
                inj = bool(self.pro_units)
                Pa, Pg = (0, 1) if (c % 2 == 0 or inj) else (2, 3)

                def inject():
                    if self.pro_units:
                        self.pro_units.pop(0)()
                self.proj(dr["conv_w_pw1_%d" % j], None, 1, KC, self.hT, lambda oi, Pa=Pa, Pg=Pg: (Pa, Pg)[oi],
                          lambda *a: None, oc_list=[c, KC + c], after_block=(inject if inj else None))
                sg, sgk = self.R3.view(8192 + (c % 2) * 4096, F32, (1024,))
                s.op("act", lambda e, sg=sg, Pg=Pg, c=c: e.activation(
                    out=sg, in_=self.P[Pg][:], func=AF.Sigmoid, bias=self.vecs[:, vo + KC + c:vo + KC + c + 1]),
                    reads=self.pk(Pg) + ["vecs"], writes=sgk)
                u, uk = self.R2.view(65536 + (c % 3) * 2048, BF16, (1024,))
                s.op("dve", lambda e, sg=sg, u=u, Pa=Pa, c=c: e.scalar_tensor_tensor(
                    out=u, in0=self.P[Pa][:], scalar=self.vecs[:, vo + c:vo + c + 1], in1=sg, op0=ALU.add, op1=ALU.mult),
                    reads=sgk + self.pk(Pa) + ["vecs"], writes=uk)
                s.dma(lambda q, u=u, c=c, hf=hf: q.dma_start(
                    out=dr["u_s"][c, :, HALO + hf * TH:HALO + (hf + 1) * TH], in_=u),
                    reads=uk, writes=[("u_s", c, hf)], key=("uo", c % 3), eng="act")
                if hf == 1:
                    s.dma(lambda q, u=u, c=c: q.dma_start(out=dr["halo_src"][:, c * HALO:(c + 1) * HALO], in_=u[:, TH - HALO:TH]),
                          reads=uk, writes=[("halo_src", c)], key=("ho", c % 3), eng="act")
        while self.pro_units:
            self.pro_units.pop(0)()
        self.allgather(dr["halo_src"], dr["halo_g"], [("halo_src", c) for c in range(KC)], ["halo_g"])

    def conv_b(self, layer, j, xin, xin_name, xout, xout_name):
        s = self.s
        dr = self.dr
        vo = dr["vec_off"]
        toks = []
        G, Gk = self.R3.view(16384, BF16, (4, KC * HALO))
        s.dma(lambda q: q.dma_start(out=G, in_=dr["halo_g"].rearrange("(r p) k -> p r k", p=128)),
              reads=["halo_g"], writes=Gk, key="halo_ld")
        hal, halk = self.R3.view(16384 + 4096, BF16, (KC * HALO,))
        mo = vo["mprev"]
        s.op("dve", lambda e: e.tensor_scalar(out=hal, in0=G[:, 0, :], scalar1=self.vecs[:, mo:mo + 1], scalar2=None,
                                               op0=ALU.mult), reads=Gk + ["vecs"], writes=halk)
        for r in range(1, 4):
            s.op("dve", lambda e, r=r: e.scalar_tensor_tensor(
                out=hal, in0=G[:, r, :], scalar=self.vecs[:, mo + r:mo + r + 1], in1=hal, op0=ALU.mult, op1=ALU.add),
                reads=Gk + halk + ["vecs"], writes=halk)
        UW = HALO + TH
        for hf in range(2):
            Us = []
            for c in range(KC):
                U, Uk = self.R2.view(c * UW * 2, BF16, (UW,))
                Us.append((U, Uk))
                s.dma(lambda q, U=U, c=c, hf=hf: q.dma_start(out=U, in_=dr["u_s"][c, :, hf * TH:hf * TH + UW]),
                      reads=[("u_s", c, 0), ("u_s", c, 1)], writes=Uk, key=("Uld", c))
                if hf == 0:
                    s.op("dve", lambda e, U=U, c=c: e.tensor_copy(out=U[:, 0:HALO], in_=hal[:, c * HALO:(c + 1) * HALO]),
                         reads=halk + Uk, writes=Uk)
            vv = lambda c: self.R1.view(c * 4096, F32, (1024,))
            wo = vo["w_dw_%d" % j]
            bo = vo["b_dw_%d" % j]
            def build_taps(c):
                dg, dgk = self.R2.view(33792 + (c % 2) * 8192, BF16, (CW, 128))
                for k in range(CW):
                    rd = ["ident", "vecs"] + ([("dggate", c % 2)] if k > 0 else [])
                    wr = [("dgtap", c % 2, k)] + (dgk + [("dggate", c % 2)] if k == 0 else [])
                    if k % 2 == 0:
                        s.op("dve", lambda e, dg=dg, k=k, c=c: e.tensor_scalar(
                            out=dg[:, k, :], in0=self.ident[:], scalar1=self.vecs[:, wo + k * KC + c:wo + k * KC + c + 1],
                            scalar2=None, op0=ALU.mult), reads=rd, writes=wr)
                    else:
                        s.op("act", lambda e, dg=dg, k=k, c=c: e.activation(
                            out=dg[:, k, :], in_=self.ident[:], func=AF.Identity,
                            scale=self.vecs[:, wo + k * KC + c:wo + k * KC + c + 1]), reads=rd, writes=wr)
            build_taps(0)
            for c in range(KC):
                dg, dgk = self.R2.view(33792 + (c % 2) * 8192, BF16, (CW, 128))
                Pi = c % 2
                U, Uk = Us[c]

                def emit(e, dg=dg, U=U, Pi=Pi):
                    ins = None
                    for tt in range(2):
                        for k in range(CW):
                            ins = e.matmul(self.P[Pi][:, tt * 512:(tt + 1) * 512], lhsT=dg[:, k, :],
                                           rhs=U[:, tt * 512 + k + 2:tt * 512 + k + 2 + 512],
                                           start=(k == 0), stop=(k == CW - 1))
                    return ins
                s.op("pe", emit, reads=[("dgtap", c % 2, k) for k in range(CW)] + dgk + [("dggate", c % 2)] + Uk,
                     writes=self.pk(Pi))
                if c + 1 < KC:
                    build_taps(c + 1)
                v, vk = vv(c)
                s.op("act", lambda e, v=v, Pi=Pi, c=c: e.activation(
                    out=v, in_=self.P[Pi][:], func=AF.Identity, bias=self.vecs[:, bo + c:bo + c + 1]),
                    reads=self.pk(Pi) + ["vecs"], writes=vk)
                sq, sqk = self.R3.view((c % 2) * 2048, BF16, (1024,))
                vb, vbk = self.R3.view(4096 + (c % 2) * 2048, BF16, (1024,))
                s.op("dve", lambda e, v=v, vb=vb: e.tensor_copy(out=vb, in_=v), reads=vk, writes=vbk)
                s.op("dve", lambda e, v=v, sq=sq: e.tensor_tensor(out=sq, in0=v, in1=v, op=ALU.mult), reads=vk, writes=sqk)
                self.flush_pending()
                self.stats_mm(2, vb, vbk, c == 0, c == KC - 1)
                self.stats_mm(3, sq, sqk, c == 0, c == KC - 1)
            self.flush_pending()
            mean, mk = self.R3.view(8192, F32, (1024,))
            rstd, rk = self.R3.view(12288, F32, (1024,))
            s.op("act", lambda e: e.activation(out=mean, in_=self.P[2][:], func=AF.Identity, scale=1.0 / D),
                 reads=self.pk(2), writes=mk)
            s.op("dve", lambda e: e.tensor_tensor(out=rstd, in0=mean, in1=mean, op=ALU.mult), reads=mk, writes=rk)
            s.op("dve", lambda e: e.scalar_tensor_tensor(out=rstd, in0=self.P[3][:], scalar=1.0 / D, in1=rstd,
                                                          op0=ALU.mult, op1=ALU.subtract), reads=self.pk(3) + rk, writes=rk)
            s.op("act", lambda e: e.activation(out=rstd, in_=rstd, func=AF.Ln, bias=self.epsc[:]),
                 reads=rk + ["epsc"], writes=rk)
            s.op("act", lambda e: e.activation(out=rstd, in_=rstd, func=AF.Exp, scale=-0.5), reads=rk, writes=rk)
            sv = lambda c: self.R2.view(50176 + c * 2048, BF16, (1024,))
            go, lbo = vo["ln_g_%d" % j], vo["ln_b_%d" % j]
            for c in range(KC):
                v, vk = vv(c)
                t1, t1k = self.R3.view(16384 + (c % 2) * 4096, F32, (1024,))
                s.op("dve", lambda e, v=v, t1=t1: e.tensor_tensor(out=t1, in0=v, in1=mean, op=ALU.subtract),
                     reads=vk + mk, writes=t1k)
                s.op("dve", lambda e, t1=t1: e.tensor_tensor(out=t1, in0=t1, in1=rstd, op=ALU.mult),
                     reads=t1k + rk, writes=t1k)
                sc, sck = sv(c)
                s.op("act", lambda e, t1=t1, sc=sc, c=c: e.activation(
                    out=sc, in_=t1, func=AF.Silu, scale=self.vecs[:, go + c:go + c + 1], bias=self.vecs[:, lbo + c:lbo + c + 1]),
                    reads=t1k + ["vecs"], writes=sck)
            yv = vv
            b2 = vo["b_pw2_%d" % j]

            def epi(oi, oc, Pi):
                y, yk = yv(oc)
                s.op("act", lambda e: e.activation(out=y, in_=self.P[Pi][:], func=AF.Identity,
                                                   bias=self.vecs[:, b2 + oc:b2 + oc + 1]),
                     reads=self.pk(Pi) + ["vecs"], writes=yk)
                sq, sqk = self.R3.view((oc % 2) * 2048, BF16, (1024,))
                s.op("dve", lambda e: e.tensor_tensor(out=sq, in0=y, in1=y, op=ALU.mult), reads=yk, writes=sqk)
                self.flush_pending()
                self.stats_mm(3, sq, sqk, oc == 0, oc == KC - 1)
            self.proj(dr["conv_w_pw2_%d" % j], KC, 1, KC, sv, lambda oi: oi % 2, epi)
            self.flush_pending()
            toks += self.postnorm(yv, xin, xin_name, xout, xout_name, hf, layer, 2, self.R2, 0)
        return toks

    def rows(self, region, off, dtype, n, nrows):
        esz = 4 if dtype == F32 else 2
        a = region.t[0:nrows, off // 4:(off + n * esz) // 4]
        if dtype != F32:
            a = a.bitcast(dtype)
        keys = [(region.name, pg) for pg in range(off // 1024, (off + n * esz + 1023) // 1024)]
        return a, keys

    def gla(self, layer, j, xin, xin_name, xout, xout_name, state_only):
        s = self.s
        dr = self.dr
        vo = dr["vec_off"]
        R1, R2, R3 = self.R1, self.R2, self.R3
        toks = []
        B = not state_only
        wup, wupk = self.rows(R3, 16384, BF16, 1024, 17)
        aT, aTk = self.rows(R3, 18944, BF16, 512, 17)
        small, smk = R3.view(18432, F32, (8 * 8,))
        bl, ebl, Bsum, Dj, Dp = [small[:, i * 8:(i + 1) * 8] for i in range(5)]
        blk_, eblk_, Bsk, Djk, Dpk = [[("gsm", i)] for i in range(5)]
        s.dma(lambda q: q.dma_start(out=wup, in_=dr["gla_wup_%d" % j]), writes=wupk, key="wup", eng="pool")
        s.op("dve", lambda e: e.memset(aT, 1.0), writes=aTk)
        S32v = lambda dc: R2.view(65536 + dc * 2048, F32, (512,))
        Sbfv = lambda dc: R2.view(81920 + dc * 1024, BF16, (512,))
        Sall, Sallk = R2.view(65536, F32, (4096,))
        Sball, Sballk = R2.view(81920, BF16, (4096,))
        def init_state():
            s.op("dve", lambda e: e.memset(Sall, 0.0), writes=Sallk)
            if state_only:
                s.op("dve", lambda e: e.memset(Bsum, 0.0), writes=Bsk)
            else:
                mo = vo["mlt"]
                for jr in range(3):
                    s.dma(lambda q, jr=jr: q.dma_start(out=Dj, in_=dr["gd_g"][jr * 128:(jr + 1) * 128, 0:8]),
                          reads=["gd_g"], writes=Djk, key="Dj")
                    s.op("dve", lambda e, jr=jr: e.tensor_scalar(out=Dp, in0=Dj, scalar1=-1.0, scalar2=self.vecs[:, mo + jr:mo + jr + 1],
                                                                  op0=ALU.add, op1=ALU.mult), reads=Djk + ["vecs"], writes=Dpk)
                    s.op("dve", lambda e: e.tensor_scalar(out=Dp, in0=Dp, scalar1=1.0, scalar2=None, op0=ALU.add),
                         reads=Dpk, writes=Dpk)
                    for dc in range(8):
                        stg, stgk = R3.view(20480 + (dc % 2) * 2048, F32, (512,))
                        s.dma(lambda q, jr=jr, dc=dc, stg=stg: q.dma_start(
                            out=stg, in_=dr["gsA_g" if dc < 4 else "gsB_g"][jr * 128:(jr + 1) * 128, (dc % 4) * 512:(dc % 4 + 1) * 512]),
                            reads=["gsA_g" if dc < 4 else "gsB_g"], writes=stgk, key=("stg", dc % 2))
                        s.op("dve", lambda e, stg=stg, jr=jr: e.tensor_scalar(
                            out=stg, in0=stg, scalar1=self.vecs[:, mo + jr:mo + jr + 1], scalar2=None, op0=ALU.mult),
                            reads=stgk + ["vecs"], writes=stgk)
                        S, Sk = S32v(dc)
                        s.op("dve", lambda e, S=S, stg=stg, dc=dc: e.scalar_tensor_tensor(
                            out=S, in0=S, scalar=Dp[:, dc:dc + 1], in1=stg, op0=ALU.mult, op1=ALU.add),
                            reads=Sk + stgk + Dpk, writes=Sk)
                s.op("act", lambda e: e.activation(out=Sball, in_=Sall, func=AF.Identity), reads=Sallk, writes=Sballk)


        init_done = [False]

        qTv = lambda c: R2.view(c * 1024, BF16, (512,))
        kTv = lambda c: R2.view(8192 + c * 1024, BF16, (512,))
        vTv = lambda c: R2.view(16384 + c * 1024, BF16, (512,))
        qT3, _ = R2.view(0, BF16, (8, 512))
        kT3, _ = R2.view(8192, BF16, (8, 512))
        qTk = R2.view(0, BF16, (4096,))[1]
        kTk = R2.view(8192, BF16, (4096,))[1]
        vTk = R2.view(16384, BF16, (8192,))[1]
        gpos, gpk = R2.view(32768, F32, (1024,))
        e1, e1k = R2.view(36864, F32, (1024,))
        tE = [R2.view(40960, F32, (8, 128)), R2.view(45056, F32, (8, 128))]
        tEf = [R2.view(40960, F32, (1024,)), R2.view(45056, F32, (1024,))]
        qd, qdk = R2.view(49152, BF16, (8, 128))
        kd, kdk = R2.view(51200, BF16, (8, 128))
        kh, khk = R2.view(53248, BF16, (8, 128))
        vtok, vtk = R2.view(55296, BF16, (2048,))
        ktok, ktk = R2.view(59392, BF16, (1024,))
        sc, sck = R2.view(61440, BF16, (512,))
        osq, osqk = R3.view(0, BF16, (2048,))
        rh, rhk = R3.view(4096, F32, (512,))
        onT = lambda c: R1.view(32768 + c * 2048, BF16, (1024,))
        P = self.P
        P1v = P[1][:].rearrange("p (a b) -> p a b", a=8)
        P2bf = P[2][:].bitcast(BF16)
        P3bf = P[3][:].bitcast(BF16)
        ngo = vo["gla_ng_%d" % j]
        cnt = [0]

        def alt():
            cnt[0] += 1
            return "act" if cnt[0] % 2 else "dve"

        def copy_op(eng, out, in_, reads, writes, scale=None):
            if eng == "act":
                if scale is None:
                    s.op("act", lambda e: e.activation(out=out, in_=in_, func=AF.Identity), reads=reads, writes=writes)
                else:
                    s.op("act", lambda e: e.activation(out=out, in_=in_, func=AF.Identity, scale=scale), reads=reads, writes=writes)
            else:
                if scale is None:
                    s.op("dve", lambda e: e.tensor_copy(out=out, in_=in_), reads=reads, writes=writes)
                else:
                    s.op("dve", lambda e: e.tensor_scalar(out=out, in0=in_, scalar1=scale, scalar2=None, op0=ALU.mult),
                         reads=reads, writes=writes)

        for hf in range(2):
            hall, hallk = R1.view(0, BF16, (KC * TH,))
            if state_only:
                self.prenorm(xin, xin_name, hf, layer, 0, 1)
                s.dma(lambda q, hf=hf: q.dma_start(out=dr["h_s"][hf], in_=hall), reads=hallk, writes=[("h_s", hf)],
                      key="hso", eng="act")
            else:
                s.dma(lambda q, hf=hf: q.dma_start(out=hall, in_=dr["h_s"][hf]), reads=[("h_s", hf)], writes=hallk, key="hsi")
            for qt in range(2):
                tl = ((qt * 512, 0),)
                def epi_qkv(oi, oc, Pi):
                    if oc < 8:
                        o_, k_ = qTv(oc)
                        copy_op(alt(), o_, P[Pi][:, 0:512], self.pk(Pi, 0), k_, scale=0.0625)
                    elif oc < 16:
                        o_, k_ = kTv(oc - 8)
                        copy_op(alt(), o_, P[Pi][:, 0:512], self.pk(Pi, 0), k_)
                    else:
                        o_, k_ = vTv(oc - 16)
                        copy_op(alt(), o_, P[Pi][:, 0:512], self.pk(Pi, 0), k_)
                qi = hf * 2 + qt
                kv3, _ = R2.view(8192, BF16, (24, 512))
                if state_only:
                    def epi_a(oi, oc, Pi):
                        s.op("act", lambda e: e.activation(out=aT[0:16, :], in_=P[Pi][0:16, 0:512], func=AF.Identity),
                             reads=self.pk(Pi, 0), writes=aTk)
                    self.proj(dr["gla_wa_%d" % j], 1, 1, KC, self.hT, lambda oi: 0, epi_a, tiles=tl, m=16)
                    self.proj(dr["gla_w_in_%d" % j], None, 1, KC, self.hT, lambda oi: 1 + oi % 3, epi_qkv,
                              oc_list=list(range(8, 32)), tiles=tl)
                    s.dma(lambda q, qi=qi: q.dma_start(out=dr["kv_s"][qi], in_=kv3), reads=kTk + vTk,
                          writes=[("kv_s", qi)], key="kvo", eng="act")
                    s.dma(lambda q, qi=qi: q.dma_start(out=dr["a_s"][qi], in_=aT[0:16, :]), reads=aTk,
                          writes=[("a_s", qi)], key="ao", eng="act")
                else:
                    s.dma(lambda q, qi=qi: q.dma_start(out=kv3[:, 0:8, :], in_=dr["kv_s"][qi][:, 0:8, :]), reads=[("kv_s", qi)],
                          writes=kTk, key="kvi")
                    self.proj(dr["gla_w_in_%d" % j], None, 1, KC, self.hT, lambda oi: 1 + oi % 3, epi_qkv,
                              oc_list=list(range(0, 8)), tiles=tl)
                if not init_done[0]:
                    init_done[0] = True
                    init_state()
                for ch in range(4):
                    c0 = ch * 128
                    tokoff = qt * 512 + c0

                    def state_mm():
                        for dc in range(8):
                            h = dc // 2
                            s.op("pe", lambda e, dc=dc, h=h: e.matmul(P[3][:, (dc % 2) * 512:(dc % 2 + 1) * 512],
                                                                     lhsT=ktok[:, dc * 128:(dc + 1) * 128],
                                                                     rhs=vtok[:, h * 512:(h + 1) * 512], start=True, stop=True),
                                 reads=ktk + vtk, writes=self.pk(3, dc % 2))
                            S, Sk = S32v(dc)
                            s.op("dve", lambda e, S=S, dc=dc: e.scalar_tensor_tensor(
                                out=S, in0=S, scalar=ebl[:, dc:dc + 1], in1=P[3][:, (dc % 2) * 512:(dc % 2 + 1) * 512],
                                op0=ALU.mult, op1=ALU.add), reads=Sk + eblk_ + self.pk(3, dc % 2), writes=Sk)

                    def state_cast():
                        for dc in range(8):
                            S, Sk = S32v(dc)
                            Sb, Sbk = Sbfv(dc)
                            s.op("act", lambda e, S=S, Sb=Sb: e.activation(out=Sb, in_=S, func=AF.Identity), reads=Sk, writes=Sbk)

                    ci = (hf * 2 + qt) * 4 + ch
                    if state_only:
                        def emit_z(e, c0=c0):
                            ins = None
                            for hh in range(2):
                                ins = e.matmul(P[0][:, hh * 512:(hh + 1) * 512], lhsT=aT[:, c0:c0 + 128],
                                               rhs=wup[:, hh * 512:(hh + 1) * 512], start=True, stop=True)
                            return ins
                        s.op("pe", emit_z, reads=aTk + wupk, writes=self.pk(0))
                        s.op("act", lambda e: e.activation(out=e1, in_=P[0][:], func=AF.Exp, scale=-1.0), reads=self.pk(0), writes=e1k)
                        s.op("act", lambda e: e.activation(out=gpos, in_=e1, func=AF.Ln, bias=self.one32[:]),
                             reads=e1k + ["one32"], writes=gpk)

                        ghi, ghk = R2.view(36864, BF16, (1024,))
                        glo, glk = R2.view(38912, BF16, (1024,))
                        s.op("dve", lambda e: e.tensor_copy(out=ghi, in_=gpos), reads=gpk, writes=ghk)
                        s.op("dve", lambda e: e.tensor_tensor(out=glo, in0=gpos, in1=ghi, op=ALU.subtract), reads=gpk + ghk, writes=glk)

                        def emit_cs(e):
                            ins = None
                            for dc in range(8):
                                e.matmul(P[1][:, dc * 128:(dc + 1) * 128], lhsT=ghi[:, dc * 128:(dc + 1) * 128],
                                         rhs=self.tribf[:], start=True, stop=False)
                                ins = e.matmul(P[1][:, dc * 128:(dc + 1) * 128], lhsT=glo[:, dc * 128:(dc + 1) * 128],
                                               rhs=self.tribf[:], start=False, stop=True)
                            return ins
                        s.op("pe", emit_cs, reads=ghk + glk + ["tribf"], writes=self.pk(1))
                        s.op("dve", lambda e: e.tensor_scalar(out=bl, in0=P1v[:, :, 127], scalar1=-0.0625, scalar2=None, op0=ALU.mult),
                             reads=self.pk(1), writes=blk_)
                        s.op("act", lambda e: e.activation(out=ebl, in_=bl, func=AF.Exp), reads=blk_, writes=eblk_)
                        if state_only:
                            s.op("dve", lambda e: e.tensor_tensor(out=Bsum, in0=Bsum, in1=bl, op=ALU.add), reads=Bsk + blk_, writes=Bsk)
                        (EK, EKk) = tE[0]

                        def emit_ek(e, EK=EK):
                            ins = None
                            for dc in range(8):
                                ins = e.activation(out=EK[:, dc, :], in_=P1v[:, dc, :], func=AF.Exp, scale=0.0625, bias=bl[:, dc:dc + 1])
                            return ins
                        s.op("act", emit_ek, reads=self.pk(1) + blk_, writes=EKk)
                        s.op("dve", lambda e, c0=c0, EK=EK: e.tensor_tensor(out=kh, in0=kT3[:, :, c0:c0 + 128], in1=EK, op=ALU.mult),
                             reads=kTk + EKk, writes=khk)
                        def emit_tk(e):
                            ins = None
                            for dc in range(8):
                                ins = e.transpose(out=P2bf[:, dc * 128:(dc + 1) * 128], in_=kh[:, dc, :], identity=self.ident[:])
                            return ins
                        s.op("pe", emit_tk, reads=khk + ["ident"], writes=self.pk(2, 0))

                        def emit_tv(e, c0=c0):
                            ins = None
                            for c in range(16):
                                ins = e.transpose(out=P3bf[:, c * 128:(c + 1) * 128], in_=vTv(c)[0][:, c0:c0 + 128], identity=self.ident[:])
                            return ins
                        s.op("pe", emit_tv, reads=vTk + ["ident"], writes=self.pk(3))
                        s.op("act", lambda e: e.activation(out=ktok, in_=P2bf[:, 0:1024], func=AF.Identity), reads=self.pk(2, 0), writes=ktk)
                        s.op("dve", lambda e: e.tensor_copy(out=vtok, in_=P3bf[:, 0:2048]), reads=self.pk(3), writes=vtk)
                        csc, csck = tEf[1]
                        s.op("act", lambda e, csc=csc: e.activation(out=csc, in_=P[1][:], func=AF.Identity), reads=self.pk(1), writes=csck)
                        s.dma(lambda q, ci=ci: q.dma_start(out=dr["ck_s"][ci], in_=ktok), reads=ktk, writes=[("ck_s", ci)], key="cko", eng="act")
                        s.dma(lambda q, ci=ci: q.dma_start(out=dr["cv_s"][ci], in_=vtok), reads=vtk, writes=[("cv_s", ci)], key="cvo", eng="act")
                        s.dma(lambda q, ci=ci, csc=csc: q.dma_start(out=dr["cs_s"][ci], in_=csc), reads=csck, writes=[("cs_s", ci)], key="cso", eng="act")
                    else:
                        csb, csbk = R2.view(32768, F32, (1024,))
                        csb3, _ = R2.view(32768, F32, (8, 128))
                        s.dma(lambda q, ci=ci: q.dma_start(out=ktok, in_=dr["ck_s"][ci]), reads=[("ck_s", ci)], writes=ktk, key="ckl")
                        s.dma(lambda q, ci=ci: q.dma_start(out=vtok, in_=dr["cv_s"][ci]), reads=[("cv_s", ci)], writes=vtk, key="cvl")
                        s.dma(lambda q, ci=ci: q.dma_start(out=csb, in_=dr["cs_s"][ci]), reads=[("cs_s", ci)], writes=csbk, key="csl")
                        s.op("dve", lambda e: e.tensor_scalar(out=bl, in0=csb3[:, :, 127], scalar1=-0.0625, scalar2=None, op0=ALU.mult),
                             reads=csbk, writes=blk_)
                        s.op("act", lambda e: e.activation(out=ebl, in_=bl, func=AF.Exp), reads=blk_, writes=eblk_)
                        EBf, EBk = tEf[1]
                        s.op("act", lambda e, EBf=EBf: e.activation(out=EBf, in_=csb, func=AF.Exp, scale=-0.0625),
                             reads=csbk, writes=EBk)
                        s.op("dve", lambda e, c0=c0: e.tensor_tensor(out=qd, in0=qT3[:, :, c0:c0 + 128], in1=tE[1][0], op=ALU.mult),
                             reads=qTk + EBk, writes=qdk)
                        ENf, ENk = tEf[0]
                        s.op("act", lambda e, ENf=ENf: e.activation(out=ENf, in_=csb, func=AF.Exp, scale=0.0625),
                             reads=csbk, writes=ENk)
                        s.op("dve", lambda e, c0=c0: e.tensor_tensor(out=kd, in0=kT3[:, :, c0:c0 + 128], in1=tE[0][0], op=ALU.mult),
                             reads=kTk + ENk, writes=kdk)
                    if B:
                        def emit_sc(e):
                            ins = None
                            for h in range(4):
                                for d2 in range(2):
                                    dc = 2 * h + d2
                                    ins = e.matmul(P[2][:, 512 + h * 128:512 + (h + 1) * 128], lhsT=kd[:, dc, :], rhs=qd[:, dc, :],
                                                   start=(d2 == 0), stop=(d2 == 1))
                            return ins
                        s.op("pe", emit_sc, reads=kdk + qdk, writes=self.pk(2, 1))
                        s.op("dve", lambda e: e.tensor_tensor(out=sc, in0=P[2][:, 512:1024], in1=self.tri4[:], op=ALU.mult),
                             reads=self.pk(2, 1) + ["tri4"], writes=sck)

                        def emit_o(e):
                            ins = None
                            for h in range(4):
                                for es in range(4):
                                    blk = h * 4 + es
                                    out = P[blk // 8][:, (blk % 8) * 128:(blk % 8 + 1) * 128]
                                    e.matmul(out, lhsT=Sbfv(2 * h)[0][:, es * 128:(es + 1) * 128], rhs=qd[:, 2 * h, :], start=True, stop=False)
                                    e.matmul(out, lhsT=Sbfv(2 * h + 1)[0][:, es * 128:(es + 1) * 128], rhs=qd[:, 2 * h + 1, :], start=False, stop=False)
                                    ins = e.matmul(out, lhsT=vtok[:, h * 512 + es * 128:h * 512 + (es + 1) * 128],
                                                   rhs=sc[:, h * 128:(h + 1) * 128], start=False, stop=True)
                            return ins
                        s.op("pe", emit_o, reads=Sballk + qdk + vtk + sck, writes=self.pk(0) + self.pk(1))
                        state_mm()
                        s.op("act", lambda e: e.activation(out=osq[:, 0:1024], in_=P[0][:], func=AF.Square), reads=self.pk(0), writes=osqk)
                        s.op("act", lambda e: e.activation(out=osq[:, 1024:2048], in_=P[1][:], func=AF.Square), reads=self.pk(1), writes=osqk)

                        def emit_hs(e):
                            ins = None
                            for h in range(4):
                                for es in range(4):
                                    ins = e.matmul(P[2][:, h * 128:(h + 1) * 128], lhsT=self.ones[:],
                                                   rhs=osq[:, (h * 4 + es) * 128:(h * 4 + es + 1) * 128], start=(es == 0), stop=(es == 3))
                            return ins
                        s.op("pe", emit_hs, reads=osqk + ["ones"], writes=self.pk(2, 0))
                        s.op("act", lambda e: e.activation(out=rh, in_=P[2][:, 0:512], func=AF.Ln, scale=1.0 / 512, bias=self.epsc[:]),
                             reads=self.pk(2, 0) + ["epsc"], writes=rhk)
                        s.op("act", lambda e: e.activation(out=rh, in_=rh, func=AF.Exp, scale=-0.5), reads=rhk, writes=rhk)
                        onT3, _ = R1.view(32768, BF16, (16, 1024))
                        for h in range(4):
                            Pv = P[h // 2][:, (h % 2) * 512:(h % 2 + 1) * 512].rearrange("p (a b) -> p a b", a=4)
                            rb = rh[:, h * 128:(h + 1) * 128].unsqueeze(1).broadcast_to([128, 4, 128])
                            okeys = []
                            for es in range(4):
                                okeys += onT(4 * h + es)[1]
                            s.op("dve", lambda e, h=h, Pv=Pv, rb=rb, tokoff=tokoff: e.tensor_tensor(
                                out=onT3[:, 4 * h:4 * h + 4, tokoff:tokoff + 128], in0=Pv, in1=rb, op=ALU.mult),
                                reads=self.pk(h // 2, h % 2) + rhk, writes=okeys)
                    if not B:
                        state_mm()
                    else:
                        state_cast()
            if B:
                def epi_r(oi, oc, Pi):
                    c = oc - 32
                    sr, srk = R3.view(8192 + (c % 2) * 4096, F32, (1024,))
                    s.op("act", lambda e: e.activation(out=sr, in_=P[Pi][:], func=AF.Silu), reads=self.pk(Pi), writes=srk)
                    o_, ok_ = onT(c)
                    s.op("dve", lambda e: e.scalar_tensor_tensor(out=o_, in0=o_, scalar=self.vecs[:, ngo + c:ngo + c + 1], in1=sr,
                                                                  op0=ALU.mult, op1=ALU.mult), reads=ok_ + srk + ["vecs"], writes=ok_)
                self.proj(dr["gla_w_in_%d" % j], None, 1, KC, self.hT, lambda oi: oi % 2, epi_r, oc_list=list(range(32, 48)))
                yv = lambda c: R2.view(c * 4096, F32, (1024,))

                def epi_o(oi, oc, Pi):
                    y, yk = yv(oc)
                    s.op("act", lambda e: e.activation(out=y, in_=P[Pi][:], func=AF.Identity), reads=self.pk(Pi), writes=yk)
                    sq, sqk = R3.view((oc % 2) * 2048, BF16, (1024,))
                    s.op("dve", lambda e: e.tensor_tensor(out=sq, in0=P[Pi][:], in1=y, op=ALU.mult), reads=self.pk(Pi) + yk, writes=sqk)
                    self.flush_pending()
                    self.stats_mm(3, sq, sqk, oc == 0, oc == KC - 1)
                self.proj(dr["gla_w_out_%d" % j], KC, 1, KC, onT, lambda oi: oi % 2, epi_o)
                self.flush_pending()
                toks += self.postnorm(yv, xin, xin_name, xout, xout_name, hf, layer, 2, R1, 0)
        if state_only:
            s.op("act", lambda e: e.activation(out=Dj, in_=Bsum, func=AF.Exp), reads=Bsk, writes=Djk)
            s.dma(lambda q: q.dma_start(out=dr["gd_src"][:, 0:8], in_=Dj), reads=Djk, writes=["gd_src"], key="gdo", eng="act")
            s.dma(lambda q: q.dma_start(out=dr["gsA_src"], in_=Sall[:, 0:2048]), reads=Sallk, writes=["gsA_src"], key="gsoA", eng="act")
            s.dma(lambda q: q.dma_start(out=dr["gsB_src"], in_=Sall[:, 2048:4096]), reads=Sallk, writes=["gsB_src"], key="gsoB", eng="act")
            self.allgather(dr["gd_src"], dr["gd_g"], ["gd_src"], ["gd_g"])
            self.allgather(dr["gsA_src"], dr["gsA_g"], ["gsA_src"], ["gsA_g"])
            self.allgather(dr["gsB_src"], dr["gsB_g"], ["gsB_src"], ["gsB_g"])
        return toks


def tile_w(W, kcb, m=128):
    K, N = W.shape
    n_kg = K // (128 * kcb)
    a = W.reshape(n_kg, kcb, 128, N // m, m).transpose(3, 0, 2, 1, 4)
    return np.ascontiguousarray(a).reshape(N // m, n_kg, 128, kcb * m)


def colvec(v):
    return np.ascontiguousarray(v.reshape(-1, 128).T)


def build_vecs(inp, b, seg):
    cols = []
    off = {}

    def add(name, arr):
        off[name] = sum(a.shape[1] for a in cols)
        cols.append(np.asarray(arr, dtype=np.float32))
    add("c", colvec(inp["c"][b]))
    for i in range(DEPTH):
        add("b_ada%d" % i, colvec(inp["b_ada"][i]))
        for nm in ["pre_mix_g", "post_mix_g", "pre_ffn_g", "post_ffn_g"]:
            add(nm + "%d" % i, colvec(inp[nm][i]))
    for j in range(2):
        add("b_pw1_%d" % j, colvec(inp["conv_b_pw1"][j]))
        add("b_dw_%d" % j, colvec(inp["conv_b_dw"][j]))
        add("ln_g_%d" % j, colvec(inp["conv_ln_g"][j]))
        add("ln_b_%d" % j, colvec(inp["conv_ln_b"][j]))
        add("b_pw2_%d" % j, colvec(inp["conv_b_pw2"][j]))
        add("w_dw_%d" % j, colvec(inp["conv_w_dw"][j].reshape(-1)))
        add("gla_ng_%d" % j, colvec(inp["gla_norm_g"][j]))
    mprev = np.zeros((128, 4), np.float32)
    if seg > 0:
        mprev[:, seg - 1] = 1.0
    mlt = np.zeros((128, 4), np.float32)
    mlt[:, :seg] = 1.0
    add("mprev", mprev)
    add("mlt", mlt)
    return np.concatenate(cols, axis=1), off


FULL = [("convA", 0), ("convB", 0), ("ffn", 0), ("glaA", 1), ("glaB", 1), ("ffn", 1),
        ("convA", 2), ("convB", 2), ("ffn", 2), ("glaA", 3), ("glaB", 3), ("ffn", 3)]
MODES = {"full": FULL, "ffn0": [("ffn", 0)], "conv0": [("convA", 0), ("convB", 0)],
         "gla1": [("glaA", 1), ("glaB", 1)], "gla1A": [("glaA", 1)], "f0g1": [("ffn", 0), ("glaA", 1), ("glaB", 1)], "l0": FULL[:3], "l01": FULL[:6]}


def weight_arrays(inp, steps):
    w = {}
    for kind, L in steps:
        j = L // 2
        w["w_ada%d" % L] = lambda L=L: np.ascontiguousarray(inp["w_ada"][L])
        if kind == "ffn":
            w["ffn_w_in_%d" % L] = lambda L=L: tile_w(inp["ffn_w_in"][L], KC)
            w["ffn_w_out_%d" % L] = lambda L=L: tile_w(inp["ffn_w_out"][L], 11)
        elif kind == "convA":
            w["conv_w_pw1_%d" % j] = lambda j=j: tile_w(inp["conv_w_pw1"][j], KC)
        elif kind == "convB":
            w["conv_w_pw2_%d" % j] = lambda j=j: tile_w(inp["conv_w_pw2"][j], KC)
        elif kind in ("glaA", "glaB"):
            w["gla_w_in_%d" % j] = lambda j=j: tile_w(inp["gla_w_in"][j][:, :6144], KC)
            w["gla_wa_%d" % j] = lambda j=j: tile_w(inp["gla_w_in"][j][:, 6144:6160], KC, m=16)
            w["gla_wup_%d" % j] = lambda j=j: np.concatenate(
                [inp["gla_w_gate_up"][j], inp["gla_b_gate"][j][None, :]], axis=0).astype(np.float32)
            if kind == "glaB":
                w["gla_w_out_%d" % j] = lambda j=j: tile_w(inp["gla_w_out"][j], KC)
    return {k: f() for k, f in w.items()}


def host_consts():
    c = np.zeros((128, 640), np.float32)
    c[:, 0:128] = np.eye(128, dtype=np.float32)
    tri = np.triu(np.ones((128, 128), np.float32))
    c[:, 128:640] = np.tile(tri, (1, 4))
    return c


def build_program(vec_off, nvec, steps, wshapes):
    nc = bass.Bass("TRN2", target_bir_lowering=False)
    dr = {"vec_off": vec_off}
    dr["xT"] = nc.dram_tensor("xT", [KC, 128, TOK], F32, kind="ExternalInput").ap()
    dr["vecs"] = nc.dram_tensor("vecs", [128, nvec], F32, kind="ExternalInput").ap()
    dr["consts"] = nc.dram_tensor("consts", [128, 640], F32, kind="ExternalInput").ap()
    for k, shp in wshapes.items():
        dr[k] = nc.dram_tensor(k, list(shp), F32, kind="ExternalInput").ap()
    dr["out"] = nc.dram_tensor("out", [KC, 128, TOK], F32, kind="ExternalOutput").ap()
    dr["xs"] = nc.dram_tensor("xs", [KC, 128, TOK], F32).ap()
    dr["u_s"] = nc.dram_tensor("u_s", [KC, 128, HALO + TOK], BF16).ap()
    dr["halo_src"] = nc.dram_tensor("halo_src", [128, KC * HALO], BF16).ap()
    dr["halo_g"] = nc.dram_tensor("halo_g", [4 * 128, KC * HALO], BF16).ap()
    dr["kv_s"] = nc.dram_tensor("kv_s", [4, 128, 24, 512], BF16).ap()
    dr["a_s"] = nc.dram_tensor("a_s", [4, 16, 512], BF16).ap()
    dr["h_s"] = nc.dram_tensor("h_s", [2, 128, KC * TH], BF16).ap()
    dr["ck_s"] = nc.dram_tensor("ck_s", [16, 128, 1024], BF16).ap()
    dr["cv_s"] = nc.dram_tensor("cv_s", [16, 128, 2048], BF16).ap()
    dr["cs_s"] = nc.dram_tensor("cs_s", [16, 128, 1024], F32).ap()
    dr["gd_src"] = nc.dram_tensor("gd_src", [128, 64], F32).ap()
    dr["gd_g"] = nc.dram_tensor("gd_g", [4 * 128, 64], F32).ap()
    dr["gsA_src"] = nc.dram_tensor("gsA_src", [128, 2048], F32).ap()
    dr["gsA_g"] = nc.dram_tensor("gsA_g", [4 * 128, 2048], F32).ap()
    dr["gsB_src"] = nc.dram_tensor("gsB_src", [128, 2048], F32).ap()
    dr["gsB_g"] = nc.dram_tensor("gsB_g", [4 * 128, 2048], F32).ap()
    with contextlib.ExitStack() as es:
        b = Builder(nc, es, dr)
        b.epsc = es.enter_context(nc.sbuf_tensor("epsc", [128, 1], F32))
        b.s.op("dve", lambda e: e.memset(b.epsc[:], EPS), writes=["epsc"])
        layers = sorted(set(L for _, L in steps))
        b.prologue_mod(layers[:1])
        if steps[0][0] != "convA":
            while b.pro_units:
                b.pro_units.pop(0)()
        resid = [i for i, (k, _) in enumerate(steps) if k in ("convB", "glaB", "ffn")]
        cur, cur_name = dr["xT"], "xT"
        toks = []
        for i, (kind, L) in enumerate(steps):
            j = L // 2
            if resid and i == resid[-1]:
                nxt, nxt_name = dr["out"], "out"
            else:
                nxt, nxt_name = dr["xs"], "xs"
            if kind == "ffn":
                nl = layers[layers.index(L) + 1] if layers.index(L) + 1 < len(layers) else None
                toks = b.ffn(L, cur, cur_name, nxt, nxt_name, next_layer=nl)
            elif kind == "convA":
                b.conv_a(L, j, cur, cur_name)
            elif kind == "convB":
                toks = b.conv_b(L, j, cur, cur_name, nxt, nxt_name)
            elif kind == "glaA":
                b.gla(L, j, cur, cur_name, None, None, True)
            elif kind == "glaB":
                toks = b.gla(L, j, cur, cur_name, nxt, nxt_name, False)
            if i in resid:
                cur, cur_name = nxt, nxt_name
        b.s.emit(final_wait_tokens=toks)
    return nc


def run(inp, mode, trace=False):
    inp = {k: np.asarray(v) for k, v in inp.items()}
    steps = MODES[mode]
    W = weight_arrays(inp, steps)
    consts = host_consts()
    maps = []
    vec_off = None
    for core in range(NCORE):
        b, seg = core // 4, core % 4
        xT = np.ascontiguousarray(inp["x"][b, seg * TOK:(seg + 1) * TOK, :].T).reshape(KC, 128, TOK)
        vecs, vec_off = build_vecs(inp, b, seg)
        m = {"xT": xT, "vecs": vecs, "consts": consts}
        m.update(W)
        maps.append(m)
    nc = build_program(vec_off, maps[0]["vecs"].shape[1], steps, {k: v.shape for k, v in W.items()})
    res = run_bass_kernel_spmd(nc, maps, core_ids=list(range(NCORE)), trace=trace)
    out = np.empty((2, SEQ, D), np.float32)
    for core in range(NCORE):
        b, seg = core // 4, core % 4
        o = res.results[core]["out"].reshape(D, TOK)
        out[b, seg * TOK:(seg + 1) * TOK, :] = o.T
    return out, res


def kernel(**inputs):
    out, _ = run(inputs, "full")
    return out
```
